# Optimizing a Trainium2 kernel written in Bass

```python
import math
import jax, jax.numpy as jnp
from jax import lax
import numpy as np

D_MODEL = 1024
BATCH = 16
SEQ = 2048
DEPTH = 4

D_FF = 2816
NORM_EPS = 1e-6
F_MIN = 1e-6
CHUNK = 64
HG_HEADS = 4
HG_DK = 128
HG_DV = 128
HG_QK = HG_HEADS * HG_DK
HG_WIDTH = HG_HEADS * HG_DV
S5_WIDTH = D_MODEL - HG_WIDTH
S5_GROUP = 16
S5_GROUPS = S5_WIDTH // S5_GROUP
S5_STATE = 64
EV_IN = 2 * HG_QK + 2 * HG_WIDTH + S5_WIDTH
GDN_HEADS = 8
GDN_DK = 128
GDN_DV = 128
GDN_QK = GDN_HEADS * GDN_DK
GDN_V = GDN_HEADS * GDN_DV
CONV_W = 4
OD_IN = 2 * GDN_QK + 2 * GDN_V + 2 * GDN_HEADS
N_EVEN = (DEPTH + 1) // 2
N_ODD = DEPTH // 2

kernel_name = 'hybrid_hgrn2_s5_gdn_macaron'


def rmsnorm(x, w):
    xf = x.astype(jnp.float32)
    y = xf * lax.rsqrt(jnp.mean(xf * xf, axis=-1, keepdims=True) + NORM_EPS)
    return (y * w.astype(jnp.float32)).astype(x.dtype)


def swiglu(h, w_gate, w_up, w_down):
    return (jax.nn.silu(h @ w_gate) * (h @ w_up)) @ w_down


def split_chunks(t, n_heads):
    b, l, _ = t.shape
    t = t.reshape(b, l // CHUNK, CHUNK, n_heads, -1)
    return t.transpose(1, 0, 3, 2, 4)


def merge_chunks(t):
    nc, b, h, c, d = t.shape
    return t.transpose(1, 0, 3, 2, 4).reshape(b, nc * c, h, d)


def masked_exp(mask, z):
    return jnp.where(mask, jnp.exp(jnp.where(mask, z, 0.0)), 0.0)


def gated_head_norm(o, gate, w):
    o = o * lax.rsqrt(jnp.mean(o * o, axis=-1, keepdims=True) + NORM_EPS) * w.astype(jnp.float32)
    o = o * jax.nn.silu(gate.astype(jnp.float32)).reshape(o.shape)
    return o.reshape(o.shape[0], o.shape[1], -1)


def l2norm(t):
    return t * lax.rsqrt(jnp.sum(t * t, axis=-1, keepdims=True) + NORM_EPS)


def hgrn2_mix(q_lin, f_lin, i_val, g_lin, lb, norm_w):
    f32 = jnp.float32
    q = jax.nn.silu(q_lin.astype(f32))
    f = lb + (1.0 - lb) * jax.nn.sigmoid(f_lin.astype(f32))
    log_f = jnp.log(jnp.maximum(f, F_MIN))
    k = 1.0 - f
    v = i_val.astype(f32)
    qc = split_chunks(q, HG_HEADS)
    kc = split_chunks(k, HG_HEADS)
    vc = split_chunks(v, HG_HEADS)
    lfc = split_chunks(log_f, HG_HEADS)
    causal = jnp.tril(jnp.ones((CHUNK, CHUNK), dtype=bool))[:, :, None]
    s0 = jnp.zeros((q.shape[0], HG_HEADS, HG_DK, HG_DV), f32)

    def step(S, inp):
        q_c, k_c, v_c, lf_c = inp
        b = jnp.cumsum(lf_c, axis=2)
        diff = b[:, :, :, None, :] - b[:, :, None, :, :]
        decay = masked_exp(causal, diff)
        attn = jnp.einsum('bhtk,bhtsk,bhsk->bhts', q_c, decay, k_c)
        o = attn @ v_c + jnp.einsum('bhtk,bhkv->bhtv', q_c * jnp.exp(b), S)
        b_last = b[:, :, -1:, :]
        S = jnp.exp(b_last[:, :, 0, :, None]) * S + jnp.einsum(
            'bhsk,bhsv->bhkv', k_c * jnp.exp(b_last - b), v_c)
        return S, o

    _, o = lax.scan(step, s0, (qc, kc, vc, lfc))
    return gated_head_norm(merge_chunks(o), g_lin, norm_w)


def s5_mix(u, a_re, a_im, b_re, b_im, c_re, c_im, d, log_dt, w_glu):
    f32 = jnp.float32
    bsz, l, _ = u.shape
    uf = u.astype(f32).reshape(bsz, l, S5_GROUPS, S5_GROUP)
    a_re = a_re.astype(f32)
    a_im = a_im.astype(f32)
    dt = jnp.exp(log_dt.astype(f32))[:, None]
    mag = jnp.exp(dt * a_re)
    ang = dt * a_im
    abar_re = mag * jnp.cos(ang)
    abar_im = mag * jnp.sin(ang)
    den = a_re * a_re + a_im * a_im
    zr = abar_re - 1.0
    zi = abar_im
    coef_re = ((zr * a_re + zi * a_im) / den)[..., None]
    coef_im = ((zi * a_re - zr * a_im) / den)[..., None]
    b_re = b_re.astype(f32)
    b_im = b_im.astype(f32)
    bb_re = coef_re * b_re - coef_im * b_im
    bb_im = coef_re * b_im + coef_im * b_re
    bu_re = jnp.einsum('blgp,gnp->blgn', uf, bb_re)
    bu_im = jnp.einsum('blgp,gnp->blgn', uf, bb_im)
    ar = jnp.broadcast_to(abar_re, (1, l, S5_GROUPS, S5_STATE))
    ai = jnp.broadcast_to(abar_im, (1, l, S5_GROUPS, S5_STATE))

    def combine(e1, e2):
        ar1, ai1, br1, bi1 = e1
        ar2, ai2, br2, bi2 = e2
        return (ar2 * ar1 - ai2 * ai1,
                ar2 * ai1 + ai2 * ar1,
                ar2 * br1 - ai2 * bi1 + br2,
                ar2 * bi1 + ai2 * br1 + bi2)

    _, _, h_re, h_im = lax.associative_scan(combine, (ar, ai, bu_re, bu_im), axis=1)
    y = (jnp.einsum('gpn,blgn->blgp', c_re.astype(f32), h_re)
         - jnp.einsum('gpn,blgn->blgp', c_im.astype(f32), h_im)
         + d.astype(f32).reshape(S5_GROUPS, S5_GROUP) * uf)
    y = jax.nn.gelu(y.reshape(bsz, l, S5_WIDTH))
    return y * jax.nn.sigmoid(y @ w_glu.astype(f32))


def causal_dwconv(x, w):
    return lax.conv_general_dilated(
        x, w[:, None, :].astype(x.dtype), window_strides=(1,), padding=[(CONV_W - 1, 0)],
        dimension_numbers=('NWC', 'WIO', 'NWC'), feature_group_count=x.shape[-1])


def gdn_mix(proj, conv_w, a_log, dt_bias, norm_w):
    f32 = jnp.float32
    n_qkv = 2 * GDN_QK + GDN_V
    qkv = jax.nn.silu(causal_dwconv(proj[..., :n_qkv], conv_w)).astype(f32)
    gate = proj[..., n_qkv:n_qkv + GDN_V]
    beta = jax.nn.sigmoid(proj[..., n_qkv + GDN_V:n_qkv + GDN_V + GDN_HEADS].astype(f32))
    a_lin = proj[..., n_qkv + GDN_V + GDN_HEADS:].astype(f32)
    log_alpha = -jnp.exp(a_log.astype(f32)) * jax.nn.softplus(a_lin + dt_bias.astype(f32))
    q = split_chunks(qkv[..., :GDN_QK], GDN_HEADS)
    k = split_chunks(qkv[..., GDN_QK:2 * GDN_QK], GDN_HEADS)
    v = split_chunks(qkv[..., 2 * GDN_QK:], GDN_HEADS)
    q = l2norm(q) * (GDN_DK ** -0.5)
    k = l2norm(k)
    beta = split_chunks(beta, GDN_HEADS)[..., 0]
    g = jnp.cumsum(split_chunks(log_alpha, GDN_HEADS)[..., 0], axis=-1)
    lower = jnp.tril(jnp.ones((CHUNK, CHUNK), dtype=bool))
    strict = jnp.tril(jnp.ones((CHUNK, CHUNK), dtype=bool), k=-1)
    l_mask = masked_exp(lower, g[..., :, None] - g[..., None, :])
    kb = k * beta[..., None]
    vb = v * beta[..., None]
    m = jnp.where(strict, jnp.einsum('...ik,...jk->...ij', kb, k) * l_mask, 0.0)
    eye = jnp.eye(CHUNK, dtype=f32)
    t_inv = lax.linalg.triangular_solve(eye + m, jnp.broadcast_to(eye, m.shape),
                                        left_side=True, lower=True, unit_diagonal=True)
    u_c = t_inv @ vb
    w_c = t_inv @ (kb * jnp.exp(g)[..., None])
    attn = jnp.einsum('...ik,...jk->...ij', q, k) * l_mask
    s0 = jnp.zeros((proj.shape[0], GDN_HEADS, GDN_DK, GDN_DV), f32)

    def step(S, inp):
        q_c, k_c, uu, ww, at, g_c = inp
        v_new = uu - ww @ S
        o = (q_c * jnp.exp(g_c)[..., None]) @ S + at @ v_new
        g_last = g_c[..., -1:]
        S = S * jnp.exp(g_last)[..., None] + jnp.einsum(
            'bhsk,bhsv->bhkv', k_c * jnp.exp(g_last - g_c)[..., None], v_new)
        return S, o

    _, o = lax.scan(step, s0, (q, k, u_c, w_c, attn, g))
    return gated_head_norm(merge_chunks(o), gate, norm_w)


def setup_inputs(seed: int = 0) -> dict:
    key = jax.random.key(seed)
    ks = jax.random.split(key, 32)
    f32 = jnp.float32

    def nrm(k, shape, scale):
        return jax.random.normal(k, shape, f32) * scale

    def gain(k, shape):
        return 1.0 + 0.02 * jax.random.normal(k, shape, f32)

    n_idx = jnp.arange(S5_STATE, dtype=f32)
    gdn_dt = jnp.exp(jax.random.uniform(ks[27], (N_ODD, GDN_HEADS), f32, math.log(1e-3), math.log(1e-1)))
    return {
        'x': nrm(ks[0], (BATCH, SEQ, D_MODEL), 1.0),
        'ffn1_norm': gain(ks[1], (DEPTH, D_MODEL)),
        'ffn1_w_gate': nrm(ks[2], (DEPTH, D_MODEL, D_FF), D_MODEL ** -0.5),
        'ffn1_w_up': nrm(ks[3], (DEPTH, D_MODEL, D_FF), D_MODEL ** -0.5),
        'ffn1_w_down': nrm(ks[4], (DEPTH, D_FF, D_MODEL), D_FF ** -0.5),
        'mix_norm': gain(ks[5], (DEPTH, D_MODEL)),
        'ffn2_norm': gain(ks[6], (DEPTH, D_MODEL)),
        'ffn2_w_gate': nrm(ks[7], (DEPTH, D_MODEL, D_FF), D_MODEL ** -0.5),
        'ffn2_w_up': nrm(ks[8], (DEPTH, D_MODEL, D_FF), D_MODEL ** -0.5),
        'ffn2_w_down': nrm(ks[9], (DEPTH, D_FF, D_MODEL), D_FF ** -0.5),
        'ev_w_in': nrm(ks[10], (N_EVEN, D_MODEL, EV_IN), D_MODEL ** -0.5),
        'hg_lb_logits': nrm(ks[11], (N_EVEN, HG_QK), 0.1),
        'hg_norm_w': gain(ks[12], (N_EVEN, HG_DV)),
        's5_a_re': -0.5 + nrm(ks[13], (N_EVEN, S5_GROUPS, S5_STATE), 0.01),
        's5_a_im': math.pi * n_idx + nrm(ks[14], (N_EVEN, S5_GROUPS, S5_STATE), 0.01),
        's5_b_re': nrm(ks[15], (N_EVEN, S5_GROUPS, S5_STATE, S5_GROUP), (2 * S5_GROUP) ** -0.5),
        's5_b_im': nrm(ks[16], (N_EVEN, S5_GROUPS, S5_STATE, S5_GROUP), (2 * S5_GROUP) ** -0.5),
        's5_c_re': nrm(ks[17], (N_EVEN, S5_GROUPS, S5_GROUP, S5_STATE), (2 * S5_STATE) ** -0.5),
        's5_c_im': nrm(ks[18], (N_EVEN, S5_GROUPS, S5_GROUP, S5_STATE), (2 * S5_STATE) ** -0.5),
        's5_d': nrm(ks[19], (N_EVEN, S5_WIDTH), 1.0),
        's5_log_dt': jax.random.uniform(ks[20], (N_EVEN, S5_GROUPS), f32, math.log(1e-3), math.log(1e-1)),
        's5_w_glu': nrm(ks[21], (N_EVEN, S5_WIDTH, S5_WIDTH), S5_WIDTH ** -0.5),
        'ev_w_out': nrm(ks[22], (N_EVEN, HG_WIDTH + S5_WIDTH, D_MODEL), (HG_WIDTH + S5_WIDTH) ** -0.5),
        'od_w_in': nrm(ks[23], (N_ODD, D_MODEL, OD_IN), D_MODEL ** -0.5),
        'gdn_conv_w': nrm(ks[24], (N_ODD, CONV_W, 2 * GDN_QK + GDN_V), CONV_W ** -0.5),
        'gdn_a_log': jnp.log(jax.random.uniform(ks[25], (N_ODD, GDN_HEADS), f32, 1.0, 16.0)),
        'gdn_dt_bias': gdn_dt + jnp.log(-jnp.expm1(-gdn_dt)),
        'gdn_norm_w': gain(ks[26], (N_ODD, GDN_DV)),
        'od_w_out': nrm(ks[28], (N_ODD, GDN_V, D_MODEL), GDN_V ** -0.5),
        'final_norm': gain(ks[29], (D_MODEL,)),
    }


def reference(x, ffn1_norm, ffn1_w_gate, ffn1_w_up, ffn1_w_down, mix_norm,
              ffn2_norm, ffn2_w_gate, ffn2_w_up, ffn2_w_down,
              ev_w_in, hg_lb_logits, hg_norm_w, s5_a_re, s5_a_im, s5_b_re, s5_b_im,
              s5_c_re, s5_c_im, s5_d, s5_log_dt, s5_w_glu, ev_w_out,
              od_w_in, gdn_conv_w, gdn_a_log, gdn_dt_bias, gdn_norm_w, od_w_out,
              final_norm):
    p = jax.nn.softmax(hg_lb_logits.astype(jnp.float32), axis=0)
    lbs = jnp.cumsum(p, axis=0) - p[0]
    for layer in range(DEPTH):
        x = x + 0.5 * swiglu(rmsnorm(x, ffn1_norm[layer]), ffn1_w_gate[layer],
                             ffn1_w_up[layer], ffn1_w_down[layer])
        h = rmsnorm(x, mix_norm[layer])
        j = layer // 2
        if layer % 2 == 0:
            proj = h @ ev_w_in[j]
            y_a = hgrn2_mix(proj[..., :HG_QK],
                            proj[..., HG_QK:2 * HG_QK],
                            proj[..., 2 * HG_QK:2 * HG_QK + HG_WIDTH],
                            proj[..., 2 * HG_QK + HG_WIDTH:2 * HG_QK + 2 * HG_WIDTH],
                            lbs[j], hg_norm_w[j])
            y_b = s5_mix(proj[..., 2 * HG_QK + 2 * HG_WIDTH:], s5_a_re[j], s5_a_im[j],
                         s5_b_re[j], s5_b_im[j], s5_c_re[j], s5_c_im[j], s5_d[j],
                         s5_log_dt[j], s5_w_glu[j])
            y = (jnp.concatenate([y_a, y_b], axis=-1) @ ev_w_out[j]).astype(x.dtype)
        else:
            proj = h @ od_w_in[j]
            y = (gdn_mix(proj, gdn_conv_w[j], gdn_a_log[j], gdn_dt_bias[j], gdn_norm_w[j])
                 @ od_w_out[j]).astype(x.dtype)
        x = x + y
        x = x + 0.5 * swiglu(rmsnorm(x, ffn2_norm[layer]), ffn2_w_gate[layer],
                             ffn2_w_up[layer], ffn2_w_down[layer])
    return rmsnorm(x, final_norm)
```

```python
import numpy as np
import concourse.bass as bass
import concourse.mybir as mybir
from concourse.bass_utils import run_bass_kernel_spmd

F32 = mybir.dt.float32
BF16 = mybir.dt.bfloat16
AF = mybir.ActivationFunctionType
ALU = mybir.AluOpType

EPOCH = 16000


class Eng:
    def __init__(self, fw, e, name, self_sync=True):
        self.fw = fw
        self.e = e
        self.name = name
        self.self_sync = self_sync
        self.sem = fw.nc.alloc_semaphore(name + "_s0")
        self.cnt = 0
        self.nep = 0
        self.seen = {}
        self.total = 0

    def _wait(self, deps):
        for d in deps:
            if d is None:
                continue
            sem, val, own = d
            if own is self and not self.self_sync:
                continue
            k = id(sem)
            if self.seen.get(k, 0) >= val:
                continue
            self.e.wait_ge(sem, val)
            self.seen[k] = val

    def emit(self, fn, deps=()):
        self._wait(deps)
        if self.cnt >= EPOCH:
            self.nep += 1
            self.sem = self.fw.nc.alloc_semaphore("%s_s%d" % (self.name, self.nep))
            self.cnt = 0
        ins = fn()
        self.cnt += 1
        self.total += 1
        ins.then_inc(self.sem, 1)
        return (self.sem, self.cnt, self)

    def dma(self, out, in_, deps=(), **kw):
        fw = self.fw
        self._wait(deps)
        slot = fw.dma_rr % len(fw.dma_sems)
        fw.dma_rr += 1
        sem = fw.dma_sems[slot]
        prev = fw.dma_vals[slot]
        if prev > 0:
            k = id(sem)
            if self.seen.get(k, 0) < prev:
                self.e.wait_ge(sem, prev)
                self.seen[k] = prev
        ins = self.e.dma_start(out=out, in_=in_, **kw)
        val = prev + 16
        fw.dma_vals[slot] = val
        ins.then_inc(sem, 16)
        return (sem, val, None)


class Buf:
    def __init__(self, name=""):
        self.name = name
        self.w = None
        self.r = {}


class FW:
    def __init__(self, n_dma_sems=40):
        self.nc = bass.Bass("TRN2", target_bir_lowering=False)
        nc = self.nc
        self.pe = Eng(self, nc.tensor, "pe", self_sync=False)
        self.act = Eng(self, nc.scalar, "act")
        self.dve = Eng(self, nc.vector, "dve")
        self.pool = Eng(self, nc.gpsimd, "pool")
        self.sp = Eng(self, nc.sync, "sp")
        self.dma_sems = [nc.alloc_semaphore("dma%d" % i) for i in range(n_dma_sems)]
        self.dma_vals = [0] * n_dma_sems
        self.dma_rr = 0

    def _deps(self, reads, writes):
        deps = []
        for b in reads:
            if b.w is not None:
                deps.append(b.w)
        for b in writes:
            if b.w is not None:
                deps.append(b.w)
            deps.extend(b.r.values())
        return deps

    def _post(self, tok, reads, writes):
        for b in reads:
            k = id(tok[0])
            o = b.r.get(k)
            if o is None or o[1] < tok[1]:
                b.r[k] = tok
        for b in writes:
            b.w = tok
            b.r = {}

    def op(self, eng, fn, reads=(), writes=()):
        tok = eng.emit(fn, self._deps(reads, writes))
        self._post(tok, reads, writes)
        return tok

    def dma(self, eng, out, in_, reads=(), writes=(), **kw):
        tok = eng.dma(out, in_, self._deps(reads, writes), **kw)
        self._post(tok, reads, writes)
        return tok

    def finish(self, bufs):
        deps = []
        for b in bufs:
            if b.w is not None:
                deps.append(b.w)
        self.sp._wait(deps)


D = 1024
DFF = 2816
KC = D // 128
FC = DFF // 128
TT = 512
EPS = 1e-6


class Scope:
    def __init__(self, nc):
        self.nc = nc
        self.guards = []

    def sb(self, name, shape, dt):
        g = self.nc.sbuf_tensor(name, shape, dt)
        t = g.__enter__()
        self.guards.append(g)
        return t

    def ps(self, name, shape, dt=F32):
        g = self.nc.psum_tensor(name, shape, dt)
        t = g.__enter__()
        self.guards.append(g)
        return t

    def close(self):
        for g in reversed(self.guards):
            g.__exit__(None, None, None)
        self.guards = []


def barrier(f):
    engs = [f.pe, f.act, f.dve, f.pool, f.sp]
    toks = []
    for e in engs:
        if e.cnt > 0:
            toks.append((e.sem, e.cnt, None))
    for s, v in zip(f.dma_sems, f.dma_vals):
        if v > 0:
            toks.append((s, v, None))
    for e in engs:
        e._wait(toks)


_uid = [0]


def uname(p):
    _uid[0] += 1
    return "%s_%d" % (p, _uid[0])


def rms_stats(f, sc, xt, sqbuf, Bx, Bsq, ones_bf, eps_t, pss, Bpss, rstd, Brstd, ncols, inv_n):
    nc = f.nc
    Bsq = Bsq if isinstance(Bsq, list) else [Bsq]
    f.op(f.act, lambda: nc.scalar.activation(out=sqbuf, in_=xt, func=AF.Square), reads=[Bx], writes=Bsq)
    for c in range(KC):
        f.op(f.pe, lambda c=c: nc.tensor.matmul(pss, ones_bf, sqbuf[:, c, :], start=(c == 0), stop=(c == KC - 1)),
             reads=Bsq, writes=[Bpss])
    f.op(f.act, lambda: nc.scalar.activation(out=rstd, in_=pss, func=AF.Sqrt, bias=eps_t, scale=inv_n),
         reads=[Bpss], writes=[Brstd])
    f.op(f.dve, lambda: nc.vector.reciprocal(out=rstd, in_=rstd), reads=[Brstd], writes=[Brstd])


def load_w_bf16(f, eng, dst_sb, src_ap, bufs, piece):
    nc = f.nc
    A, Bn = src_ap.shape[1], src_ap.shape[2]
    toks = []
    i = 0
    for b0 in range(0, Bn, piece):
        b1 = min(Bn, b0 + piece)
        f.dma(eng, dst_sb[:, :, b0:b1], src_ap[:, :, b0:b1], writes=[bufs[i]])
        i += 1


def ffn_phase(f, src, dst, wn_d, wg_d, wu_d, wd_d, NT):
    nc = f.nc
    sc = Scope(nc)
    Wg = sc.sb(uname("Wg"), [128, KC, DFF], BF16)
    Wu = sc.sb(uname("Wu"), [128, KC, DFF], BF16)
    Wd = sc.sb(uname("Wd"), [128, FC, D], BF16)
    wn = sc.sb(uname("wn"), [128, KC], F32)
    xt = sc.sb(uname("xt"), [128, KC, TT], F32)
    hT = sc.sb(uname("hT"), [128, KC, TT], BF16)
    act = sc.sb(uname("act"), [128, FC, TT], BF16)
    rstd = sc.sb(uname("rstd"), [128, TT], F32)
    sg = [sc.sb(uname("sg"), [128, TT], F32) for _ in range(2)]
    ones_bf = sc.sb(uname("ones"), [128, 128], BF16)
    eps_t = sc.sb(uname("eps"), [128, 1], F32)
    pg = [sc.ps(uname("pg"), [128, TT]) for _ in range(2)]
    pu = [sc.ps(uname("pu"), [128, TT]) for _ in range(2)]
    pd = [sc.ps(uname("pd"), [128, TT]) for _ in range(2)]
    pss = sc.ps(uname("pss"), [128, TT])

    PW = 512
    npc = (DFF + PW - 1) // PW
    BWg = [Buf() for _ in range(npc)]
    BWu = [Buf() for _ in range(npc)]
    BWd = [Buf() for _ in range(FC)]
    Bc, Bx, Bh, Bpss, Brstd = Buf(), Buf(), Buf(), Buf(), Buf()
    Bact = [Buf() for _ in range(FC)]
    Bpg = [Buf(), Buf()]
    Bpu = [Buf(), Buf()]
    Bsg = [Buf(), Buf()]
    Bpd = [Buf(), Buf()]
    Bdst = Buf()

    f.op(f.dve, lambda: nc.vector.memset(ones_bf[:], 1.0), writes=[Bc])
    f.op(f.dve, lambda: nc.vector.memset(eps_t[:], EPS), writes=[Bc])
    f.dma(f.sp, wn[:], wn_d.rearrange("(c p) -> p c", p=128), writes=[Bc], allow_slow_non_contiguous=True)
    wgv = wg_d.rearrange("(kc p) f -> p kc f", p=128)
    wuv = wu_d.rearrange("(kc p) f -> p kc f", p=128)
    wdv = wd_d.rearrange("(fc p) d -> p fc d", p=128)
    for i in range(npc):
        b0, b1 = i * PW, min(DFF, (i + 1) * PW)
        f.dma(f.pool, Wg[:, :, b0:b1], wgv[:, :, b0:b1], writes=[BWg[i]])
        f.dma(f.pool, Wu[:, :, b0:b1], wuv[:, :, b0:b1], writes=[BWu[i]])
    for i in range(0, FC, 2):
        f.dma(f.pool, Wd[:, i:i + 2, :], wdv[:, i:i + 2, :], writes=[BWd[i], BWd[i + 1]])

    srcv = src.rearrange("(c p) t -> p c t", p=128)
    dstv = dst.rearrange("(c p) t -> p c t", p=128)
    for t in range(NT // TT):
        cs = slice(t * TT, (t + 1) * TT)
        f.dma(f.sp, xt[:], srcv[:, :, cs], writes=[Bx])
        rms_stats(f, sc, xt[:], act[:, 0:KC, :], Bx, Bact[0:KC], ones_bf[:], eps_t[:], pss[:], Bpss, rstd[:], Brstd, TT, 1.0 / D)
        for c in range(KC):
            f.op(f.dve, lambda c=c: nc.vector.scalar_tensor_tensor(out=hT[:, c, :], in0=xt[:, c, :], scalar=wn[:, c:c + 1],
                                                                 in1=rstd[:], op0=ALU.mult, op1=ALU.mult),
                 reads=[Bx, Brstd, Bc], writes=[Bh])
        for fc in range(FC):
            b = fc % 2
            wi = (fc * 128) // PW
            for kc in range(KC):
                f.op(f.pe, lambda kc=kc, fc=fc, b=b: nc.tensor.matmul(pg[b][:], Wg[:, kc, fc * 128:(fc + 1) * 128], hT[:, kc, :],
                                                                      start=(kc == 0), stop=(kc == KC - 1)),
                     reads=[BWg[wi], Bh], writes=[Bpg[b]])
            for kc in range(KC):
                f.op(f.pe, lambda kc=kc, fc=fc, b=b: nc.tensor.matmul(pu[b][:], Wu[:, kc, fc * 128:(fc + 1) * 128], hT[:, kc, :],
                                                                      start=(kc == 0), stop=(kc == KC - 1)),
                     reads=[BWu[wi], Bh], writes=[Bpu[b]])
            f.op(f.act, lambda b=b: nc.scalar.activation(out=sg[b][:], in_=pg[b][:], func=AF.Silu), reads=[Bpg[b]], writes=[Bsg[b]])
            f.op(f.dve, lambda b=b, fc=fc: nc.vector.tensor_tensor(out=act[:, fc, :], in0=pu[b][:], in1=sg[b][:], op=ALU.mult),
                 reads=[Bpu[b], Bsg[b]], writes=[Bact[fc]])
        for dc in range(KC):
            b = dc % 2
            for fc in range(FC):
                f.op(f.pe, lambda dc=dc, fc=fc, b=b: nc.tensor.matmul(pd[b][:], Wd[:, fc, dc * 128:(dc + 1) * 128], act[:, fc, :],
                                                                      start=(fc == 0), stop=(fc == FC - 1)),
                     reads=[BWd[fc], Bact[fc]], writes=[Bpd[b]])
            f.op(f.dve, lambda dc=dc, b=b: nc.vector.scalar_tensor_tensor(out=xt[:, dc, :], in0=pd[b][:], scalar=0.5, in1=xt[:, dc, :],
                                                                       op0=ALU.mult, op1=ALU.add),
                 reads=[Bpd[b]], writes=[Bx])
        f.dma(f.sp, dstv[:, :, cs], xt[:], reads=[Bx], writes=[Bdst])
    barrier(f)
    sc.close()


def final_phase(f, src, dst, wn_d, NT):
    nc = f.nc
    sc = Scope(nc)
    wn = sc.sb(uname("wn"), [128, KC], F32)
    xt = sc.sb(uname("xt"), [128, KC, TT], F32)
    sq = sc.sb(uname("sq"), [128, KC, TT], BF16)
    rstd = sc.sb(uname("rstd"), [128, TT], F32)
    ones_bf = sc.sb(uname("ones"), [128, 128], BF16)
    eps_t = sc.sb(uname("eps"), [128, 1], F32)
    pss = sc.ps(uname("pss"), [128, TT])
    Bc, Bx, Bsq, Bpss, Brstd, Bdst = [Buf() for _ in range(6)]
    f.op(f.dve, lambda: nc.vector.memset(ones_bf[:], 1.0), writes=[Bc])
    f.op(f.dve, lambda: nc.vector.memset(eps_t[:], EPS), writes=[Bc])
    f.dma(f.sp, wn[:], wn_d.rearrange("(c p) -> p c", p=128), writes=[Bc], allow_slow_non_contiguous=True)
    srcv = src.rearrange("(c p) t -> p c t", p=128)
    dstv = dst.rearrange("(c p) t -> p c t", p=128)
    for t in range(NT // TT):
        cs = slice(t * TT, (t + 1) * TT)
        f.dma(f.sp, xt[:], srcv[:, :, cs], writes=[Bx])
        rms_stats(f, sc, xt[:], sq[:], Bx, Bsq, ones_bf[:], eps_t[:], pss[:], Bpss, rstd[:], Brstd, TT, 1.0 / D)
        for c in range(KC):
            f.op(f.dve, lambda c=c: nc.vector.scalar_tensor_tensor(out=xt[:, c, :], in0=xt[:, c, :], scalar=wn[:, c:c + 1],
                                                                 in1=rstd[:], op0=ALU.mult, op1=ALU.mult),
                 reads=[Brstd, Bc], writes=[Bx])
        f.dma(f.sp, dstv[:, :, cs], xt[:], reads=[Bx], writes=[Bdst])
    barrier(f)
    sc.close()
    return Bdst


NEG = -30000.0


class Consts:
    def __init__(self, f, sc):
        nc = f.nc
        self.B = Buf()
        B = self.B
        mk = lambda n, shp, dt=F32: sc.sb(uname(n), shp, dt)
        self.ones32 = mk("ones32", [128, 128])
        self.ones_bf = mk("onesbf", [128, 128], BF16)
        self.tri = mk("tri", [128, 128])
        self.low = mk("low", [128, 128])
        self.ident = mk("ident", [128, 128])
        self.ident_bf = mk("identbf", [128, 128], BF16)
        self.negincT = mk("negincT", [128, 128])
        self.negstr = mk("negstr", [128, 128])
        self.strT01 = mk("strT01", [128, 128])
        self.bd = mk("bd", [128, 128])
        self.cind = mk("cind", [128, 2, 128])
        self.eps = mk("epsc", [128, 1])
        self.one = mk("onec", [128, 1])
        P = f.pool
        f.op(P, lambda: nc.gpsimd.memset(self.ones32[:], 1.0), writes=[B])
        f.op(P, lambda: nc.gpsimd.memset(self.ones_bf[:], 1.0), writes=[B])
        f.op(P, lambda: nc.gpsimd.memset(self.eps[:], EPS), writes=[B])
        f.op(P, lambda: nc.gpsimd.memset(self.one[:], 1.0), writes=[B])
        f.op(P, lambda: nc.gpsimd.affine_select(out=self.tri[:], in_=self.ones32[:], pattern=[[1, 128]], compare_op=ALU.is_ge,
                                                fill=0.0, base=0, channel_multiplier=-1), reads=[B], writes=[B])
        f.op(P, lambda: nc.gpsimd.memset(self.tri[0:64, 64:128], 0.0), writes=[B])
        f.op(P, lambda: nc.gpsimd.affine_select(out=self.low[:], in_=self.ones32[:], pattern=[[-1, 128]], compare_op=ALU.is_gt,
                                                fill=0.0, base=0, channel_multiplier=1), reads=[B], writes=[B])
        f.op(P, lambda: nc.gpsimd.memset(self.low[64:128, 0:64], 0.0), writes=[B])
        f.op(P, lambda: nc.gpsimd.affine_select(out=self.ident[:], in_=self.ones32[:], pattern=[[-1, 128]], compare_op=ALU.is_equal,
                                                fill=0.0, base=0, channel_multiplier=1), reads=[B], writes=[B])
        f.op(P, lambda: nc.gpsimd.tensor_copy(out=self.ident_bf[:], in_=self.ident[:]), reads=[B], writes=[B])
        f.op(P, lambda: nc.gpsimd.tensor_scalar(self.negincT[:], self.tri[:], -1.0, -NEG, ALU.add, ALU.mult), reads=[B], writes=[B])
        f.op(P, lambda: nc.gpsimd.tensor_scalar(self.negstr[:], self.low[:], -1.0, -NEG, ALU.add, ALU.mult), reads=[B], writes=[B])
        f.op(P, lambda: nc.gpsimd.tensor_tensor(out=self.strT01[:], in0=self.tri[:], in1=self.ident[:], op=ALU.subtract), reads=[B], writes=[B])
        f.op(P, lambda: nc.gpsimd.memset(self.bd[:], 0.0), writes=[B])
        f.op(P, lambda: nc.gpsimd.memset(self.bd[0:64, 0:64], 1.0), writes=[B])
        f.op(P, lambda: nc.gpsimd.memset(self.bd[64:128, 64:128], 1.0), writes=[B])
        f.op(P, lambda: nc.gpsimd.memset(self.cind[:], 0.0), writes=[B])
        f.op(P, lambda: nc.gpsimd.memset(self.cind[0:64, 0, :], 1.0), writes=[B])
        f.op(P, lambda: nc.gpsimd.memset(self.cind[64:128, 1, :], 1.0), writes=[B])


def bc_h(ap2, H=8):
    return ap2.unsqueeze(1).to_broadcast([ap2.shape[0], H, ap2.shape[1]])


def bc_i(ap2, n=128):
    return ap2.unsqueeze(2).to_broadcast([ap2.shape[0], ap2.shape[1], n])


GH = 8
ODIN = 4112


def gdn_proj_phase(f, X, wn_d, win_d, convw_d, alog_d, dtb_d, scr, NT, L):
    nc = f.nc
    sc = Scope(nc)
    C = Consts(f, sc)
    Win = sc.sb(uname("Win"), [128, KC, ODIN], BF16)
    wn = sc.sb(uname("wn"), [128, KC], F32)
    xt = sc.sb(uname("xt"), [128, KC, TT], F32)
    hT = sc.sb(uname("hT"), [128, KC, TT], BF16)
    sq = sc.sb(uname("sq"), [128, KC, TT], BF16)
    rstd = sc.sb(uname("rstd"), [128, TT], F32)
    cw = sc.sb(uname("cw"), [128, 24, 4], F32)
    halo = sc.sb(uname("halo"), [128, 24, 3], F32)
    pre = [sc.sb(uname("pre"), [128, TT + 3], F32) for _ in range(2)]
    cv = [sc.sb(uname("cv"), [128, TT], F32) for _ in range(2)]
    s32 = [sc.sb(uname("s32"), [128, TT], F32) for _ in range(2)]
    sq2 = [sc.sb(uname("sq2"), [128, TT], BF16) for _ in range(2)]
    r2 = [sc.sb(uname("r2"), [128, TT], F32) for _ in range(2)]
    ob = [sc.sb(uname("ob"), [128, TT], BF16) for _ in range(2)]
    tk = [sc.sb(uname("tk"), [128, 4, 128], BF16) for _ in range(2)]
    blt = sc.sb(uname("blt"), [128, 4, 16], F32)
    tmpb = sc.sb(uname("tmpb"), [128, 4, 8], F32)
    dtb = sc.sb(uname("dtb"), [128, 8], F32)
    negA = sc.sb(uname("negA"), [128, 8], F32)
    pp = [sc.ps(uname("pp"), [128, TT]) for _ in range(2)]
    pn = [sc.ps(uname("pn"), [128, TT]) for _ in range(2)]
    ptr = [sc.ps(uname("ptr"), [128, 4, 128], BF16) for _ in range(2)]
    pss = sc.ps(uname("pss"), [128, TT])
    pb = sc.ps(uname("pb"), [128, 4, 16])

    Bc, Bx, Bh, Bsq, Bpss, Brstd, Bhalo, Bpb, Bblt, Btmpb = [Buf() for _ in range(10)]
    NW = 9
    BW = [Buf() for _ in range(NW)]
    Bpp = [Buf(), Buf()]; Bpn = [Buf(), Buf()]; Bptr = [Buf(), Buf()]
    Bpre = [Buf(), Buf()]; Bcv = [Buf(), Buf()]; Bs32 = [Buf(), Buf()]; Bsq2 = [Buf(), Buf()]
    Br2 = [Buf(), Buf()]; Bob = [Buf(), Buf()]; Btk = [Buf(), Buf()]
    Bscr = Buf()

    f.dma(f.sp, wn[:], wn_d.rearrange("(c p) -> p c", p=128), writes=[Bc], allow_slow_non_contiguous=True)
    for j in range(4):
        f.dma(f.sp, cw[:, :, j], convw_d[j, :].rearrange("(c p) -> p c", p=128), writes=[Bc], allow_slow_non_contiguous=True)
    f.dma(f.sp, dtb[:], dtb_d.partition_broadcast(128), writes=[Bc])
    f.dma(f.sp, negA[:], alog_d.partition_broadcast(128), writes=[Bc])
    f.op(f.act, lambda: nc.scalar.activation(out=negA[:], in_=negA[:], func=AF.Exp), reads=[Bc], writes=[Bc])
    f.op(f.dve, lambda: nc.vector.tensor_scalar(negA[:], negA[:], -1.0, None, ALU.mult), reads=[Bc], writes=[Bc])
    winv = win_d.rearrange("(kc p) f -> p kc f", p=128)
    for i in range(NW):
        b0, b1 = i * 512, min(ODIN, (i + 1) * 512)
        f.dma(f.pool, Win[:, :, b0:b1], winv[:, :, b0:b1], writes=[BW[i]])

    Xv = X.rearrange("(c p) t -> p c t", p=128)
    qTv, kTv, gTv = scr["qT"], scr["kT"], scr["gT"]
    for t in range(NT // TT):
        cs = slice(t * TT, (t + 1) * TT)
        seq_start = (t * TT) % L == 0
        f.dma(f.sp, xt[:], Xv[:, :, cs], writes=[Bx])
        rms_stats(f, sc, xt[:], sq[:], Bx, Bsq, C.ones_bf[:], C.eps[:], pss[:], Bpss, rstd[:], Brstd, TT, 1.0 / D)
        for c in range(KC):
            f.op(f.dve, lambda c=c: nc.vector.scalar_tensor_tensor(out=hT[:, c, :], in0=xt[:, c, :], scalar=wn[:, c:c + 1],
                                                                 in1=rstd[:], op0=ALU.mult, op1=ALU.mult),
                 reads=[Bx, Brstd, Bc], writes=[Bh])
        if seq_start:
            f.op(f.dve, lambda: nc.vector.memset(halo[:], 0.0), writes=[Bhalo])
        for oc in range(24):
            b = oc % 2
            wi = (oc * 128) // 512
            for kc in range(KC):
                f.op(f.pe, lambda kc=kc, oc=oc, b=b: nc.tensor.matmul(pp[b][:], Win[:, kc, oc * 128:(oc + 1) * 128], hT[:, kc, :],
                                                                      start=(kc == 0), stop=(kc == KC - 1)),
                     reads=[BW[wi], Bh], writes=[Bpp[b]])
            f.op(f.act, lambda b=b: nc.scalar.copy(out=pre[b][:, 3:TT + 3], in_=pp[b][:]), reads=[Bpp[b]], writes=[Bpre[b]])
            f.op(f.dve, lambda b=b, oc=oc: nc.vector.tensor_copy(out=pre[b][:, 0:3], in_=halo[:, oc, :]), reads=[Bhalo], writes=[Bpre[b]])
            f.op(f.dve, lambda b=b, oc=oc: nc.vector.tensor_copy(out=halo[:, oc, :], in_=pre[b][:, TT:TT + 3]), reads=[Bpre[b]], writes=[Bhalo])
            f.op(f.dve, lambda b=b, oc=oc: nc.vector.tensor_scalar(cv[b][:], pre[b][:, 0:TT], cw[:, oc, 0:1], None, ALU.mult),
                 reads=[Bpre[b], Bc], writes=[Bcv[b]])
            for j in range(1, 4):
                f.op(f.dve, lambda b=b, oc=oc, j=j: nc.vector.scalar_tensor_tensor(out=cv[b][:], in0=pre[b][:, j:TT + j], scalar=cw[:, oc, j:j + 1],
                                                                                  in1=cv[b][:], op0=ALU.mult, op1=ALU.add),
                     reads=[Bpre[b], Bc], writes=[Bcv[b]])
            if oc < 16:
                f.op(f.act, lambda b=b: nc.scalar.activation(out=s32[b][:], in_=cv[b][:], func=AF.Silu), reads=[Bcv[b]], writes=[Bs32[b]])
                f.op(f.act, lambda b=b: nc.scalar.activation(out=sq2[b][:], in_=s32[b][:], func=AF.Square), reads=[Bs32[b]], writes=[Bsq2[b]])
                f.op(f.pe, lambda b=b: nc.tensor.matmul(pn[b][:], C.ones_bf[:], sq2[b][:], start=True, stop=True), reads=[Bsq2[b], C.B], writes=[Bpn[b]])
                f.op(f.act, lambda b=b: nc.scalar.activation(out=r2[b][:], in_=pn[b][:], func=AF.Sqrt, bias=C.eps[:], scale=1.0),
                     reads=[Bpn[b], C.B], writes=[Br2[b]])
                f.op(f.dve, lambda b=b: nc.vector.reciprocal(out=r2[b][:], in_=r2[b][:]), reads=[Br2[b]], writes=[Br2[b]])
                scl = (128.0 ** -0.5) if oc < 8 else 1.0
                f.op(f.dve, lambda b=b, scl=scl: nc.vector.scalar_tensor_tensor(out=ob[b][:], in0=s32[b][:], scalar=scl, in1=r2[b][:],
                                                                               op0=ALU.mult, op1=ALU.mult),
                     reads=[Bs32[b], Br2[b]], writes=[Bob[b]])
                dstT = qTv if oc < 8 else kTv
                hc = oc % 8
                f.dma(f.sp, dstT[hc * 128:(hc + 1) * 128, cs], ob[b][:], reads=[Bob[b]], writes=[Bscr])
            else:
                f.op(f.act, lambda b=b: nc.scalar.activation(out=ob[b][:], in_=cv[b][:], func=AF.Silu), reads=[Bcv[b]], writes=[Bob[b]])
            if oc >= 8:
                hc = oc % 8
                for s in range(4):
                    f.op(f.pe, lambda b=b, s=s: nc.tensor.transpose(ptr[b][:, s, :], ob[b][:, s * 128:(s + 1) * 128], C.ident_bf[:]),
                         reads=[Bob[b], C.B], writes=[Bptr[b]])
                f.op(f.act, lambda b=b: nc.scalar.copy(out=tk[b][:], in_=ptr[b][:]), reads=[Bptr[b]], writes=[Btk[b]])
                dsttok = scr["ktok"] if oc < 16 else scr["vtok"]
                f.dma(f.sp, dsttok[cs, hc * 128:(hc + 1) * 128].rearrange("(s p) d -> p s d", p=128), tk[b][:], reads=[Btk[b]], writes=[Bscr])
        for oc in range(24, 32):
            b = oc % 2
            wi = (oc * 128) // 512
            for kc in range(KC):
                f.op(f.pe, lambda kc=kc, oc=oc, b=b: nc.tensor.matmul(pp[b][:], Win[:, kc, oc * 128:(oc + 1) * 128], hT[:, kc, :],
                                                                      start=(kc == 0), stop=(kc == KC - 1)),
                     reads=[BW[wi], Bh], writes=[Bpp[b]])
            f.op(f.act, lambda b=b: nc.scalar.activation(out=ob[b][:], in_=pp[b][:], func=AF.Silu), reads=[Bpp[b]], writes=[Bob[b]])
            hc = oc - 24
            f.dma(f.sp, gTv[hc * 128:(hc + 1) * 128, cs], ob[b][:], reads=[Bob[b]], writes=[Bscr])
        for s in range(4):
            for kc in range(KC):
                f.op(f.pe, lambda kc=kc, s=s: nc.tensor.matmul(pb[:, s, :], hT[:, kc, s * 128:(s + 1) * 128], Win[:, kc, 4096:4112],
                                                               start=(kc == 0), stop=(kc == KC - 1)),
                     reads=[BW[8], Bh], writes=[Bpb])
        f.op(f.act, lambda: nc.scalar.activation(out=blt[:, :, 0:8], in_=pb[:, :, 0:8], func=AF.Sigmoid), reads=[Bpb], writes=[Bblt])
        f.op(f.dve, lambda: nc.vector.tensor_tensor(out=tmpb[:], in0=pb[:, :, 8:16], in1=dtb[:].unsqueeze(1).to_broadcast([128, 4, 8]), op=ALU.add),
             reads=[Bpb, Bc], writes=[Btmpb])
        f.op(f.act, lambda: nc.scalar.activation(out=tmpb[:], in_=tmpb[:], func=AF.Exp), reads=[Btmpb], writes=[Btmpb])
        f.op(f.act, lambda: nc.scalar.activation(out=tmpb[:], in_=tmpb[:], func=AF.Ln, bias=C.one[:], scale=1.0), reads=[Btmpb, C.B], writes=[Btmpb])
        f.op(f.dve, lambda: nc.vector.tensor_tensor(out=blt[:, :, 8:16], in0=tmpb[:], in1=negA[:].unsqueeze(1).to_broadcast([128, 4, 8]), op=ALU.mult),
             reads=[Btmpb, Bc], writes=[Bblt])
        f.dma(f.sp, scr["bl"][cs, :].rearrange("(s p) c -> p s c", p=128), blt[:], reads=[Bblt], writes=[Bscr])
    barrier(f)
    sc.close()


def gdn_core_phase(f, X, gnw_d, wout_d, scr, NT, L):
    nc = f.nc
    sc = Scope(nc)
    C = Consts(f, sc)
    H = GH
    mk = lambda n, shp, dt=F32: sc.sb(uname(n), shp, dt)
    Wout = mk("Wout", [128, H, D], BF16)
    gnw = mk("gnw", [128, 1])
    qTb = mk("qTb", [128, H, 128], BF16)
    kTb = mk("kTb", [128, H, 128], BF16)
    ktokb = mk("ktokb", [128, H, 128], BF16)
    vtokb = mk("vtokb", [128, H, 128], BF16)
    gTb = mk("gTb", [128, H, 128], BF16)
    bl = mk("bl", [128, 16])
    Xs = mk("Xs", [128, H, 128])
    sm = mk("sm", [128, 32])
    sme = mk("sme", [128, 40])
    nbeta = mk("nbeta", [128, 8])
    tmp1 = mk("tmp1", [128, H, 128])
    tmp2 = mk("tmp2", [128, H, 128])
    LmT = mk("LmT", [128, H, 128])
    LmS = mk("LmS", [128, H, 128])
    WTN = mk("WTN", [128, H, 128])
    E = mk("E", [128, H, 128])
    Pm = [mk("Pm", [128, H, 128]) for _ in range(2)]
    PTm = [mk("PTm", [128, H, 128]) for _ in range(2)]
    TTm = [mk("TTm", [128, H, 128]) for _ in range(2)]
    TTb = mk("TTb", [128, H, 128], BF16)
    attnT = mk("attnT", [128, H, 128], BF16)
    qgT = mk("qgT", [128, H, 128], BF16)
    ktil = mk("ktil", [128, H, 128], BF16)
    vb = mk("vb", [128, H, 128])
    R = mk("R", [128, H, 128], BF16)
    vnew = mk("vnew", [128, H, 128], BF16)
    S = mk("S", [128, H, 128])
    Sb = mk("Sb", [128, H, 128], BF16)
    sqo = mk("sqo", [128, H, 128], BF16)
    rs = mk("rs", [128, H, 128])
    of32 = mk("of32", [128, H, 128])
    ofb = mk("ofb", [128, H, 128], BF16)
    xt = mk("xtb", [128, KC, 128])
    PA = sc.ps(uname("PA"), [128, H, 128])
    PB = sc.ps(uname("PB"), [128, H, 128])
    PC = sc.ps(uname("PC"), [128, H, 128])
    PD = sc.ps(uname("PD"), [128, H, 128])
    names = "c W in bl Xs sm sme nb t1 t2 LmT LmS WTN E TTb attnT qgT ktil vb R vnew S Sb sqo rs of32 ofb xt PA PB PC PD scr P0 P1 PT0 PT1 TT0 TT1 qin kin ktin vtin gin"
    Bf = {n: Buf(n) for n in names.split()}
    g = lambda *ns: [Bf[n] for n in ns]
    Bc = Bf["c"]

    f.dma(f.sp, gnw[:], gnw_d.rearrange("(p o) -> p o", o=1), writes=[Bc])
    woutv = wout_d.rearrange("(h p) d -> p h d", p=128)
    f.dma(f.pool, Wout[:, 0:4, :], woutv[:, 0:4, :], writes=[Bf["W"]])
    f.dma(f.pool, Wout[:, 4:8, :], woutv[:, 4:8, :], writes=[Bf["W"]])
    Xv = X.rearrange("(c p) t -> p c t", p=128)
    V, A, P_ = f.dve, f.act, f.pe

    nblk = NT // 128
    for blk in range(nblk):
        t0 = blk * 128
        ts = slice(t0, t0 + 128)
        if t0 % L == 0:
            f.op(V, lambda: nc.vector.memset(S[:], 0.0), writes=g("S"))
            f.op(V, lambda: nc.vector.memset(Sb[:], 0.0), writes=g("Sb"))
        f.dma(f.sp, qTb[:], scr["qT"][:, ts].rearrange("(h p) t -> p h t", p=128), reads=g("scr"), writes=g("qin"))
        f.dma(f.sp, kTb[:], scr["kT"][:, ts].rearrange("(h p) t -> p h t", p=128), reads=g("scr"), writes=g("kin"))
        f.dma(f.sp, gTb[:], scr["gT"][:, ts].rearrange("(h p) t -> p h t", p=128), reads=g("scr"), writes=g("gin"))
        f.dma(f.sp, ktokb[:], scr["ktok"][ts, :].rearrange("p (h d) -> p h d", d=128), reads=g("scr"), writes=g("ktin"))
        f.dma(f.sp, vtokb[:], scr["vtok"][ts, :].rearrange("p (h d) -> p h d", d=128), reads=g("scr"), writes=g("vtin"))
        f.dma(f.sp, bl[:], scr["bl"][ts, :], reads=g("scr"), writes=g("bl"))
        f.dma(f.sp, xt[:], Xv[:, :, ts], writes=g("xt"))
        beta = bl[:, 0:8]
        la = bl[:, 8:16]
        f.op(V, lambda: nc.vector.tensor_tensor(out=Xs[:], in0=bc_i(la), in1=bc_h(C.tri[:]), op=ALU.mult), reads=g("bl") + [C.B], writes=g("Xs"))
        for hh in range(2):
            f.op(P_, lambda hh=hh: nc.tensor.matmul(PA[:, 4 * hh:4 * hh + 4, :], C.ones32[:], Xs[:, 4 * hh:4 * hh + 4, :], start=True, stop=True),
                 reads=g("Xs") + [C.B], writes=g("PA"))
        f.op(P_, lambda: nc.tensor.matmul(PD[:, 0, 0:8], C.tri[:], la, start=True, stop=True), reads=g("bl") + [C.B], writes=g("PD"))
        f.op(P_, lambda: nc.tensor.matmul(PD[:, 0, 8:16], C.bd[:], la, start=True, stop=True), reads=g("bl") + [C.B], writes=g("PD"))
        f.op(P_, lambda: nc.tensor.matmul(PD[:, 0, 16:24], C.cind[:, 0, :], la, start=True, stop=True), reads=g("bl") + [C.B], writes=g("PD"))
        f.op(P_, lambda: nc.tensor.matmul(PD[:, 0, 24:32], C.cind[:, 1, :], la, start=True, stop=True), reads=g("bl") + [C.B], writes=g("PD"))
        f.op(V, lambda: nc.vector.tensor_copy(out=sm[:], in_=PD[:, 0, 0:32]), reads=g("PD"), writes=g("sm"))
        gcol = sm[:, 0:8]
        f.op(A, lambda: nc.scalar.activation(out=sme[:, 0:8], in_=sm[:, 0:8], func=AF.Exp), reads=g("sm"), writes=g("sme"))
        f.op(V, lambda: nc.vector.tensor_tensor(out=sme[:, 8:16], in0=sm[:, 8:16], in1=sm[:, 0:8], op=ALU.subtract), reads=g("sm"), writes=g("sme"))
        f.op(A, lambda: nc.scalar.activation(out=sme[:, 8:16], in_=sme[:, 8:16], func=AF.Exp), reads=g("sme"), writes=g("sme"))
        f.op(A, lambda: nc.scalar.activation(out=sme[:, 16:32], in_=sm[:, 16:32], func=AF.Exp), reads=g("sm"), writes=g("sme"))
        f.op(V, lambda: nc.vector.scalar_tensor_tensor(out=sme[:, 32:40], in0=sme[:, 0:8], scalar=-1.0, in1=beta, op0=ALU.mult, op1=ALU.mult),
             reads=g("sme", "bl"), writes=g("sme"))
        f.op(V, lambda: nc.vector.tensor_scalar(nbeta[:], beta, -1.0, None, ALU.mult), reads=g("bl"), writes=g("nb"))
        f.op(V, lambda: nc.vector.tensor_tensor(out=tmp1[:], in0=PA[:], in1=bc_i(gcol), op=ALU.subtract), reads=g("PA", "sm"), writes=g("t1"))
        f.op(V, lambda: nc.vector.scalar_tensor_tensor(out=tmp2[:], in0=tmp1[:], scalar=-1.0, in1=bc_h(C.negstr[:]), op0=ALU.mult, op1=ALU.add),
             reads=g("t1") + [C.B], writes=g("t2"))
        f.op(V, lambda: nc.vector.tensor_tensor(out=tmp1[:], in0=tmp1[:], in1=bc_h(C.negincT[:]), op=ALU.add), reads=g("t1") + [C.B], writes=g("t1"))
        f.op(A, lambda: nc.scalar.activation(out=LmT[:], in_=tmp1[:], func=AF.Exp), reads=g("t1"), writes=g("LmT"))
        f.op(A, lambda: nc.scalar.activation(out=LmS[:], in_=tmp2[:], func=AF.Exp), reads=g("t2"), writes=g("LmS"))
        f.op(A, lambda: nc.scalar.activation(out=E[:], in_=PA[:], func=AF.Exp), reads=g("PA"), writes=g("E"))
        f.op(V, lambda: nc.vector.tensor_tensor(out=LmS[:], in0=LmS[:], in1=bc_i(nbeta[:]), op=ALU.mult), reads=g("LmS", "nb"), writes=g("LmS"))
        f.op(V, lambda: nc.vector.tensor_tensor(out=Xs[:], in0=bc_i(beta), in1=bc_h(C.ident[:]), op=ALU.mult), reads=g("bl") + [C.B], writes=g("Xs"))
        for hh in range(2):
            f.op(P_, lambda hh=hh: nc.tensor.matmul(PB[:, 4 * hh:4 * hh + 4, :], C.ones32[:], Xs[:, 4 * hh:4 * hh + 4, :], start=True, stop=True),
                 reads=g("Xs") + [C.B], writes=g("PB"))
        f.op(V, lambda: nc.vector.tensor_tensor(out=WTN[:], in0=LmT[:], in1=bc_h(C.strT01[:]), op=ALU.mult), reads=g("LmT") + [C.B], writes=g("WTN"))
        f.op(V, lambda: nc.vector.scalar_tensor_tensor(out=WTN[:], in0=PB[:], scalar=-1.0, in1=WTN[:], op0=ALU.mult, op1=ALU.mult),
             reads=g("PB"), writes=g("WTN"))
        for h in range(H):
            f.op(P_, lambda h=h: nc.tensor.matmul(PC[:, h, :], kTb[:, h, :], kTb[:, h, :], start=True, stop=True), reads=g("kin"), writes=g("PC"))
        f.op(V, lambda: nc.vector.tensor_tensor(out=Pm[0][:], in0=PC[:], in1=LmS[:], op=ALU.mult), reads=g("PC", "LmS"), writes=g("P0"))
        f.op(V, lambda: nc.vector.tensor_tensor(out=PTm[0][:], in0=PC[:], in1=WTN[:], op=ALU.mult), reads=g("PC", "WTN"), writes=g("PT0"))
        for h in range(H):
            f.op(P_, lambda h=h: nc.tensor.matmul(PB[:, h, :], kTb[:, h, :], qTb[:, h, :], start=True, stop=True), reads=g("kin", "qin"), writes=g("PB"))
        f.op(V, lambda: nc.vector.tensor_tensor(out=attnT[:], in0=PB[:], in1=LmT[:], op=ALU.mult), reads=g("PB", "LmT"), writes=g("attnT"))
        f.op(V, lambda: nc.vector.tensor_tensor(out=TTm[0][:], in0=PTm[0][:], in1=bc_h(C.ident[:]), op=ALU.add), reads=g("PT0") + [C.B], writes=g("TT0"))
        for k in range(1, 6):
            cur, nxt = (k - 1) % 2, k % 2
            Pc, PTc, Pn, PTn = "P%d" % cur, "PT%d" % cur, "P%d" % nxt, "PT%d" % nxt
            TTc, TTn = "TT%d" % cur, "TT%d" % nxt
            for h in range(H):
                f.op(P_, lambda h=h, cur=cur: nc.tensor.matmul(PC[:, h, :], PTm[cur][:, h, :], Pm[cur][:, h, :], start=True, stop=True),
                     reads=g(Pc, PTc), writes=g("PC"))
            f.op(A, lambda nxt=nxt: nc.scalar.copy(out=Pm[nxt][:], in_=PC[:]), reads=g("PC"), writes=g(Pn))
            if k < 5:
                for h in range(H):
                    f.op(P_, lambda h=h, cur=cur: nc.tensor.matmul(PB[:, h, :], Pm[cur][:, h, :], PTm[cur][:, h, :], start=True, stop=True),
                         reads=g(Pc, PTc), writes=g("PB"))
                f.op(A, lambda nxt=nxt: nc.scalar.copy(out=PTm[nxt][:], in_=PB[:]), reads=g("PB"), writes=g(PTn))
            for h in range(H):
                f.op(P_, lambda h=h, cur=cur, nxt=nxt: nc.tensor.matmul(PA[:, h, :], Pm[nxt][:, h, :], TTm[cur][:, h, :], start=True, stop=True),
                     reads=g(Pn, TTc), writes=g("PA"))
            f.op(V, lambda cur=cur, nxt=nxt: nc.vector.tensor_tensor(out=TTm[nxt][:], in0=PA[:], in1=TTm[cur][:], op=ALU.add),
                 reads=g("PA", TTc), writes=g(TTn))
        f.op(A, lambda: nc.scalar.copy(out=TTb[:], in_=TTm[1][:]), reads=g("TT1"), writes=g("TTb"))
        f.op(V, lambda: nc.vector.tensor_tensor(out=qgT[:], in0=qTb[:], in1=E[:], op=ALU.mult), reads=g("qin", "E"), writes=g("qgT"))
        f.op(V, lambda: nc.vector.tensor_tensor(out=ktil[:], in0=ktokb[:], in1=bc_i(sme[:, 8:16]), op=ALU.mult), reads=g("ktin", "sme"), writes=g("ktil"))
        f.op(V, lambda: nc.vector.tensor_tensor(out=vb[:], in0=vtokb[:], in1=bc_i(beta), op=ALU.mult), reads=g("vtin", "bl"), writes=g("vb"))
        for c in range(2):
            r = slice(64 * c, 64 * c + 64)
            for h in range(H):
                f.op(P_, lambda h=h, r=r: nc.tensor.matmul(PC[r, h, :], kTb[:, h, r], Sb[:, h, :], start=True, stop=True),
                     reads=g("kin", "Sb"), writes=g("PC"))
            for h in range(H):
                f.op(V, lambda h=h, r=r: nc.vector.scalar_tensor_tensor(out=R[r, h, :], in0=PC[r, h, :], scalar=sme[r, 32 + h:33 + h], in1=vb[r, h, :],
                                                                       op0=ALU.mult, op1=ALU.add),
                     reads=g("PC", "sme", "vb"), writes=g("R"))
            for h in range(H):
                f.op(P_, lambda h=h, r=r: nc.tensor.matmul(PB[r, h, :], TTb[r, h, r], R[r, h, :], start=True, stop=True),
                     reads=g("TTb", "R"), writes=g("PB"))
            f.op(A, lambda r=r: nc.scalar.copy(out=vnew[r, :, :], in_=PB[r, :, :]), reads=g("PB"), writes=g("vnew"))
            for h in range(H):
                f.op(P_, lambda h=h, r=r: nc.tensor.matmul(PD[:, h, r], Sb[:, h, :], qgT[:, h, r], start=True, stop=False),
                     reads=g("Sb", "qgT"), writes=g("PD"))
                f.op(P_, lambda h=h, r=r: nc.tensor.matmul(PD[:, h, r], vnew[r, h, :], attnT[r, h, r], start=False, stop=True),
                     reads=g("vnew", "attnT"), writes=g("PD"))
            for h in range(H):
                f.op(P_, lambda h=h, r=r: nc.tensor.matmul(PA[:, h, :], ktil[r, h, :], vnew[r, h, :], start=True, stop=True),
                     reads=g("ktil", "vnew"), writes=g("PA"))
            for h in range(H):
                f.op(V, lambda h=h, c=c: nc.vector.scalar_tensor_tensor(out=S[:, h, :], in0=S[:, h, :], scalar=sme[:, 16 + 8 * c + h:17 + 8 * c + h],
                                                                       in1=PA[:, h, :], op0=ALU.mult, op1=ALU.add),
                     reads=g("PA", "sme"), writes=g("S"))
            f.op(A, lambda: nc.scalar.copy(out=Sb[:], in_=S[:]), reads=g("S"), writes=g("Sb"))
        f.op(A, lambda: nc.scalar.activation(out=sqo[:], in_=PD[:], func=AF.Square), reads=g("PD"), writes=g("sqo"))
        for hh in range(2):
            f.op(P_, lambda hh=hh: nc.tensor.matmul(PC[:, 4 * hh:4 * hh + 4, :], C.ones_bf[:], sqo[:, 4 * hh:4 * hh + 4, :], start=True, stop=True),
                 reads=g("sqo") + [C.B], writes=g("PC"))
        f.op(A, lambda: nc.scalar.activation(out=rs[:], in_=PC[:], func=AF.Sqrt, bias=C.eps[:], scale=1.0 / 128), reads=g("PC") + [C.B], writes=g("rs"))
        f.op(V, lambda: nc.vector.reciprocal(out=rs[:], in_=rs[:]), reads=g("rs"), writes=g("rs"))
        f.op(V, lambda: nc.vector.scalar_tensor_tensor(out=of32[:], in0=PD[:], scalar=gnw[:, 0:1], in1=rs[:], op0=ALU.mult, op1=ALU.mult),
             reads=g("PD", "rs", "c"), writes=g("of32"))
        f.op(V, lambda: nc.vector.tensor_tensor(out=ofb[:], in0=of32[:], in1=gTb[:], op=ALU.mult), reads=g("of32", "gin"), writes=g("ofb"))
        for dc in range(KC):
            for h in range(H):
                f.op(P_, lambda dc=dc, h=h: nc.tensor.matmul(PB[:, dc, :], Wout[:, h, dc * 128:(dc + 1) * 128], ofb[:, h, :],
                                                            start=(h == 0), stop=(h == H - 1)),
                     reads=g("W", "ofb"), writes=g("PB"))
        f.op(V, lambda: nc.vector.tensor_tensor(out=xt[:], in0=PB[:], in1=xt[:], op=ALU.add), reads=g("PB"), writes=g("xt"))
        f.dma(f.sp, Xv[:, :, ts], xt[:], reads=g("xt"), writes=g("scr"))
    barrier(f)
    sc.close()


EVIN = 2560
HH = 4


def ev_proj_phase(f, X, wn_d, win_d, lbl_d, j, scr, NT, L):
    nc = f.nc
    sc = Scope(nc)
    C = Consts(f, sc)
    mk = lambda n, shp, dt=F32: sc.sb(uname(n), shp, dt)
    Win = mk("Win", [128, KC, EVIN], BF16)
    wn = mk("wn", [128, KC])
    xt = mk("xt", [128, KC, TT])
    hT = mk("hT", [128, KC, TT], BF16)
    sq = mk("sq", [128, KC, TT], BF16)
    rstd = mk("rstd", [128, TT])
    lg = mk("lg", [128, 2, 4])
    lb = mk("lb", [128, 4])
    oml = mk("oml", [128, 4])
    ob = [mk("ob", [128, TT], BF16) for _ in range(2)]
    fs = [mk("fs", [128, TT]) for _ in range(2)]
    lf = [mk("lf", [128, TT]) for _ in range(2)]
    vt = [mk("vt", [128, 512], BF16) for _ in range(2)]
    pp = [sc.ps(uname("pp"), [128, TT]) for _ in range(2)]
    pv = [sc.ps(uname("pv"), [128, 512]) for _ in range(2)]
    pss = sc.ps(uname("pss"), [128, TT])
    Bc, Bx, Bh, Bsq, Bpss, Brstd, Bscr = [Buf() for _ in range(7)]
    BW = [Buf() for _ in range(5)]
    Bpp = [Buf(), Buf()]; Bpv = [Buf(), Buf()]; Bob = [Buf(), Buf()]; Bfs = [Buf(), Buf()]; Blf = [Buf(), Buf()]; Bvt = [Buf(), Buf()]
    V, A, P_ = f.dve, f.act, f.pe

    f.dma(f.sp, wn[:], wn_d.rearrange("(c p) -> p c", p=128), writes=[Bc], allow_slow_non_contiguous=True)
    for l in range(2):
        f.dma(f.sp, lg[:, l, :], lbl_d[l, :].rearrange("(c p) -> p c", p=128), writes=[Bc], allow_slow_non_contiguous=True)
    if j == 0:
        f.op(V, lambda: nc.vector.memset(lb[:], 0.0), writes=[Bc])
    else:
        f.op(V, lambda: nc.vector.tensor_tensor(out=lb[:], in0=lg[:, 1, :], in1=lg[:, 0, :], op=ALU.subtract), reads=[Bc], writes=[Bc])
        f.op(A, lambda: nc.scalar.activation(out=lb[:], in_=lb[:], func=AF.Sigmoid), reads=[Bc], writes=[Bc])
    f.op(V, lambda: nc.vector.tensor_scalar(oml[:], lb[:], -1.0, 1.0, ALU.mult, ALU.add), reads=[Bc], writes=[Bc])
    winv = win_d.rearrange("(kc p) f -> p kc f", p=128)
    for i in range(5):
        f.dma(f.pool, Win[:, :, i * 512:(i + 1) * 512], winv[:, :, i * 512:(i + 1) * 512], writes=[BW[i]])
    Xv = X.rearrange("(c p) t -> p c t", p=128)
    for t in range(NT // TT):
        cs = slice(t * TT, (t + 1) * TT)
        f.dma(f.sp, xt[:], Xv[:, :, cs], writes=[Bx])
        rms_stats(f, sc, xt[:], sq[:], Bx, Bsq, C.ones_bf[:], C.eps[:], pss[:], Bpss, rstd[:], Brstd, TT, 1.0 / D)
        for c in range(KC):
            f.op(V, lambda c=c: nc.vector.scalar_tensor_tensor(out=hT[:, c, :], in0=xt[:, c, :], scalar=wn[:, c:c + 1],
                                                             in1=rstd[:], op0=ALU.mult, op1=ALU.mult),
                 reads=[Bx, Brstd, Bc], writes=[Bh])
        for oc in list(range(0, 8)) + list(range(12, 20)):
            b = oc % 2
            wi = oc // 4
            hc = oc % 4
            for kc in range(KC):
                f.op(P_, lambda kc=kc, oc=oc, b=b: nc.tensor.matmul(pp[b][:], Win[:, kc, oc * 128:(oc + 1) * 128], hT[:, kc, :],
                                                                    start=(kc == 0), stop=(kc == KC - 1)),
                     reads=[BW[wi], Bh], writes=[Bpp[b]])
            rows = slice(hc * 128, (hc + 1) * 128)
            if oc < 4 or 12 <= oc < 16:
                f.op(A, lambda b=b: nc.scalar.activation(out=ob[b][:], in_=pp[b][:], func=AF.Silu), reads=[Bpp[b]], writes=[Bob[b]])
                dst = scr["qT"] if oc < 4 else scr["gT"]
                f.dma(f.sp, dst[rows, cs], ob[b][:], reads=[Bob[b]], writes=[Bscr])
            elif oc >= 16:
                f.op(A, lambda b=b: nc.scalar.copy(out=ob[b][:], in_=pp[b][:]), reads=[Bpp[b]], writes=[Bob[b]])
                f.dma(f.sp, scr["uT"][rows, cs], ob[b][:], reads=[Bob[b]], writes=[Bscr])
            else:
                f.op(A, lambda b=b: nc.scalar.activation(out=fs[b][:], in_=pp[b][:], func=AF.Sigmoid), reads=[Bpp[b]], writes=[Bfs[b]])
                f.op(V, lambda b=b, hc=hc: nc.vector.tensor_scalar(fs[b][:], fs[b][:], oml[:, hc:hc + 1], lb[:, hc:hc + 1], ALU.mult, ALU.add),
                     reads=[Bc], writes=[Bfs[b]])
                f.op(V, lambda b=b: nc.vector.tensor_scalar(ob[b][:], fs[b][:], -1.0, 1.0, ALU.mult, ALU.add), reads=[Bfs[b]], writes=[Bob[b]])
                f.dma(f.sp, scr["kT"][rows, cs], ob[b][:], reads=[Bob[b]], writes=[Bscr])
                f.op(V, lambda b=b: nc.vector.tensor_scalar(lf[b][:], fs[b][:], 1e-6, None, ALU.max), reads=[Bfs[b]], writes=[Blf[b]])
                f.op(A, lambda b=b: nc.scalar.activation(out=lf[b][:], in_=lf[b][:], func=AF.Ln), reads=[Blf[b]], writes=[Blf[b]])
                f.dma(f.sp, scr["lfT"][rows, cs], lf[b][:], reads=[Blf[b]], writes=[Bscr])
        for s in range(4):
            b = s % 2
            for kc in range(KC):
                f.op(P_, lambda kc=kc, s=s, b=b: nc.tensor.matmul(pv[b][:], hT[:, kc, s * 128:(s + 1) * 128], Win[:, kc, 1024:1536],
                                                                 start=(kc == 0), stop=(kc == KC - 1)),
                     reads=[BW[2], Bh], writes=[Bpv[b]])
            f.op(A, lambda b=b: nc.scalar.copy(out=vt[b][:], in_=pv[b][:]), reads=[Bpv[b]], writes=[Bvt[b]])
            f.dma(f.sp, scr["vtok"][t * TT + s * 128:t * TT + (s + 1) * 128, 0:512], vt[b][:], reads=[Bvt[b]], writes=[Bscr])
    barrier(f)
    sc.close()


def hgrn_core_phase(f, hnw_d, scr, NT, L):
    nc = f.nc
    sc = Scope(nc)
    C = Consts(f, sc)
    mk = lambda n, shp, dt=F32: sc.sb(uname(n), shp, dt)
    NCH = L // 64
    NB = L // 128
    hnw = mk("hnw", [128, 1])
    onesL = mk("onesL", [128, L])
    qh = mk("qh", [128, L], BF16)
    kh = mk("kh", [128, L], BF16)
    gh = mk("gh", [128, L], BF16)
    lfh = mk("lfh", [128, L])
    Bcs = mk("Bcs", [128, L])
    dif = mk("dif", [128, L])
    eq = mk("eq", [128, L])
    ek = mk("ek", [128, L])
    qt = mk("qt", [128, L], BF16)
    kt = mk("kt", [128, L], BF16)
    bprev = mk("bprev", [128, NCH])
    sca = mk("sca", [128, 3, NCH])
    vb_ = [mk("vblk", [128, 128], BF16) for _ in range(2)]
    ktok = [mk("ktokh", [128, 128], BF16) for _ in range(2)]
    attnT = [mk("attnTh", [128, 128], BF16) for _ in range(2)]
    S = mk("Sh", [128, 128])
    St = mk("Sth", [128, 128], BF16)
    dSs = mk("dSs", [128, 128])
    sqo = mk("sqoh", [128, 128], BF16)
    rs = mk("rsh", [128, 128])
    o32 = mk("o32h", [128, 128])
    yo = [mk("yoh", [128, 128], BF16) for _ in range(2)]
    pat = [sc.ps(uname("pat"), [128, 512])[:, 0:128] for _ in range(2)]
    ptr = [sc.ps(uname("ptrh"), [128, 1024], BF16)[:, 0:128] for _ in range(2)]
    po = [sc.ps(uname("po"), [128, 512])[:, 0:128] for _ in range(2)]
    pds = sc.ps(uname("pds"), [128, 512])[:, 0:128]
    pn = sc.ps(uname("pnh"), [128, 512])[:, 0:128]
    names = "c q k g lf B dif eq ek qt kt bp sca S St dSs sqo rs o32 pds pn scr"
    Bf = {n: Buf(n) for n in names.split()}
    for n in ["v", "ktok", "attnT", "yo", "pat", "ptr", "po"]:
        Bf[n + "0"] = Buf(); Bf[n + "1"] = Buf()
    g = lambda *ns: [Bf[n] for n in ns]
    V, A, P_ = f.dve, f.act, f.pe
    f.dma(f.sp, hnw[:], hnw_d.rearrange("(p o) -> p o", o=1), writes=g("c"))
    f.op(V, lambda: nc.vector.memset(onesL[:], 1.0), writes=g("c"))
    for sq_ in range(NT // L):
        s0 = sq_ * L
        for h in range(HH):
            rows = slice(h * 128, (h + 1) * 128)
            f.dma(f.sp, qh[:], scr["qT"][rows, s0:s0 + L], writes=g("q"))
            f.dma(f.sp, kh[:], scr["kT"][rows, s0:s0 + L], writes=g("k"))
            f.dma(f.sp, gh[:], scr["gT"][rows, s0:s0 + L], writes=g("g"))
            f.dma(f.sp, lfh[:], scr["lfT"][rows, s0:s0 + L], writes=g("lf"))
            f.op(V, lambda: nc.vector.tensor_tensor_scan(out=Bcs[:], data0=onesL[:], data1=lfh[:], initial=0.0, op0=ALU.mult, op1=ALU.add),
                 reads=g("lf", "c"), writes=g("B"))
            B3 = Bcs[:].rearrange("p (c s) -> p c s", s=64)
            bmid = B3[:, :, 31]
            blast = B3[:, :, 63]
            f.op(V, lambda: nc.vector.memset(bprev[:, 0:1], 0.0), writes=g("bp"))
            f.op(V, lambda: nc.vector.tensor_copy(out=bprev[:, 1:NCH], in_=B3[:, 0:NCH - 1, 63]), reads=g("B"), writes=g("bp"))
            f.op(V, lambda: nc.vector.tensor_tensor(out=sca[:, 0, :], in0=blast, in1=bprev[:], op=ALU.subtract), reads=g("B", "bp"), writes=g("sca"))
            f.op(V, lambda: nc.vector.tensor_tensor(out=sca[:, 1, :], in0=blast, in1=bmid, op=ALU.subtract), reads=g("B"), writes=g("sca"))
            f.op(V, lambda: nc.vector.tensor_tensor(out=sca[:, 2, :], in0=bmid, in1=bprev[:], op=ALU.subtract), reads=g("B", "bp"), writes=g("sca"))
            f.op(A, lambda: nc.scalar.activation(out=sca[:], in_=sca[:], func=AF.Exp), reads=g("sca"), writes=g("sca"))
            f.op(V, lambda: nc.vector.tensor_tensor(out=dif[:].rearrange("p (c s) -> p c s", s=64), in0=B3,
                                                    in1=bmid.unsqueeze(2).to_broadcast([128, NCH, 64]), op=ALU.subtract),
                 reads=g("B"), writes=g("dif"))
            f.op(A, lambda: nc.scalar.activation(out=eq[:], in_=dif[:], func=AF.Exp), reads=g("dif"), writes=g("eq"))
            f.op(A, lambda: nc.scalar.activation(out=ek[:], in_=dif[:], func=AF.Exp, scale=-1.0), reads=g("dif"), writes=g("ek"))
            f.op(V, lambda: nc.vector.tensor_tensor(out=qt[:], in0=qh[:], in1=eq[:], op=ALU.mult), reads=g("q", "eq"), writes=g("qt"))
            f.op(V, lambda: nc.vector.tensor_tensor(out=kt[:], in0=kh[:], in1=ek[:], op=ALU.mult), reads=g("k", "ek"), writes=g("kt"))
            f.op(V, lambda: nc.vector.memset(S[:], 0.0), writes=g("S"))
            for blk in range(NB):
                b = blk % 2
                bs = slice(blk * 128, (blk + 1) * 128)
                sb_ = str(b)
                f.dma(f.sp, vb_[b][:], scr["vtok"][s0 + blk * 128:s0 + (blk + 1) * 128, h * 128:(h + 1) * 128], writes=g("v" + sb_))
                f.op(P_, lambda b=b, bs=bs: nc.tensor.matmul(pat[b][:], kt[:, bs], qt[:, bs], start=True, stop=True), reads=g("kt", "qt"), writes=g("pat" + sb_))
                f.op(V, lambda b=b: nc.vector.tensor_tensor(out=attnT[b][:], in0=pat[b][:], in1=C.tri[:], op=ALU.mult),
                     reads=g("pat" + sb_) + [C.B], writes=g("attnT" + sb_))
                f.op(P_, lambda b=b, bs=bs: nc.tensor.transpose(ptr[b][:], kt[:, bs], C.ident_bf[:]), reads=g("kt") + [C.B], writes=g("ptr" + sb_))
                f.op(A, lambda b=b: nc.scalar.copy(out=ktok[b][:], in_=ptr[b][:]), reads=g("ptr" + sb_), writes=g("ktok" + sb_))
                f.op(P_, lambda b=b: nc.tensor.matmul(po[b][:], vb_[b][:], attnT[b][:], start=True, stop=False),
                     reads=g("v" + sb_, "attnT" + sb_), writes=g("po" + sb_))
                for c in range(2):
                    ci = blk * 2 + c
                    r = slice(64 * c, 64 * c + 64)
                    cols = slice(blk * 128 + 64 * c, blk * 128 + 64 * c + 64)
                    f.op(V, lambda ci=ci: nc.vector.tensor_scalar(St[:], S[:], sca[:, 2, ci:ci + 1], None, ALU.mult), reads=g("S", "sca"), writes=g("St"))
                    f.op(P_, lambda b=b, r=r, cols=cols, c=c: nc.tensor.matmul(po[b][:, r], St[:], qt[:, cols], start=False, stop=(c == 1)),
                         reads=g("St", "qt"), writes=g("po" + sb_))
                    f.op(P_, lambda b=b, r=r: nc.tensor.matmul(pds[:], ktok[b][r, :], vb_[b][r, :], start=True, stop=True),
                         reads=g("ktok" + sb_, "v" + sb_), writes=g("pds"))
                    f.op(V, lambda ci=ci: nc.vector.tensor_scalar(dSs[:], pds[:], sca[:, 1, ci:ci + 1], None, ALU.mult), reads=g("pds", "sca"), writes=g("dSs"))
                    f.op(V, lambda ci=ci: nc.vector.scalar_tensor_tensor(out=S[:], in0=S[:], scalar=sca[:, 0, ci:ci + 1], in1=dSs[:],
                                                                       op0=ALU.mult, op1=ALU.add), reads=g("dSs", "sca"), writes=g("S"))
                f.op(A, lambda b=b: nc.scalar.activation(out=sqo[:], in_=po[b][:], func=AF.Square), reads=g("po" + sb_), writes=g("sqo"))
                f.op(P_, lambda: nc.tensor.matmul(pn[:], C.ones_bf[:], sqo[:], start=True, stop=True), reads=g("sqo") + [C.B], writes=g("pn"))
                f.op(A, lambda: nc.scalar.activation(out=rs[:], in_=pn[:], func=AF.Sqrt, bias=C.eps[:], scale=1.0 / 128), reads=g("pn") + [C.B], writes=g("rs"))
                f.op(V, lambda: nc.vector.reciprocal(out=rs[:], in_=rs[:]), reads=g("rs"), writes=g("rs"))
                f.op(V, lambda b=b: nc.vector.scalar_tensor_tensor(out=o32[:], in0=po[b][:], scalar=hnw[:, 0:1], in1=rs[:], op0=ALU.mult, op1=ALU.mult),
                     reads=g("po" + sb_, "rs", "c"), writes=g("o32"))
                f.op(V, lambda b=b, bs=bs: nc.vector.tensor_tensor(out=yo[b][:], in0=o32[:], in1=gh[:, bs], op=ALU.mult), reads=g("o32", "g"), writes=g("yo" + sb_))
                f.dma(f.sp, scr["yT"][h * 128:(h + 1) * 128, s0 + blk * 128:s0 + (blk + 1) * 128], yo[b][:], reads=g("yo" + sb_), writes=g("scr"))
    barrier(f)
    sc.close()


PI = 3.14159265358979


def s5_phase(f, p, j, scr, NT, L):
    nc = f.nc
    sc = Scope(nc)
    C = Consts(f, sc)
    mk = lambda n, shp, dt=F32: sc.sb(uname(n), shp, dt)
    V, A, P_ = f.dve, f.act, f.pe
    NS = 16
    NCH = NT // 64
    CPS = L // 64
    NSEQ = NT // L
    Bp = Buf("prep")
    gp = [Bp]
    ar = mk("ar", [128, NS]); ai = mk("ai", [128, NS]); nai = mk("nai", [128, NS])
    pwr = mk("pwr", [128, NS, 64]); pwi = mk("pwi", [128, NS, 64]); npwi = mk("npwi", [128, NS, 64])
    a64r = mk("a64r", [128, NS]); a64i = mk("a64i", [128, NS]); na64i = mk("na64i", [128, NS])
    Btab = [mk("Btab", [128, NS, 128], BF16) for _ in range(2)]
    TCre = mk("TCre", [128, NS, 128], BF16); TCimn = mk("TCimn", [128, NS, 128], BF16)
    dv = mk("dvec", [128, 4])
    Wglu = mk("Wglu", [128, 4, 512], BF16)
    scp = Scope(nc)
    mkp = lambda n, shp, dt=F32: scp.sb(uname(n), shp, dt)
    are = mkp("are", [128, NS]); aim = mkp("aim", [128, NS]); dtl = mkp("dtl", [128, NS])
    mag = mkp("mag", [128, NS]); ang = mkp("ang", [128, NS]); ang2 = mkp("ang2", [128, NS]); kk = mkp("kk", [128, NS]); tmpa = mkp("tmpa", [128, NS])
    cre = mkp("cre", [128, NS]); cim = mkp("cim", [128, NS]); den = mkp("den", [128, NS]); zr = mkp("zr", [128, NS])
    a2r = mkp("a2r", [128, NS]); a2i = mkp("a2i", [128, NS]); t3a = mkp("t3a", [128, NS, 32])
    Braw = [mkp("Braw", [128, NS, 128]) for _ in range(2)]
    Craw = [mkp("Craw", [128, NS, 128]) for _ in range(2)]
    c1 = mkp("c1", [128, NS, 128]); c2 = mkp("c2", [128, NS, 128])
    f.dma(f.sp, are[:], p["a_re"].rearrange("(t g) n -> (g n) t", g=2), writes=gp, allow_slow_non_contiguous=True)
    f.dma(f.sp, aim[:], p["a_im"].rearrange("(t g) n -> (g n) t", g=2), writes=gp, allow_slow_non_contiguous=True)
    ldv = p["log_dt"].rearrange("(t g) -> g t", g=2)
    for g2 in range(2):
        f.dma(f.sp, dtl[g2 * 64:(g2 + 1) * 64, :], ldv[g2:g2 + 1, :].to_broadcast([64, NS]), writes=gp, allow_slow_non_contiguous=True)
    op = lambda eng, fn: f.op(eng, fn, reads=gp, writes=gp)
    op(A, lambda: nc.scalar.activation(out=dtl[:], in_=dtl[:], func=AF.Exp))
    op(V, lambda: nc.vector.tensor_tensor(out=mag[:], in0=dtl[:], in1=are[:], op=ALU.mult))
    op(A, lambda: nc.scalar.activation(out=mag[:], in_=mag[:], func=AF.Exp))
    op(V, lambda: nc.vector.tensor_tensor(out=ang[:], in0=dtl[:], in1=aim[:], op=ALU.mult))
    op(V, lambda: nc.vector.tensor_scalar(ang2[:], ang[:], PI / 2, None, ALU.add))
    for a_ in (ang, ang2):
        op(V, lambda: nc.vector.memset(kk[:], 0.0))
        for m in (1, 3, 5, 7, 9):
            op(V, lambda a_=a_, m=m: nc.vector.tensor_scalar(tmpa[:], a_[:], m * PI, None, ALU.is_gt))
            op(V, lambda: nc.vector.tensor_tensor(out=kk[:], in0=kk[:], in1=tmpa[:], op=ALU.add))
        op(V, lambda a_=a_: nc.vector.scalar_tensor_tensor(out=a_[:], in0=kk[:], scalar=-2 * PI, in1=a_[:], op0=ALU.mult, op1=ALU.add))
        op(V, lambda a_=a_: nc.vector.tensor_scalar(a_[:], a_[:], PI, -PI, ALU.min, ALU.max))
    op(A, lambda: nc.scalar.activation(out=ai[:], in_=ang[:], func=AF.Sin))
    op(A, lambda: nc.scalar.activation(out=ar[:], in_=ang2[:], func=AF.Sin))
    op(V, lambda: nc.vector.tensor_tensor(out=ai[:], in0=ai[:], in1=mag[:], op=ALU.mult))
    op(V, lambda: nc.vector.tensor_tensor(out=ar[:], in0=ar[:], in1=mag[:], op=ALU.mult))
    op(V, lambda: nc.vector.tensor_scalar(nai[:], ai[:], -1.0, None, ALU.mult))
    op(V, lambda: nc.vector.tensor_tensor(out=den[:], in0=are[:], in1=are[:], op=ALU.mult))
    op(V, lambda: nc.vector.tensor_tensor(out=tmpa[:], in0=aim[:], in1=aim[:], op=ALU.mult))
    op(V, lambda: nc.vector.tensor_tensor(out=den[:], in0=den[:], in1=tmpa[:], op=ALU.add))
    op(V, lambda: nc.vector.reciprocal(out=den[:], in_=den[:]))
    op(V, lambda: nc.vector.tensor_scalar(zr[:], ar[:], -1.0, None, ALU.add))
    op(V, lambda: nc.vector.tensor_tensor(out=cre[:], in0=zr[:], in1=are[:], op=ALU.mult))
    op(V, lambda: nc.vector.tensor_tensor(out=tmpa[:], in0=ai[:], in1=aim[:], op=ALU.mult))
    op(V, lambda: nc.vector.tensor_tensor(out=cre[:], in0=cre[:], in1=tmpa[:], op=ALU.add))
    op(V, lambda: nc.vector.tensor_tensor(out=cre[:], in0=cre[:], in1=den[:], op=ALU.mult))
    op(V, lambda: nc.vector.tensor_tensor(out=cim[:], in0=ai[:], in1=are[:], op=ALU.mult))
    op(V, lambda: nc.vector.tensor_tensor(out=tmpa[:], in0=zr[:], in1=aim[:], op=ALU.mult))
    op(V, lambda: nc.vector.tensor_tensor(out=cim[:], in0=cim[:], in1=tmpa[:], op=ALU.subtract))
    op(V, lambda: nc.vector.tensor_tensor(out=cim[:], in0=cim[:], in1=den[:], op=ALU.mult))
    op(V, lambda: nc.vector.tensor_copy(out=pwr[:, :, 0], in_=ar[:]))
    op(V, lambda: nc.vector.tensor_copy(out=pwi[:, :, 0], in_=ai[:]))
    op(V, lambda: nc.vector.tensor_copy(out=a2r[:], in_=ar[:]))
    op(V, lambda: nc.vector.tensor_copy(out=a2i[:], in_=ai[:]))
    n = 1
    while n < 64:
        br = bc_i(a2r[:], n); bi = bc_i(a2i[:], n)
        op(V, lambda n=n, br=br: nc.vector.tensor_tensor(out=pwr[:, :, n:2 * n], in0=pwr[:, :, 0:n], in1=br, op=ALU.mult))
        op(V, lambda n=n, bi=bi: nc.vector.tensor_tensor(out=t3a[:, :, 0:n], in0=pwi[:, :, 0:n], in1=bi, op=ALU.mult))
        op(V, lambda n=n: nc.vector.tensor_tensor(out=pwr[:, :, n:2 * n], in0=pwr[:, :, n:2 * n], in1=t3a[:, :, 0:n], op=ALU.subtract))
        op(V, lambda n=n, bi=bi: nc.vector.tensor_tensor(out=pwi[:, :, n:2 * n], in0=pwr[:, :, 0:n], in1=bi, op=ALU.mult))
        op(V, lambda n=n, br=br: nc.vector.tensor_tensor(out=t3a[:, :, 0:n], in0=pwi[:, :, 0:n], in1=br, op=ALU.mult))
        op(V, lambda n=n: nc.vector.tensor_tensor(out=pwi[:, :, n:2 * n], in0=pwi[:, :, n:2 * n], in1=t3a[:, :, 0:n], op=ALU.add))
        op(V, lambda: nc.vector.tensor_tensor(out=tmpa[:], in0=a2r[:], in1=a2i[:], op=ALU.mult))
        op(V, lambda: nc.vector.tensor_tensor(out=kk[:], in0=a2i[:], in1=a2i[:], op=ALU.mult))
        op(V, lambda: nc.vector.tensor_tensor(out=a2r[:], in0=a2r[:], in1=a2r[:], op=ALU.mult))
        op(V, lambda: nc.vector.tensor_tensor(out=a2r[:], in0=a2r[:], in1=kk[:], op=ALU.subtract))
        op(V, lambda: nc.vector.tensor_scalar(a2i[:], tmpa[:], 2.0, None, ALU.mult))
        n *= 2
    op(V, lambda: nc.vector.tensor_scalar(npwi[:], pwi[:], -1.0, None, ALU.mult))
    op(V, lambda: nc.vector.tensor_copy(out=a64r[:], in_=pwr[:, :, 63]))
    op(V, lambda: nc.vector.tensor_copy(out=a64i[:], in_=pwi[:, :, 63]))
    op(V, lambda: nc.vector.tensor_scalar(na64i[:], a64i[:], -1.0, None, ALU.mult))
    for ri, key in enumerate(("b_re", "b_im")):
        op(V, lambda ri=ri: nc.vector.memset(Braw[ri][:], 0.0))
        for g_ in range(32):
            st_, p0, g2 = g_ // 2, (g_ % 8) * 16, g_ % 2
            f.dma(f.sp, Braw[ri][p0:p0 + 16, st_, g2 * 64:(g2 + 1) * 64], p[key][g_].rearrange("n q -> q n"), reads=gp, writes=gp,
                  allow_slow_non_contiguous=True)
        op(V, lambda ri=ri: nc.vector.tensor_copy(out=Btab[ri][:], in_=Braw[ri][:]))
    for ri, key in enumerate(("c_re", "c_im")):
        op(V, lambda ri=ri: nc.vector.memset(Craw[ri][:], 0.0))
        cv_ = p[key].rearrange("(t g) q n -> g n t q", g=2)
        for g2 in range(2):
            for t_ in range(NS):
                c0 = 32 * (t_ % 4) + 16 * g2
                f.dma(f.sp, Craw[ri][g2 * 64:(g2 + 1) * 64, t_, c0:c0 + 16], cv_[g2, :, t_, :], reads=gp, writes=gp,
                      allow_slow_non_contiguous=True)
    op(V, lambda: nc.vector.tensor_tensor(out=c1[:], in0=Craw[0][:], in1=bc_i(cre[:], 128), op=ALU.mult))
    op(V, lambda: nc.vector.tensor_tensor(out=c2[:], in0=Craw[1][:], in1=bc_i(cim[:], 128), op=ALU.mult))
    op(V, lambda: nc.vector.tensor_tensor(out=TCre[:], in0=c1[:], in1=c2[:], op=ALU.subtract))
    op(V, lambda: nc.vector.tensor_tensor(out=c1[:], in0=Craw[0][:], in1=bc_i(cim[:], 128), op=ALU.mult))
    op(V, lambda: nc.vector.tensor_tensor(out=c2[:], in0=Craw[1][:], in1=bc_i(cre[:], 128), op=ALU.mult))
    op(V, lambda: nc.vector.tensor_tensor(out=c1[:], in0=c1[:], in1=c2[:], op=ALU.add))
    op(V, lambda: nc.vector.tensor_scalar(TCimn[:], c1[:], -1.0, None, ALU.mult))
    f.dma(f.sp, dv[:], p["d"].rearrange("(c p) -> p c", p=128), writes=gp, allow_slow_non_contiguous=True)
    f.dma(f.pool, Wglu[:], p["w_glu"].rearrange("(kc p) f -> p kc f", p=128), writes=gp)
    barrier(f)
    scp.close()

    uT = mk("uTkt", [128, NT], BF16)
    yg = mk("yg", [128, 4, NT], BF16)
    bu = [mk("bu", [128, 64, NCH]) for _ in range(2)]
    hb = [[mk("hb", [128, 64, NCH], BF16) for _ in range(2)] for _ in range(4)]
    Hs = [mk("Hs", [128, NSEQ, CPS + 1]) for _ in range(2)]
    ysb = mk("ysb", [128, 512])
    x2 = mk("x2g", [128, 512]); zz = mk("zzg", [128, 512])
    pbu = [[sc.ps(uname("pbu"), [128, 512]) for _ in range(2)] for _ in range(2)]
    py = [sc.ps(uname("py"), [128, 512]) for _ in range(2)]
    pgl = [sc.ps(uname("pgl"), [128, 512]) for _ in range(2)]
    names = "u yg bur bui H ysb x2 zz scr"
    Bf = {n: Buf(n) for n in names.split()}
    for n in ["pbu0", "pbu1", "py", "pgl"]:
        Bf[n + "0"] = Buf(); Bf[n + "1"] = Buf()
    for sl in range(4):
        Bf["hbr%d" % sl] = Buf(); Bf["hbi%d" % sl] = Buf()
    g = lambda *ns: [Bf[n] for n in ns]
    bun = ("bur", "bui")
    for ot in range(4):
      f.dma(f.sp, uT[:], scr["uT"][ot * 128:(ot + 1) * 128, :], writes=g("u"))
      for sl in range(4):
        st = 4 * ot + sl
        arS, aiS, naiS = ar[:, st:st + 1], ai[:, st:st + 1], nai[:, st:st + 1]
        hbn = ("hbr%d" % sl, "hbi%d" % sl)
        for pc in range(NT // 512):
            b = pc % 2
            for ri in range(2):
                f.op(P_, lambda ri=ri, b=b, pc=pc: nc.tensor.matmul(pbu[ri][b][:], Btab[ri][:, st, :], uT[:, pc * 512:(pc + 1) * 512],
                                                                    start=True, stop=True),
                     reads=g("u") + gp, writes=g("pbu%d%d" % (ri, b)))
                f.op(A, lambda ri=ri, b=b, pc=pc: nc.scalar.copy(out=bu[ri][:, :, pc * 8:(pc + 1) * 8].rearrange("p s c -> p c s"),
                                                                in_=pbu[ri][b][:].rearrange("p (c s) -> p c s", s=64)),
                     reads=g("pbu%d%d" % (ri, b)), writes=g(bun[ri]))
        for tau in range(1, 64):
            f.op(V, lambda tau=tau: nc.vector.scalar_tensor_tensor(out=bu[0][:, tau, :], in0=bu[1][:, tau - 1, :], scalar=naiS, in1=bu[0][:, tau, :],
                                                                   op0=ALU.mult, op1=ALU.add), reads=g("bui") + gp, writes=g("bur"))
            f.op(V, lambda tau=tau: nc.vector.scalar_tensor_tensor(out=bu[1][:, tau, :], in0=bu[0][:, tau - 1, :], scalar=aiS, in1=bu[1][:, tau, :],
                                                                   op0=ALU.mult, op1=ALU.add), reads=g("bur") + gp, writes=g("bui"))
            f.op(V, lambda tau=tau: nc.vector.scalar_tensor_tensor(out=bu[0][:, tau, :], in0=bu[0][:, tau - 1, :], scalar=arS, in1=bu[0][:, tau, :],
                                                                   op0=ALU.mult, op1=ALU.add), reads=gp, writes=g("bur"))
            f.op(V, lambda tau=tau: nc.vector.scalar_tensor_tensor(out=bu[1][:, tau, :], in0=bu[1][:, tau - 1, :], scalar=arS, in1=bu[1][:, tau, :],
                                                                   op0=ALU.mult, op1=ALU.add), reads=gp, writes=g("bui"))
        f.op(V, lambda: nc.vector.memset(Hs[0][:, :, 0:1], 0.0), writes=g("H"))
        f.op(V, lambda: nc.vector.memset(Hs[1][:, :, 0:1], 0.0), writes=g("H"))
        lastr = bu[0][:, 63, :].rearrange("p (b c) -> p b c", c=CPS)
        lasti = bu[1][:, 63, :].rearrange("p (b c) -> p b c", c=CPS)
        A64r, A64i, NA64i = a64r[:, st:st + 1], a64i[:, st:st + 1], na64i[:, st:st + 1]
        for c in range(CPS):
            f.op(V, lambda c=c: nc.vector.scalar_tensor_tensor(out=Hs[0][:, :, c + 1], in0=Hs[1][:, :, c], scalar=NA64i, in1=lastr[:, :, c],
                                                               op0=ALU.mult, op1=ALU.add), reads=g("bur") + gp, writes=g("H"))
            f.op(V, lambda c=c: nc.vector.scalar_tensor_tensor(out=Hs[0][:, :, c + 1], in0=Hs[0][:, :, c], scalar=A64r, in1=Hs[0][:, :, c + 1],
                                                               op0=ALU.mult, op1=ALU.add), reads=gp, writes=g("H"))
            f.op(V, lambda c=c: nc.vector.scalar_tensor_tensor(out=Hs[1][:, :, c + 1], in0=Hs[0][:, :, c], scalar=A64i, in1=lasti[:, :, c],
                                                               op0=ALU.mult, op1=ALU.add), reads=g("bui") + gp, writes=g("H"))
            f.op(V, lambda c=c: nc.vector.scalar_tensor_tensor(out=Hs[1][:, :, c + 1], in0=Hs[1][:, :, c], scalar=A64r, in1=Hs[1][:, :, c + 1],
                                                               op0=ALU.mult, op1=ALU.add), reads=gp, writes=g("H"))
        Hr = Hs[0][:, :, 0:CPS]; Hi = Hs[1][:, :, 0:CPS]
        for tau in range(64):
            pr, pi_, npi = pwr[:, st, tau:tau + 1], pwi[:, st, tau:tau + 1], npwi[:, st, tau:tau + 1]
            br3 = bu[0][:, tau, :].rearrange("p (b c) -> p b c", c=CPS)
            bi3 = bu[1][:, tau, :].rearrange("p (b c) -> p b c", c=CPS)
            hr3 = hb[sl][0][:, tau, :].rearrange("p (b c) -> p b c", c=CPS)
            hi3 = hb[sl][1][:, tau, :].rearrange("p (b c) -> p b c", c=CPS)
            f.op(V, lambda: nc.vector.scalar_tensor_tensor(out=br3, in0=Hi, scalar=npi, in1=br3, op0=ALU.mult, op1=ALU.add), reads=g("H") + gp, writes=g("bur"))
            f.op(V, lambda: nc.vector.scalar_tensor_tensor(out=hr3, in0=Hr, scalar=pr, in1=br3, op0=ALU.mult, op1=ALU.add), reads=g("H", "bur") + gp, writes=g(hbn[0]))
            f.op(V, lambda: nc.vector.scalar_tensor_tensor(out=bi3, in0=Hr, scalar=pi_, in1=bi3, op0=ALU.mult, op1=ALU.add), reads=g("H") + gp, writes=g("bui"))
            f.op(V, lambda: nc.vector.scalar_tensor_tensor(out=hi3, in0=Hi, scalar=pr, in1=bi3, op0=ALU.mult, op1=ALU.add), reads=g("H", "bui") + gp, writes=g(hbn[1]))
      tpp = 512 // NCH
      for pc in range(64 * NCH // 512):
        b = pc % 2
        k = 0
        for sl in range(4):
            st = 4 * ot + sl
            for ri, TC in enumerate((TCre, TCimn)):
                hbf = hb[sl][ri][:].rearrange("p s c -> p (s c)")
                f.op(P_, lambda pc=pc, b=b, TC=TC, st=st, hbf=hbf, k=k: nc.tensor.matmul(py[b][:], TC[:, st, :], hbf[:, pc * 512:(pc + 1) * 512],
                                                                                      start=(k == 0), stop=(k == 7)),
                     reads=g("hbr%d" % sl, "hbi%d" % sl) + gp, writes=g("py%d" % b))
                k += 1
        uview = uT[:, :].rearrange("p (c s) -> p s c", s=64)[:, pc * tpp:(pc + 1) * tpp, :]
        ygview = yg[:, ot, :].rearrange("p (c s) -> p s c", s=64)[:, pc * tpp:(pc + 1) * tpp, :]
        y3 = ysb[:, :].rearrange("p (s c) -> p s c", c=NCH)
        z3 = zz[:, :].rearrange("p (s c) -> p s c", c=NCH)
        f.op(V, lambda uview=uview, y3=y3, b=b: nc.vector.scalar_tensor_tensor(out=y3, in0=uview, scalar=dv[:, ot:ot + 1],
                                                                              in1=py[b][:].rearrange("p (s c) -> p s c", c=NCH),
                                                                              op0=ALU.mult, op1=ALU.add),
             reads=g("py%d" % b, "u") + gp, writes=g("ysb"))
        f.op(A, lambda: nc.scalar.activation(out=x2[:], in_=ysb[:], func=AF.Square), reads=g("ysb"), writes=g("x2"))
        f.op(V, lambda: nc.vector.tensor_scalar(x2[:], x2[:], 0.044715, 1.0, ALU.mult, ALU.add), reads=g("x2"), writes=g("x2"))
        f.op(V, lambda: nc.vector.tensor_tensor(out=zz[:], in0=x2[:], in1=ysb[:], op=ALU.mult), reads=g("x2", "ysb"), writes=g("zz"))
        f.op(A, lambda: nc.scalar.activation(out=zz[:], in_=zz[:], func=AF.Sigmoid, scale=1.5957691216), reads=g("zz"), writes=g("zz"))
        f.op(V, lambda ygview=ygview, y3=y3, z3=z3: nc.vector.tensor_tensor(out=ygview, in0=y3, in1=z3, op=ALU.mult), reads=g("zz", "ysb"), writes=g("yg"))
    sgl = [mk("sgl", [128, 512]) for _ in range(2)]
    og = [mk("og", [128, 512], BF16) for _ in range(2)]
    Bs = [Buf(), Buf()]; Bo = [Buf(), Buf()]
    for t in range(NT // 512):
        cs = slice(t * 512, (t + 1) * 512)
        for oc in range(4):
            b = oc % 2
            for kc in range(4):
                f.op(P_, lambda kc=kc, oc=oc, b=b, cs=cs: nc.tensor.matmul(pgl[b][:], Wglu[:, kc, oc * 128:(oc + 1) * 128], yg[:, kc, cs],
                                                                        start=(kc == 0), stop=(kc == 3)),
                     reads=g("yg") + gp, writes=g("pgl%d" % b))
            f.op(A, lambda b=b: nc.scalar.activation(out=sgl[b][:], in_=pgl[b][:], func=AF.Sigmoid), reads=g("pgl%d" % b), writes=[Bs[b]])
            f.op(V, lambda b=b, oc=oc, cs=cs: nc.vector.tensor_tensor(out=og[b][:], in0=sgl[b][:], in1=yg[:, oc, cs], op=ALU.mult),
                 reads=[Bs[b]] + g("yg"), writes=[Bo[b]])
            f.dma(f.sp, scr["yT"][512 + oc * 128:512 + (oc + 1) * 128, cs], og[b][:], reads=[Bo[b]], writes=g("scr"))
    barrier(f)
    sc.close()


def ev_out_phase(f, X, wout_d, scr, NT):
    nc = f.nc
    sc = Scope(nc)
    mk = lambda n, shp, dt=F32: sc.sb(uname(n), shp, dt)
    Wo = mk("Wo", [128, KC, D], BF16)
    yt = mk("yt", [128, KC, TT], BF16)
    xt = mk("xt", [128, KC, TT])
    po = [sc.ps(uname("pout"), [128, TT]) for _ in range(2)]
    BW, By, Bx, Bs = Buf(), Buf(), Buf(), Buf()
    Bp = [Buf(), Buf()]
    wv = wout_d.rearrange("(kc p) d -> p kc d", p=128)
    f.dma(f.pool, Wo[:, 0:4, :], wv[:, 0:4, :], writes=[BW])
    f.dma(f.pool, Wo[:, 4:8, :], wv[:, 4:8, :], writes=[BW])
    Xv = X.rearrange("(c p) t -> p c t", p=128)
    yv = scr["yT"].rearrange("(c p) t -> p c t", p=128)
    for t in range(NT // TT):
        cs = slice(t * TT, (t + 1) * TT)
        f.dma(f.sp, yt[:], yv[:, :, cs], writes=[By])
        f.dma(f.sp, xt[:], Xv[:, :, cs], writes=[Bx])
        for dc in range(KC):
            b = dc % 2
            for kc in range(KC):
                f.op(f.pe, lambda dc=dc, kc=kc, b=b: nc.tensor.matmul(po[b][:], Wo[:, kc, dc * 128:(dc + 1) * 128], yt[:, kc, :],
                                                                      start=(kc == 0), stop=(kc == KC - 1)),
                     reads=[BW, By], writes=[Bp[b]])
            f.op(f.dve, lambda dc=dc, b=b: nc.vector.tensor_tensor(out=xt[:, dc, :], in0=po[b][:], in1=xt[:, dc, :], op=ALU.add),
                 reads=[Bp[b]], writes=[Bx])
        f.dma(f.sp, Xv[:, :, cs], xt[:], reads=[Bx], writes=[Bs])
    barrier(f)
    sc.close()


SEQ = 2048
NSEQ_CORE = 2
NCORES = 8
DEPTH = 4

_IN_SHAPES = {
    "ffn1_norm": [4, 1024], "ffn1_w_gate": [4, 1024, 2816], "ffn1_w_up": [4, 1024, 2816], "ffn1_w_down": [4, 2816, 1024],
    "mix_norm": [4, 1024], "ffn2_norm": [4, 1024], "ffn2_w_gate": [4, 1024, 2816], "ffn2_w_up": [4, 1024, 2816],
    "ffn2_w_down": [4, 2816, 1024], "ev_w_in": [2, 1024, 2560], "hg_lb_logits": [2, 512], "hg_norm_w": [2, 128],
    "s5_a_re": [2, 32, 64], "s5_a_im": [2, 32, 64], "s5_b_re": [2, 32, 64, 16], "s5_b_im": [2, 32, 64, 16],
    "s5_c_re": [2, 32, 16, 64], "s5_c_im": [2, 32, 16, 64], "s5_d": [2, 512], "s5_log_dt": [2, 32], "s5_w_glu": [2, 512, 512],
    "ev_w_out": [2, 1024, 1024], "od_w_in": [2, 1024, 4112], "gdn_conv_w": [2, 4, 3072], "gdn_a_log": [2, 8], "gdn_dt_bias": [2, 8],
    "gdn_norm_w": [2, 128], "od_w_out": [2, 1024, 1024], "final_norm": [1024],
}


def build_program(L=SEQ, nseq=NSEQ_CORE, depth=DEPTH):
    NT = L * nseq
    f = FW()
    nc = f.nc
    I = {k: nc.dram_tensor(k, list(shp), F32, kind="ExternalInput").ap() for k, shp in _IN_SHAPES.items()}
    xT = nc.dram_tensor("xT", [D, NT], F32, kind="ExternalInput").ap()
    oT = nc.dram_tensor("oT", [D, NT], F32, kind="ExternalOutput").ap()
    X = nc.dram_tensor("Xres", [D, NT], F32).ap()
    dt_ = lambda n, shp, t: nc.dram_tensor(n, shp, t).ap()
    scr = {
        "qT": dt_("s_qT", [D, NT], BF16), "kT": dt_("s_kT", [D, NT], BF16), "gT": dt_("s_gT", [D, NT], BF16),
        "ktok": dt_("s_ktok", [NT, D], BF16), "vtok": dt_("s_vtok", [NT, D], BF16), "bl": dt_("s_bl", [NT, 16], F32),
        "uT": dt_("s_uT", [512, NT], BF16), "lfT": dt_("s_lfT", [512, NT], F32), "yT": dt_("s_yT", [D, NT], BF16),
    }
    src = xT
    for layer in range(depth):
        j = layer // 2
        ffn_phase(f, src, X, I["ffn1_norm"][layer], I["ffn1_w_gate"][layer], I["ffn1_w_up"][layer], I["ffn1_w_down"][layer], NT)
        src = X
        if layer % 2 == 0:
            ev_proj_phase(f, X, I["mix_norm"][layer], I["ev_w_in"][j], I["hg_lb_logits"], j, scr, NT, L)
            hgrn_core_phase(f, I["hg_norm_w"][j], scr, NT, L)
            p = {"a_re": I["s5_a_re"][j], "a_im": I["s5_a_im"][j], "b_re": I["s5_b_re"][j], "b_im": I["s5_b_im"][j],
                 "c_re": I["s5_c_re"][j], "c_im": I["s5_c_im"][j], "d": I["s5_d"][j], "log_dt": I["s5_log_dt"][j], "w_glu": I["s5_w_glu"][j]}
            s5_phase(f, p, j, scr, NT, L)
            ev_out_phase(f, X, I["ev_w_out"][j], scr, NT)
        else:
            gdn_proj_phase(f, X, I["mix_norm"][layer], I["od_w_in"][j], I["gdn_conv_w"][j], I["gdn_a_log"][j], I["gdn_dt_bias"][j], scr, NT, L)
            gdn_core_phase(f, X, I["gdn_norm_w"][j], I["od_w_out"][j], scr, NT, L)
        ffn_phase(f, X, X, I["ffn2_norm"][layer], I["ffn2_w_gate"][layer], I["ffn2_w_up"][layer], I["ffn2_w_down"][layer], NT)
    Bo = final_phase(f, src, oT, I["final_norm"], NT)
    f.finish([Bo])
    return f


def kernel(**inputs):
    x = np.asarray(inputs["x"], dtype=np.float32)
    Bsz, L, Dm = x.shape
    nseq = Bsz // NCORES
    f = build_program(L, nseq, DEPTH)
    shared = {k: np.ascontiguousarray(np.asarray(inputs[k], dtype=np.float32)) for k in _IN_SHAPES}
    in_maps = []
    for c in range(NCORES):
        m = dict(shared)
        m["xT"] = np.ascontiguousarray(x[c * nseq:(c + 1) * nseq].reshape(nseq * L, Dm).T)
        in_maps.append(m)
    res = run_bass_kernel_spmd(f.nc, in_maps, core_ids=list(range(NCORES)))
    out = np.empty((Bsz, L, Dm), dtype=np.float32)
    for c in range(NCORES):
        oT = np.asarray(res.results[c]["oT"])
        out[c * nseq:(c + 1) * nseq] = oT.T.reshape(nseq, L, Dm)
    return out
```

```python
import numpy as np
import concourse.bass as bass
import concourse.mybir as mybir
from concourse.bass_utils import run_bass_kernel_spmd

F32 = mybir.dt.float32
BF16 = mybir.dt.bfloat16
AF = mybir.ActivationFunctionType
ALU = mybir.AluOpType

EPOCH = 16000


class Eng:
    def __init__(self, fw, e, name, self_sync=True):
        self.fw = fw
        self.e = e
        self.name = name
        self.self_sync = self_sync
        self.sem = fw.nc.alloc_semaphore(name + "_s0")
        self.cnt = 0
        self.nep = 0
        self.seen = {}
        self.total = 0

    def _wait(self, deps):
        for d in deps:
            if d is None:
                continue
            sem, val, own = d
            if own is self and not self.self_sync:
                continue
            k = id(sem)
            if self.seen.get(k, 0) >= val:
                continue
            self.e.wait_ge(sem, val)
            self.seen[k] = val

    def emit(self, fn, deps=()):
        self._wait(deps)
        if self.cnt >= EPOCH:
            self.nep += 1
            self.sem = self.fw.nc.alloc_semaphore("%s_s%d" % (self.name, self.nep))
            self.cnt = 0
        ins = fn()
        self.cnt += 1
        self.total += 1
        ins.then_inc(self.sem, 1)
        return (self.sem, self.cnt, self)

    def dma(self, out, in_, deps=(), **kw):
        fw = self.fw
        self._wait(deps)
        slot = fw.dma_rr % len(fw.dma_sems)
        fw.dma_rr += 1
        sem = fw.dma_sems[slot]
        prev = fw.dma_vals[slot]
        if prev > 0:
            k = id(sem)
            if self.seen.get(k, 0) < prev:
                self.e.wait_ge(sem, prev)
                self.seen[k] = prev
        ins = self.e.dma_start(out=out, in_=in_, **kw)
        val = prev + 16
        fw.dma_vals[slot] = val
        ins.then_inc(sem, 16)
        return (sem, val, None)


class Buf:
    def __init__(self, name=""):
        self.name = name
        self.w = None
        self.r = {}


class FW:
    def __init__(self, n_dma_sems=40):
        self.nc = bass.Bass("TRN2", target_bir_lowering=False)
        nc = self.nc
        self.pe = Eng(self, nc.tensor, "pe", self_sync=False)
        self.act = Eng(self, nc.scalar, "act")
        self.dve = Eng(self, nc.vector, "dve")
        self.pool = Eng(self, nc.gpsimd, "pool")
        self.sp = Eng(self, nc.sync, "sp")
        self.dma_sems = [nc.alloc_semaphore("dma%d" % i) for i in range(n_dma_sems)]
        self.dma_vals = [0] * n_dma_sems
        self.dma_rr = 0

    def _deps(self, reads, writes):
        deps = []
        for b in reads:
            if b.w is not None:
                deps.append(b.w)
        for b in writes:
            if b.w is not None:
                deps.append(b.w)
            deps.extend(b.r.values())
        return deps

    def _post(self, tok, reads, writes):
        for b in reads:
            k = id(tok[0])
            o = b.r.get(k)
            if o is None or o[1] < tok[1]:
                b.r[k] = tok
        for b in writes:
            b.w = tok
            b.r = {}

    def op(self, eng, fn, reads=(), writes=()):
        tok = eng.emit(fn, self._deps(reads, writes))
        self._post(tok, reads, writes)
        return tok

    def dma(self, eng, out, in_, reads=(), writes=(), **kw):
        tok = eng.dma(out, in_, self._deps(reads, writes), **kw)
        self._post(tok, reads, writes)
        return tok

    def finish(self, bufs):
        deps = []
        for b in bufs:
            if b.w is not None:
                deps.append(b.w)
        self.sp._wait(deps)


D = 1024
DFF = 2816
KC = D // 128
FC = DFF // 128
TT = 512
EPS = 1e-6


class Scope:
    def __init__(self, nc):
        self.nc = nc
        self.guards = []

    def sb(self, name, shape, dt):
        g = self.nc.sbuf_tensor(name, shape, dt)
        t = g.__enter__()
        self.guards.append(g)
        return t

    def ps(self, name, shape, dt=F32):
        g = self.nc.psum_tensor(name, shape, dt)
        t = g.__enter__()
        self.guards.append(g)
        return t

    def close(self):
        for g in reversed(self.guards):
            g.__exit__(None, None, None)
        self.guards = []


def barrier(f):
    engs = [f.pe, f.act, f.dve, f.pool, f.sp]
    toks = []
    for e in engs:
        if e.cnt > 0:
            toks.append((e.sem, e.cnt, None))
    for s, v in zip(f.dma_sems, f.dma_vals):
        if v > 0:
            toks.append((s, v, None))
    for e in engs:
        e._wait(toks)


_uid = [0]


def uname(p):
    _uid[0] += 1
    return "%s_%d" % (p, _uid[0])


def rms_stats(f, sc, xt, sqbuf, Bx, Bsq, ones_bf, eps_t, pss, Bpss, rstd, Brstd, ncols, inv_n):
    nc = f.nc
    Bsq = Bsq if isinstance(Bsq, list) else [Bsq]
    f.op(f.act, lambda: nc.scalar.activation(out=sqbuf, in_=xt, func=AF.Square), reads=[Bx], writes=Bsq)
    for c in range(KC):
        f.op(f.pe, lambda c=c: nc.tensor.matmul(pss, ones_bf, sqbuf[:, c, :], start=(c == 0), stop=(c == KC - 1)),
             reads=Bsq, writes=[Bpss])
    f.op(f.act, lambda: nc.scalar.activation(out=rstd, in_=pss, func=AF.Sqrt, bias=eps_t, scale=inv_n),
         reads=[Bpss], writes=[Brstd])
    f.op(f.dve, lambda: nc.vector.reciprocal(out=rstd, in_=rstd), reads=[Brstd], writes=[Brstd])


def load_w_bf16(f, eng, dst_sb, src_ap, bufs, piece):
    nc = f.nc
    A, Bn = src_ap.shape[1], src_ap.shape[2]
    toks = []
    i = 0
    for b0 in range(0, Bn, piece):
        b1 = min(Bn, b0 + piece)
        f.dma(eng, dst_sb[:, :, b0:b1], src_ap[:, :, b0:b1], writes=[bufs[i]])
        i += 1


def ffn_phase(f, src, dst, wn_d, wg_d, wu_d, wd_d, NT):
    nc = f.nc
    sc = Scope(nc)
    Wg = sc.sb(uname("Wg"), [128, KC, DFF], BF16)
    Wu = sc.sb(uname("Wu"), [128, KC, DFF], BF16)
    Wd = sc.sb(uname("Wd"), [128, FC, D], BF16)
    wn = sc.sb(uname("wn"), [128, KC], F32)
    xt = sc.sb(uname("xt"), [128, KC, TT], F32)
    hT = sc.sb(uname("hT"), [128, KC, TT], BF16)
    act = sc.sb(uname("act"), [128, FC, TT], BF16)
    rstd = sc.sb(uname("rstd"), [128, TT], F32)
    sg = [sc.sb(uname("sg"), [128, TT], F32) for _ in range(2)]
    ones_bf = sc.sb(uname("ones"), [128, 128], BF16)
    eps_t = sc.sb(uname("eps"), [128, 1], F32)
    pg = [sc.ps(uname("pg"), [128, TT]) for _ in range(2)]
    pu = [sc.ps(uname("pu"), [128, TT]) for _ in range(2)]
    pd = [sc.ps(uname("pd"), [128, TT]) for _ in range(2)]
    pss = sc.ps(uname("pss"), [128, TT])

    PW = 512
    npc = (DFF + PW - 1) // PW
    BWg = [Buf() for _ in range(npc)]
    BWu = [Buf() for _ in range(npc)]
    BWd = [Buf() for _ in range(FC)]
    Bc, Bx, Bh, Bpss, Brstd = Buf(), Buf(), Buf(), Buf(), Buf()
    Bact = [Buf() for _ in range(FC)]
    Bpg = [Buf(), Buf()]
    Bpu = [Buf(), Buf()]
    Bsg = [Buf(), Buf()]
    Bpd = [Buf(), Buf()]
    Bdst = Buf()

    f.op(f.dve, lambda: nc.vector.memset(ones_bf[:], 1.0), writes=[Bc])
    f.op(f.dve, lambda: nc.vector.memset(eps_t[:], EPS), writes=[Bc])
    f.dma(f.sp, wn[:], wn_d.rearrange("(c p) -> p c", p=128), writes=[Bc], allow_slow_non_contiguous=True)
    wgv = wg_d.rearrange("(kc p) f -> p kc f", p=128)
    wuv = wu_d.rearrange("(kc p) f -> p kc f", p=128)
    wdv = wd_d.rearrange("(fc p) d -> p fc d", p=128)
    for i in range(npc):
        b0, b1 = i * PW, min(DFF, (i + 1) * PW)
        f.dma(f.pool, Wg[:, :, b0:b1], wgv[:, :, b0:b1], writes=[BWg[i]])
        f.dma(f.pool, Wu[:, :, b0:b1], wuv[:, :, b0:b1], writes=[BWu[i]])
    for i in range(0, FC, 2):
        f.dma(f.pool, Wd[:, i:i + 2, :], wdv[:, i:i + 2, :], writes=[BWd[i], BWd[i + 1]])

    srcv = src.rearrange("(c p) t -> p c t", p=128)
    dstv = dst.rearrange("(c p) t -> p c t", p=128)
    for t in range(NT // TT):
        cs = slice(t * TT, (t + 1) * TT)
        f.dma(f.sp, xt[:], srcv[:, :, cs], writes=[Bx])
        rms_stats(f, sc, xt[:], act[:, 0:KC, :], Bx, Bact[0:KC], ones_bf[:], eps_t[:], pss[:], Bpss, rstd[:], Brstd, TT, 1.0 / D)
        for c in range(KC):
            f.op(f.dve, lambda c=c: nc.vector.scalar_tensor_tensor(out=hT[:, c, :], in0=xt[:, c, :], scalar=wn[:, c:c + 1],
                                                                 in1=rstd[:], op0=ALU.mult, op1=ALU.mult),
                 reads=[Bx, Brstd, Bc], writes=[Bh])
        for fc in range(FC):
            b = fc % 2
            wi = (fc * 128) // PW
            for kc in range(KC):
                f.op(f.pe, lambda kc=kc, fc=fc, b=b: nc.tensor.matmul(pg[b][:], Wg[:, kc, fc * 128:(fc + 1) * 128], hT[:, kc, :],
                                                                      start=(kc == 0), stop=(kc == KC - 1)),
                     reads=[BWg[wi], Bh], writes=[Bpg[b]])
            for kc in range(KC):
                f.op(f.pe, lambda kc=kc, fc=fc, b=b: nc.tensor.matmul(pu[b][:], Wu[:, kc, fc * 128:(fc + 1) * 128], hT[:, kc, :],
                                                                      start=(kc == 0), stop=(kc == KC - 1)),
                     reads=[BWu[wi], Bh], writes=[Bpu[b]])
            f.op(f.act, lambda b=b: nc.scalar.activation(out=sg[b][:], in_=pg[b][:], func=AF.Silu), reads=[Bpg[b]], writes=[Bsg[b]])
            f.op(f.dve, lambda b=b, fc=fc: nc.vector.tensor_tensor(out=act[:, fc, :], in0=pu[b][:], in1=sg[b][:], op=ALU.mult),
                 reads=[Bpu[b], Bsg[b]], writes=[Bact[fc]])
        for dc in range(KC):
            b = dc % 2
            for fc in range(FC):
                f.op(f.pe, lambda dc=dc, fc=fc, b=b: nc.tensor.matmul(pd[b][:], Wd[:, fc, dc * 128:(dc + 1) * 128], act[:, fc, :],
                                                                      start=(fc == 0), stop=(fc == FC - 1)),
                     reads=[BWd[fc], Bact[fc]], writes=[Bpd[b]])
            f.op(f.dve, lambda dc=dc, b=b: nc.vector.scalar_tensor_tensor(out=xt[:, dc, :], in0=pd[b][:], scalar=0.5, in1=xt[:, dc, :],
                                                                       op0=ALU.mult, op1=ALU.add),
                 reads=[Bpd[b]], writes=[Bx])
        f.dma(f.sp, dstv[:, :, cs], xt[:], reads=[Bx], writes=[Bdst])
    barrier(f)
    sc.close()


def final_phase(f, src, dst, wn_d, NT):
    nc = f.nc
    sc = Scope(nc)
    wn = sc.sb(uname("wn"), [128, KC], F32)
    xt = sc.sb(uname("xt"), [128, KC, TT], F32)
    sq = sc.sb(uname("sq"), [128, KC, TT], BF16)
    rstd = sc.sb(uname("rstd"), [128, TT], F32)
    ones_bf = sc.sb(uname("ones"), [128, 128], BF16)
    eps_t = sc.sb(uname("eps"), [128, 1], F32)
    pss = sc.ps(uname("pss"), [128, TT])
    Bc, Bx, Bsq, Bpss, Brstd, Bdst = [Buf() for _ in range(6)]
    f.op(f.dve, lambda: nc.vector.memset(ones_bf[:], 1.0), writes=[Bc])
    f.op(f.dve, lambda: nc.vector.memset(eps_t[:], EPS), writes=[Bc])
    f.dma(f.sp, wn[:], wn_d.rearrange("(c p) -> p c", p=128), writes=[Bc], allow_slow_non_contiguous=True)
    srcv = src.rearrange("(c p) t -> p c t", p=128)
    dstv = dst.rearrange("(c p) t -> p c t", p=128)
    for t in range(NT // TT):
        cs = slice(t * TT, (t + 1) * TT)
        f.dma(f.sp, xt[:], srcv[:, :, cs], writes=[Bx])
        rms_stats(f, sc, xt[:], sq[:], Bx, Bsq, ones_bf[:], eps_t[:], pss[:], Bpss, rstd[:], Brstd, TT, 1.0 / D)
        for c in range(KC):
            f.op(f.dve, lambda c=c: nc.vector.scalar_tensor_tensor(out=xt[:, c, :], in0=xt[:, c, :], scalar=wn[:, c:c + 1],
                                                                 in1=rstd[:], op0=ALU.mult, op1=ALU.mult),
                 reads=[Brstd, Bc], writes=[Bx])
        f.dma(f.sp, dstv[:, :, cs], xt[:], reads=[Bx], writes=[Bdst])
    barrier(f)
    sc.close()
    return Bdst


NEG = -30000.0


class Consts:
    def __init__(self, f, sc):
        nc = f.nc
        self.B = Buf()
        B = self.B
        mk = lambda n, shp, dt=F32: sc.sb(uname(n), shp, dt)
        self.ones32 = mk("ones32", [128, 128])
        self.ones_bf = mk("onesbf", [128, 128], BF16)
        self.tri = mk("tri", [128, 128])
        self.low = mk("low", [128, 128])
        self.ident = mk("ident", [128, 128])
        self.ident_bf = mk("identbf", [128, 128], BF16)
        self.negincT = mk("negincT", [128, 128])
        self.negstr = mk("negstr", [128, 128])
        self.strT01 = mk("strT01", [128, 128])
        self.bd = mk("bd", [128, 128])
        self.cind = mk("cind", [128, 2, 128])
        self.eps = mk("epsc", [128, 1])
        self.one = mk("onec", [128, 1])
        P = f.pool
        f.op(P, lambda: nc.gpsimd.memset(self.ones32[:], 1.0), writes=[B])
        f.op(P, lambda: nc.gpsimd.memset(self.ones_bf[:], 1.0), writes=[B])
        f.op(P, lambda: nc.gpsimd.memset(self.eps[:], EPS), writes=[B])
        f.op(P, lambda: nc.gpsimd.memset(self.one[:], 1.0), writes=[B])
        f.op(P, lambda: nc.gpsimd.affine_select(out=self.tri[:], in_=self.ones32[:], pattern=[[1, 128]], compare_op=ALU.is_ge,
                                                fill=0.0, base=0, channel_multiplier=-1), reads=[B], writes=[B])
        f.op(P, lambda: nc.gpsimd.memset(self.tri[0:64, 64:128], 0.0), writes=[B])
        f.op(P, lambda: nc.gpsimd.affine_select(out=self.low[:], in_=self.ones32[:], pattern=[[-1, 128]], compare_op=ALU.is_gt,
                                                fill=0.0, base=0, channel_multiplier=1), reads=[B], writes=[B])
        f.op(P, lambda: nc.gpsimd.memset(self.low[64:128, 0:64], 0.0), writes=[B])
        f.op(P, lambda: nc.gpsimd.affine_select(out=self.ident[:], in_=self.ones32[:], pattern=[[-1, 128]], compare_op=ALU.is_equal,
                                                fill=0.0, base=0, channel_multiplier=1), reads=[B], writes=[B])
        f.op(P, lambda: nc.gpsimd.tensor_copy(out=self.ident_bf[:], in_=self.ident[:]), reads=[B], writes=[B])
        f.op(P, lambda: nc.gpsimd.tensor_scalar(self.negincT[:], self.tri[:], -1.0, -NEG, ALU.add, ALU.mult), reads=[B], writes=[B])
        f.op(P, lambda: nc.gpsimd.tensor_scalar(self.negstr[:], self.low[:], -1.0, -NEG, ALU.add, ALU.mult), reads=[B], writes=[B])
        f.op(P, lambda: nc.gpsimd.tensor_tensor(out=self.strT01[:], in0=self.tri[:], in1=self.ident[:], op=ALU.subtract), reads=[B], writes=[B])
        f.op(P, lambda: nc.gpsimd.memset(self.bd[:], 0.0), writes=[B])
        f.op(P, lambda: nc.gpsimd.memset(self.bd[0:64, 0:64], 1.0), writes=[B])
        f.op(P, lambda: nc.gpsimd.memset(self.bd[64:128, 64:128], 1.0), writes=[B])
        f.op(P, lambda: nc.gpsimd.memset(self.cind[:], 0.0), writes=[B])
        f.op(P, lambda: nc.gpsimd.memset(self.cind[0:64, 0, :], 1.0), writes=[B])
        f.op(P, lambda: nc.gpsimd.memset(self.cind[64:128, 1, :], 1.0), writes=[B])


def bc_h(ap2, H=8):
    return ap2.unsqueeze(1).to_broadcast([ap2.shape[0], H, ap2.shape[1]])


def bc_i(ap2, n=128):
    return ap2.unsqueeze(2).to_broadcast([ap2.shape[0], ap2.shape[1], n])


GH = 8
ODIN = 4112


def gdn_proj_phase(f, X, wn_d, win_d, convw_d, alog_d, dtb_d, scr, NT, L):
    nc = f.nc
    sc = Scope(nc)
    C = Consts(f, sc)
    Win = sc.sb(uname("Win"), [128, KC, ODIN], BF16)
    wn = sc.sb(uname("wn"), [128, KC], F32)
    xt = sc.sb(uname("xt"), [128, KC, TT], F32)
    hT = sc.sb(uname("hT"), [128, KC, TT], BF16)
    sq = sc.sb(uname("sq"), [128, KC, TT], BF16)
    rstd = sc.sb(uname("rstd"), [128, TT], F32)
    cw = sc.sb(uname("cw"), [128, 24, 4], F32)
    halo = sc.sb(uname("halo"), [128, 24, 3], F32)
    pre = [sc.sb(uname("pre"), [128, TT + 3], F32) for _ in range(2)]
    cv = [sc.sb(uname("cv"), [128, TT], F32) for _ in range(2)]
    s32 = [sc.sb(uname("s32"), [128, TT], F32) for _ in range(2)]
    sq2 = [sc.sb(uname("sq2"), [128, TT], BF16) for _ in range(2)]
    r2 = [sc.sb(uname("r2"), [128, TT], F32) for _ in range(2)]
    ob = [sc.sb(uname("ob"), [128, TT], BF16) for _ in range(2)]
    tk = [sc.sb(uname("tk"), [128, 4, 128], BF16) for _ in range(2)]
    blt = sc.sb(uname("blt"), [128, 4, 16], F32)
    tmpb = sc.sb(uname("tmpb"), [128, 4, 8], F32)
    dtb = sc.sb(uname("dtb"), [128, 8], F32)
    negA = sc.sb(uname("negA"), [128, 8], F32)
    pp = [sc.ps(uname("pp"), [128, TT]) for _ in range(2)]
    pn = [sc.ps(uname("pn"), [128, TT]) for _ in range(2)]
    ptr = [sc.ps(uname("ptr"), [128, 4, 128], BF16) for _ in range(2)]
    pss = sc.ps(uname("pss"), [128, TT])
    pb = sc.ps(uname("pb"), [128, 4, 16])

    Bc, Bx, Bh, Bsq, Bpss, Brstd, Bhalo, Bpb, Bblt, Btmpb = [Buf() for _ in range(10)]
    NW = 9
    BW = [Buf() for _ in range(NW)]
    Bpp = [Buf(), Buf()]; Bpn = [Buf(), Buf()]; Bptr = [Buf(), Buf()]
    Bpre = [Buf(), Buf()]; Bcv = [Buf(), Buf()]; Bs32 = [Buf(), Buf()]; Bsq2 = [Buf(), Buf()]
    Bcvh = [[Buf(), Buf()], [Buf(), Buf()]]
    Br2 = [Buf(), Buf()]; Bob = [Buf(), Buf()]; Btk = [Buf(), Buf()]
    Bscr = Buf()

    f.dma(f.sp, wn[:], wn_d.rearrange("(c p) -> p c", p=128), writes=[Bc], allow_slow_non_contiguous=True)
    for j in range(4):
        f.dma(f.sp, cw[:, :, j], convw_d[j, :].rearrange("(c p) -> p c", p=128), writes=[Bc], allow_slow_non_contiguous=True)
    f.dma(f.sp, dtb[:], dtb_d.partition_broadcast(128), writes=[Bc])
    f.dma(f.sp, negA[:], alog_d.partition_broadcast(128), writes=[Bc])
    f.op(f.act, lambda: nc.scalar.activation(out=negA[:], in_=negA[:], func=AF.Exp), reads=[Bc], writes=[Bc])
    f.op(f.dve, lambda: nc.vector.tensor_scalar(negA[:], negA[:], -1.0, None, ALU.mult), reads=[Bc], writes=[Bc])
    winv = win_d.rearrange("(kc p) f -> p kc f", p=128)
    for i in range(NW):
        b0, b1 = i * 512, min(ODIN, (i + 1) * 512)
        f.dma(f.pool, Win[:, :, b0:b1], winv[:, :, b0:b1], writes=[BW[i]])

    Xv = X.rearrange("(c p) t -> p c t", p=128)
    qTv, kTv, gTv = scr["qT"], scr["kT"], scr["gT"]
    for t in range(NT // TT):
        cs = slice(t * TT, (t + 1) * TT)
        seq_start = (t * TT) % L == 0
        f.dma(f.sp, xt[:], Xv[:, :, cs], writes=[Bx])
        rms_stats(f, sc, xt[:], sq[:], Bx, Bsq, C.ones_bf[:], C.eps[:], pss[:], Bpss, rstd[:], Brstd, TT, 1.0 / D)
        for c in range(KC):
            f.op(f.dve, lambda c=c: nc.vector.scalar_tensor_tensor(out=hT[:, c, :], in0=xt[:, c, :], scalar=wn[:, c:c + 1],
                                                                 in1=rstd[:], op0=ALU.mult, op1=ALU.mult),
                 reads=[Bx, Brstd, Bc], writes=[Bh])
        if seq_start:
            f.op(f.dve, lambda: nc.vector.memset(halo[:], 0.0), writes=[Bhalo])
        HT = TT // 2

        def stA(oc):
            b = oc % 2
            wi = (oc * 128) // 512
            for kc in range(KC):
                f.op(f.pe, lambda kc=kc: nc.tensor.matmul(pp[b][:], Win[:, kc, oc * 128:(oc + 1) * 128], hT[:, kc, :],
                                                          start=(kc == 0), stop=(kc == KC - 1)),
                     reads=[BW[wi], Bh], writes=[Bpp[b]])
            f.op(f.act, lambda: nc.scalar.copy(out=pre[b][:, 3:TT + 3], in_=pp[b][:]), reads=[Bpp[b]], writes=[Bpre[b]])

        def stB(oc):
            b = oc % 2
            f.op(f.dve, lambda: nc.vector.tensor_copy(out=pre[b][:, 0:3], in_=halo[:, oc, :]), reads=[Bhalo], writes=[Bpre[b]])
            f.op(f.dve, lambda: nc.vector.tensor_copy(out=halo[:, oc, :], in_=pre[b][:, TT:TT + 3]), reads=[Bpre[b]], writes=[Bhalo])
            for j in range(4):
                for hf in range(2):
                    c0 = hf * HT
                    if j == 0:
                        f.op(f.dve, lambda c0=c0: nc.vector.tensor_scalar(cv[b][:, c0:c0 + HT], pre[b][:, c0:c0 + HT], cw[:, oc, 0:1], None, ALU.mult),
                             reads=[Bpre[b], Bc], writes=[Bcvh[b][hf]])
                    else:
                        f.op(f.dve, lambda c0=c0, j=j: nc.vector.scalar_tensor_tensor(out=cv[b][:, c0:c0 + HT], in0=pre[b][:, c0 + j:c0 + HT + j],
                                                                                      scalar=cw[:, oc, j:j + 1], in1=cv[b][:, c0:c0 + HT],
                                                                                      op0=ALU.mult, op1=ALU.add),
                             reads=[Bpre[b], Bc], writes=[Bcvh[b][hf]])
            if oc < 16:
                f.op(f.act, lambda: nc.scalar.activation(out=s32[b][:], in_=cv[b][:], func=AF.Silu), reads=Bcvh[b], writes=[Bs32[b]])
                f.op(f.act, lambda: nc.scalar.activation(out=sq2[b][:], in_=s32[b][:], func=AF.Square), reads=[Bs32[b]], writes=[Bsq2[b]])
            else:
                f.op(f.act, lambda: nc.scalar.activation(out=ob[b][:], in_=cv[b][:], func=AF.Silu), reads=Bcvh[b], writes=[Bob[b]])

        def stC1(oc):
            b = oc % 2
            if oc < 16:
                f.op(f.pe, lambda: nc.tensor.matmul(pn[b][:], C.ones_bf[:], sq2[b][:], start=True, stop=True), reads=[Bsq2[b], C.B], writes=[Bpn[b]])
                f.op(f.act, lambda: nc.scalar.activation(out=r2[b][:], in_=pn[b][:], func=AF.Sqrt, bias=C.eps[:], scale=1.0),
                     reads=[Bpn[b], C.B], writes=[Br2[b]])

        def stC2(oc):
            b = oc % 2
            hc = oc % 8
            if oc < 16:
                f.op(f.dve, lambda: nc.vector.reciprocal(out=r2[b][:], in_=r2[b][:]), reads=[Br2[b]], writes=[Br2[b]])
                scl = (128.0 ** -0.5) if oc < 8 else 1.0
                f.op(f.dve, lambda: nc.vector.scalar_tensor_tensor(out=ob[b][:], in0=s32[b][:], scalar=scl, in1=r2[b][:],
                                                                   op0=ALU.mult, op1=ALU.mult),
                     reads=[Bs32[b], Br2[b]], writes=[Bob[b]])
                dstT = qTv if oc < 8 else kTv
                f.dma(f.sp, dstT[hc * 128:(hc + 1) * 128, cs], ob[b][:], reads=[Bob[b]], writes=[Bscr])
            if oc >= 8:
                for s_ in range(4):
                    f.op(f.pe, lambda s_=s_: nc.tensor.transpose(ptr[b][:, s_, :], ob[b][:, s_ * 128:(s_ + 1) * 128], C.ident_bf[:]),
                         reads=[Bob[b], C.B], writes=[Bptr[b]])
                f.op(f.act, lambda: nc.scalar.copy(out=tk[b][:], in_=ptr[b][:]), reads=[Bptr[b]], writes=[Btk[b]])
                dsttok = scr["ktok"] if oc < 16 else scr["vtok"]
                f.dma(f.sp, dsttok[cs, hc * 128:(hc + 1) * 128].rearrange("(s p) d -> p s d", p=128), tk[b][:], reads=[Btk[b]], writes=[Bscr])

        NQ = 24
        stA(0); stA(1); stB(0)
        for oc in range(NQ):
            if oc + 2 < NQ:
                stA(oc + 2)
            stC1(oc)
            if oc + 1 < NQ:
                stB(oc + 1)
            stC2(oc)
        for oc in range(24, 32):
            b = oc % 2
            wi = (oc * 128) // 512
            for kc in range(KC):
                f.op(f.pe, lambda kc=kc, oc=oc, b=b: nc.tensor.matmul(pp[b][:], Win[:, kc, oc * 128:(oc + 1) * 128], hT[:, kc, :],
                                                                      start=(kc == 0), stop=(kc == KC - 1)),
                     reads=[BW[wi], Bh], writes=[Bpp[b]])
            f.op(f.act, lambda b=b: nc.scalar.activation(out=ob[b][:], in_=pp[b][:], func=AF.Silu), reads=[Bpp[b]], writes=[Bob[b]])
            hc = oc - 24
            f.dma(f.sp, gTv[hc * 128:(hc + 1) * 128, cs], ob[b][:], reads=[Bob[b]], writes=[Bscr])
        for s in range(4):
            for kc in range(KC):
                f.op(f.pe, lambda kc=kc, s=s: nc.tensor.matmul(pb[:, s, :], hT[:, kc, s * 128:(s + 1) * 128], Win[:, kc, 4096:4112],
                                                               start=(kc == 0), stop=(kc == KC - 1)),
                     reads=[BW[8], Bh], writes=[Bpb])
        f.op(f.act, lambda: nc.scalar.activation(out=blt[:, :, 0:8], in_=pb[:, :, 0:8], func=AF.Sigmoid), reads=[Bpb], writes=[Bblt])
        f.op(f.dve, lambda: nc.vector.tensor_tensor(out=tmpb[:], in0=pb[:, :, 8:16], in1=dtb[:].unsqueeze(1).to_broadcast([128, 4, 8]), op=ALU.add),
             reads=[Bpb, Bc], writes=[Btmpb])
        f.op(f.act, lambda: nc.scalar.activation(out=tmpb[:], in_=tmpb[:], func=AF.Exp), reads=[Btmpb], writes=[Btmpb])
        f.op(f.act, lambda: nc.scalar.activation(out=tmpb[:], in_=tmpb[:], func=AF.Ln, bias=C.one[:], scale=1.0), reads=[Btmpb, C.B], writes=[Btmpb])
        f.op(f.dve, lambda: nc.vector.tensor_tensor(out=blt[:, :, 8:16], in0=tmpb[:], in1=negA[:].unsqueeze(1).to_broadcast([128, 4, 8]), op=ALU.mult),
             reads=[Btmpb, Bc], writes=[Bblt])
        f.dma(f.sp, scr["bl"][cs, :].rearrange("(s p) c -> p s c", p=128), blt[:], reads=[Bblt], writes=[Bscr])
    barrier(f)
    sc.close()


def gdn_core_phase(f, X, gnw_d, wout_d, scr, NT, L):
    nc = f.nc
    sc = Scope(nc)
    C = Consts(f, sc)
    H = GH
    mk = lambda n, shp, dt=F32: sc.sb(uname(n), shp, dt)
    Wout = mk("Wout", [128, H, D], BF16)
    gnw = mk("gnw", [128, 1])
    qTb = mk("qTb", [128, H, 128], BF16)
    kTb = mk("kTb", [128, H, 128], BF16)
    ktokb = mk("ktokb", [128, H, 128], BF16)
    vtokb = mk("vtokb", [128, H, 128], BF16)
    gTb = mk("gTb", [128, H, 128], BF16)
    bl = mk("bl", [128, 16])
    Xs = mk("Xs", [128, H, 128])
    sm = mk("sm", [128, 32])
    sme = mk("sme", [128, 40])
    nbeta = mk("nbeta", [128, 8])
    tmp1 = mk("tmp1", [128, H, 128])
    tmp2 = mk("tmp2", [128, H, 128])
    LmT = mk("LmT", [128, H, 128])
    LmS = mk("LmS", [128, H, 128])
    WTN = mk("WTN", [128, H, 128])
    E = mk("E", [128, H, 128])
    Pm = [mk("Pm", [128, H, 128]) for _ in range(2)]
    PTm = [mk("PTm", [128, H, 128]) for _ in range(2)]
    TTm = [mk("TTm", [128, H, 128]) for _ in range(2)]
    TTb = mk("TTb", [128, H, 128], BF16)
    attnT = mk("attnT", [128, H, 128], BF16)
    qgT = mk("qgT", [128, H, 128], BF16)
    ktil = mk("ktil", [128, H, 128], BF16)
    vb = mk("vb", [128, H, 128])
    R = mk("R", [128, H, 128], BF16)
    vnew = mk("vnew", [128, H, 128], BF16)
    S = mk("S", [128, H, 128])
    Sb = mk("Sb", [128, H, 128], BF16)
    sqo = mk("sqo", [128, H, 128], BF16)
    rs = mk("rs", [128, H, 128])
    of32 = mk("of32", [128, H, 128])
    ofb = mk("ofb", [128, H, 128], BF16)
    xt = mk("xtb", [128, KC, 128])
    PA = sc.ps(uname("PA"), [128, H, 128])
    PB = sc.ps(uname("PB"), [128, H, 128])
    PC = sc.ps(uname("PC"), [128, H, 128])
    PD = sc.ps(uname("PD"), [128, H, 128])
    shared = "c W bl sm sme nb xt scr qin kin ktin vtin gin"
    grouped = "Xs t1 t2 LmT LmS WTN E TTb attnT qgT ktil vb R vnew S Sb sqo rs of32 ofb PA PB PC PD P0 P1 PT0 PT1 TT0 TT1"
    Bf = {n: Buf(n) for n in shared.split()}
    for n in grouped.split():
        for gi in range(2):
            Bf[n + "#%d" % gi] = Buf(n)
    g = lambda *ns: [Bf[n] for n in ns]

    def gg(gi, *ns):
        return [Bf[n + "#%d" % gi] for n in ns]
    gall = lambda *ns: [Bf[n + "#%d" % gi] for n in ns for gi in range(2)]
    Bc = Bf["c"]

    f.dma(f.sp, gnw[:], gnw_d.rearrange("(p o) -> p o", o=1), writes=[Bc])
    woutv = wout_d.rearrange("(h p) d -> p h d", p=128)
    f.dma(f.pool, Wout[:, 0:4, :], woutv[:, 0:4, :], writes=g("W"))
    f.dma(f.pool, Wout[:, 4:8, :], woutv[:, 4:8, :], writes=g("W"))
    Xv = X.rearrange("(c p) t -> p c t", p=128)
    V, A, P_, G_ = f.dve, f.act, f.pe, f.pool
    HS = [slice(0, 4), slice(4, 8)]

    nblk = NT // 128
    for blk in range(nblk):
        t0 = blk * 128
        ts = slice(t0, t0 + 128)
        if t0 % L == 0:
            for gi in range(2):
                f.op(G_, lambda gi=gi: nc.gpsimd.memset(S[:, HS[gi], :], 0.0), writes=gg(gi, "S"))
                f.op(G_, lambda gi=gi: nc.gpsimd.memset(Sb[:, HS[gi], :], 0.0), writes=gg(gi, "Sb"))
        f.dma(f.sp, qTb[:], scr["qT"][:, ts].rearrange("(h p) t -> p h t", p=128), writes=g("qin"))
        f.dma(f.sp, kTb[:], scr["kT"][:, ts].rearrange("(h p) t -> p h t", p=128), writes=g("kin"))
        f.dma(f.sp, gTb[:], scr["gT"][:, ts].rearrange("(h p) t -> p h t", p=128), writes=g("gin"))
        f.dma(f.sp, ktokb[:], scr["ktok"][ts, :].rearrange("p (h d) -> p h d", d=128), writes=g("ktin"))
        f.dma(f.sp, vtokb[:], scr["vtok"][ts, :].rearrange("p (h d) -> p h d", d=128), writes=g("vtin"))
        f.dma(f.sp, bl[:], scr["bl"][ts, :], writes=g("bl"))
        f.dma(f.sp, xt[:], Xv[:, :, ts], writes=g("xt"))
        beta = bl[:, 0:8]
        la = bl[:, 8:16]
        for gi in range(2):
            hs = HS[gi]
            f.op(G_, lambda hs=hs: nc.gpsimd.tensor_tensor(out=Xs[:, hs, :], in0=bc_i(la[:, hs]), in1=bc_h(C.tri[:], 4), op=ALU.mult),
                 reads=g("bl") + [C.B], writes=gg(gi, "Xs"))
            f.op(P_, lambda hs=hs: nc.tensor.matmul(PA[:, hs, :], C.ones32[:], Xs[:, hs, :], start=True, stop=True),
                 reads=gg(gi, "Xs") + [C.B], writes=gg(gi, "PA"))
        f.op(P_, lambda: nc.tensor.matmul(PD[:, 0, 0:8], C.tri[:], la, start=True, stop=True), reads=g("bl") + [C.B], writes=gg(0, "PD"))
        f.op(P_, lambda: nc.tensor.matmul(PD[:, 0, 8:16], C.bd[:], la, start=True, stop=True), reads=g("bl") + [C.B], writes=gg(0, "PD"))
        f.op(P_, lambda: nc.tensor.matmul(PD[:, 0, 16:24], C.cind[:, 0, :], la, start=True, stop=True), reads=g("bl") + [C.B], writes=gg(0, "PD"))
        f.op(P_, lambda: nc.tensor.matmul(PD[:, 0, 24:32], C.cind[:, 1, :], la, start=True, stop=True), reads=g("bl") + [C.B], writes=gg(0, "PD"))
        f.op(V, lambda: nc.vector.tensor_copy(out=sm[:], in_=PD[:, 0, 0:32]), reads=gg(0, "PD"), writes=g("sm"))
        gcol = sm[:, 0:8]
        f.op(A, lambda: nc.scalar.activation(out=sme[:, 0:8], in_=sm[:, 0:8], func=AF.Exp), reads=g("sm"), writes=g("sme"))
        f.op(V, lambda: nc.vector.tensor_tensor(out=sme[:, 8:16], in0=sm[:, 8:16], in1=sm[:, 0:8], op=ALU.subtract), reads=g("sm"), writes=g("sme"))
        f.op(A, lambda: nc.scalar.activation(out=sme[:, 8:16], in_=sme[:, 8:16], func=AF.Exp), reads=g("sme"), writes=g("sme"))
        f.op(A, lambda: nc.scalar.activation(out=sme[:, 16:32], in_=sm[:, 16:32], func=AF.Exp), reads=g("sm"), writes=g("sme"))
        f.op(V, lambda: nc.vector.scalar_tensor_tensor(out=sme[:, 32:40], in0=sme[:, 0:8], scalar=-1.0, in1=beta, op0=ALU.mult, op1=ALU.mult),
             reads=g("sme", "bl"), writes=g("sme"))
        f.op(V, lambda: nc.vector.tensor_scalar(nbeta[:], beta, -1.0, None, ALU.mult), reads=g("bl"), writes=g("nb"))
        for gi in range(2):
            hs = HS[gi]
            f.op(V, lambda hs=hs: nc.vector.tensor_tensor(out=tmp1[:, hs, :], in0=PA[:, hs, :], in1=bc_i(gcol[:, hs]), op=ALU.subtract),
                 reads=gg(gi, "PA") + g("sm"), writes=gg(gi, "t1"))
            f.op(V, lambda hs=hs: nc.vector.scalar_tensor_tensor(out=tmp2[:, hs, :], in0=tmp1[:, hs, :], scalar=-1.0, in1=bc_h(C.negstr[:], 4),
                                                                 op0=ALU.mult, op1=ALU.add),
                 reads=gg(gi, "t1") + [C.B], writes=gg(gi, "t2"))
            f.op(G_, lambda hs=hs: nc.gpsimd.tensor_tensor(out=tmp1[:, hs, :], in0=tmp1[:, hs, :], in1=bc_h(C.negincT[:], 4), op=ALU.add),
                 reads=gg(gi, "t2") + [C.B], writes=gg(gi, "t1"))
            f.op(A, lambda hs=hs: nc.scalar.activation(out=LmT[:, hs, :], in_=tmp1[:, hs, :], func=AF.Exp), reads=gg(gi, "t1"), writes=gg(gi, "LmT"))
            f.op(A, lambda hs=hs: nc.scalar.activation(out=LmS[:, hs, :], in_=tmp2[:, hs, :], func=AF.Exp), reads=gg(gi, "t2"), writes=gg(gi, "LmS"))
            f.op(A, lambda hs=hs: nc.scalar.activation(out=E[:, hs, :], in_=PA[:, hs, :], func=AF.Exp), reads=gg(gi, "PA"), writes=gg(gi, "E"))
            f.op(G_, lambda hs=hs: nc.gpsimd.tensor_tensor(out=LmS[:, hs, :], in0=LmS[:, hs, :], in1=bc_i(nbeta[:, hs]), op=ALU.mult),
                 reads=gg(gi, "LmS") + g("nb"), writes=gg(gi, "LmS"))
            f.op(G_, lambda hs=hs: nc.gpsimd.tensor_tensor(out=Xs[:, hs, :], in0=bc_i(beta[:, hs]), in1=bc_h(C.ident[:], 4), op=ALU.mult),
                 reads=g("bl") + [C.B], writes=gg(gi, "Xs"))
            f.op(P_, lambda hs=hs: nc.tensor.matmul(PB[:, hs, :], C.ones32[:], Xs[:, hs, :], start=True, stop=True),
                 reads=gg(gi, "Xs") + [C.B], writes=gg(gi, "PB"))
            f.op(G_, lambda hs=hs: nc.gpsimd.tensor_tensor(out=WTN[:, hs, :], in0=LmT[:, hs, :], in1=bc_h(C.strT01[:], 4), op=ALU.mult),
                 reads=gg(gi, "LmT") + [C.B], writes=gg(gi, "WTN"))
            f.op(V, lambda hs=hs: nc.vector.scalar_tensor_tensor(out=WTN[:, hs, :], in0=PB[:, hs, :], scalar=-1.0, in1=WTN[:, hs, :],
                                                                 op0=ALU.mult, op1=ALU.mult),
                 reads=gg(gi, "PB"), writes=gg(gi, "WTN"))
        for gi in range(2):
            hs = HS[gi]
            for h in range(4 * gi, 4 * gi + 4):
                f.op(P_, lambda h=h: nc.tensor.matmul(PC[:, h, :], kTb[:, h, :], kTb[:, h, :], start=True, stop=True), reads=g("kin"), writes=gg(gi, "PC"))
            f.op(V, lambda hs=hs: nc.vector.tensor_tensor(out=Pm[0][:, hs, :], in0=PC[:, hs, :], in1=LmS[:, hs, :], op=ALU.mult),
                 reads=gg(gi, "PC", "LmS"), writes=gg(gi, "P0"))
            f.op(V, lambda hs=hs: nc.vector.tensor_tensor(out=PTm[0][:, hs, :], in0=PC[:, hs, :], in1=WTN[:, hs, :], op=ALU.mult),
                 reads=gg(gi, "PC", "WTN"), writes=gg(gi, "PT0"))
            for h in range(4 * gi, 4 * gi + 4):
                f.op(P_, lambda h=h: nc.tensor.matmul(PB[:, h, :], kTb[:, h, :], qTb[:, h, :], start=True, stop=True), reads=g("kin", "qin"), writes=gg(gi, "PB"))
            f.op(V, lambda hs=hs: nc.vector.tensor_tensor(out=attnT[:, hs, :], in0=PB[:, hs, :], in1=LmT[:, hs, :], op=ALU.mult),
                 reads=gg(gi, "PB", "LmT"), writes=gg(gi, "attnT"))
            f.op(G_, lambda hs=hs: nc.gpsimd.tensor_tensor(out=TTm[0][:, hs, :], in0=PTm[0][:, hs, :], in1=bc_h(C.ident[:], 4), op=ALU.add),
                 reads=gg(gi, "PT0") + [C.B], writes=gg(gi, "TT0"))
        for k in range(1, 6):
            cur, nxt = (k - 1) % 2, k % 2
            Pc, PTc, Pn, PTn = "P%d" % cur, "PT%d" % cur, "P%d" % nxt, "PT%d" % nxt
            TTc, TTn = "TT%d" % cur, "TT%d" % nxt
            for gi in range(2):
                hs = HS[gi]
                for h in range(4 * gi, 4 * gi + 4):
                    f.op(P_, lambda h=h: nc.tensor.matmul(PC[:, h, :], PTm[cur][:, h, :], Pm[cur][:, h, :], start=True, stop=True),
                         reads=gg(gi, Pc, PTc), writes=gg(gi, "PC"))
                f.op(A, lambda hs=hs: nc.scalar.copy(out=Pm[nxt][:, hs, :], in_=PC[:, hs, :]), reads=gg(gi, "PC"), writes=gg(gi, Pn))
                if k < 5:
                    for h in range(4 * gi, 4 * gi + 4):
                        f.op(P_, lambda h=h: nc.tensor.matmul(PB[:, h, :], Pm[cur][:, h, :], PTm[cur][:, h, :], start=True, stop=True),
                             reads=gg(gi, Pc, PTc), writes=gg(gi, "PB"))
                    f.op(A, lambda hs=hs: nc.scalar.copy(out=PTm[nxt][:, hs, :], in_=PB[:, hs, :]), reads=gg(gi, "PB"), writes=gg(gi, PTn))
            for gi in range(2):
                hs = HS[gi]
                for h in range(4 * gi, 4 * gi + 4):
                    f.op(P_, lambda h=h: nc.tensor.matmul(PA[:, h, :], Pm[nxt][:, h, :], TTm[cur][:, h, :], start=True, stop=True),
                         reads=gg(gi, Pn, TTc), writes=gg(gi, "PA"))
                f.op(V, lambda hs=hs: nc.vector.tensor_tensor(out=TTm[nxt][:, hs, :], in0=PA[:, hs, :], in1=TTm[cur][:, hs, :], op=ALU.add),
                     reads=gg(gi, "PA", TTc), writes=gg(gi, TTn))
        for gi in range(2):
            hs = HS[gi]
            f.op(A, lambda hs=hs: nc.scalar.copy(out=TTb[:, hs, :], in_=TTm[1][:, hs, :]), reads=gg(gi, "TT1"), writes=gg(gi, "TTb"))
            f.op(G_, lambda hs=hs: nc.gpsimd.tensor_tensor(out=qgT[:, hs, :], in0=qTb[:, hs, :], in1=E[:, hs, :], op=ALU.mult),
                 reads=g("qin") + gg(gi, "E"), writes=gg(gi, "qgT"))
            f.op(G_, lambda hs=hs: nc.gpsimd.tensor_tensor(out=ktil[:, hs, :], in0=ktokb[:, hs, :], in1=bc_i(sme[:, 8:16][:, hs]), op=ALU.mult),
                 reads=g("ktin", "sme"), writes=gg(gi, "ktil"))
            f.op(G_, lambda hs=hs: nc.gpsimd.tensor_tensor(out=vb[:, hs, :], in0=vtokb[:, hs, :], in1=bc_i(beta[:, hs]), op=ALU.mult),
                 reads=g("vtin", "bl"), writes=gg(gi, "vb"))
        for c in range(2):
            r = slice(64 * c, 64 * c + 64)
            for gi in range(2):
                hs = HS[gi]
                for h in range(4 * gi, 4 * gi + 4):
                    f.op(P_, lambda h=h: nc.tensor.matmul(PC[r, h, :], kTb[:, h, r], Sb[:, h, :], start=True, stop=True),
                         reads=g("kin") + gg(gi, "Sb"), writes=gg(gi, "PC"))
            for gi in range(2):
                for h in range(4 * gi, 4 * gi + 4):
                    f.op(V, lambda h=h: nc.vector.scalar_tensor_tensor(out=R[r, h, :], in0=PC[r, h, :], scalar=sme[r, 32 + h:33 + h], in1=vb[r, h, :],
                                                                       op0=ALU.mult, op1=ALU.add),
                         reads=gg(gi, "PC", "vb") + g("sme"), writes=gg(gi, "R"))
                for h in range(4 * gi, 4 * gi + 4):
                    f.op(P_, lambda h=h: nc.tensor.matmul(PB[r, h, :], TTb[r, h, r], R[r, h, :], start=True, stop=True),
                         reads=gg(gi, "TTb", "R"), writes=gg(gi, "PB"))
                f.op(A, lambda gi=gi: nc.scalar.copy(out=vnew[r, HS[gi], :], in_=PB[r, HS[gi], :]), reads=gg(gi, "PB"), writes=gg(gi, "vnew"))
            for gi in range(2):
                for h in range(4 * gi, 4 * gi + 4):
                    f.op(P_, lambda h=h: nc.tensor.matmul(PD[:, h, r], Sb[:, h, :], qgT[:, h, r], start=True, stop=False),
                         reads=gg(gi, "Sb", "qgT"), writes=gg(gi, "PD"))
                    f.op(P_, lambda h=h: nc.tensor.matmul(PD[:, h, r], vnew[r, h, :], attnT[r, h, r], start=False, stop=True),
                         reads=gg(gi, "vnew", "attnT"), writes=gg(gi, "PD"))
                for h in range(4 * gi, 4 * gi + 4):
                    f.op(P_, lambda h=h: nc.tensor.matmul(PA[:, h, :], ktil[r, h, :], vnew[r, h, :], start=True, stop=True),
                         reads=gg(gi, "ktil", "vnew"), writes=gg(gi, "PA"))
            for gi in range(2):
                for h in range(4 * gi, 4 * gi + 4):
                    f.op(V, lambda h=h: nc.vector.scalar_tensor_tensor(out=S[:, h, :], in0=S[:, h, :], scalar=sme[:, 16 + 8 * c + h:17 + 8 * c + h],
                                                                       in1=PA[:, h, :], op0=ALU.mult, op1=ALU.add),
                         reads=gg(gi, "PA") + g("sme"), writes=gg(gi, "S"))
                f.op(A, lambda gi=gi: nc.scalar.copy(out=Sb[:, HS[gi], :], in_=S[:, HS[gi], :]), reads=gg(gi, "S"), writes=gg(gi, "Sb"))
        for gi in range(2):
            hs = HS[gi]
            f.op(A, lambda hs=hs: nc.scalar.activation(out=sqo[:, hs, :], in_=PD[:, hs, :], func=AF.Square), reads=gg(gi, "PD"), writes=gg(gi, "sqo"))
            f.op(P_, lambda hs=hs: nc.tensor.matmul(PC[:, hs, :], C.ones_bf[:], sqo[:, hs, :], start=True, stop=True),
                 reads=gg(gi, "sqo") + [C.B], writes=gg(gi, "PC"))
            f.op(A, lambda hs=hs: nc.scalar.activation(out=rs[:, hs, :], in_=PC[:, hs, :], func=AF.Sqrt, bias=C.eps[:], scale=1.0 / 128),
                 reads=gg(gi, "PC") + [C.B], writes=gg(gi, "rs"))
            f.op(V, lambda hs=hs: nc.vector.reciprocal(out=rs[:, hs, :], in_=rs[:, hs, :]), reads=gg(gi, "rs"), writes=gg(gi, "rs"))
            f.op(V, lambda hs=hs: nc.vector.scalar_tensor_tensor(out=of32[:, hs, :], in0=PD[:, hs, :], scalar=gnw[:, 0:1], in1=rs[:, hs, :],
                                                                 op0=ALU.mult, op1=ALU.mult),
                 reads=gg(gi, "PD", "rs") + [Bc], writes=gg(gi, "of32"))
            f.op(G_, lambda hs=hs: nc.gpsimd.tensor_tensor(out=ofb[:, hs, :], in0=of32[:, hs, :], in1=gTb[:, hs, :], op=ALU.mult),
                 reads=gg(gi, "of32") + g("gin"), writes=gg(gi, "ofb"))
        for dc in range(KC):
            for h in range(H):
                f.op(P_, lambda dc=dc, h=h: nc.tensor.matmul(PB[:, dc, :], Wout[:, h, dc * 128:(dc + 1) * 128], ofb[:, h, :],
                                                            start=(h == 0), stop=(h == H - 1)),
                     reads=g("W") + gall("ofb"), writes=gg(dc // 4, "PB"))
        f.op(V, lambda: nc.vector.tensor_tensor(out=xt[:], in0=PB[:], in1=xt[:], op=ALU.add), reads=gall("PB"), writes=g("xt"))
        f.dma(f.sp, Xv[:, :, ts], xt[:], reads=g("xt"), writes=g("scr"))
    barrier(f)
    sc.close()


EVIN = 2560
HH = 4


def ev_proj_phase(f, X, wn_d, win_d, lbl_d, j, scr, NT, L):
    nc = f.nc
    sc = Scope(nc)
    C = Consts(f, sc)
    mk = lambda n, shp, dt=F32: sc.sb(uname(n), shp, dt)
    Win = mk("Win", [128, KC, EVIN], BF16)
    wn = mk("wn", [128, KC])
    xt = mk("xt", [128, KC, TT])
    hT = mk("hT", [128, KC, TT], BF16)
    sq = mk("sq", [128, KC, TT], BF16)
    rstd = mk("rstd", [128, TT])
    lg = mk("lg", [128, 2, 4])
    lb = mk("lb", [128, 4])
    oml = mk("oml", [128, 4])
    ob = [mk("ob", [128, TT], BF16) for _ in range(2)]
    fs = [mk("fs", [128, TT]) for _ in range(2)]
    lf = [mk("lf", [128, TT]) for _ in range(2)]
    vt = [mk("vt", [128, 512], BF16) for _ in range(2)]
    pp = [sc.ps(uname("pp"), [128, TT]) for _ in range(2)]
    pv = [sc.ps(uname("pv"), [128, 512]) for _ in range(2)]
    pss = sc.ps(uname("pss"), [128, TT])
    Bc, Bx, Bh, Bsq, Bpss, Brstd, Bscr = [Buf() for _ in range(7)]
    BW = [Buf() for _ in range(5)]
    Bpp = [Buf(), Buf()]; Bpv = [Buf(), Buf()]; Bob = [Buf(), Buf()]; Bfs = [Buf(), Buf()]; Blf = [Buf(), Buf()]; Bvt = [Buf(), Buf()]
    V, A, P_ = f.dve, f.act, f.pe

    f.dma(f.sp, wn[:], wn_d.rearrange("(c p) -> p c", p=128), writes=[Bc], allow_slow_non_contiguous=True)
    for l in range(2):
        f.dma(f.sp, lg[:, l, :], lbl_d[l, :].rearrange("(c p) -> p c", p=128), writes=[Bc], allow_slow_non_contiguous=True)
    if j == 0:
        f.op(V, lambda: nc.vector.memset(lb[:], 0.0), writes=[Bc])
    else:
        f.op(V, lambda: nc.vector.tensor_tensor(out=lb[:], in0=lg[:, 1, :], in1=lg[:, 0, :], op=ALU.subtract), reads=[Bc], writes=[Bc])
        f.op(A, lambda: nc.scalar.activation(out=lb[:], in_=lb[:], func=AF.Sigmoid), reads=[Bc], writes=[Bc])
    f.op(V, lambda: nc.vector.tensor_scalar(oml[:], lb[:], -1.0, 1.0, ALU.mult, ALU.add), reads=[Bc], writes=[Bc])
    winv = win_d.rearrange("(kc p) f -> p kc f", p=128)
    for i in range(5):
        f.dma(f.pool, Win[:, :, i * 512:(i + 1) * 512], winv[:, :, i * 512:(i + 1) * 512], writes=[BW[i]])
    Xv = X.rearrange("(c p) t -> p c t", p=128)
    for t in range(NT // TT):
        cs = slice(t * TT, (t + 1) * TT)
        f.dma(f.sp, xt[:], Xv[:, :, cs], writes=[Bx])
        rms_stats(f, sc, xt[:], sq[:], Bx, Bsq, C.ones_bf[:], C.eps[:], pss[:], Bpss, rstd[:], Brstd, TT, 1.0 / D)
        for c in range(KC):
            f.op(V, lambda c=c: nc.vector.scalar_tensor_tensor(out=hT[:, c, :], in0=xt[:, c, :], scalar=wn[:, c:c + 1],
                                                             in1=rstd[:], op0=ALU.mult, op1=ALU.mult),
                 reads=[Bx, Brstd, Bc], writes=[Bh])
        for oc in list(range(0, 8)) + list(range(12, 20)):
            b = oc % 2
            wi = oc // 4
            hc = oc % 4
            for kc in range(KC):
                f.op(P_, lambda kc=kc, oc=oc, b=b: nc.tensor.matmul(pp[b][:], Win[:, kc, oc * 128:(oc + 1) * 128], hT[:, kc, :],
                                                                    start=(kc == 0), stop=(kc == KC - 1)),
                     reads=[BW[wi], Bh], writes=[Bpp[b]])
            rows = slice(hc * 128, (hc + 1) * 128)
            if oc < 4 or 12 <= oc < 16:
                f.op(A, lambda b=b: nc.scalar.activation(out=ob[b][:], in_=pp[b][:], func=AF.Silu), reads=[Bpp[b]], writes=[Bob[b]])
                dst = scr["qT"] if oc < 4 else scr["gT"]
                f.dma(f.sp, dst[rows, cs], ob[b][:], reads=[Bob[b]], writes=[Bscr])
            elif oc >= 16:
                f.op(A, lambda b=b: nc.scalar.copy(out=ob[b][:], in_=pp[b][:]), reads=[Bpp[b]], writes=[Bob[b]])
                f.dma(f.sp, scr["uT"][rows, cs], ob[b][:], reads=[Bob[b]], writes=[Bscr])
            else:
                f.op(A, lambda b=b: nc.scalar.activation(out=fs[b][:], in_=pp[b][:], func=AF.Sigmoid), reads=[Bpp[b]], writes=[Bfs[b]])
                f.op(V, lambda b=b, hc=hc: nc.vector.tensor_scalar(fs[b][:], fs[b][:], oml[:, hc:hc + 1], lb[:, hc:hc + 1], ALU.mult, ALU.add),
                     reads=[Bc], writes=[Bfs[b]])
                f.op(V, lambda b=b: nc.vector.tensor_scalar(ob[b][:], fs[b][:], -1.0, 1.0, ALU.mult, ALU.add), reads=[Bfs[b]], writes=[Bob[b]])
                f.dma(f.sp, scr["kT"][rows, cs], ob[b][:], reads=[Bob[b]], writes=[Bscr])
                f.op(V, lambda b=b: nc.vector.tensor_scalar(lf[b][:], fs[b][:], 1e-6, None, ALU.max), reads=[Bfs[b]], writes=[Blf[b]])
                f.op(A, lambda b=b: nc.scalar.activation(out=lf[b][:], in_=lf[b][:], func=AF.Ln), reads=[Blf[b]], writes=[Blf[b]])
                f.dma(f.sp, scr["lfT"][rows, cs], lf[b][:], reads=[Blf[b]], writes=[Bscr])
        for s in range(4):
            b = s % 2
            for kc in range(KC):
                f.op(P_, lambda kc=kc, s=s, b=b: nc.tensor.matmul(pv[b][:], hT[:, kc, s * 128:(s + 1) * 128], Win[:, kc, 1024:1536],
                                                                 start=(kc == 0), stop=(kc == KC - 1)),
                     reads=[BW[2], Bh], writes=[Bpv[b]])
            f.op(A, lambda b=b: nc.scalar.copy(out=vt[b][:], in_=pv[b][:]), reads=[Bpv[b]], writes=[Bvt[b]])
            f.dma(f.sp, scr["vtok"][t * TT + s * 128:t * TT + (s + 1) * 128, 0:512], vt[b][:], reads=[Bvt[b]], writes=[Bscr])
    barrier(f)
    sc.close()


def hgrn_core_phase(f, hnw_d, scr, NT, L):
    nc = f.nc
    sc = Scope(nc)
    C = Consts(f, sc)
    mk = lambda n, shp, dt=F32: sc.sb(uname(n), shp, dt)
    NCH = L // 64
    NB = L // 128
    hnw = mk("hnw", [128, 1])
    onesL = mk("onesL", [128, L])
    qh = mk("qh", [128, L], BF16)
    kh = mk("kh", [128, L], BF16)
    gh = mk("gh", [128, L], BF16)
    lfh = mk("lfh", [128, L])
    Bcs = mk("Bcs", [128, L])
    dif = mk("dif", [128, L])
    eq = mk("eq", [128, L])
    ek = mk("ek", [128, L])
    qt = mk("qt", [128, L], BF16)
    kt = mk("kt", [128, L], BF16)
    bprev = mk("bprev", [128, NCH])
    sca = mk("sca", [128, 3, NCH])
    vb_ = [mk("vblk", [128, 128], BF16) for _ in range(2)]
    ktok = [mk("ktokh", [128, 128], BF16) for _ in range(2)]
    attnT = [mk("attnTh", [128, 128], BF16) for _ in range(2)]
    S = mk("Sh", [128, 128])
    St = mk("Sth", [128, 128], BF16)
    dSs = mk("dSs", [128, 128])
    sqo = mk("sqoh", [128, 128], BF16)
    rs = mk("rsh", [128, 128])
    o32 = mk("o32h", [128, 128])
    yo = [mk("yoh", [128, 128], BF16) for _ in range(2)]
    pat = [sc.ps(uname("pat"), [128, 512])[:, 0:128] for _ in range(2)]
    ptr = [sc.ps(uname("ptrh"), [128, 1024], BF16)[:, 0:128] for _ in range(2)]
    po = [sc.ps(uname("po"), [128, 512])[:, 0:128] for _ in range(2)]
    pds = sc.ps(uname("pds"), [128, 512])[:, 0:128]
    pn = sc.ps(uname("pnh"), [128, 512])[:, 0:128]
    names = "c q k g lf B dif eq ek qt kt bp sca S St dSs sqo rs o32 pds pn scr"
    Bf = {n: Buf(n) for n in names.split()}
    for n in ["v", "ktok", "attnT", "yo", "pat", "ptr", "po"]:
        Bf[n + "0"] = Buf(); Bf[n + "1"] = Buf()
    g = lambda *ns: [Bf[n] for n in ns]
    V, A, P_ = f.dve, f.act, f.pe
    f.dma(f.sp, hnw[:], hnw_d.rearrange("(p o) -> p o", o=1), writes=g("c"))
    f.op(V, lambda: nc.vector.memset(onesL[:], 1.0), writes=g("c"))
    for sq_ in range(NT // L):
        s0 = sq_ * L
        for h in range(HH):
            rows = slice(h * 128, (h + 1) * 128)
            f.dma(f.sp, qh[:], scr["qT"][rows, s0:s0 + L], writes=g("q"))
            f.dma(f.sp, kh[:], scr["kT"][rows, s0:s0 + L], writes=g("k"))
            f.dma(f.sp, gh[:], scr["gT"][rows, s0:s0 + L], writes=g("g"))
            f.dma(f.sp, lfh[:], scr["lfT"][rows, s0:s0 + L], writes=g("lf"))
            f.op(V, lambda: nc.vector.tensor_tensor_scan(out=Bcs[:], data0=onesL[:], data1=lfh[:], initial=0.0, op0=ALU.mult, op1=ALU.add),
                 reads=g("lf", "c"), writes=g("B"))
            B3 = Bcs[:].rearrange("p (c s) -> p c s", s=64)
            bmid = B3[:, :, 31]
            blast = B3[:, :, 63]
            f.op(V, lambda: nc.vector.memset(bprev[:, 0:1], 0.0), writes=g("bp"))
            f.op(V, lambda: nc.vector.tensor_copy(out=bprev[:, 1:NCH], in_=B3[:, 0:NCH - 1, 63]), reads=g("B"), writes=g("bp"))
            f.op(V, lambda: nc.vector.tensor_tensor(out=sca[:, 0, :], in0=blast, in1=bprev[:], op=ALU.subtract), reads=g("B", "bp"), writes=g("sca"))
            f.op(V, lambda: nc.vector.tensor_tensor(out=sca[:, 1, :], in0=blast, in1=bmid, op=ALU.subtract), reads=g("B"), writes=g("sca"))
            f.op(V, lambda: nc.vector.tensor_tensor(out=sca[:, 2, :], in0=bmid, in1=bprev[:], op=ALU.subtract), reads=g("B", "bp"), writes=g("sca"))
            f.op(A, lambda: nc.scalar.activation(out=sca[:], in_=sca[:], func=AF.Exp), reads=g("sca"), writes=g("sca"))
            f.op(V, lambda: nc.vector.tensor_tensor(out=dif[:].rearrange("p (c s) -> p c s", s=64), in0=B3,
                                                    in1=bmid.unsqueeze(2).to_broadcast([128, NCH, 64]), op=ALU.subtract),
                 reads=g("B"), writes=g("dif"))
            f.op(A, lambda: nc.scalar.activation(out=eq[:], in_=dif[:], func=AF.Exp), reads=g("dif"), writes=g("eq"))
            f.op(A, lambda: nc.scalar.activation(out=ek[:], in_=dif[:], func=AF.Exp, scale=-1.0), reads=g("dif"), writes=g("ek"))
            f.op(V, lambda: nc.vector.tensor_tensor(out=qt[:], in0=qh[:], in1=eq[:], op=ALU.mult), reads=g("q", "eq"), writes=g("qt"))
            f.op(V, lambda: nc.vector.tensor_tensor(out=kt[:], in0=kh[:], in1=ek[:], op=ALU.mult), reads=g("k", "ek"), writes=g("kt"))
            f.op(V, lambda: nc.vector.memset(S[:], 0.0), writes=g("S"))
            for blk in range(NB):
                b = blk % 2
                bs = slice(blk * 128, (blk + 1) * 128)
                sb_ = str(b)
                f.dma(f.sp, vb_[b][:], scr["vtok"][s0 + blk * 128:s0 + (blk + 1) * 128, h * 128:(h + 1) * 128], writes=g("v" + sb_))
                f.op(P_, lambda b=b, bs=bs: nc.tensor.matmul(pat[b][:], kt[:, bs], qt[:, bs], start=True, stop=True), reads=g("kt", "qt"), writes=g("pat" + sb_))
                f.op(V, lambda b=b: nc.vector.tensor_tensor(out=attnT[b][:], in0=pat[b][:], in1=C.tri[:], op=ALU.mult),
                     reads=g("pat" + sb_) + [C.B], writes=g("attnT" + sb_))
                f.op(P_, lambda b=b, bs=bs: nc.tensor.transpose(ptr[b][:], kt[:, bs], C.ident_bf[:]), reads=g("kt") + [C.B], writes=g("ptr" + sb_))
                f.op(A, lambda b=b: nc.scalar.copy(out=ktok[b][:], in_=ptr[b][:]), reads=g("ptr" + sb_), writes=g("ktok" + sb_))
                f.op(P_, lambda b=b: nc.tensor.matmul(po[b][:], vb_[b][:], attnT[b][:], start=True, stop=False),
                     reads=g("v" + sb_, "attnT" + sb_), writes=g("po" + sb_))
                for c in range(2):
                    ci = blk * 2 + c
                    r = slice(64 * c, 64 * c + 64)
                    cols = slice(blk * 128 + 64 * c, blk * 128 + 64 * c + 64)
                    f.op(V, lambda ci=ci: nc.vector.tensor_scalar(St[:], S[:], sca[:, 2, ci:ci + 1], None, ALU.mult), reads=g("S", "sca"), writes=g("St"))
                    f.op(P_, lambda b=b, r=r, cols=cols, c=c: nc.tensor.matmul(po[b][:, r], St[:], qt[:, cols], start=False, stop=(c == 1)),
                         reads=g("St", "qt"), writes=g("po" + sb_))
                    f.op(P_, lambda b=b, r=r: nc.tensor.matmul(pds[:], ktok[b][r, :], vb_[b][r, :], start=True, stop=True),
                         reads=g("ktok" + sb_, "v" + sb_), writes=g("pds"))
                    f.op(V, lambda ci=ci: nc.vector.tensor_scalar(dSs[:], pds[:], sca[:, 1, ci:ci + 1], None, ALU.mult), reads=g("pds", "sca"), writes=g("dSs"))
                    f.op(V, lambda ci=ci: nc.vector.scalar_tensor_tensor(out=S[:], in0=S[:], scalar=sca[:, 0, ci:ci + 1], in1=dSs[:],
                                                                       op0=ALU.mult, op1=ALU.add), reads=g("dSs", "sca"), writes=g("S"))
                f.op(A, lambda b=b: nc.scalar.activation(out=sqo[:], in_=po[b][:], func=AF.Square), reads=g("po" + sb_), writes=g("sqo"))
                f.op(P_, lambda: nc.tensor.matmul(pn[:], C.ones_bf[:], sqo[:], start=True, stop=True), reads=g("sqo") + [C.B], writes=g("pn"))
                f.op(A, lambda: nc.scalar.activation(out=rs[:], in_=pn[:], func=AF.Sqrt, bias=C.eps[:], scale=1.0 / 128), reads=g("pn") + [C.B], writes=g("rs"))
                f.op(V, lambda: nc.vector.reciprocal(out=rs[:], in_=rs[:]), reads=g("rs"), writes=g("rs"))
                f.op(V, lambda b=b: nc.vector.scalar_tensor_tensor(out=o32[:], in0=po[b][:], scalar=hnw[:, 0:1], in1=rs[:], op0=ALU.mult, op1=ALU.mult),
                     reads=g("po" + sb_, "rs", "c"), writes=g("o32"))
                f.op(V, lambda b=b, bs=bs: nc.vector.tensor_tensor(out=yo[b][:], in0=o32[:], in1=gh[:, bs], op=ALU.mult), reads=g("o32", "g"), writes=g("yo" + sb_))
                f.dma(f.sp, scr["yT"][h * 128:(h + 1) * 128, s0 + blk * 128:s0 + (blk + 1) * 128], yo[b][:], reads=g("yo" + sb_), writes=g("scr"))
    barrier(f)
    sc.close()


PI = 3.14159265358979


def s5_phase(f, p, j, scr, NT, L):
    nc = f.nc
    sc = Scope(nc)
    C = Consts(f, sc)
    mk = lambda n, shp, dt=F32: sc.sb(uname(n), shp, dt)
    V, A, P_ = f.dve, f.act, f.pe
    NS = 16
    NCH = NT // 64
    CPS = L // 64
    NSEQ = NT // L
    Bp = Buf("prep")
    gp = [Bp]
    ar = mk("ar", [128, NS]); ai = mk("ai", [128, NS]); nai = mk("nai", [128, NS])
    pwr = mk("pwr", [128, NS, 64]); pwi = mk("pwi", [128, NS, 64]); npwi = mk("npwi", [128, NS, 64])
    a64r = mk("a64r", [128, NS]); a64i = mk("a64i", [128, NS]); na64i = mk("na64i", [128, NS])
    Btab = [mk("Btab", [128, NS, 128], BF16) for _ in range(2)]
    TCre = mk("TCre", [128, NS, 128], BF16); TCimn = mk("TCimn", [128, NS, 128], BF16)
    dv = mk("dvec", [128, 4])
    Wglu = mk("Wglu", [128, 4, 512], BF16)
    scp = Scope(nc)
    mkp = lambda n, shp, dt=F32: scp.sb(uname(n), shp, dt)
    are = mkp("are", [128, NS]); aim = mkp("aim", [128, NS]); dtl = mkp("dtl", [128, NS])
    mag = mkp("mag", [128, NS]); ang = mkp("ang", [128, NS]); ang2 = mkp("ang2", [128, NS]); kk = mkp("kk", [128, NS]); tmpa = mkp("tmpa", [128, NS])
    cre = mkp("cre", [128, NS]); cim = mkp("cim", [128, NS]); den = mkp("den", [128, NS]); zr = mkp("zr", [128, NS])
    a2r = mkp("a2r", [128, NS]); a2i = mkp("a2i", [128, NS]); t3a = mkp("t3a", [128, NS, 32])
    Braw = [mkp("Braw", [128, NS, 128]) for _ in range(2)]
    Craw = [mkp("Craw", [128, NS, 128]) for _ in range(2)]
    c1 = mkp("c1", [128, NS, 128]); c2 = mkp("c2", [128, NS, 128])
    f.dma(f.sp, are[:], p["a_re"].rearrange("(t g) n -> (g n) t", g=2), writes=gp, allow_slow_non_contiguous=True)
    f.dma(f.sp, aim[:], p["a_im"].rearrange("(t g) n -> (g n) t", g=2), writes=gp, allow_slow_non_contiguous=True)
    ldv = p["log_dt"].rearrange("(t g) -> g t", g=2)
    for g2 in range(2):
        f.dma(f.sp, dtl[g2 * 64:(g2 + 1) * 64, :], ldv[g2:g2 + 1, :].to_broadcast([64, NS]), writes=gp, allow_slow_non_contiguous=True)
    op = lambda eng, fn: f.op(eng, fn, reads=gp, writes=gp)
    op(A, lambda: nc.scalar.activation(out=dtl[:], in_=dtl[:], func=AF.Exp))
    op(V, lambda: nc.vector.tensor_tensor(out=mag[:], in0=dtl[:], in1=are[:], op=ALU.mult))
    op(A, lambda: nc.scalar.activation(out=mag[:], in_=mag[:], func=AF.Exp))
    op(V, lambda: nc.vector.tensor_tensor(out=ang[:], in0=dtl[:], in1=aim[:], op=ALU.mult))
    op(V, lambda: nc.vector.tensor_scalar(ang2[:], ang[:], PI / 2, None, ALU.add))
    for a_ in (ang, ang2):
        op(V, lambda: nc.vector.memset(kk[:], 0.0))
        for m in (1, 3, 5, 7, 9):
            op(V, lambda a_=a_, m=m: nc.vector.tensor_scalar(tmpa[:], a_[:], m * PI, None, ALU.is_gt))
            op(V, lambda: nc.vector.tensor_tensor(out=kk[:], in0=kk[:], in1=tmpa[:], op=ALU.add))
        op(V, lambda a_=a_: nc.vector.scalar_tensor_tensor(out=a_[:], in0=kk[:], scalar=-2 * PI, in1=a_[:], op0=ALU.mult, op1=ALU.add))
        op(V, lambda a_=a_: nc.vector.tensor_scalar(a_[:], a_[:], PI, -PI, ALU.min, ALU.max))
    op(A, lambda: nc.scalar.activation(out=ai[:], in_=ang[:], func=AF.Sin))
    op(A, lambda: nc.scalar.activation(out=ar[:], in_=ang2[:], func=AF.Sin))
    op(V, lambda: nc.vector.tensor_tensor(out=ai[:], in0=ai[:], in1=mag[:], op=ALU.mult))
    op(V, lambda: nc.vector.tensor_tensor(out=ar[:], in0=ar[:], in1=mag[:], op=ALU.mult))
    op(V, lambda: nc.vector.tensor_scalar(nai[:], ai[:], -1.0, None, ALU.mult))
    op(V, lambda: nc.vector.tensor_tensor(out=den[:], in0=are[:], in1=are[:], op=ALU.mult))
    op(V, lambda: nc.vector.tensor_tensor(out=tmpa[:], in0=aim[:], in1=aim[:], op=ALU.mult))
    op(V, lambda: nc.vector.tensor_tensor(out=den[:], in0=den[:], in1=tmpa[:], op=ALU.add))
    op(V, lambda: nc.vector.reciprocal(out=den[:], in_=den[:]))
    op(V, lambda: nc.vector.tensor_scalar(zr[:], ar[:], -1.0, None, ALU.add))
    op(V, lambda: nc.vector.tensor_tensor(out=cre[:], in0=zr[:], in1=are[:], op=ALU.mult))
    op(V, lambda: nc.vector.tensor_tensor(out=tmpa[:], in0=ai[:], in1=aim[:], op=ALU.mult))
    op(V, lambda: nc.vector.tensor_tensor(out=cre[:], in0=cre[:], in1=tmpa[:], op=ALU.add))
    op(V, lambda: nc.vector.tensor_tensor(out=cre[:], in0=cre[:], in1=den[:], op=ALU.mult))
    op(V, lambda: nc.vector.tensor_tensor(out=cim[:], in0=ai[:], in1=are[:], op=ALU.mult))
    op(V, lambda: nc.vector.tensor_tensor(out=tmpa[:], in0=zr[:], in1=aim[:], op=ALU.mult))
    op(V, lambda: nc.vector.tensor_tensor(out=cim[:], in0=cim[:], in1=tmpa[:], op=ALU.subtract))
    op(V, lambda: nc.vector.tensor_tensor(out=cim[:], in0=cim[:], in1=den[:], op=ALU.mult))
    op(V, lambda: nc.vector.tensor_copy(out=pwr[:, :, 0], in_=ar[:]))
    op(V, lambda: nc.vector.tensor_copy(out=pwi[:, :, 0], in_=ai[:]))
    op(V, lambda: nc.vector.tensor_copy(out=a2r[:], in_=ar[:]))
    op(V, lambda: nc.vector.tensor_copy(out=a2i[:], in_=ai[:]))
    n = 1
    while n < 64:
        br = bc_i(a2r[:], n); bi = bc_i(a2i[:], n)
        op(V, lambda n=n, br=br: nc.vector.tensor_tensor(out=pwr[:, :, n:2 * n], in0=pwr[:, :, 0:n], in1=br, op=ALU.mult))
        op(V, lambda n=n, bi=bi: nc.vector.tensor_tensor(out=t3a[:, :, 0:n], in0=pwi[:, :, 0:n], in1=bi, op=ALU.mult))
        op(V, lambda n=n: nc.vector.tensor_tensor(out=pwr[:, :, n:2 * n], in0=pwr[:, :, n:2 * n], in1=t3a[:, :, 0:n], op=ALU.subtract))
        op(V, lambda n=n, bi=bi: nc.vector.tensor_tensor(out=pwi[:, :, n:2 * n], in0=pwr[:, :, 0:n], in1=bi, op=ALU.mult))
        op(V, lambda n=n, br=br: nc.vector.tensor_tensor(out=t3a[:, :, 0:n], in0=pwi[:, :, 0:n], in1=br, op=ALU.mult))
        op(V, lambda n=n: nc.vector.tensor_tensor(out=pwi[:, :, n:2 * n], in0=pwi[:, :, n:2 * n], in1=t3a[:, :, 0:n], op=ALU.add))
        op(V, lambda: nc.vector.tensor_tensor(out=tmpa[:], in0=a2r[:], in1=a2i[:], op=ALU.mult))
        op(V, lambda: nc.vector.tensor_tensor(out=kk[:], in0=a2i[:], in1=a2i[:], op=ALU.mult))
        op(V, lambda: nc.vector.tensor_tensor(out=a2r[:], in0=a2r[:], in1=a2r[:], op=ALU.mult))
        op(V, lambda: nc.vector.tensor_tensor(out=a2r[:], in0=a2r[:], in1=kk[:], op=ALU.subtract))
        op(V, lambda: nc.vector.tensor_scalar(a2i[:], tmpa[:], 2.0, None, ALU.mult))
        n *= 2
    op(V, lambda: nc.vector.tensor_scalar(npwi[:], pwi[:], -1.0, None, ALU.mult))
    op(V, lambda: nc.vector.tensor_copy(out=a64r[:], in_=pwr[:, :, 63]))
    op(V, lambda: nc.vector.tensor_copy(out=a64i[:], in_=pwi[:, :, 63]))
    op(V, lambda: nc.vector.tensor_scalar(na64i[:], a64i[:], -1.0, None, ALU.mult))
    for ri, key in enumerate(("b_re", "b_im")):
        op(V, lambda ri=ri: nc.vector.memset(Braw[ri][:], 0.0))
        for g_ in range(32):
            st_, p0, g2 = g_ // 2, (g_ % 8) * 16, g_ % 2
            f.dma(f.sp, Braw[ri][p0:p0 + 16, st_, g2 * 64:(g2 + 1) * 64], p[key][g_].rearrange("n q -> q n"), reads=gp, writes=gp,
                  allow_slow_non_contiguous=True)
        op(V, lambda ri=ri: nc.vector.tensor_copy(out=Btab[ri][:], in_=Braw[ri][:]))
    for ri, key in enumerate(("c_re", "c_im")):
        op(V, lambda ri=ri: nc.vector.memset(Craw[ri][:], 0.0))
        cv_ = p[key].rearrange("(t g) q n -> g n t q", g=2)
        for g2 in range(2):
            for t_ in range(NS):
                c0 = 32 * (t_ % 4) + 16 * g2
                f.dma(f.sp, Craw[ri][g2 * 64:(g2 + 1) * 64, t_, c0:c0 + 16], cv_[g2, :, t_, :], reads=gp, writes=gp,
                      allow_slow_non_contiguous=True)
    op(V, lambda: nc.vector.tensor_tensor(out=c1[:], in0=Craw[0][:], in1=bc_i(cre[:], 128), op=ALU.mult))
    op(V, lambda: nc.vector.tensor_tensor(out=c2[:], in0=Craw[1][:], in1=bc_i(cim[:], 128), op=ALU.mult))
    op(V, lambda: nc.vector.tensor_tensor(out=TCre[:], in0=c1[:], in1=c2[:], op=ALU.subtract))
    op(V, lambda: nc.vector.tensor_tensor(out=c1[:], in0=Craw[0][:], in1=bc_i(cim[:], 128), op=ALU.mult))
    op(V, lambda: nc.vector.tensor_tensor(out=c2[:], in0=Craw[1][:], in1=bc_i(cre[:], 128), op=ALU.mult))
    op(V, lambda: nc.vector.tensor_tensor(out=c1[:], in0=c1[:], in1=c2[:], op=ALU.add))
    op(V, lambda: nc.vector.tensor_scalar(TCimn[:], c1[:], -1.0, None, ALU.mult))
    f.dma(f.sp, dv[:], p["d"].rearrange("(c p) -> p c", p=128), writes=gp, allow_slow_non_contiguous=True)
    f.dma(f.pool, Wglu[:], p["w_glu"].rearrange("(kc p) f -> p kc f", p=128), writes=gp)
    barrier(f)
    scp.close()

    uT = mk("uTkt", [128, NT], BF16)
    yg = mk("yg", [128, 4, NT], BF16)
    bu = [mk("bu", [128, 64, NCH]) for _ in range(2)]
    hb = [[mk("hb", [128, 64, NCH], BF16) for _ in range(2)] for _ in range(4)]
    Hs = [mk("Hs", [128, NSEQ, CPS + 1]) for _ in range(2)]
    ysb = mk("ysb", [128, 512])
    x2 = mk("x2g", [128, 512]); zz = mk("zzg", [128, 512])
    pbu = [[sc.ps(uname("pbu"), [128, 512]) for _ in range(2)] for _ in range(2)]
    py = [sc.ps(uname("py"), [128, 512]) for _ in range(2)]
    pgl = [sc.ps(uname("pgl"), [128, 512]) for _ in range(2)]
    names = "u yg ysb x2 zz scr"
    Bf = {n: Buf(n) for n in names.split()}
    Bur = [Buf() for _ in range(64)]; Bui = [Buf() for _ in range(64)]
    BHr = [Buf() for _ in range(CPS + 1)]; BHi = [Buf() for _ in range(CPS + 1)]
    for n in ["pbu0", "pbu1", "py", "pgl"]:
        Bf[n + "0"] = Buf(); Bf[n + "1"] = Buf()
    for sl in range(4):
        Bf["hbr%d" % sl] = Buf(); Bf["hbi%d" % sl] = Buf()
    g = lambda *ns: [Bf[n] for n in ns]
    bun = (Bur, Bui)
    for ot in range(4):
      f.dma(f.sp, uT[:], scr["uT"][ot * 128:(ot + 1) * 128, :], writes=g("u"))
      for sl in range(4):
        st = 4 * ot + sl
        arS, aiS, naiS = ar[:, st:st + 1], ai[:, st:st + 1], nai[:, st:st + 1]
        hbn = ("hbr%d" % sl, "hbi%d" % sl)
        for pc in range(NT // 512):
            b = pc % 2
            for ri in range(2):
                f.op(P_, lambda ri=ri, b=b, pc=pc: nc.tensor.matmul(pbu[ri][b][:], Btab[ri][:, st, :], uT[:, pc * 512:(pc + 1) * 512],
                                                                    start=True, stop=True),
                     reads=g("u") + gp, writes=g("pbu%d%d" % (ri, b)))
                f.op(A, lambda ri=ri, b=b, pc=pc: nc.scalar.copy(out=bu[ri][:, :, pc * 8:(pc + 1) * 8].rearrange("p s c -> p c s"),
                                                                in_=pbu[ri][b][:].rearrange("p (c s) -> p c s", s=64)),
                     reads=g("pbu%d%d" % (ri, b)), writes=bun[ri])
        for tau in range(1, 64):
            X1 = lambda tau=tau: f.op(V, lambda: nc.vector.scalar_tensor_tensor(out=bu[0][:, tau, :], in0=bu[1][:, tau - 1, :], scalar=naiS, in1=bu[0][:, tau, :],
                                                                   op0=ALU.mult, op1=ALU.add), reads=[Bui[tau - 1]] + gp, writes=[Bur[tau]])
            X2 = lambda tau=tau: f.op(V, lambda: nc.vector.scalar_tensor_tensor(out=bu[1][:, tau, :], in0=bu[0][:, tau - 1, :], scalar=aiS, in1=bu[1][:, tau, :],
                                                                   op0=ALU.mult, op1=ALU.add), reads=[Bur[tau - 1]] + gp, writes=[Bui[tau]])
            D1 = lambda tau=tau: f.op(V, lambda: nc.vector.scalar_tensor_tensor(out=bu[0][:, tau, :], in0=bu[0][:, tau - 1, :], scalar=arS, in1=bu[0][:, tau, :],
                                                                   op0=ALU.mult, op1=ALU.add), reads=[Bur[tau - 1]] + gp, writes=[Bur[tau]])
            D2 = lambda tau=tau: f.op(V, lambda: nc.vector.scalar_tensor_tensor(out=bu[1][:, tau, :], in0=bu[1][:, tau - 1, :], scalar=arS, in1=bu[1][:, tau, :],
                                                                   op0=ALU.mult, op1=ALU.add), reads=[Bui[tau - 1]] + gp, writes=[Bui[tau]])
            for o_ in ((X1, X2, D1, D2) if tau % 2 == 1 else (X2, X1, D2, D1)):
                o_()
        f.op(V, lambda: nc.vector.memset(Hs[0][:, :, 0:1], 0.0), writes=[BHr[0]])
        f.op(V, lambda: nc.vector.memset(Hs[1][:, :, 0:1], 0.0), writes=[BHi[0]])
        lastr = bu[0][:, 63, :].rearrange("p (b c) -> p b c", c=CPS)
        lasti = bu[1][:, 63, :].rearrange("p (b c) -> p b c", c=CPS)
        A64r, A64i, NA64i = a64r[:, st:st + 1], a64i[:, st:st + 1], na64i[:, st:st + 1]
        for c in range(CPS):
            C1 = lambda c=c: f.op(V, lambda: nc.vector.scalar_tensor_tensor(out=Hs[0][:, :, c + 1], in0=Hs[1][:, :, c], scalar=NA64i, in1=lastr[:, :, c],
                                                               op0=ALU.mult, op1=ALU.add), reads=[Bur[63], BHi[c]] + gp, writes=[BHr[c + 1]])
            C2 = lambda c=c: f.op(V, lambda: nc.vector.scalar_tensor_tensor(out=Hs[0][:, :, c + 1], in0=Hs[0][:, :, c], scalar=A64r, in1=Hs[0][:, :, c + 1],
                                                               op0=ALU.mult, op1=ALU.add), reads=[BHr[c]] + gp, writes=[BHr[c + 1]])
            C3 = lambda c=c: f.op(V, lambda: nc.vector.scalar_tensor_tensor(out=Hs[1][:, :, c + 1], in0=Hs[0][:, :, c], scalar=A64i, in1=lasti[:, :, c],
                                                               op0=ALU.mult, op1=ALU.add), reads=[Bui[63], BHr[c]] + gp, writes=[BHi[c + 1]])
            C4 = lambda c=c: f.op(V, lambda: nc.vector.scalar_tensor_tensor(out=Hs[1][:, :, c + 1], in0=Hs[1][:, :, c], scalar=A64r, in1=Hs[1][:, :, c + 1],
                                                               op0=ALU.mult, op1=ALU.add), reads=[BHi[c]] + gp, writes=[BHi[c + 1]])
            for o_ in ((C1, C3, C2, C4) if c % 2 == 0 else (C3, C1, C4, C2)):
                o_()
        Hr = Hs[0][:, :, 0:CPS]; Hi = Hs[1][:, :, 0:CPS]
        for tau in range(64):
            pr, pi_, npi = pwr[:, st, tau:tau + 1], pwi[:, st, tau:tau + 1], npwi[:, st, tau:tau + 1]
            br3 = bu[0][:, tau, :].rearrange("p (b c) -> p b c", c=CPS)
            bi3 = bu[1][:, tau, :].rearrange("p (b c) -> p b c", c=CPS)
            hr3 = hb[sl][0][:, tau, :].rearrange("p (b c) -> p b c", c=CPS)
            hi3 = hb[sl][1][:, tau, :].rearrange("p (b c) -> p b c", c=CPS)
            f.op(V, lambda: nc.vector.scalar_tensor_tensor(out=br3, in0=Hi, scalar=npi, in1=br3, op0=ALU.mult, op1=ALU.add), reads=BHi[0:CPS] + gp, writes=[Bur[tau]])
            f.op(V, lambda: nc.vector.scalar_tensor_tensor(out=bi3, in0=Hr, scalar=pi_, in1=bi3, op0=ALU.mult, op1=ALU.add), reads=BHr[0:CPS] + gp, writes=[Bui[tau]])
            f.op(V, lambda: nc.vector.scalar_tensor_tensor(out=hr3, in0=Hr, scalar=pr, in1=br3, op0=ALU.mult, op1=ALU.add), reads=BHr[0:CPS] + [Bur[tau]] + gp, writes=g(hbn[0]))
            f.op(V, lambda: nc.vector.scalar_tensor_tensor(out=hi3, in0=Hi, scalar=pr, in1=bi3, op0=ALU.mult, op1=ALU.add), reads=BHi[0:CPS] + [Bui[tau]] + gp, writes=g(hbn[1]))
      tpp = 512 // NCH
      for pc in range(64 * NCH // 512):
        b = pc % 2
        k = 0
        for sl in range(4):
            st = 4 * ot + sl
            for ri, TC in enumerate((TCre, TCimn)):
                hbf = hb[sl][ri][:].rearrange("p s c -> p (s c)")
                f.op(P_, lambda pc=pc, b=b, TC=TC, st=st, hbf=hbf, k=k: nc.tensor.matmul(py[b][:], TC[:, st, :], hbf[:, pc * 512:(pc + 1) * 512],
                                                                                      start=(k == 0), stop=(k == 7)),
                     reads=g("hbr%d" % sl, "hbi%d" % sl) + gp, writes=g("py%d" % b))
                k += 1
        uview = uT[:, :].rearrange("p (c s) -> p s c", s=64)[:, pc * tpp:(pc + 1) * tpp, :]
        ygview = yg[:, ot, :].rearrange("p (c s) -> p s c", s=64)[:, pc * tpp:(pc + 1) * tpp, :]
        y3 = ysb[:, :].rearrange("p (s c) -> p s c", c=NCH)
        z3 = zz[:, :].rearrange("p (s c) -> p s c", c=NCH)
        f.op(V, lambda uview=uview, y3=y3, b=b: nc.vector.scalar_tensor_tensor(out=y3, in0=uview, scalar=dv[:, ot:ot + 1],
                                                                              in1=py[b][:].rearrange("p (s c) -> p s c", c=NCH),
                                                                              op0=ALU.mult, op1=ALU.add),
             reads=g("py%d" % b, "u") + gp, writes=g("ysb"))
        f.op(A, lambda: nc.scalar.activation(out=x2[:], in_=ysb[:], func=AF.Square), reads=g("ysb"), writes=g("x2"))
        f.op(V, lambda: nc.vector.tensor_scalar(x2[:], x2[:], 0.044715, 1.0, ALU.mult, ALU.add), reads=g("x2"), writes=g("x2"))
        f.op(V, lambda: nc.vector.tensor_tensor(out=zz[:], in0=x2[:], in1=ysb[:], op=ALU.mult), reads=g("x2", "ysb"), writes=g("zz"))
        f.op(A, lambda: nc.scalar.activation(out=zz[:], in_=zz[:], func=AF.Sigmoid, scale=1.5957691216), reads=g("zz"), writes=g("zz"))
        f.op(V, lambda ygview=ygview, y3=y3, z3=z3: nc.vector.tensor_tensor(out=ygview, in0=y3, in1=z3, op=ALU.mult), reads=g("zz", "ysb"), writes=g("yg"))
    sgl = [mk("sgl", [128, 512]) for _ in range(2)]
    og = [mk("og", [128, 512], BF16) for _ in range(2)]
    Bs = [Buf(), Buf()]; Bo = [Buf(), Buf()]
    for t in range(NT // 512):
        cs = slice(t * 512, (t + 1) * 512)
        for oc in range(4):
            b = oc % 2
            for kc in range(4):
                f.op(P_, lambda kc=kc, oc=oc, b=b, cs=cs: nc.tensor.matmul(pgl[b][:], Wglu[:, kc, oc * 128:(oc + 1) * 128], yg[:, kc, cs],
                                                                        start=(kc == 0), stop=(kc == 3)),
                     reads=g("yg") + gp, writes=g("pgl%d" % b))
            f.op(A, lambda b=b: nc.scalar.activation(out=sgl[b][:], in_=pgl[b][:], func=AF.Sigmoid), reads=g("pgl%d" % b), writes=[Bs[b]])
            f.op(V, lambda b=b, oc=oc, cs=cs: nc.vector.tensor_tensor(out=og[b][:], in0=sgl[b][:], in1=yg[:, oc, cs], op=ALU.mult),
                 reads=[Bs[b]] + g("yg"), writes=[Bo[b]])
            f.dma(f.sp, scr["yT"][512 + oc * 128:512 + (oc + 1) * 128, cs], og[b][:], reads=[Bo[b]], writes=g("scr"))
    barrier(f)
    sc.close()


def ev_out_phase(f, X, wout_d, scr, NT):
    nc = f.nc
    sc = Scope(nc)
    mk = lambda n, shp, dt=F32: sc.sb(uname(n), shp, dt)
    Wo = mk("Wo", [128, KC, D], BF16)
    yt = mk("yt", [128, KC, TT], BF16)
    xt = mk("xt", [128, KC, TT])
    po = [sc.ps(uname("pout"), [128, TT]) for _ in range(2)]
    BW, By, Bx, Bs = Buf(), Buf(), Buf(), Buf()
    Bp = [Buf(), Buf()]
    wv = wout_d.rearrange("(kc p) d -> p kc d", p=128)
    f.dma(f.pool, Wo[:, 0:4, :], wv[:, 0:4, :], writes=[BW])
    f.dma(f.pool, Wo[:, 4:8, :], wv[:, 4:8, :], writes=[BW])
    Xv = X.rearrange("(c p) t -> p c t", p=128)
    yv = scr["yT"].rearrange("(c p) t -> p c t", p=128)
    for t in range(NT // TT):
        cs = slice(t * TT, (t + 1) * TT)
        f.dma(f.sp, yt[:], yv[:, :, cs], writes=[By])
        f.dma(f.sp, xt[:], Xv[:, :, cs], writes=[Bx])
        for dc in range(KC):
            b = dc % 2
            for kc in range(KC):
                f.op(f.pe, lambda dc=dc, kc=kc, b=b: nc.tensor.matmul(po[b][:], Wo[:, kc, dc * 128:(dc + 1) * 128], yt[:, kc, :],
                                                                      start=(kc == 0), stop=(kc == KC - 1)),
                     reads=[BW, By], writes=[Bp[b]])
            f.op(f.dve, lambda dc=dc, b=b: nc.vector.tensor_tensor(out=xt[:, dc, :], in0=po[b][:], in1=xt[:, dc, :], op=ALU.add),
                 reads=[Bp[b]], writes=[Bx])
        f.dma(f.sp, Xv[:, :, cs], xt[:], reads=[Bx], writes=[Bs])
    barrier(f)
    sc.close()


SEQ = 2048
NSEQ_CORE = 2
NCORES = 8
DEPTH = 4

_IN_SHAPES = {
    "ffn1_norm": [4, 1024], "ffn1_w_gate": [4, 1024, 2816], "ffn1_w_up": [4, 1024, 2816], "ffn1_w_down": [4, 2816, 1024],
    "mix_norm": [4, 1024], "ffn2_norm": [4, 1024], "ffn2_w_gate": [4, 1024, 2816], "ffn2_w_up": [4, 1024, 2816],
    "ffn2_w_down": [4, 2816, 1024], "ev_w_in": [2, 1024, 2560], "hg_lb_logits": [2, 512], "hg_norm_w": [2, 128],
    "s5_a_re": [2, 32, 64], "s5_a_im": [2, 32, 64], "s5_b_re": [2, 32, 64, 16], "s5_b_im": [2, 32, 64, 16],
    "s5_c_re": [2, 32, 16, 64], "s5_c_im": [2, 32, 16, 64], "s5_d": [2, 512], "s5_log_dt": [2, 32], "s5_w_glu": [2, 512, 512],
    "ev_w_out": [2, 1024, 1024], "od_w_in": [2, 1024, 4112], "gdn_conv_w": [2, 4, 3072], "gdn_a_log": [2, 8], "gdn_dt_bias": [2, 8],
    "gdn_norm_w": [2, 128], "od_w_out": [2, 1024, 1024], "final_norm": [1024],
}


def build_program(L=SEQ, nseq=NSEQ_CORE, depth=DEPTH):
    NT = L * nseq
    f = FW()
    nc = f.nc
    I = {k: nc.dram_tensor(k, list(shp), F32, kind="ExternalInput").ap() for k, shp in _IN_SHAPES.items()}
    xT = nc.dram_tensor("xT", [D, NT], F32, kind="ExternalInput").ap()
    oT = nc.dram_tensor("oT", [D, NT], F32, kind="ExternalOutput").ap()
    X = nc.dram_tensor("Xres", [D, NT], F32).ap()
    dt_ = lambda n, shp, t: nc.dram_tensor(n, shp, t).ap()
    scr = {
        "qT": dt_("s_qT", [D, NT], BF16), "kT": dt_("s_kT", [D, NT], BF16), "gT": dt_("s_gT", [D, NT], BF16),
        "ktok": dt_("s_ktok", [NT, D], BF16), "vtok": dt_("s_vtok", [NT, D], BF16), "bl": dt_("s_bl", [NT, 16], F32),
        "uT": dt_("s_uT", [512, NT], BF16), "lfT": dt_("s_lfT", [512, NT], F32), "yT": dt_("s_yT", [D, NT], BF16),
    }
    src = xT
    for layer in range(depth):
        j = layer // 2
        ffn_phase(f, src, X, I["ffn1_norm"][layer], I["ffn1_w_gate"][layer], I["ffn1_w_up"][layer], I["ffn1_w_down"][layer], NT)
        src = X
        if layer % 2 == 0:
            ev_proj_phase(f, X, I["mix_norm"][layer], I["ev_w_in"][j], I["hg_lb_logits"], j, scr, NT, L)
            hgrn_core_phase(f, I["hg_norm_w"][j], scr, NT, L)
            p = {"a_re": I["s5_a_re"][j], "a_im": I["s5_a_im"][j], "b_re": I["s5_b_re"][j], "b_im": I["s5_b_im"][j],
                 "c_re": I["s5_c_re"][j], "c_im": I["s5_c_im"][j], "d": I["s5_d"][j], "log_dt": I["s5_log_dt"][j], "w_glu": I["s5_w_glu"][j]}
            s5_phase(f, p, j, scr, NT, L)
            ev_out_phase(f, X, I["ev_w_out"][j], scr, NT)
        else:
            gdn_proj_phase(f, X, I["mix_norm"][layer], I["od_w_in"][j], I["gdn_conv_w"][j], I["gdn_a_log"][j], I["gdn_dt_bias"][j], scr, NT, L)
            gdn_core_phase(f, X, I["gdn_norm_w"][j], I["od_w_out"][j], scr, NT, L)
        ffn_phase(f, X, X, I["ffn2_norm"][layer], I["ffn2_w_gate"][layer], I["ffn2_w_up"][layer], I["ffn2_w_down"][layer], NT)
    Bo = final_phase(f, src, oT, I["final_norm"], NT)
    f.finish([Bo])
    return f


def kernel(**inputs):
    x = np.asarray(inputs["x"], dtype=np.float32)
    Bsz, L, Dm = x.shape
    nseq = Bsz // NCORES
    f = build_program(L, nseq, DEPTH)
    shared = {k: np.ascontiguousarray(np.asarray(inputs[k], dtype=np.float32)) for k in _IN_SHAPES}
    in_maps = []
    for c in range(NCORES):
        m = dict(shared)
        m["xT"] = np.ascontiguousarray(x[c * nseq:(c + 1) * nseq].reshape(nseq * L, Dm).T)
        in_maps.append(m)
    res = run_bass_kernel_spmd(f.nc, in_maps, core_ids=list(range(NCORES)))
    out = np.empty((Bsz, L, Dm), dtype=np.float32)
    for c in range(NCORES):
        oT = np.asarray(res.results[c]["oT"])
        out[c * nseq:(c + 1) * nseq] = oT.T.reshape(nseq, L, Dm)
    return out
```

```python
import numpy as np
import concourse.bass as bass
import concourse.mybir as mybir
from concourse.bass_utils import run_bass_kernel_spmd

F32 = mybir.dt.float32
BF16 = mybir.dt.bfloat16
AF = mybir.ActivationFunctionType
ALU = mybir.AluOpType

EPOCH = 16000


class Eng:
    def __init__(self, fw, e, name, self_sync=True):
        self.fw = fw
        self.e = e
        self.name = name
        self.self_sync = self_sync
        self.sem = fw.nc.alloc_semaphore(name + "_s0")
        self.cnt = 0
        self.nep = 0
        self.seen = {}
        self.total = 0

    def _wait(self, deps):
        for d in deps:
            if d is None:
                continue
            sem, val, own = d
            if own is self and not self.self_sync:
                continue
            k = id(sem)
            if self.seen.get(k, 0) >= val:
                continue
            self.e.wait_ge(sem, val)
            self.seen[k] = val

    def emit(self, fn, deps=()):
        self._wait(deps)
        if self.cnt >= EPOCH:
            self.nep += 1
            self.sem = self.fw.nc.alloc_semaphore("%s_s%d" % (self.name, self.nep))
            self.cnt = 0
        ins = fn()
        self.cnt += 1
        self.total += 1
        ins.then_inc(self.sem, 1)
        return (self.sem, self.cnt, self)

    def dma(self, out, in_, deps=(), **kw):
        fw = self.fw
        self._wait(deps)
        slot = fw.dma_rr % len(fw.dma_sems)
        fw.dma_rr += 1
        sem = fw.dma_sems[slot]
        prev = fw.dma_vals[slot]
        if prev > 0:
            k = id(sem)
            if self.seen.get(k, 0) < prev:
                self.e.wait_ge(sem, prev)
                self.seen[k] = prev
        ins = self.e.dma_start(out=out, in_=in_, **kw)
        val = prev + 16
        fw.dma_vals[slot] = val
        ins.then_inc(sem, 16)
        return (sem, val, None)


class Buf:
    def __init__(self, name=""):
        self.name = name
        self.w = None
        self.r = {}


class FW:
    def __init__(self, n_dma_sems=40):
        self.nc = bass.Bass("TRN2", target_bir_lowering=False)
        nc = self.nc
        self.pe = Eng(self, nc.tensor, "pe", self_sync=False)
        self.act = Eng(self, nc.scalar, "act")
        self.dve = Eng(self, nc.vector, "dve")
        self.pool = Eng(self, nc.gpsimd, "pool")
        self.sp = Eng(self, nc.sync, "sp")
        self.dma_sems = [nc.alloc_semaphore("dma%d" % i) for i in range(n_dma_sems)]
        self.dma_vals = [0] * n_dma_sems
        self.dma_rr = 0

    def _deps(self, reads, writes):
        deps = []
        for b in reads:
            if b.w is not None:
                deps.append(b.w)
        for b in writes:
            if b.w is not None:
                deps.append(b.w)
            deps.extend(b.r.values())
        return deps

    def _post(self, tok, reads, writes):
        for b in reads:
            k = id(tok[0])
            o = b.r.get(k)
            if o is None or o[1] < tok[1]:
                b.r[k] = tok
        for b in writes:
            b.w = tok
            b.r = {}

    def op(self, eng, fn, reads=(), writes=()):
        tok = eng.emit(fn, self._deps(reads, writes))
        self._post(tok, reads, writes)
        return tok

    def dma(self, eng, out, in_, reads=(), writes=(), **kw):
        tok = eng.dma(out, in_, self._deps(reads, writes), **kw)
        self._post(tok, reads, writes)
        return tok

    def finish(self, bufs):
        deps = []
        for b in bufs:
            if b.w is not None:
                deps.append(b.w)
        self.sp._wait(deps)


D = 1024
DFF = 2816
KC = D // 128
FC = DFF // 128
TT = 512
EPS = 1e-6


class Scope:
    def __init__(self, nc):
        self.nc = nc
        self.guards = []

    def sb(self, name, shape, dt):
        g = self.nc.sbuf_tensor(name, shape, dt)
        t = g.__enter__()
        self.guards.append(g)
        return t

    def ps(self, name, shape, dt=F32):
        g = self.nc.psum_tensor(name, shape, dt)
        t = g.__enter__()
        self.guards.append(g)
        return t

    def close(self):
        for g in reversed(self.guards):
            g.__exit__(None, None, None)
        self.guards = []


def barrier(f):
    engs = [f.pe, f.act, f.dve, f.pool, f.sp]
    toks = []
    for e in engs:
        if e.cnt > 0:
            toks.append((e.sem, e.cnt, None))
    for s, v in zip(f.dma_sems, f.dma_vals):
        if v > 0:
            toks.append((s, v, None))
    for e in engs:
        e._wait(toks)


_uid = [0]


def uname(p):
    _uid[0] += 1
    return "%s_%d" % (p, _uid[0])


def rms_stats(f, sc, xt, sqbuf, Bx, Bsq, ones_bf, eps_t, pss, Bpss, rstd, Brstd, ncols, inv_n):
    nc = f.nc
    Bsq = Bsq if isinstance(Bsq, list) else [Bsq]
    f.op(f.act, lambda: nc.scalar.activation(out=sqbuf, in_=xt, func=AF.Square), reads=[Bx], writes=Bsq)
    for c in range(KC):
        f.op(f.pe, lambda c=c: nc.tensor.matmul(pss, ones_bf, sqbuf[:, c, :], start=(c == 0), stop=(c == KC - 1)),
             reads=Bsq, writes=[Bpss])
    f.op(f.act, lambda: nc.scalar.activation(out=rstd, in_=pss, func=AF.Sqrt, bias=eps_t, scale=inv_n),
         reads=[Bpss], writes=[Brstd])
    f.op(f.dve, lambda: nc.vector.reciprocal(out=rstd, in_=rstd), reads=[Brstd], writes=[Brstd])


def load_w_bf16(f, eng, dst_sb, src_ap, bufs, piece):
    nc = f.nc
    A, Bn = src_ap.shape[1], src_ap.shape[2]
    toks = []
    i = 0
    for b0 in range(0, Bn, piece):
        b1 = min(Bn, b0 + piece)
        f.dma(eng, dst_sb[:, :, b0:b1], src_ap[:, :, b0:b1], writes=[bufs[i]])
        i += 1


def ffn_phase(f, src, dst, wn_d, wg_d, wu_d, wd_d, NT):
    nc = f.nc
    sc = Scope(nc)
    Wg = sc.sb(uname("Wg"), [128, KC, DFF], BF16)
    Wu = sc.sb(uname("Wu"), [128, KC, DFF], BF16)
    Wd = sc.sb(uname("Wd"), [128, FC, D], BF16)
    wn = sc.sb(uname("wn"), [128, KC], F32)
    xt = [sc.sb(uname("xt"), [128, KC, TT], F32) for _ in range(2)]
    hT = sc.sb(uname("hT"), [128, KC, TT], BF16)
    sq = [sc.sb(uname("sq"), [128, TT], BF16) for _ in range(2)]
    act = sc.sb(uname("act"), [128, FC, TT], BF16)
    rstd = sc.sb(uname("rstd"), [128, TT], F32)
    sg = [sc.sb(uname("sg"), [128, TT], F32) for _ in range(2)]
    ones_bf = sc.sb(uname("ones"), [128, 128], BF16)
    eps_t = sc.sb(uname("eps"), [128, 1], F32)
    pg = [sc.ps(uname("pg"), [128, TT]) for _ in range(2)]
    pu = [sc.ps(uname("pu"), [128, TT]) for _ in range(2)]
    pd = [sc.ps(uname("pd"), [128, TT]) for _ in range(2)]
    pss = sc.ps(uname("pss"), [128, TT])

    PW = 512
    npc = (DFF + PW - 1) // PW
    BWg = [Buf() for _ in range(npc)]
    BWu = [Buf() for _ in range(npc)]
    BWd = [Buf() for _ in range(FC)]
    Bc, Bh, Bpss, Brstd = Buf(), Buf(), Buf(), Buf()
    Bsq = [Buf(), Buf()]
    Bx = [Buf(), Buf()]
    Bact = [Buf() for _ in range(FC)]
    Bpg = [Buf(), Buf()]
    Bpu = [Buf(), Buf()]
    Bsg = [Buf(), Buf()]
    Bpd = [Buf(), Buf()]
    Bdst = Buf()

    f.op(f.dve, lambda: nc.vector.memset(ones_bf[:], 1.0), writes=[Bc])
    f.op(f.dve, lambda: nc.vector.memset(eps_t[:], EPS), writes=[Bc])
    f.dma(f.sp, wn[:], wn_d.rearrange("(c p) -> p c", p=128), writes=[Bc], allow_slow_non_contiguous=True)
    srcv = src.rearrange("(c p) t -> p c t", p=128)
    dstv = dst.rearrange("(c p) t -> p c t", p=128)
    ntile = NT // TT

    def load(t):
        f.dma(f.sp, xt[t % 2][:], srcv[:, :, t * TT:(t + 1) * TT], writes=[Bx[t % 2]])

    def norm(t):
        x_ = xt[t % 2]
        for c in range(KC):
            f.op(f.act, lambda c=c: nc.scalar.activation(out=sq[c % 2][:], in_=x_[:, c, :], func=AF.Square), reads=[Bx[t % 2]], writes=[Bsq[c % 2]])
            f.op(f.pe, lambda c=c: nc.tensor.matmul(pss[:], ones_bf[:], sq[c % 2][:], start=(c == 0), stop=(c == KC - 1)),
                 reads=[Bsq[c % 2], Bc], writes=[Bpss])
        f.op(f.act, lambda: nc.scalar.activation(out=rstd[:], in_=pss[:], func=AF.Sqrt, bias=eps_t[:], scale=1.0 / D),
             reads=[Bpss, Bc], writes=[Brstd])
        f.op(f.dve, lambda: nc.vector.reciprocal(out=rstd[:], in_=rstd[:]), reads=[Brstd], writes=[Brstd])
        for c in range(KC):
            f.op(f.dve, lambda c=c: nc.vector.scalar_tensor_tensor(out=hT[:, c, :], in0=x_[:, c, :], scalar=wn[:, c:c + 1],
                                                                 in1=rstd[:], op0=ALU.mult, op1=ALU.mult),
                 reads=[Bx[t % 2], Brstd, Bc], writes=[Bh])

    load(0)
    wgv = wg_d.rearrange("(kc p) f -> p kc f", p=128)
    wuv = wu_d.rearrange("(kc p) f -> p kc f", p=128)
    wdv = wd_d.rearrange("(fc p) d -> p fc d", p=128)
    for i in range(npc):
        b0, b1 = i * PW, min(DFF, (i + 1) * PW)
        f.dma(f.pool, Wg[:, :, b0:b1], wgv[:, :, b0:b1], writes=[BWg[i]])
        f.dma(f.pool, Wu[:, :, b0:b1], wuv[:, :, b0:b1], writes=[BWu[i]])
    for i in range(0, FC, 2):
        f.dma(f.pool, Wd[:, i:i + 2, :], wdv[:, i:i + 2, :], writes=[BWd[i], BWd[i + 1]])
    norm(0)
    for t in range(ntile):
        x_ = xt[t % 2]
        if t + 1 < ntile:
            load(t + 1)
        for fc in range(FC):
            b = fc % 2
            wi = (fc * 128) // PW
            for kc in range(KC):
                f.op(f.pe, lambda kc=kc, fc=fc, b=b: nc.tensor.matmul(pg[b][:], Wg[:, kc, fc * 128:(fc + 1) * 128], hT[:, kc, :],
                                                                      start=(kc == 0), stop=(kc == KC - 1)),
                     reads=[BWg[wi], Bh], writes=[Bpg[b]])
            for kc in range(KC):
                f.op(f.pe, lambda kc=kc, fc=fc, b=b: nc.tensor.matmul(pu[b][:], Wu[:, kc, fc * 128:(fc + 1) * 128], hT[:, kc, :],
                                                                      start=(kc == 0), stop=(kc == KC - 1)),
                     reads=[BWu[wi], Bh], writes=[Bpu[b]])
            f.op(f.act, lambda b=b: nc.scalar.activation(out=sg[b][:], in_=pg[b][:], func=AF.Silu), reads=[Bpg[b]], writes=[Bsg[b]])
            f.op(f.dve, lambda b=b, fc=fc: nc.vector.tensor_tensor(out=act[:, fc, :], in0=pu[b][:], in1=sg[b][:], op=ALU.mult),
                 reads=[Bpu[b], Bsg[b]], writes=[Bact[fc]])
        if t + 1 < ntile:
            norm(t + 1)
        for dc in range(KC):
            b = dc % 2
            for fc in range(FC):
                f.op(f.pe, lambda dc=dc, fc=fc, b=b: nc.tensor.matmul(pd[b][:], Wd[:, fc, dc * 128:(dc + 1) * 128], act[:, fc, :],
                                                                      start=(fc == 0), stop=(fc == FC - 1)),
                     reads=[BWd[fc], Bact[fc]], writes=[Bpd[b]])
            f.op(f.dve, lambda dc=dc, b=b: nc.vector.scalar_tensor_tensor(out=x_[:, dc, :], in0=pd[b][:], scalar=0.5, in1=x_[:, dc, :],
                                                                       op0=ALU.mult, op1=ALU.add),
                 reads=[Bpd[b]], writes=[Bx[t % 2]])
        f.dma(f.sp, dstv[:, :, t * TT:(t + 1) * TT], x_[:], reads=[Bx[t % 2]], writes=[Bdst])
    barrier(f)
    sc.close()


def final_phase(f, src, dst, wn_d, NT):
    nc = f.nc
    sc = Scope(nc)
    wn = sc.sb(uname("wn"), [128, KC], F32)
    xt = sc.sb(uname("xt"), [128, KC, TT], F32)
    sq = sc.sb(uname("sq"), [128, KC, TT], BF16)
    rstd = sc.sb(uname("rstd"), [128, TT], F32)
    ones_bf = sc.sb(uname("ones"), [128, 128], BF16)
    eps_t = sc.sb(uname("eps"), [128, 1], F32)
    pss = sc.ps(uname("pss"), [128, TT])
    Bc, Bx, Bsq, Bpss, Brstd, Bdst = [Buf() for _ in range(6)]
    f.op(f.dve, lambda: nc.vector.memset(ones_bf[:], 1.0), writes=[Bc])
    f.op(f.dve, lambda: nc.vector.memset(eps_t[:], EPS), writes=[Bc])
    f.dma(f.sp, wn[:], wn_d.rearrange("(c p) -> p c", p=128), writes=[Bc], allow_slow_non_contiguous=True)
    srcv = src.rearrange("(c p) t -> p c t", p=128)
    dstv = dst.rearrange("(c p) t -> p c t", p=128)
    for t in range(NT // TT):
        cs = slice(t * TT, (t + 1) * TT)
        f.dma(f.sp, xt[:], srcv[:, :, cs], writes=[Bx])
        rms_stats(f, sc, xt[:], sq[:], Bx, Bsq, ones_bf[:], eps_t[:], pss[:], Bpss, rstd[:], Brstd, TT, 1.0 / D)
        for c in range(KC):
            f.op(f.dve, lambda c=c: nc.vector.scalar_tensor_tensor(out=xt[:, c, :], in0=xt[:, c, :], scalar=wn[:, c:c + 1],
                                                                 in1=rstd[:], op0=ALU.mult, op1=ALU.mult),
                 reads=[Brstd, Bc], writes=[Bx])
        f.dma(f.sp, dstv[:, :, cs], xt[:], reads=[Bx], writes=[Bdst])
    barrier(f)
    sc.close()
    return Bdst


NEG = -30000.0


class Consts:
    def __init__(self, f, sc):
        nc = f.nc
        self.B = Buf()
        B = self.B
        mk = lambda n, shp, dt=F32: sc.sb(uname(n), shp, dt)
        self.ones32 = mk("ones32", [128, 128])
        self.ones_bf = mk("onesbf", [128, 128], BF16)
        self.tri = mk("tri", [128, 128])
        self.low = mk("low", [128, 128])
        self.ident = mk("ident", [128, 128])
        self.ident_bf = mk("identbf", [128, 128], BF16)
        self.negincT = mk("negincT", [128, 128])
        self.negstr = mk("negstr", [128, 128])
        self.strT01 = mk("strT01", [128, 128])
        self.bd = mk("bd", [128, 128])
        self.cind = mk("cind", [128, 2, 128])
        self.eps = mk("epsc", [128, 1])
        self.one = mk("onec", [128, 1])
        P = f.pool
        f.op(P, lambda: nc.gpsimd.memset(self.ones32[:], 1.0), writes=[B])
        f.op(P, lambda: nc.gpsimd.memset(self.ones_bf[:], 1.0), writes=[B])
        f.op(P, lambda: nc.gpsimd.memset(self.eps[:], EPS), writes=[B])
        f.op(P, lambda: nc.gpsimd.memset(self.one[:], 1.0), writes=[B])
        f.op(P, lambda: nc.gpsimd.affine_select(out=self.tri[:], in_=self.ones32[:], pattern=[[1, 128]], compare_op=ALU.is_ge,
                                                fill=0.0, base=0, channel_multiplier=-1), reads=[B], writes=[B])
        f.op(P, lambda: nc.gpsimd.memset(self.tri[0:64, 64:128], 0.0), writes=[B])
        f.op(P, lambda: nc.gpsimd.affine_select(out=self.low[:], in_=self.ones32[:], pattern=[[-1, 128]], compare_op=ALU.is_gt,
                                                fill=0.0, base=0, channel_multiplier=1), reads=[B], writes=[B])
        f.op(P, lambda: nc.gpsimd.memset(self.low[64:128, 0:64], 0.0), writes=[B])
        f.op(P, lambda: nc.gpsimd.affine_select(out=self.ident[:], in_=self.ones32[:], pattern=[[-1, 128]], compare_op=ALU.is_equal,
                                                fill=0.0, base=0, channel_multiplier=1), reads=[B], writes=[B])
        f.op(P, lambda: nc.gpsimd.tensor_copy(out=self.ident_bf[:], in_=self.ident[:]), reads=[B], writes=[B])
        f.op(P, lambda: nc.gpsimd.tensor_scalar(self.negincT[:], self.tri[:], -1.0, -NEG, ALU.add, ALU.mult), reads=[B], writes=[B])
        f.op(P, lambda: nc.gpsimd.tensor_scalar(self.negstr[:], self.low[:], -1.0, -NEG, ALU.add, ALU.mult), reads=[B], writes=[B])
        f.op(P, lambda: nc.gpsimd.tensor_tensor(out=self.strT01[:], in0=self.tri[:], in1=self.ident[:], op=ALU.subtract), reads=[B], writes=[B])
        f.op(P, lambda: nc.gpsimd.memset(self.bd[:], 0.0), writes=[B])
        f.op(P, lambda: nc.gpsimd.memset(self.bd[0:64, 0:64], 1.0), writes=[B])
        f.op(P, lambda: nc.gpsimd.memset(self.bd[64:128, 64:128], 1.0), writes=[B])
        f.op(P, lambda: nc.gpsimd.memset(self.cind[:], 0.0), writes=[B])
        f.op(P, lambda: nc.gpsimd.memset(self.cind[0:64, 0, :], 1.0), writes=[B])
        f.op(P, lambda: nc.gpsimd.memset(self.cind[64:128, 1, :], 1.0), writes=[B])


def bc_h(ap2, H=8):
    return ap2.unsqueeze(1).to_broadcast([ap2.shape[0], H, ap2.shape[1]])


def bc_i(ap2, n=128):
    return ap2.unsqueeze(2).to_broadcast([ap2.shape[0], ap2.shape[1], n])


GH = 8
ODIN = 4112


def gdn_proj_phase(f, X, wn_d, win_d, convw_d, alog_d, dtb_d, scr, NT, L):
    nc = f.nc
    sc = Scope(nc)
    C = Consts(f, sc)
    Win = sc.sb(uname("Win"), [128, KC, ODIN], BF16)
    wn = sc.sb(uname("wn"), [128, KC], F32)
    xt = sc.sb(uname("xt"), [128, KC, TT], F32)
    hT = sc.sb(uname("hT"), [128, KC, TT], BF16)
    sq = sc.sb(uname("sq"), [128, KC, TT], BF16)
    rstd = sc.sb(uname("rstd"), [128, TT], F32)
    cw = sc.sb(uname("cw"), [128, 24, 4], F32)
    halo = sc.sb(uname("halo"), [128, 24, 3], F32)
    pre = [sc.sb(uname("pre"), [128, TT + 3], F32) for _ in range(2)]
    cv = [sc.sb(uname("cv"), [128, TT], F32) for _ in range(2)]
    s32 = [sc.sb(uname("s32"), [128, TT], F32) for _ in range(2)]
    sq2 = [sc.sb(uname("sq2"), [128, TT], BF16) for _ in range(2)]
    r2 = [sc.sb(uname("r2"), [128, TT], F32) for _ in range(2)]
    ob = [sc.sb(uname("ob"), [128, TT], BF16) for _ in range(2)]
    tk = [sc.sb(uname("tk"), [128, 4, 128], BF16) for _ in range(2)]
    blt = sc.sb(uname("blt"), [128, 4, 16], F32)
    tmpb = sc.sb(uname("tmpb"), [128, 4, 8], F32)
    dtb = sc.sb(uname("dtb"), [128, 8], F32)
    negA = sc.sb(uname("negA"), [128, 8], F32)
    eps128 = sc.sb(uname("eps128"), [128, 1], F32)
    pp = [sc.ps(uname("pp"), [128, TT]) for _ in range(2)]
    pn = [sc.ps(uname("pn"), [128, TT]) for _ in range(2)]
    ptr = [sc.ps(uname("ptr"), [128, 4, 128], BF16) for _ in range(2)]
    pss = sc.ps(uname("pss"), [128, TT])
    pb = sc.ps(uname("pb"), [128, 4, 16])

    Bc, Bx, Bh, Bsq, Bpss, Brstd, Bhalo, Bpb, Bblt, Btmpb = [Buf() for _ in range(10)]
    NW = 9
    BW = [Buf() for _ in range(NW)]
    Bpp = [Buf(), Buf()]; Bpn = [Buf(), Buf()]; Bptr = [Buf(), Buf()]
    Bpre = [Buf(), Buf()]; Bcv = [Buf(), Buf()]; Bs32 = [Buf(), Buf()]; Bsq2 = [Buf(), Buf()]
    Bcvh = [[Buf(), Buf()], [Buf(), Buf()]]
    Br2 = [Buf(), Buf()]; Bob = [Buf(), Buf()]; Btk = [Buf(), Buf()]
    Bscr = Buf()

    f.dma(f.sp, wn[:], wn_d.rearrange("(c p) -> p c", p=128), writes=[Bc], allow_slow_non_contiguous=True)
    for j in range(4):
        f.dma(f.sp, cw[:, :, j], convw_d[j, :].rearrange("(c p) -> p c", p=128), writes=[Bc], allow_slow_non_contiguous=True)
    f.op(f.dve, lambda: nc.vector.memset(eps128[:], EPS * 128.0), writes=[Bc])
    f.dma(f.sp, dtb[:], dtb_d.partition_broadcast(128), writes=[Bc])
    f.dma(f.sp, negA[:], alog_d.partition_broadcast(128), writes=[Bc])
    f.op(f.act, lambda: nc.scalar.activation(out=negA[:], in_=negA[:], func=AF.Exp), reads=[Bc], writes=[Bc])
    f.op(f.dve, lambda: nc.vector.tensor_scalar(negA[:], negA[:], -1.0, None, ALU.mult), reads=[Bc], writes=[Bc])
    winv = win_d.rearrange("(kc p) f -> p kc f", p=128)
    for i in range(NW):
        b0, b1 = i * 512, min(ODIN, (i + 1) * 512)
        f.dma(f.pool, Win[:, :, b0:b1], winv[:, :, b0:b1], writes=[BW[i]])

    Xv = X.rearrange("(c p) t -> p c t", p=128)
    qTv, kTv, gTv = scr["qT"], scr["kT"], scr["gT"]
    for t in range(NT // TT):
        cs = slice(t * TT, (t + 1) * TT)
        seq_start = (t * TT) % L == 0
        f.dma(f.sp, xt[:], Xv[:, :, cs], writes=[Bx])
        rms_stats(f, sc, xt[:], sq[:], Bx, Bsq, C.ones_bf[:], C.eps[:], pss[:], Bpss, rstd[:], Brstd, TT, 1.0 / D)
        for c in range(KC):
            f.op(f.dve, lambda c=c: nc.vector.scalar_tensor_tensor(out=hT[:, c, :], in0=xt[:, c, :], scalar=wn[:, c:c + 1],
                                                                 in1=rstd[:], op0=ALU.mult, op1=ALU.mult),
                 reads=[Bx, Brstd, Bc], writes=[Bh])
        if seq_start:
            f.op(f.dve, lambda: nc.vector.memset(halo[:], 0.0), writes=[Bhalo])
        HT = TT // 2

        def stA(oc):
            b = oc % 2
            wi = (oc * 128) // 512
            for kc in range(KC):
                f.op(f.pe, lambda kc=kc: nc.tensor.matmul(pp[b][:], Win[:, kc, oc * 128:(oc + 1) * 128], hT[:, kc, :],
                                                          start=(kc == 0), stop=(kc == KC - 1)),
                     reads=[BW[wi], Bh], writes=[Bpp[b]])
            f.op(f.act, lambda: nc.scalar.copy(out=pre[b][:, 3:TT + 3], in_=pp[b][:]), reads=[Bpp[b]], writes=[Bpre[b]])

        def stB(oc):
            b = oc % 2
            f.op(f.dve, lambda: nc.vector.tensor_copy(out=pre[b][:, 0:3], in_=halo[:, oc, :]), reads=[Bhalo], writes=[Bpre[b]])
            f.op(f.dve, lambda: nc.vector.tensor_copy(out=halo[:, oc, :], in_=pre[b][:, TT:TT + 3]), reads=[Bpre[b]], writes=[Bhalo])
            for j in range(4):
                for hf in range(2):
                    c0 = hf * HT
                    if j == 0:
                        f.op(f.dve, lambda c0=c0: nc.vector.tensor_scalar(cv[b][:, c0:c0 + HT], pre[b][:, c0:c0 + HT], cw[:, oc, 0:1], None, ALU.mult),
                             reads=[Bpre[b], Bc], writes=[Bcvh[b][hf]])
                    else:
                        f.op(f.dve, lambda c0=c0, j=j: nc.vector.scalar_tensor_tensor(out=cv[b][:, c0:c0 + HT], in0=pre[b][:, c0 + j:c0 + HT + j],
                                                                                      scalar=cw[:, oc, j:j + 1], in1=cv[b][:, c0:c0 + HT],
                                                                                      op0=ALU.mult, op1=ALU.add),
                             reads=[Bpre[b], Bc], writes=[Bcvh[b][hf]])
            if oc < 16:
                f.op(f.act, lambda: nc.scalar.activation(out=s32[b][:], in_=cv[b][:], func=AF.Silu), reads=Bcvh[b], writes=[Bs32[b]])
                f.op(f.act, lambda: nc.scalar.activation(out=sq2[b][:], in_=s32[b][:], func=AF.Square), reads=[Bs32[b]], writes=[Bsq2[b]])
            else:
                f.op(f.act, lambda: nc.scalar.activation(out=ob[b][:], in_=cv[b][:], func=AF.Silu), reads=Bcvh[b], writes=[Bob[b]])

        def stC1(oc):
            b = oc % 2
            if oc < 16:
                f.op(f.pe, lambda: nc.tensor.matmul(pn[b][:], C.ones_bf[:], sq2[b][:], start=True, stop=True), reads=[Bsq2[b], C.B], writes=[Bpn[b]])
                if oc < 8:
                    f.op(f.act, lambda: nc.scalar.activation(out=r2[b][:], in_=pn[b][:], func=AF.Sqrt, bias=eps128[:], scale=128.0),
                         reads=[Bpn[b], Bc], writes=[Br2[b]])
                else:
                    f.op(f.act, lambda: nc.scalar.activation(out=r2[b][:], in_=pn[b][:], func=AF.Sqrt, bias=C.eps[:], scale=1.0),
                         reads=[Bpn[b], C.B], writes=[Br2[b]])

        def stC2(oc):
            b = oc % 2
            hc = oc % 8
            if oc < 16:
                f.op(f.dve, lambda: nc.vector.reciprocal(out=r2[b][:], in_=r2[b][:]), reads=[Br2[b]], writes=[Br2[b]])
                f.op(f.pool, lambda: nc.gpsimd.tensor_tensor(out=ob[b][:], in0=s32[b][:], in1=r2[b][:], op=ALU.mult),
                     reads=[Bs32[b], Br2[b]], writes=[Bob[b]])
                dstT = qTv if oc < 8 else kTv
                f.dma(f.sp, dstT[hc * 128:(hc + 1) * 128, cs], ob[b][:], reads=[Bob[b]], writes=[Bscr])
            if oc >= 8:
                for s_ in range(4):
                    f.op(f.pe, lambda s_=s_: nc.tensor.transpose(ptr[b][:, s_, :], ob[b][:, s_ * 128:(s_ + 1) * 128], C.ident_bf[:]),
                         reads=[Bob[b], C.B], writes=[Bptr[b]])
                f.op(f.act, lambda: nc.scalar.copy(out=tk[b][:], in_=ptr[b][:]), reads=[Bptr[b]], writes=[Btk[b]])
                dsttok = scr["ktok"] if oc < 16 else scr["vtok"]
                f.dma(f.sp, dsttok[cs, hc * 128:(hc + 1) * 128].rearrange("(s p) d -> p s d", p=128), tk[b][:], reads=[Btk[b]], writes=[Bscr])

        NQ = 24
        stA(0); stA(1); stB(0)
        for oc in range(NQ):
            if oc + 2 < NQ:
                stA(oc + 2)
            stC1(oc)
            if oc + 1 < NQ:
                stB(oc + 1)
            stC2(oc)
        for oc in range(24, 32):
            b = oc % 2
            wi = (oc * 128) // 512
            for kc in range(KC):
                f.op(f.pe, lambda kc=kc, oc=oc, b=b: nc.tensor.matmul(pp[b][:], Win[:, kc, oc * 128:(oc + 1) * 128], hT[:, kc, :],
                                                                      start=(kc == 0), stop=(kc == KC - 1)),
                     reads=[BW[wi], Bh], writes=[Bpp[b]])
            f.op(f.act, lambda b=b: nc.scalar.activation(out=ob[b][:], in_=pp[b][:], func=AF.Silu), reads=[Bpp[b]], writes=[Bob[b]])
            hc = oc - 24
            f.dma(f.sp, gTv[hc * 128:(hc + 1) * 128, cs], ob[b][:], reads=[Bob[b]], writes=[Bscr])
        for s in range(4):
            for kc in range(KC):
                f.op(f.pe, lambda kc=kc, s=s: nc.tensor.matmul(pb[:, s, :], hT[:, kc, s * 128:(s + 1) * 128], Win[:, kc, 4096:4112],
                                                               start=(kc == 0), stop=(kc == KC - 1)),
                     reads=[BW[8], Bh], writes=[Bpb])
        f.op(f.act, lambda: nc.scalar.activation(out=blt[:, :, 0:8], in_=pb[:, :, 0:8], func=AF.Sigmoid), reads=[Bpb], writes=[Bblt])
        f.op(f.dve, lambda: nc.vector.tensor_tensor(out=tmpb[:], in0=pb[:, :, 8:16], in1=dtb[:].unsqueeze(1).to_broadcast([128, 4, 8]), op=ALU.add),
             reads=[Bpb, Bc], writes=[Btmpb])
        f.op(f.act, lambda: nc.scalar.activation(out=tmpb[:], in_=tmpb[:], func=AF.Exp), reads=[Btmpb], writes=[Btmpb])
        f.op(f.act, lambda: nc.scalar.activation(out=tmpb[:], in_=tmpb[:], func=AF.Ln, bias=C.one[:], scale=1.0), reads=[Btmpb, C.B], writes=[Btmpb])
        f.op(f.dve, lambda: nc.vector.tensor_tensor(out=blt[:, :, 8:16], in0=tmpb[:], in1=negA[:].unsqueeze(1).to_broadcast([128, 4, 8]), op=ALU.mult),
             reads=[Btmpb, Bc], writes=[Bblt])
        f.dma(f.sp, scr["bl"][cs, :].rearrange("(s p) c -> p s c", p=128), blt[:], reads=[Bblt], writes=[Bscr])
    barrier(f)
    sc.close()


def gdn_core_phase(f, X, gnw_d, wout_d, scr, NT, L):
    nc = f.nc
    sc = Scope(nc)
    C = Consts(f, sc)
    H = GH
    mk = lambda n, shp, dt=F32: sc.sb(uname(n), shp, dt)
    Wout = mk("Wout", [128, H, D], BF16)
    gnw = mk("gnw", [128, 1])
    qTb = mk("qTb", [128, H, 128], BF16)
    kTb = mk("kTb", [128, H, 128], BF16)
    ktokb = mk("ktokb", [128, H, 128], BF16)
    vtokb = mk("vtokb", [128, H, 128], BF16)
    gTb = mk("gTb", [128, H, 128], BF16)
    bl = mk("bl", [128, 16])
    Xs = mk("Xs", [128, H, 128])
    sm = mk("sm", [128, 32])
    sme = mk("sme", [128, 40])
    nbeta = mk("nbeta", [128, 8])
    tmp1 = mk("tmp1", [128, H, 128])
    tmp2 = mk("tmp2", [128, H, 128])
    LmT = mk("LmT", [128, H, 128])
    LmS = mk("LmS", [128, H, 128])
    WTN = mk("WTN", [128, H, 128])
    E = mk("E", [128, H, 128])
    Pm = [mk("Pm", [128, H, 128]) for _ in range(2)]
    PTm = [mk("PTm", [128, H, 128]) for _ in range(2)]
    TTm = [mk("TTm", [128, H, 128]) for _ in range(2)]
    TTb = mk("TTb", [128, H, 128], BF16)
    attnT = mk("attnT", [128, H, 128], BF16)
    qgT = mk("qgT", [128, H, 128], BF16)
    ktil = mk("ktil", [128, H, 128], BF16)
    vb = mk("vb", [128, H, 128])
    R = mk("R", [128, H, 128], BF16)
    vnew = mk("vnew", [128, H, 128], BF16)
    S = mk("S", [128, H, 128])
    Sb = mk("Sb", [128, H, 128], BF16)
    sqo = mk("sqo", [128, H, 128], BF16)
    rs = mk("rs", [128, H, 128])
    of32 = mk("of32", [128, H, 128])
    ofb = mk("ofb", [128, H, 128], BF16)
    xt = mk("xtb", [128, KC, 128])
    PA = sc.ps(uname("PA"), [128, H, 128])
    PB = sc.ps(uname("PB"), [128, H, 128])
    PC = sc.ps(uname("PC"), [128, H, 128])
    PD = sc.ps(uname("PD"), [128, H, 128])
    shared = "c W bl sm sme nb xt scr qin kin ktin vtin gin"
    grouped = "Xs t1 t2 LmT LmS WTN E TTb attnT qgT ktil vb R vnew S Sb sqo rs of32 ofb PA PB PC PD P0 P1 PT0 PT1 TT0 TT1"
    Bf = {n: Buf(n) for n in shared.split()}
    for n in grouped.split():
        for gi in range(2):
            Bf[n + "#%d" % gi] = Buf(n)
    g = lambda *ns: [Bf[n] for n in ns]

    def gg(gi, *ns):
        return [Bf[n + "#%d" % gi] for n in ns]
    gall = lambda *ns: [Bf[n + "#%d" % gi] for n in ns for gi in range(2)]
    Bc = Bf["c"]

    f.dma(f.sp, gnw[:], gnw_d.rearrange("(p o) -> p o", o=1), writes=[Bc])
    woutv = wout_d.rearrange("(h p) d -> p h d", p=128)
    f.dma(f.pool, Wout[:, 0:4, :], woutv[:, 0:4, :], writes=g("W"))
    f.dma(f.pool, Wout[:, 4:8, :], woutv[:, 4:8, :], writes=g("W"))
    Xv = X.rearrange("(c p) t -> p c t", p=128)
    V, A, P_, G_ = f.dve, f.act, f.pe, f.pool
    HS = [slice(0, 4), slice(4, 8)]

    nblk = NT // 128
    for blk in range(nblk):
        t0 = blk * 128
        ts = slice(t0, t0 + 128)
        if t0 % L == 0:
            for gi in range(2):
                f.op(G_, lambda gi=gi: nc.gpsimd.memset(S[:, HS[gi], :], 0.0), writes=gg(gi, "S"))
                f.op(G_, lambda gi=gi: nc.gpsimd.memset(Sb[:, HS[gi], :], 0.0), writes=gg(gi, "Sb"))
        f.dma(f.sp, qTb[:], scr["qT"][:, ts].rearrange("(h p) t -> p h t", p=128), writes=g("qin"))
        f.dma(f.sp, kTb[:], scr["kT"][:, ts].rearrange("(h p) t -> p h t", p=128), writes=g("kin"))
        f.dma(f.sp, gTb[:], scr["gT"][:, ts].rearrange("(h p) t -> p h t", p=128), writes=g("gin"))
        f.dma(f.sp, ktokb[:], scr["ktok"][ts, :].rearrange("p (h d) -> p h d", d=128), writes=g("ktin"))
        f.dma(f.sp, vtokb[:], scr["vtok"][ts, :].rearrange("p (h d) -> p h d", d=128), writes=g("vtin"))
        f.dma(f.sp, bl[:], scr["bl"][ts, :], writes=g("bl"))
        f.dma(f.sp, xt[:], Xv[:, :, ts], writes=g("xt"))
        beta = bl[:, 0:8]
        la = bl[:, 8:16]
        for gi in range(2):
            hs = HS[gi]
            f.op(G_, lambda hs=hs: nc.gpsimd.tensor_tensor(out=Xs[:, hs, :], in0=bc_i(la[:, hs]), in1=bc_h(C.tri[:], 4), op=ALU.mult),
                 reads=g("bl") + [C.B], writes=gg(gi, "Xs"))
            f.op(P_, lambda hs=hs: nc.tensor.matmul(PA[:, hs, :], C.ones32[:], Xs[:, hs, :], start=True, stop=True),
                 reads=gg(gi, "Xs") + [C.B], writes=gg(gi, "PA"))
        f.op(P_, lambda: nc.tensor.matmul(PD[:, 0, 0:8], C.tri[:], la, start=True, stop=True), reads=g("bl") + [C.B], writes=gg(0, "PD"))
        f.op(P_, lambda: nc.tensor.matmul(PD[:, 0, 8:16], C.bd[:], la, start=True, stop=True), reads=g("bl") + [C.B], writes=gg(0, "PD"))
        f.op(P_, lambda: nc.tensor.matmul(PD[:, 0, 16:24], C.cind[:, 0, :], la, start=True, stop=True), reads=g("bl") + [C.B], writes=gg(0, "PD"))
        f.op(P_, lambda: nc.tensor.matmul(PD[:, 0, 24:32], C.cind[:, 1, :], la, start=True, stop=True), reads=g("bl") + [C.B], writes=gg(0, "PD"))
        f.op(V, lambda: nc.vector.tensor_copy(out=sm[:], in_=PD[:, 0, 0:32]), reads=gg(0, "PD"), writes=g("sm"))
        gcol = sm[:, 0:8]
        f.op(A, lambda: nc.scalar.activation(out=sme[:, 0:8], in_=sm[:, 0:8], func=AF.Exp), reads=g("sm"), writes=g("sme"))
        f.op(V, lambda: nc.vector.tensor_tensor(out=sme[:, 8:16], in0=sm[:, 8:16], in1=sm[:, 0:8], op=ALU.subtract), reads=g("sm"), writes=g("sme"))
        f.op(A, lambda: nc.scalar.activation(out=sme[:, 8:16], in_=sme[:, 8:16], func=AF.Exp), reads=g("sme"), writes=g("sme"))
        f.op(A, lambda: nc.scalar.activation(out=sme[:, 16:32], in_=sm[:, 16:32], func=AF.Exp), reads=g("sm"), writes=g("sme"))
        f.op(V, lambda: nc.vector.scalar_tensor_tensor(out=sme[:, 32:40], in0=sme[:, 0:8], scalar=-1.0, in1=beta, op0=ALU.mult, op1=ALU.mult),
             reads=g("sme", "bl"), writes=g("sme"))
        f.op(V, lambda: nc.vector.tensor_scalar(nbeta[:], beta, -1.0, None, ALU.mult), reads=g("bl"), writes=g("nb"))
        for gi in range(2):
            hs = HS[gi]
            f.op(V, lambda hs=hs: nc.vector.tensor_tensor(out=tmp1[:, hs, :], in0=PA[:, hs, :], in1=bc_i(gcol[:, hs]), op=ALU.subtract),
                 reads=gg(gi, "PA") + g("sm"), writes=gg(gi, "t1"))
            f.op(V, lambda hs=hs: nc.vector.scalar_tensor_tensor(out=tmp2[:, hs, :], in0=tmp1[:, hs, :], scalar=-1.0, in1=bc_h(C.negstr[:], 4),
                                                                 op0=ALU.mult, op1=ALU.add),
                 reads=gg(gi, "t1") + [C.B], writes=gg(gi, "t2"))
            f.op(G_, lambda hs=hs: nc.gpsimd.tensor_tensor(out=tmp1[:, hs, :], in0=tmp1[:, hs, :], in1=bc_h(C.negincT[:], 4), op=ALU.add),
                 reads=gg(gi, "t2") + [C.B], writes=gg(gi, "t1"))
            f.op(A, lambda hs=hs: nc.scalar.activation(out=LmT[:, hs, :], in_=tmp1[:, hs, :], func=AF.Exp), reads=gg(gi, "t1"), writes=gg(gi, "LmT"))
            f.op(A, lambda hs=hs: nc.scalar.activation(out=LmS[:, hs, :], in_=tmp2[:, hs, :], func=AF.Exp), reads=gg(gi, "t2"), writes=gg(gi, "LmS"))
            f.op(A, lambda hs=hs: nc.scalar.activation(out=E[:, hs, :], in_=PA[:, hs, :], func=AF.Exp), reads=gg(gi, "PA"), writes=gg(gi, "E"))
            f.op(G_, lambda hs=hs: nc.gpsimd.tensor_tensor(out=LmS[:, hs, :], in0=LmS[:, hs, :], in1=bc_i(nbeta[:, hs]), op=ALU.mult),
                 reads=gg(gi, "LmS") + g("nb"), writes=gg(gi, "LmS"))
            f.op(G_, lambda hs=hs: nc.gpsimd.tensor_tensor(out=Xs[:, hs, :], in0=bc_i(beta[:, hs]), in1=bc_h(C.ident[:], 4), op=ALU.mult),
                 reads=g("bl") + [C.B], writes=gg(gi, "Xs"))
            f.op(P_, lambda hs=hs: nc.tensor.matmul(PB[:, hs, :], C.ones32[:], Xs[:, hs, :], start=True, stop=True),
                 reads=gg(gi, "Xs") + [C.B], writes=gg(gi, "PB"))
            f.op(G_, lambda hs=hs: nc.gpsimd.tensor_tensor(out=WTN[:, hs, :], in0=LmT[:, hs, :], in1=bc_h(C.strT01[:], 4), op=ALU.mult),
                 reads=gg(gi, "LmT") + [C.B], writes=gg(gi, "WTN"))
            f.op(V, lambda hs=hs: nc.vector.scalar_tensor_tensor(out=WTN[:, hs, :], in0=PB[:, hs, :], scalar=-1.0, in1=WTN[:, hs, :],
                                                                 op0=ALU.mult, op1=ALU.mult),
                 reads=gg(gi, "PB"), writes=gg(gi, "WTN"))
        for gi in range(2):
            hs = HS[gi]
            for h in range(4 * gi, 4 * gi + 4):
                f.op(P_, lambda h=h: nc.tensor.matmul(PC[:, h, :], kTb[:, h, :], kTb[:, h, :], start=True, stop=True), reads=g("kin"), writes=gg(gi, "PC"))
            f.op(V, lambda hs=hs: nc.vector.tensor_tensor(out=Pm[0][:, hs, :], in0=PC[:, hs, :], in1=LmS[:, hs, :], op=ALU.mult),
                 reads=gg(gi, "PC", "LmS"), writes=gg(gi, "P0"))
            f.op(V, lambda hs=hs: nc.vector.tensor_tensor(out=PTm[0][:, hs, :], in0=PC[:, hs, :], in1=WTN[:, hs, :], op=ALU.mult),
                 reads=gg(gi, "PC", "WTN"), writes=gg(gi, "PT0"))
            for h in range(4 * gi, 4 * gi + 4):
                f.op(P_, lambda h=h: nc.tensor.matmul(PB[:, h, :], kTb[:, h, :], qTb[:, h, :], start=True, stop=True), reads=g("kin", "qin"), writes=gg(gi, "PB"))
            f.op(V, lambda hs=hs: nc.vector.tensor_tensor(out=attnT[:, hs, :], in0=PB[:, hs, :], in1=LmT[:, hs, :], op=ALU.mult),
                 reads=gg(gi, "PB", "LmT"), writes=gg(gi, "attnT"))
            f.op(G_, lambda hs=hs: nc.gpsimd.tensor_tensor(out=TTm[0][:, hs, :], in0=PTm[0][:, hs, :], in1=bc_h(C.ident[:], 4), op=ALU.add),
                 reads=gg(gi, "PT0") + [C.B], writes=gg(gi, "TT0"))
        for k in range(1, 6):
            cur, nxt = (k - 1) % 2, k % 2
            Pc, PTc, Pn, PTn = "P%d" % cur, "PT%d" % cur, "P%d" % nxt, "PT%d" % nxt
            TTc, TTn = "TT%d" % cur, "TT%d" % nxt
            for gi in range(2):
                hs = HS[gi]
                for h in range(4 * gi, 4 * gi + 4):
                    f.op(P_, lambda h=h: nc.tensor.matmul(PC[:, h, :], PTm[cur][:, h, :], Pm[cur][:, h, :], start=True, stop=True),
                         reads=gg(gi, Pc, PTc), writes=gg(gi, "PC"))
                f.op(A, lambda hs=hs: nc.scalar.copy(out=Pm[nxt][:, hs, :], in_=PC[:, hs, :]), reads=gg(gi, "PC"), writes=gg(gi, Pn))
                if k < 5:
                    for h in range(4 * gi, 4 * gi + 4):
                        f.op(P_, lambda h=h: nc.tensor.matmul(PB[:, h, :], Pm[cur][:, h, :], PTm[cur][:, h, :], start=True, stop=True),
                             reads=gg(gi, Pc, PTc), writes=gg(gi, "PB"))
                    f.op(A, lambda hs=hs: nc.scalar.copy(out=PTm[nxt][:, hs, :], in_=PB[:, hs, :]), reads=gg(gi, "PB"), writes=gg(gi, PTn))
            for gi in range(2):
                hs = HS[gi]
                for h in range(4 * gi, 4 * gi + 4):
                    f.op(P_, lambda h=h: nc.tensor.matmul(PA[:, h, :], Pm[nxt][:, h, :], TTm[cur][:, h, :], start=True, stop=True),
                         reads=gg(gi, Pn, TTc), writes=gg(gi, "PA"))
                f.op(V, lambda hs=hs: nc.vector.tensor_tensor(out=TTm[nxt][:, hs, :], in0=PA[:, hs, :], in1=TTm[cur][:, hs, :], op=ALU.add),
                     reads=gg(gi, "PA", TTc), writes=gg(gi, TTn))
        for gi in range(2):
            hs = HS[gi]
            f.op(A, lambda hs=hs: nc.scalar.copy(out=TTb[:, hs, :], in_=TTm[1][:, hs, :]), reads=gg(gi, "TT1"), writes=gg(gi, "TTb"))
            f.op(G_, lambda hs=hs: nc.gpsimd.tensor_tensor(out=qgT[:, hs, :], in0=qTb[:, hs, :], in1=E[:, hs, :], op=ALU.mult),
                 reads=g("qin") + gg(gi, "E"), writes=gg(gi, "qgT"))
            f.op(G_, lambda hs=hs: nc.gpsimd.tensor_tensor(out=ktil[:, hs, :], in0=ktokb[:, hs, :], in1=bc_i(sme[:, 8:16][:, hs]), op=ALU.mult),
                 reads=g("ktin", "sme"), writes=gg(gi, "ktil"))
            f.op(G_, lambda hs=hs: nc.gpsimd.tensor_tensor(out=vb[:, hs, :], in0=vtokb[:, hs, :], in1=bc_i(beta[:, hs]), op=ALU.mult),
                 reads=g("vtin", "bl"), writes=gg(gi, "vb"))
        for c in range(2):
            r = slice(64 * c, 64 * c + 64)
            for gi in range(2):
                hs = HS[gi]
                for h in range(4 * gi, 4 * gi + 4):
                    f.op(P_, lambda h=h: nc.tensor.matmul(PC[r, h, :], kTb[:, h, r], Sb[:, h, :], start=True, stop=True),
                         reads=g("kin") + gg(gi, "Sb"), writes=gg(gi, "PC"))
            for gi in range(2):
                for h in range(4 * gi, 4 * gi + 4):
                    f.op(V, lambda h=h: nc.vector.scalar_tensor_tensor(out=R[r, h, :], in0=PC[r, h, :], scalar=sme[r, 32 + h:33 + h], in1=vb[r, h, :],
                                                                       op0=ALU.mult, op1=ALU.add),
                         reads=gg(gi, "PC", "vb") + g("sme"), writes=gg(gi, "R"))
                for h in range(4 * gi, 4 * gi + 4):
                    f.op(P_, lambda h=h: nc.tensor.matmul(PB[r, h, :], TTb[r, h, r], R[r, h, :], start=True, stop=True),
                         reads=gg(gi, "TTb", "R"), writes=gg(gi, "PB"))
                f.op(A, lambda gi=gi: nc.scalar.copy(out=vnew[r, HS[gi], :], in_=PB[r, HS[gi], :]), reads=gg(gi, "PB"), writes=gg(gi, "vnew"))
            for gi in range(2):
                for h in range(4 * gi, 4 * gi + 4):
                    f.op(P_, lambda h=h: nc.tensor.matmul(PD[:, h, r], Sb[:, h, :], qgT[:, h, r], start=True, stop=False),
                         reads=gg(gi, "Sb", "qgT"), writes=gg(gi, "PD"))
                    f.op(P_, lambda h=h: nc.tensor.matmul(PD[:, h, r], vnew[r, h, :], attnT[r, h, r], start=False, stop=True),
                         reads=gg(gi, "vnew", "attnT"), writes=gg(gi, "PD"))
                for h in range(4 * gi, 4 * gi + 4):
                    f.op(P_, lambda h=h: nc.tensor.matmul(PA[:, h, :], ktil[r, h, :], vnew[r, h, :], start=True, stop=True),
                         reads=gg(gi, "ktil", "vnew"), writes=gg(gi, "PA"))
            for gi in range(2):
                for h in range(4 * gi, 4 * gi + 4):
                    f.op(V, lambda h=h: nc.vector.scalar_tensor_tensor(out=S[:, h, :], in0=S[:, h, :], scalar=sme[:, 16 + 8 * c + h:17 + 8 * c + h],
                                                                       in1=PA[:, h, :], op0=ALU.mult, op1=ALU.add),
                         reads=gg(gi, "PA") + g("sme"), writes=gg(gi, "S"))
                f.op(A, lambda gi=gi: nc.scalar.copy(out=Sb[:, HS[gi], :], in_=S[:, HS[gi], :]), reads=gg(gi, "S"), writes=gg(gi, "Sb"))
        for gi in range(2):
            hs = HS[gi]
            f.op(A, lambda hs=hs: nc.scalar.activation(out=sqo[:, hs, :], in_=PD[:, hs, :], func=AF.Square), reads=gg(gi, "PD"), writes=gg(gi, "sqo"))
            f.op(P_, lambda hs=hs: nc.tensor.matmul(PC[:, hs, :], C.ones_bf[:], sqo[:, hs, :], start=True, stop=True),
                 reads=gg(gi, "sqo") + [C.B], writes=gg(gi, "PC"))
            f.op(A, lambda hs=hs: nc.scalar.activation(out=rs[:, hs, :], in_=PC[:, hs, :], func=AF.Sqrt, bias=C.eps[:], scale=1.0 / 128),
                 reads=gg(gi, "PC") + [C.B], writes=gg(gi, "rs"))
            f.op(V, lambda hs=hs: nc.vector.reciprocal(out=rs[:, hs, :], in_=rs[:, hs, :]), reads=gg(gi, "rs"), writes=gg(gi, "rs"))
            f.op(V, lambda hs=hs: nc.vector.scalar_tensor_tensor(out=of32[:, hs, :], in0=PD[:, hs, :], scalar=gnw[:, 0:1], in1=rs[:, hs, :],
                                                                 op0=ALU.mult, op1=ALU.mult),
                 reads=gg(gi, "PD", "rs") + [Bc], writes=gg(gi, "of32"))
            f.op(G_, lambda hs=hs: nc.gpsimd.tensor_tensor(out=ofb[:, hs, :], in0=of32[:, hs, :], in1=gTb[:, hs, :], op=ALU.mult),
                 reads=gg(gi, "of32") + g("gin"), writes=gg(gi, "ofb"))
        for dc in range(KC):
            for h in range(H):
                f.op(P_, lambda dc=dc, h=h: nc.tensor.matmul(PB[:, dc, :], Wout[:, h, dc * 128:(dc + 1) * 128], ofb[:, h, :],
                                                            start=(h == 0), stop=(h == H - 1)),
                     reads=g("W") + gall("ofb"), writes=gg(dc // 4, "PB"))
        f.op(V, lambda: nc.vector.tensor_tensor(out=xt[:], in0=PB[:], in1=xt[:], op=ALU.add), reads=gall("PB"), writes=g("xt"))
        f.dma(f.sp, Xv[:, :, ts], xt[:], reads=g("xt"), writes=g("scr"))
    barrier(f)
    sc.close()


EVIN = 2560
HH = 4


def ev_proj_phase(f, X, wn_d, win_d, lbl_d, j, scr, NT, L):
    nc = f.nc
    sc = Scope(nc)
    C = Consts(f, sc)
    mk = lambda n, shp, dt=F32: sc.sb(uname(n), shp, dt)
    Win = mk("Win", [128, KC, EVIN], BF16)
    wn = mk("wn", [128, KC])
    xt = mk("xt", [128, KC, TT])
    hT = mk("hT", [128, KC, TT], BF16)
    sq = mk("sq", [128, KC, TT], BF16)
    rstd = mk("rstd", [128, TT])
    lg = mk("lg", [128, 2, 4])
    lb = mk("lb", [128, 4])
    oml = mk("oml", [128, 4])
    ob = [mk("ob", [128, TT], BF16) for _ in range(2)]
    fs = [mk("fs", [128, TT]) for _ in range(2)]
    lf = [mk("lf", [128, TT]) for _ in range(2)]
    vt = [mk("vt", [128, 512], BF16) for _ in range(2)]
    pp = [sc.ps(uname("pp"), [128, TT]) for _ in range(2)]
    pv = [sc.ps(uname("pv"), [128, 512]) for _ in range(2)]
    pss = sc.ps(uname("pss"), [128, TT])
    Bc, Bx, Bh, Bsq, Bpss, Brstd, Bscr = [Buf() for _ in range(7)]
    BW = [Buf() for _ in range(5)]
    Bpp = [Buf(), Buf()]; Bpv = [Buf(), Buf()]; Bob = [Buf(), Buf()]; Bfs = [Buf(), Buf()]; Blf = [Buf(), Buf()]; Bvt = [Buf(), Buf()]
    V, A, P_ = f.dve, f.act, f.pe

    f.dma(f.sp, wn[:], wn_d.rearrange("(c p) -> p c", p=128), writes=[Bc], allow_slow_non_contiguous=True)
    for l in range(2):
        f.dma(f.sp, lg[:, l, :], lbl_d[l, :].rearrange("(c p) -> p c", p=128), writes=[Bc], allow_slow_non_contiguous=True)
    if j == 0:
        f.op(V, lambda: nc.vector.memset(lb[:], 0.0), writes=[Bc])
    else:
        f.op(V, lambda: nc.vector.tensor_tensor(out=lb[:], in0=lg[:, 1, :], in1=lg[:, 0, :], op=ALU.subtract), reads=[Bc], writes=[Bc])
        f.op(A, lambda: nc.scalar.activation(out=lb[:], in_=lb[:], func=AF.Sigmoid), reads=[Bc], writes=[Bc])
    f.op(V, lambda: nc.vector.tensor_scalar(oml[:], lb[:], -1.0, 1.0, ALU.mult, ALU.add), reads=[Bc], writes=[Bc])
    winv = win_d.rearrange("(kc p) f -> p kc f", p=128)
    for i in range(5):
        f.dma(f.pool, Win[:, :, i * 512:(i + 1) * 512], winv[:, :, i * 512:(i + 1) * 512], writes=[BW[i]])
    Xv = X.rearrange("(c p) t -> p c t", p=128)
    for t in range(NT // TT):
        cs = slice(t * TT, (t + 1) * TT)
        f.dma(f.sp, xt[:], Xv[:, :, cs], writes=[Bx])
        rms_stats(f, sc, xt[:], sq[:], Bx, Bsq, C.ones_bf[:], C.eps[:], pss[:], Bpss, rstd[:], Brstd, TT, 1.0 / D)
        for c in range(KC):
            f.op(V, lambda c=c: nc.vector.scalar_tensor_tensor(out=hT[:, c, :], in0=xt[:, c, :], scalar=wn[:, c:c + 1],
                                                             in1=rstd[:], op0=ALU.mult, op1=ALU.mult),
                 reads=[Bx, Brstd, Bc], writes=[Bh])
        for oc in list(range(0, 8)) + list(range(12, 20)):
            b = oc % 2
            wi = oc // 4
            hc = oc % 4
            for kc in range(KC):
                f.op(P_, lambda kc=kc, oc=oc, b=b: nc.tensor.matmul(pp[b][:], Win[:, kc, oc * 128:(oc + 1) * 128], hT[:, kc, :],
                                                                    start=(kc == 0), stop=(kc == KC - 1)),
                     reads=[BW[wi], Bh], writes=[Bpp[b]])
            rows = slice(hc * 128, (hc + 1) * 128)
            if oc < 4 or 12 <= oc < 16:
                f.op(A, lambda b=b: nc.scalar.activation(out=ob[b][:], in_=pp[b][:], func=AF.Silu), reads=[Bpp[b]], writes=[Bob[b]])
                dst = scr["qT"] if oc < 4 else scr["gT"]
                f.dma(f.sp, dst[rows, cs], ob[b][:], reads=[Bob[b]], writes=[Bscr])
            elif oc >= 16:
                f.op(A, lambda b=b: nc.scalar.copy(out=ob[b][:], in_=pp[b][:]), reads=[Bpp[b]], writes=[Bob[b]])
                f.dma(f.sp, scr["uT"][rows, cs], ob[b][:], reads=[Bob[b]], writes=[Bscr])
            else:
                f.op(A, lambda b=b: nc.scalar.activation(out=fs[b][:], in_=pp[b][:], func=AF.Sigmoid), reads=[Bpp[b]], writes=[Bfs[b]])
                f.op(V, lambda b=b, hc=hc: nc.vector.tensor_scalar(fs[b][:], fs[b][:], oml[:, hc:hc + 1], lb[:, hc:hc + 1], ALU.mult, ALU.add),
                     reads=[Bc], writes=[Bfs[b]])
                f.op(V, lambda b=b: nc.vector.tensor_scalar(ob[b][:], fs[b][:], -1.0, 1.0, ALU.mult, ALU.add), reads=[Bfs[b]], writes=[Bob[b]])
                f.dma(f.sp, scr["kT"][rows, cs], ob[b][:], reads=[Bob[b]], writes=[Bscr])
                f.op(V, lambda b=b: nc.vector.tensor_scalar(lf[b][:], fs[b][:], 1e-6, None, ALU.max), reads=[Bfs[b]], writes=[Blf[b]])
                f.op(A, lambda b=b: nc.scalar.activation(out=lf[b][:], in_=lf[b][:], func=AF.Ln), reads=[Blf[b]], writes=[Blf[b]])
                f.dma(f.sp, scr["lfT"][rows, cs], lf[b][:], reads=[Blf[b]], writes=[Bscr])
        for s in range(4):
            b = s % 2
            for kc in range(KC):
                f.op(P_, lambda kc=kc, s=s, b=b: nc.tensor.matmul(pv[b][:], hT[:, kc, s * 128:(s + 1) * 128], Win[:, kc, 1024:1536],
                                                                 start=(kc == 0), stop=(kc == KC - 1)),
                     reads=[BW[2], Bh], writes=[Bpv[b]])
            f.op(A, lambda b=b: nc.scalar.copy(out=vt[b][:], in_=pv[b][:]), reads=[Bpv[b]], writes=[Bvt[b]])
            f.dma(f.sp, scr["vtok"][t * TT + s * 128:t * TT + (s + 1) * 128, 0:512], vt[b][:], reads=[Bvt[b]], writes=[Bscr])
    barrier(f)
    sc.close()


def hgrn_core_phase(f, hnw_d, scr, NT, L):
    nc = f.nc
    sc = Scope(nc)
    C = Consts(f, sc)
    mk = lambda n, shp, dt=F32: sc.sb(uname(n), shp, dt)
    NCH = L // 64
    NB = L // 128
    NSQ = NT // L
    hnw = mk("hnw", [128, 1])
    onesL = mk("onesL", [128, L])
    V, A, P_, G_ = f.dve, f.act, f.pe, f.pool
    Bcst = Buf()
    f.dma(f.sp, hnw[:], hnw_d.rearrange("(p o) -> p o", o=1), writes=[Bcst])
    f.op(V, lambda: nc.vector.memset(onesL[:], 1.0), writes=[Bcst])

    class SeqState:
        pass
    SS = []
    for q_ in range(NSQ):
        st = SeqState()
        st.qh = mk("qh", [128, L], BF16); st.kh = mk("kh", [128, L], BF16); st.gh = mk("gh", [128, L], BF16)
        st.lfh = mk("lfh", [128, L]); st.Bcs = mk("Bcs", [128, L]); st.dif = mk("dif", [128, L])
        st.ee = mk("ee", [128, L])
        st.qt = mk("qt", [128, L], BF16); st.kt = mk("kt", [128, L], BF16)
        st.bprev = mk("bprev", [128, NCH]); st.sca = mk("sca", [128, 3, NCH])
        st.vb = [mk("vblk", [128, 128], BF16) for _ in range(2)]
        st.ktok = [mk("ktokh", [128, 128], BF16) for _ in range(2)]
        st.attnT = [mk("attnTh", [128, 128], BF16) for _ in range(2)]
        st.S = mk("Sh", [128, 128]); st.St = mk("Sth", [128, 128], BF16); st.dSs = mk("dSs", [128, 128])
        st.sqo = mk("sqoh", [128, 128], BF16); st.rs = mk("rsh", [128, 128]); st.o32 = mk("o32h", [128, 128])
        st.yo = [mk("yoh", [128, 128], BF16) for _ in range(2)]
        st.pat = sc.ps(uname("pat"), [128, 512])[:, 0:128]
        st.ptr = sc.ps(uname("ptrh"), [128, 1024], BF16)[:, 0:128]
        st.po = sc.ps(uname("po"), [128, 512])[:, 0:128]
        st.pds = sc.ps(uname("pds"), [128, 512])
        names = "q k g lf B dif ee qt kt bp sca S St dSs sqo rs o32 pds pn pat ptr po v0 v1 ktok0 ktok1 attnT0 attnT1 yo0 yo1 scr"
        st.Bf = {n: Buf(n) for n in names.split()}
        SS.append(st)

    for h in range(HH):
        rows = slice(h * 128, (h + 1) * 128)
        for q_, st in enumerate(SS):
            g = lambda *ns, st=st: [st.Bf[n] for n in ns]
            s0 = q_ * L
            f.dma(f.sp, st.qh[:], scr["qT"][rows, s0:s0 + L], writes=g("q"))
            f.dma(f.sp, st.kh[:], scr["kT"][rows, s0:s0 + L], writes=g("k"))
            f.dma(f.sp, st.gh[:], scr["gT"][rows, s0:s0 + L], writes=g("g"))
            f.dma(f.sp, st.lfh[:], scr["lfT"][rows, s0:s0 + L], writes=g("lf"))
            f.op(V, lambda st=st: nc.vector.tensor_tensor_scan(out=st.Bcs[:], data0=onesL[:], data1=st.lfh[:], initial=0.0, op0=ALU.mult, op1=ALU.add),
                 reads=g("lf") + [Bcst], writes=g("B"))
            B3 = st.Bcs[:].rearrange("p (c s) -> p c s", s=64)
            bmid = B3[:, :, 31]
            blast = B3[:, :, 63]
            f.op(G_, lambda st=st: nc.gpsimd.memset(st.bprev[:, 0:1], 0.0), writes=g("bp"))
            f.op(G_, lambda st=st, B3=B3: nc.gpsimd.tensor_copy(out=st.bprev[:, 1:NCH], in_=B3[:, 0:NCH - 1, 63]), reads=g("B"), writes=g("bp"))
            f.op(G_, lambda st=st, blast=blast: nc.gpsimd.tensor_tensor(out=st.sca[:, 0, :], in0=blast, in1=st.bprev[:], op=ALU.subtract), reads=g("B", "bp"), writes=g("sca"))
            f.op(G_, lambda st=st, blast=blast, bmid=bmid: nc.gpsimd.tensor_tensor(out=st.sca[:, 1, :], in0=blast, in1=bmid, op=ALU.subtract), reads=g("B"), writes=g("sca"))
            f.op(G_, lambda st=st, bmid=bmid: nc.gpsimd.tensor_tensor(out=st.sca[:, 2, :], in0=bmid, in1=st.bprev[:], op=ALU.subtract), reads=g("B", "bp"), writes=g("sca"))
            f.op(A, lambda st=st: nc.scalar.activation(out=st.sca[:], in_=st.sca[:], func=AF.Exp), reads=g("sca"), writes=g("sca"))
            f.op(G_, lambda st=st, B3=B3, bmid=bmid: nc.gpsimd.tensor_tensor(out=st.dif[:].rearrange("p (c s) -> p c s", s=64), in0=B3,
                                                                          in1=bmid.unsqueeze(2).to_broadcast([128, NCH, 64]), op=ALU.subtract),
                 reads=g("B"), writes=g("dif"))
            f.op(A, lambda st=st: nc.scalar.activation(out=st.ee[:], in_=st.dif[:], func=AF.Exp), reads=g("dif"), writes=g("ee"))
            f.op(V, lambda st=st: nc.vector.tensor_tensor(out=st.qt[:], in0=st.qh[:], in1=st.ee[:], op=ALU.mult), reads=g("q", "ee"), writes=g("qt"))
            f.op(A, lambda st=st: nc.scalar.activation(out=st.ee[:], in_=st.dif[:], func=AF.Exp, scale=-1.0), reads=g("dif", "qt"), writes=g("ee"))
            f.op(V, lambda st=st: nc.vector.tensor_tensor(out=st.kt[:], in0=st.kh[:], in1=st.ee[:], op=ALU.mult), reads=g("k", "ee"), writes=g("kt"))
            f.op(G_, lambda st=st: nc.gpsimd.memset(st.S[:], 0.0), writes=g("S"))
        for blk in range(NB):
            b = blk % 2
            bs = slice(blk * 128, (blk + 1) * 128)
            sb_ = str(b)
            for q_, st in enumerate(SS):
                g = lambda *ns, st=st: [st.Bf[n] for n in ns]
                s0 = q_ * L
                f.dma(f.sp, st.vb[b][:], scr["vtok"][s0 + blk * 128:s0 + (blk + 1) * 128, h * 128:(h + 1) * 128], writes=g("v" + sb_))
                f.op(P_, lambda st=st: nc.tensor.matmul(st.pat, st.kt[:, bs], st.qt[:, bs], start=True, stop=True), reads=g("kt", "qt"), writes=g("pat"))
                f.op(V, lambda st=st: nc.vector.tensor_tensor(out=st.attnT[b][:], in0=st.pat, in1=C.tri[:], op=ALU.mult),
                     reads=g("pat") + [C.B], writes=g("attnT" + sb_))
                f.op(P_, lambda st=st: nc.tensor.transpose(st.ptr, st.kt[:, bs], C.ident_bf[:]), reads=g("kt") + [C.B], writes=g("ptr"))
                f.op(A, lambda st=st: nc.scalar.copy(out=st.ktok[b][:], in_=st.ptr), reads=g("ptr"), writes=g("ktok" + sb_))
                f.op(P_, lambda st=st: nc.tensor.matmul(st.po, st.vb[b][:], st.attnT[b][:], start=True, stop=False),
                     reads=g("v" + sb_, "attnT" + sb_), writes=g("po"))
            for c in range(2):
                ci = blk * 2 + c
                r = slice(64 * c, 64 * c + 64)
                cols = slice(blk * 128 + 64 * c, blk * 128 + 64 * c + 64)
                for q_, st in enumerate(SS):
                    g = lambda *ns, st=st: [st.Bf[n] for n in ns]
                    f.op(V, lambda st=st: nc.vector.tensor_scalar(st.St[:], st.S[:], st.sca[:, 2, ci:ci + 1], None, ALU.mult), reads=g("S", "sca"), writes=g("St"))
                    f.op(P_, lambda st=st: nc.tensor.matmul(st.po[:, r], st.St[:], st.qt[:, cols], start=False, stop=(c == 1)),
                         reads=g("St", "qt"), writes=g("po"))
                    f.op(P_, lambda st=st: nc.tensor.matmul(st.pds[:, 0:128], st.ktok[b][r, :], st.vb[b][r, :], start=True, stop=True),
                         reads=g("ktok" + sb_, "v" + sb_), writes=g("pds"))
                for q_, st in enumerate(SS):
                    g = lambda *ns, st=st: [st.Bf[n] for n in ns]
                    f.op(A, lambda st=st: nc.scalar.activation(out=st.dSs[:], in_=st.pds[:, 0:128], func=AF.Identity, scale=st.sca[:, 1, ci:ci + 1]),
                         reads=g("pds", "sca"), writes=g("dSs"))
                for q_, st in enumerate(SS):
                    g = lambda *ns, st=st: [st.Bf[n] for n in ns]
                    f.op(V, lambda st=st: nc.vector.scalar_tensor_tensor(out=st.S[:], in0=st.S[:], scalar=st.sca[:, 0, ci:ci + 1], in1=st.dSs[:],
                                                                       op0=ALU.mult, op1=ALU.add), reads=g("dSs", "sca"), writes=g("S"))
            for q_, st in enumerate(SS):
                g = lambda *ns, st=st: [st.Bf[n] for n in ns]
                f.op(A, lambda st=st: nc.scalar.activation(out=st.sqo[:], in_=st.po, func=AF.Square), reads=g("po"), writes=g("sqo"))
                f.op(P_, lambda st=st: nc.tensor.matmul(st.pds[:, 128:256], C.ones_bf[:], st.sqo[:], start=True, stop=True), reads=g("sqo") + [C.B], writes=g("pds"))
                f.op(A, lambda st=st: nc.scalar.activation(out=st.rs[:], in_=st.pds[:, 128:256], func=AF.Sqrt, bias=C.eps[:], scale=1.0 / 128),
                     reads=g("pds") + [C.B], writes=g("rs"))
            for q_, st in enumerate(SS):
                g = lambda *ns, st=st: [st.Bf[n] for n in ns]
                s0 = q_ * L
                f.op(V, lambda st=st: nc.vector.reciprocal(out=st.rs[:], in_=st.rs[:]), reads=g("rs"), writes=g("rs"))
                f.op(V, lambda st=st: nc.vector.scalar_tensor_tensor(out=st.o32[:], in0=st.po, scalar=hnw[:, 0:1], in1=st.rs[:], op0=ALU.mult, op1=ALU.mult),
                     reads=g("po", "rs") + [Bcst], writes=g("o32"))
                f.op(G_, lambda st=st: nc.gpsimd.tensor_tensor(out=st.yo[b][:], in0=st.o32[:], in1=st.gh[:, bs], op=ALU.mult), reads=g("o32", "g"), writes=g("yo" + sb_))
                f.dma(f.sp, scr["yT"][h * 128:(h + 1) * 128, s0 + blk * 128:s0 + (blk + 1) * 128], st.yo[b][:], reads=g("yo" + sb_), writes=g("scr"))
    barrier(f)
    sc.close()


PI = 3.14159265358979


def s5_phase(f, p, j, scr, NT, L):
    nc = f.nc
    sc = Scope(nc)
    C = Consts(f, sc)
    mk = lambda n, shp, dt=F32: sc.sb(uname(n), shp, dt)
    V, A, P_ = f.dve, f.act, f.pe
    NS = 16
    NCH = NT // 64
    CPS = L // 64
    NSEQ = NT // L
    Bp = Buf("prep")
    gp = [Bp]
    ar = mk("ar", [128, NS]); ai = mk("ai", [128, NS]); nai = mk("nai", [128, NS])
    pwr = mk("pwr", [128, NS, 64]); pwi = mk("pwi", [128, NS, 64]); npwi = mk("npwi", [128, NS, 64])
    a64r = mk("a64r", [128, NS]); a64i = mk("a64i", [128, NS]); na64i = mk("na64i", [128, NS])
    Btab = [mk("Btab", [128, NS, 128], BF16) for _ in range(2)]
    TCre = mk("TCre", [128, NS, 128], BF16); TCimn = mk("TCimn", [128, NS, 128], BF16)
    dv = mk("dvec", [128, 4])
    Wglu = mk("Wglu", [128, 4, 512], BF16)
    scp = Scope(nc)
    mkp = lambda n, shp, dt=F32: scp.sb(uname(n), shp, dt)
    are = mkp("are", [128, NS]); aim = mkp("aim", [128, NS]); dtl = mkp("dtl", [128, NS])
    mag = mkp("mag", [128, NS]); ang = mkp("ang", [128, NS]); ang2 = mkp("ang2", [128, NS]); kk = mkp("kk", [128, NS]); tmpa = mkp("tmpa", [128, NS])
    cre = mkp("cre", [128, NS]); cim = mkp("cim", [128, NS]); den = mkp("den", [128, NS]); zr = mkp("zr", [128, NS])
    a2r = mkp("a2r", [128, NS]); a2i = mkp("a2i", [128, NS]); t3a = mkp("t3a", [128, NS, 32])
    Braw = [mkp("Braw", [128, NS, 128]) for _ in range(2)]
    Craw = [mkp("Craw", [128, NS, 128]) for _ in range(2)]
    c1 = mkp("c1", [128, NS, 128]); c2 = mkp("c2", [128, NS, 128])
    f.dma(f.sp, are[:], p["a_re"].rearrange("(t g) n -> (g n) t", g=2), writes=gp, allow_slow_non_contiguous=True)
    f.dma(f.sp, aim[:], p["a_im"].rearrange("(t g) n -> (g n) t", g=2), writes=gp, allow_slow_non_contiguous=True)
    ldv = p["log_dt"].rearrange("(t g) -> g t", g=2)
    for g2 in range(2):
        f.dma(f.sp, dtl[g2 * 64:(g2 + 1) * 64, :], ldv[g2:g2 + 1, :].to_broadcast([64, NS]), writes=gp, allow_slow_non_contiguous=True)
    op = lambda eng, fn: f.op(eng, fn, reads=gp, writes=gp)
    op(A, lambda: nc.scalar.activation(out=dtl[:], in_=dtl[:], func=AF.Exp))
    op(V, lambda: nc.vector.tensor_tensor(out=mag[:], in0=dtl[:], in1=are[:], op=ALU.mult))
    op(A, lambda: nc.scalar.activation(out=mag[:], in_=mag[:], func=AF.Exp))
    op(V, lambda: nc.vector.tensor_tensor(out=ang[:], in0=dtl[:], in1=aim[:], op=ALU.mult))
    op(V, lambda: nc.vector.tensor_scalar(ang2[:], ang[:], PI / 2, None, ALU.add))
    for a_ in (ang, ang2):
        op(V, lambda: nc.vector.memset(kk[:], 0.0))
        for m in (1, 3, 5, 7, 9):
            op(V, lambda a_=a_, m=m: nc.vector.tensor_scalar(tmpa[:], a_[:], m * PI, None, ALU.is_gt))
            op(V, lambda: nc.vector.tensor_tensor(out=kk[:], in0=kk[:], in1=tmpa[:], op=ALU.add))
        op(V, lambda a_=a_: nc.vector.scalar_tensor_tensor(out=a_[:], in0=kk[:], scalar=-2 * PI, in1=a_[:], op0=ALU.mult, op1=ALU.add))
        op(V, lambda a_=a_: nc.vector.tensor_scalar(a_[:], a_[:], PI, -PI, ALU.min, ALU.max))
    op(A, lambda: nc.scalar.activation(out=ai[:], in_=ang[:], func=AF.Sin))
    op(A, lambda: nc.scalar.activation(out=ar[:], in_=ang2[:], func=AF.Sin))
    op(V, lambda: nc.vector.tensor_tensor(out=ai[:], in0=ai[:], in1=mag[:], op=ALU.mult))
    op(V, lambda: nc.vector.tensor_tensor(out=ar[:], in0=ar[:], in1=mag[:], op=ALU.mult))
    op(V, lambda: nc.vector.tensor_scalar(nai[:], ai[:], -1.0, None, ALU.mult))
    op(V, lambda: nc.vector.tensor_tensor(out=den[:], in0=are[:], in1=are[:], op=ALU.mult))
    op(V, lambda: nc.vector.tensor_tensor(out=tmpa[:], in0=aim[:], in1=aim[:], op=ALU.mult))
    op(V, lambda: nc.vector.tensor_tensor(out=den[:], in0=den[:], in1=tmpa[:], op=ALU.add))
    op(V, lambda: nc.vector.reciprocal(out=den[:], in_=den[:]))
    op(V, lambda: nc.vector.tensor_scalar(zr[:], ar[:], -1.0, None, ALU.add))
    op(V, lambda: nc.vector.tensor_tensor(out=cre[:], in0=zr[:], in1=are[:], op=ALU.mult))
    op(V, lambda: nc.vector.tensor_tensor(out=tmpa[:], in0=ai[:], in1=aim[:], op=ALU.mult))
    op(V, lambda: nc.vector.tensor_tensor(out=cre[:], in0=cre[:], in1=tmpa[:], op=ALU.add))
    op(V, lambda: nc.vector.tensor_tensor(out=cre[:], in0=cre[:], in1=den[:], op=ALU.mult))
    op(V, lambda: nc.vector.tensor_tensor(out=cim[:], in0=ai[:], in1=are[:], op=ALU.mult))
    op(V, lambda: nc.vector.tensor_tensor(out=tmpa[:], in0=zr[:], in1=aim[:], op=ALU.mult))
    op(V, lambda: nc.vector.tensor_tensor(out=cim[:], in0=cim[:], in1=tmpa[:], op=ALU.subtract))
    op(V, lambda: nc.vector.tensor_tensor(out=cim[:], in0=cim[:], in1=den[:], op=ALU.mult))
    op(V, lambda: nc.vector.tensor_copy(out=pwr[:, :, 0], in_=ar[:]))
    op(V, lambda: nc.vector.tensor_copy(out=pwi[:, :, 0], in_=ai[:]))
    op(V, lambda: nc.vector.tensor_copy(out=a2r[:], in_=ar[:]))
    op(V, lambda: nc.vector.tensor_copy(out=a2i[:], in_=ai[:]))
    n = 1
    while n < 64:
        br = bc_i(a2r[:], n); bi = bc_i(a2i[:], n)
        op(V, lambda n=n, br=br: nc.vector.tensor_tensor(out=pwr[:, :, n:2 * n], in0=pwr[:, :, 0:n], in1=br, op=ALU.mult))
        op(V, lambda n=n, bi=bi: nc.vector.tensor_tensor(out=t3a[:, :, 0:n], in0=pwi[:, :, 0:n], in1=bi, op=ALU.mult))
        op(V, lambda n=n: nc.vector.tensor_tensor(out=pwr[:, :, n:2 * n], in0=pwr[:, :, n:2 * n], in1=t3a[:, :, 0:n], op=ALU.subtract))
        op(V, lambda n=n, bi=bi: nc.vector.tensor_tensor(out=pwi[:, :, n:2 * n], in0=pwr[:, :, 0:n], in1=bi, op=ALU.mult))
        op(V, lambda n=n, br=br: nc.vector.tensor_tensor(out=t3a[:, :, 0:n], in0=pwi[:, :, 0:n], in1=br, op=ALU.mult))
        op(V, lambda n=n: nc.vector.tensor_tensor(out=pwi[:, :, n:2 * n], in0=pwi[:, :, n:2 * n], in1=t3a[:, :, 0:n], op=ALU.add))
        op(V, lambda: nc.vector.tensor_tensor(out=tmpa[:], in0=a2r[:], in1=a2i[:], op=ALU.mult))
        op(V, lambda: nc.vector.tensor_tensor(out=kk[:], in0=a2i[:], in1=a2i[:], op=ALU.mult))
        op(V, lambda: nc.vector.tensor_tensor(out=a2r[:], in0=a2r[:], in1=a2r[:], op=ALU.mult))
        op(V, lambda: nc.vector.tensor_tensor(out=a2r[:], in0=a2r[:], in1=kk[:], op=ALU.subtract))
        op(V, lambda: nc.vector.tensor_scalar(a2i[:], tmpa[:], 2.0, None, ALU.mult))
        n *= 2
    op(V, lambda: nc.vector.tensor_scalar(npwi[:], pwi[:], -1.0, None, ALU.mult))
    op(V, lambda: nc.vector.tensor_copy(out=a64r[:], in_=pwr[:, :, 63]))
    op(V, lambda: nc.vector.tensor_copy(out=a64i[:], in_=pwi[:, :, 63]))
    op(V, lambda: nc.vector.tensor_scalar(na64i[:], a64i[:], -1.0, None, ALU.mult))
    for ri, key in enumerate(("b_re", "b_im")):
        op(V, lambda ri=ri: nc.vector.memset(Braw[ri][:], 0.0))
        for g_ in range(32):
            st_, p0, g2 = g_ // 2, (g_ % 8) * 16, g_ % 2
            f.dma(f.sp, Braw[ri][p0:p0 + 16, st_, g2 * 64:(g2 + 1) * 64], p[key][g_].rearrange("n q -> q n"), reads=gp, writes=gp,
                  allow_slow_non_contiguous=True)
        op(V, lambda ri=ri: nc.vector.tensor_copy(out=Btab[ri][:], in_=Braw[ri][:]))
    for ri, key in enumerate(("c_re", "c_im")):
        op(V, lambda ri=ri: nc.vector.memset(Craw[ri][:], 0.0))
        cv_ = p[key].rearrange("(t g) q n -> g n t q", g=2)
        for g2 in range(2):
            for t_ in range(NS):
                c0 = 32 * (t_ % 4) + 16 * g2
                f.dma(f.sp, Craw[ri][g2 * 64:(g2 + 1) * 64, t_, c0:c0 + 16], cv_[g2, :, t_, :], reads=gp, writes=gp,
                      allow_slow_non_contiguous=True)
    op(V, lambda: nc.vector.tensor_tensor(out=c1[:], in0=Craw[0][:], in1=bc_i(cre[:], 128), op=ALU.mult))
    op(V, lambda: nc.vector.tensor_tensor(out=c2[:], in0=Craw[1][:], in1=bc_i(cim[:], 128), op=ALU.mult))
    op(V, lambda: nc.vector.tensor_tensor(out=TCre[:], in0=c1[:], in1=c2[:], op=ALU.subtract))
    op(V, lambda: nc.vector.tensor_tensor(out=c1[:], in0=Craw[0][:], in1=bc_i(cim[:], 128), op=ALU.mult))
    op(V, lambda: nc.vector.tensor_tensor(out=c2[:], in0=Craw[1][:], in1=bc_i(cre[:], 128), op=ALU.mult))
    op(V, lambda: nc.vector.tensor_tensor(out=c1[:], in0=c1[:], in1=c2[:], op=ALU.add))
    op(V, lambda: nc.vector.tensor_scalar(TCimn[:], c1[:], -1.0, None, ALU.mult))
    f.dma(f.sp, dv[:], p["d"].rearrange("(c p) -> p c", p=128), writes=gp, allow_slow_non_contiguous=True)
    f.dma(f.pool, Wglu[:], p["w_glu"].rearrange("(kc p) f -> p kc f", p=128), writes=gp)
    barrier(f)
    scp.close()

    uT = mk("uTkt", [128, NT], BF16)
    yg = mk("yg", [128, 4, NT], BF16)
    bu = [mk("bu", [128, 64, NCH]) for _ in range(2)]
    hb = [[mk("hb", [128, 64, NCH], BF16) for _ in range(2)] for _ in range(4)]
    Hs = [mk("Hs", [128, NSEQ, CPS + 1]) for _ in range(2)]
    ysb = mk("ysb", [128, 512])
    x2 = mk("x2g", [128, 512]); zz = mk("zzg", [128, 512])
    pbu = [[sc.ps(uname("pbu"), [128, 512]) for _ in range(2)] for _ in range(2)]
    py = [sc.ps(uname("py"), [128, 512]) for _ in range(2)]
    pgl = [sc.ps(uname("pgl"), [128, 512]) for _ in range(2)]
    names = "u yg ysb x2 zz scr"
    Bf = {n: Buf(n) for n in names.split()}
    Bur = [Buf() for _ in range(64)]; Bui = [Buf() for _ in range(64)]
    BHr = [Buf() for _ in range(CPS + 1)]; BHi = [Buf() for _ in range(CPS + 1)]
    for n in ["pbu0", "pbu1", "py", "pgl"]:
        Bf[n + "0"] = Buf(); Bf[n + "1"] = Buf()
    for sl in range(4):
        Bf["hbr%d" % sl] = Buf(); Bf["hbi%d" % sl] = Buf()
    g = lambda *ns: [Bf[n] for n in ns]
    bun = (Bur, Bui)
    for ot in range(4):
      f.dma(f.sp, uT[:], scr["uT"][ot * 128:(ot + 1) * 128, :], writes=g("u"))
      for sl in range(4):
        st = 4 * ot + sl
        arS, aiS, naiS = ar[:, st:st + 1], ai[:, st:st + 1], nai[:, st:st + 1]
        hbn = ("hbr%d" % sl, "hbi%d" % sl)
        for pc in range(NT // 512):
            b = pc % 2
            for ri in range(2):
                f.op(P_, lambda ri=ri, b=b, pc=pc: nc.tensor.matmul(pbu[ri][b][:], Btab[ri][:, st, :], uT[:, pc * 512:(pc + 1) * 512],
                                                                    start=True, stop=True),
                     reads=g("u") + gp, writes=g("pbu%d%d" % (ri, b)))
                f.op(A, lambda ri=ri, b=b, pc=pc: nc.scalar.copy(out=bu[ri][:, :, pc * 8:(pc + 1) * 8].rearrange("p s c -> p c s"),
                                                                in_=pbu[ri][b][:].rearrange("p (c s) -> p c s", s=64)),
                     reads=g("pbu%d%d" % (ri, b)), writes=bun[ri])
        for tau in range(1, 64):
            X1 = lambda tau=tau: f.op(V, lambda: nc.vector.scalar_tensor_tensor(out=bu[0][:, tau, :], in0=bu[1][:, tau - 1, :], scalar=naiS, in1=bu[0][:, tau, :],
                                                                   op0=ALU.mult, op1=ALU.add), reads=[Bui[tau - 1]] + gp, writes=[Bur[tau]])
            X2 = lambda tau=tau: f.op(V, lambda: nc.vector.scalar_tensor_tensor(out=bu[1][:, tau, :], in0=bu[0][:, tau - 1, :], scalar=aiS, in1=bu[1][:, tau, :],
                                                                   op0=ALU.mult, op1=ALU.add), reads=[Bur[tau - 1]] + gp, writes=[Bui[tau]])
            D1 = lambda tau=tau: f.op(V, lambda: nc.vector.scalar_tensor_tensor(out=bu[0][:, tau, :], in0=bu[0][:, tau - 1, :], scalar=arS, in1=bu[0][:, tau, :],
                                                                   op0=ALU.mult, op1=ALU.add), reads=[Bur[tau - 1]] + gp, writes=[Bur[tau]])
            D2 = lambda tau=tau: f.op(V, lambda: nc.vector.scalar_tensor_tensor(out=bu[1][:, tau, :], in0=bu[1][:, tau - 1, :], scalar=arS, in1=bu[1][:, tau, :],
                                                                   op0=ALU.mult, op1=ALU.add), reads=[Bui[tau - 1]] + gp, writes=[Bui[tau]])
            for o_ in ((X1, X2, D1, D2) if tau % 2 == 1 else (X2, X1, D2, D1)):
                o_()
        f.op(V, lambda: nc.vector.memset(Hs[0][:, :, 0:1], 0.0), writes=[BHr[0]])
        f.op(V, lambda: nc.vector.memset(Hs[1][:, :, 0:1], 0.0), writes=[BHi[0]])
        lastr = bu[0][:, 63, :].rearrange("p (b c) -> p b c", c=CPS)
        lasti = bu[1][:, 63, :].rearrange("p (b c) -> p b c", c=CPS)
        A64r, A64i, NA64i = a64r[:, st:st + 1], a64i[:, st:st + 1], na64i[:, st:st + 1]
        for c in range(CPS):
            C1 = lambda c=c: f.op(V, lambda: nc.vector.scalar_tensor_tensor(out=Hs[0][:, :, c + 1], in0=Hs[1][:, :, c], scalar=NA64i, in1=lastr[:, :, c],
                                                               op0=ALU.mult, op1=ALU.add), reads=[Bur[63], BHi[c]] + gp, writes=[BHr[c + 1]])
            C2 = lambda c=c: f.op(V, lambda: nc.vector.scalar_tensor_tensor(out=Hs[0][:, :, c + 1], in0=Hs[0][:, :, c], scalar=A64r, in1=Hs[0][:, :, c + 1],
                                                               op0=ALU.mult, op1=ALU.add), reads=[BHr[c]] + gp, writes=[BHr[c + 1]])
            C3 = lambda c=c: f.op(V, lambda: nc.vector.scalar_tensor_tensor(out=Hs[1][:, :, c + 1], in0=Hs[0][:, :, c], scalar=A64i, in1=lasti[:, :, c],
                                                               op0=ALU.mult, op1=ALU.add), reads=[Bui[63], BHr[c]] + gp, writes=[BHi[c + 1]])
            C4 = lambda c=c: f.op(V, lambda: nc.vector.scalar_tensor_tensor(out=Hs[1][:, :, c + 1], in0=Hs[1][:, :, c], scalar=A64r, in1=Hs[1][:, :, c + 1],
                                                               op0=ALU.mult, op1=ALU.add), reads=[BHi[c]] + gp, writes=[BHi[c + 1]])
            for o_ in ((C1, C3, C2, C4) if c % 2 == 0 else (C3, C1, C4, C2)):
                o_()
        Hr = Hs[0][:, :, 0:CPS]; Hi = Hs[1][:, :, 0:CPS]
        for tau in range(64):
            pr, pi_, npi = pwr[:, st, tau:tau + 1], pwi[:, st, tau:tau + 1], npwi[:, st, tau:tau + 1]
            br3 = bu[0][:, tau, :].rearrange("p (b c) -> p b c", c=CPS)
            bi3 = bu[1][:, tau, :].rearrange("p (b c) -> p b c", c=CPS)
            hr3 = hb[sl][0][:, tau, :].rearrange("p (b c) -> p b c", c=CPS)
            hi3 = hb[sl][1][:, tau, :].rearrange("p (b c) -> p b c", c=CPS)
            f.op(V, lambda: nc.vector.scalar_tensor_tensor(out=br3, in0=Hi, scalar=npi, in1=br3, op0=ALU.mult, op1=ALU.add), reads=BHi[0:CPS] + gp, writes=[Bur[tau]])
            f.op(V, lambda: nc.vector.scalar_tensor_tensor(out=bi3, in0=Hr, scalar=pi_, in1=bi3, op0=ALU.mult, op1=ALU.add), reads=BHr[0:CPS] + gp, writes=[Bui[tau]])
            f.op(V, lambda: nc.vector.scalar_tensor_tensor(out=hr3, in0=Hr, scalar=pr, in1=br3, op0=ALU.mult, op1=ALU.add), reads=BHr[0:CPS] + [Bur[tau]] + gp, writes=g(hbn[0]))
            f.op(V, lambda: nc.vector.scalar_tensor_tensor(out=hi3, in0=Hi, scalar=pr, in1=bi3, op0=ALU.mult, op1=ALU.add), reads=BHi[0:CPS] + [Bui[tau]] + gp, writes=g(hbn[1]))
      tpp = 512 // NCH
      for pc in range(64 * NCH // 512):
        b = pc % 2
        k = 0
        for sl in range(4):
            st = 4 * ot + sl
            for ri, TC in enumerate((TCre, TCimn)):
                hbf = hb[sl][ri][:].rearrange("p s c -> p (s c)")
                f.op(P_, lambda pc=pc, b=b, TC=TC, st=st, hbf=hbf, k=k: nc.tensor.matmul(py[b][:], TC[:, st, :], hbf[:, pc * 512:(pc + 1) * 512],
                                                                                      start=(k == 0), stop=(k == 7)),
                     reads=g("hbr%d" % sl, "hbi%d" % sl) + gp, writes=g("py%d" % b))
                k += 1
        uview = uT[:, :].rearrange("p (c s) -> p s c", s=64)[:, pc * tpp:(pc + 1) * tpp, :]
        ygview = yg[:, ot, :].rearrange("p (c s) -> p s c", s=64)[:, pc * tpp:(pc + 1) * tpp, :]
        y3 = ysb[:, :].rearrange("p (s c) -> p s c", c=NCH)
        z3 = zz[:, :].rearrange("p (s c) -> p s c", c=NCH)
        f.op(V, lambda uview=uview, y3=y3, b=b: nc.vector.scalar_tensor_tensor(out=y3, in0=uview, scalar=dv[:, ot:ot + 1],
                                                                              in1=py[b][:].rearrange("p (s c) -> p s c", c=NCH),
                                                                              op0=ALU.mult, op1=ALU.add),
             reads=g("py%d" % b, "u") + gp, writes=g("ysb"))
        f.op(A, lambda: nc.scalar.activation(out=x2[:], in_=ysb[:], func=AF.Square), reads=g("ysb"), writes=g("x2"))
        f.op(V, lambda: nc.vector.tensor_scalar(x2[:], x2[:], 0.044715, 1.0, ALU.mult, ALU.add), reads=g("x2"), writes=g("x2"))
        f.op(V, lambda: nc.vector.tensor_tensor(out=zz[:], in0=x2[:], in1=ysb[:], op=ALU.mult), reads=g("x2", "ysb"), writes=g("zz"))
        f.op(A, lambda: nc.scalar.activation(out=zz[:], in_=zz[:], func=AF.Sigmoid, scale=1.5957691216), reads=g("zz"), writes=g("zz"))
        f.op(V, lambda ygview=ygview, y3=y3, z3=z3: nc.vector.tensor_tensor(out=ygview, in0=y3, in1=z3, op=ALU.mult), reads=g("zz", "ysb"), writes=g("yg"))
    sgl = [mk("sgl", [128, 512]) for _ in range(2)]
    og = [mk("og", [128, 512], BF16) for _ in range(2)]
    Bs = [Buf(), Buf()]; Bo = [Buf(), Buf()]
    for t in range(NT // 512):
        cs = slice(t * 512, (t + 1) * 512)
        for oc in range(4):
            b = oc % 2
            for kc in range(4):
                f.op(P_, lambda kc=kc, oc=oc, b=b, cs=cs: nc.tensor.matmul(pgl[b][:], Wglu[:, kc, oc * 128:(oc + 1) * 128], yg[:, kc, cs],
                                                                        start=(kc == 0), stop=(kc == 3)),
                     reads=g("yg") + gp, writes=g("pgl%d" % b))
            f.op(A, lambda b=b: nc.scalar.activation(out=sgl[b][:], in_=pgl[b][:], func=AF.Sigmoid), reads=g("pgl%d" % b), writes=[Bs[b]])
            f.op(V, lambda b=b, oc=oc, cs=cs: nc.vector.tensor_tensor(out=og[b][:], in0=sgl[b][:], in1=yg[:, oc, cs], op=ALU.mult),
                 reads=[Bs[b]] + g("yg"), writes=[Bo[b]])
            f.dma(f.sp, scr["yT"][512 + oc * 128:512 + (oc + 1) * 128, cs], og[b][:], reads=[Bo[b]], writes=g("scr"))
    barrier(f)
    sc.close()


def ev_out_phase(f, X, wout_d, scr, NT):
    nc = f.nc
    sc = Scope(nc)
    mk = lambda n, shp, dt=F32: sc.sb(uname(n), shp, dt)
    Wo = mk("Wo", [128, KC, D], BF16)
    yt = mk("yt", [128, KC, TT], BF16)
    xt = mk("xt", [128, KC, TT])
    po = [sc.ps(uname("pout"), [128, TT]) for _ in range(2)]
    BW, By, Bx, Bs = Buf(), Buf(), Buf(), Buf()
    Bp = [Buf(), Buf()]
    wv = wout_d.rearrange("(kc p) d -> p kc d", p=128)
    f.dma(f.pool, Wo[:, 0:4, :], wv[:, 0:4, :], writes=[BW])
    f.dma(f.pool, Wo[:, 4:8, :], wv[:, 4:8, :], writes=[BW])
    Xv = X.rearrange("(c p) t -> p c t", p=128)
    yv = scr["yT"].rearrange("(c p) t -> p c t", p=128)
    for t in range(NT // TT):
        cs = slice(t * TT, (t + 1) * TT)
        f.dma(f.sp, yt[:], yv[:, :, cs], writes=[By])
        f.dma(f.sp, xt[:], Xv[:, :, cs], writes=[Bx])
        for dc in range(KC):
            b = dc % 2
            for kc in range(KC):
                f.op(f.pe, lambda dc=dc, kc=kc, b=b: nc.tensor.matmul(po[b][:], Wo[:, kc, dc * 128:(dc + 1) * 128], yt[:, kc, :],
                                                                      start=(kc == 0), stop=(kc == KC - 1)),
                     reads=[BW, By], writes=[Bp[b]])
            f.op(f.dve, lambda dc=dc, b=b: nc.vector.tensor_tensor(out=xt[:, dc, :], in0=po[b][:], in1=xt[:, dc, :], op=ALU.add),
                 reads=[Bp[b]], writes=[Bx])
        f.dma(f.sp, Xv[:, :, cs], xt[:], reads=[Bx], writes=[Bs])
    barrier(f)
    sc.close()


SEQ = 2048
NSEQ_CORE = 2
NCORES = 8
DEPTH = 4

_IN_SHAPES = {
    "ffn1_norm": [4, 1024], "ffn1_w_gate": [4, 1024, 2816], "ffn1_w_up": [4, 1024, 2816], "ffn1_w_down": [4, 2816, 1024],
    "mix_norm": [4, 1024], "ffn2_norm": [4, 1024], "ffn2_w_gate": [4, 1024, 2816], "ffn2_w_up": [4, 1024, 2816],
    "ffn2_w_down": [4, 2816, 1024], "ev_w_in": [2, 1024, 2560], "hg_lb_logits": [2, 512], "hg_norm_w": [2, 128],
    "s5_a_re": [2, 32, 64], "s5_a_im": [2, 32, 64], "s5_b_re": [2, 32, 64, 16], "s5_b_im": [2, 32, 64, 16],
    "s5_c_re": [2, 32, 16, 64], "s5_c_im": [2, 32, 16, 64], "s5_d": [2, 512], "s5_log_dt": [2, 32], "s5_w_glu": [2, 512, 512],
    "ev_w_out": [2, 1024, 1024], "od_w_in": [2, 1024, 4112], "gdn_conv_w": [2, 4, 3072], "gdn_a_log": [2, 8], "gdn_dt_bias": [2, 8],
    "gdn_norm_w": [2, 128], "od_w_out": [2, 1024, 1024], "final_norm": [1024],
}


def build_program(L=SEQ, nseq=NSEQ_CORE, depth=DEPTH):
    NT = L * nseq
    f = FW()
    nc = f.nc
    I = {k: nc.dram_tensor(k, list(shp), F32, kind="ExternalInput").ap() for k, shp in _IN_SHAPES.items()}
    xT = nc.dram_tensor("xT", [D, NT], F32, kind="ExternalInput").ap()
    oT = nc.dram_tensor("oT", [D, NT], F32, kind="ExternalOutput").ap()
    X = nc.dram_tensor("Xres", [D, NT], F32).ap()
    dt_ = lambda n, shp, t: nc.dram_tensor(n, shp, t).ap()
    scr = {
        "qT": dt_("s_qT", [D, NT], BF16), "kT": dt_("s_kT", [D, NT], BF16), "gT": dt_("s_gT", [D, NT], BF16),
        "ktok": dt_("s_ktok", [NT, D], BF16), "vtok": dt_("s_vtok", [NT, D], BF16), "bl": dt_("s_bl", [NT, 16], F32),
        "uT": dt_("s_uT", [512, NT], BF16), "lfT": dt_("s_lfT", [512, NT], F32), "yT": dt_("s_yT", [D, NT], BF16),
    }
    src = xT
    for layer in range(depth):
        j = layer // 2
        ffn_phase(f, src, X, I["ffn1_norm"][layer], I["ffn1_w_gate"][layer], I["ffn1_w_up"][layer], I["ffn1_w_down"][layer], NT)
        src = X
        if layer % 2 == 0:
            ev_proj_phase(f, X, I["mix_norm"][layer], I["ev_w_in"][j], I["hg_lb_logits"], j, scr, NT, L)
            hgrn_core_phase(f, I["hg_norm_w"][j], scr, NT, L)
            p = {"a_re": I["s5_a_re"][j], "a_im": I["s5_a_im"][j], "b_re": I["s5_b_re"][j], "b_im": I["s5_b_im"][j],
                 "c_re": I["s5_c_re"][j], "c_im": I["s5_c_im"][j], "d": I["s5_d"][j], "log_dt": I["s5_log_dt"][j], "w_glu": I["s5_w_glu"][j]}
            s5_phase(f, p, j, scr, NT, L)
            ev_out_phase(f, X, I["ev_w_out"][j], scr, NT)
        else:
            gdn_proj_phase(f, X, I["mix_norm"][layer], I["od_w_in"][j], I["gdn_conv_w"][j], I["gdn_a_log"][j], I["gdn_dt_bias"][j], scr, NT, L)
            gdn_core_phase(f, X, I["gdn_norm_w"][j], I["od_w_out"][j], scr, NT, L)
        ffn_phase(f, X, X, I["ffn2_norm"][layer], I["ffn2_w_gate"][layer], I["ffn2_w_up"][layer], I["ffn2_w_down"][layer], NT)
    Bo = final_phase(f, src, oT, I["final_norm"], NT)
    f.finish([Bo])
    return f


def kernel(**inputs):
    x = np.asarray(inputs["x"], dtype=np.float32)
    Bsz, L, Dm = x.shape
    nseq = Bsz // NCORES
    f = build_program(L, nseq, DEPTH)
    shared = {k: np.ascontiguousarray(np.asarray(inputs[k], dtype=np.float32)) for k in _IN_SHAPES}
    in_maps = []
    for c in range(NCORES):
        m = dict(shared)
        m["xT"] = np.ascontiguousarray(x[c * nseq:(c + 1) * nseq].reshape(nseq * L, Dm).T)
        in_maps.append(m)
    res = run_bass_kernel_spmd(f.nc, in_maps, core_ids=list(range(NCORES)))
    out = np.empty((Bsz, L, Dm), dtype=np.float32)
    for c in range(NCORES):
        oT = np.asarray(res.results[c]["oT"])
        out[c * nseq:(c + 1) * nseq] = oT.T.reshape(nseq, L, Dm)
    return out
```

```python
import numpy as np
import concourse.bass as bass
import concourse.mybir as mybir
from concourse.bass_utils import run_bass_kernel_spmd

F32 = mybir.dt.float32
BF16 = mybir.dt.bfloat16
AF = mybir.ActivationFunctionType
ALU = mybir.AluOpType

EPOCH = 16000


class Eng:
    def __init__(self, fw, e, name, self_sync=True):
        self.fw = fw
        self.e = e
        self.name = name
        self.self_sync = self_sync
        self.sem = fw.nc.alloc_semaphore(name + "_s0")
        self.cnt = 0
        self.nep = 0
        self.seen = {}
        self.total = 0

    def _wait(self, deps):
        for d in deps:
            if d is None:
                continue
            sem, val, own = d
            if own is self and not self.self_sync:
                continue
            k = id(sem)
            if self.seen.get(k, 0) >= val:
                continue
            self.e.wait_ge(sem, val)
            self.seen[k] = val

    def emit(self, fn, deps=()):
        self._wait(deps)
        if self.cnt >= EPOCH:
            self.nep += 1
            self.sem = self.fw.nc.alloc_semaphore("%s_s%d" % (self.name, self.nep))
            self.cnt = 0
        ins = fn()
        self.cnt += 1
        self.total += 1
        ins.then_inc(self.sem, 1)
        return (self.sem, self.cnt, self)

    def dma(self, out, in_, deps=(), **kw):
        fw = self.fw
        self._wait(deps)
        slot = fw.dma_rr % len(fw.dma_sems)
        fw.dma_rr += 1
        sem = fw.dma_sems[slot]
        prev = fw.dma_vals[slot]
        if prev > 0:
            k = id(sem)
            if self.seen.get(k, 0) < prev:
                self.e.wait_ge(sem, prev)
                self.seen[k] = prev
        ins = self.e.dma_start(out=out, in_=in_, **kw)
        val = prev + 16
        fw.dma_vals[slot] = val
        ins.then_inc(sem, 16)
        return (sem, val, None)


class Buf:
    def __init__(self, name=""):
        self.name = name
        self.w = None
        self.r = {}


class FW:
    def __init__(self, n_dma_sems=40):
        self.nc = bass.Bass("TRN2", target_bir_lowering=False)
        nc = self.nc
        self.pe = Eng(self, nc.tensor, "pe", self_sync=False)
        self.act = Eng(self, nc.scalar, "act")
        self.dve = Eng(self, nc.vector, "dve")
        self.pool = Eng(self, nc.gpsimd, "pool")
        self.sp = Eng(self, nc.sync, "sp")
        self.dma_sems = [nc.alloc_semaphore("dma%d" % i) for i in range(n_dma_sems)]
        self.dma_vals = [0] * n_dma_sems
        self.dma_rr = 0

    def _deps(self, reads, writes):
        deps = []
        for b in reads:
            if b.w is not None:
                deps.append(b.w)
        for b in writes:
            if b.w is not None:
                deps.append(b.w)
            deps.extend(b.r.values())
        return deps

    def _post(self, tok, reads, writes):
        for b in reads:
            k = id(tok[0])
            o = b.r.get(k)
            if o is None or o[1] < tok[1]:
                b.r[k] = tok
        for b in writes:
            b.w = tok
            b.r = {}

    def op(self, eng, fn, reads=(), writes=()):
        tok = eng.emit(fn, self._deps(reads, writes))
        self._post(tok, reads, writes)
        return tok

    def dma(self, eng, out, in_, reads=(), writes=(), **kw):
        tok = eng.dma(out, in_, self._deps(reads, writes), **kw)
        self._post(tok, reads, writes)
        return tok

    def finish(self, bufs):
        deps = []
        for b in bufs:
            if b.w is not None:
                deps.append(b.w)
        self.sp._wait(deps)


D = 1024
DFF = 2816
KC = D // 128
FC = DFF // 128
TT = 512
EPS = 1e-6


class Scope:
    def __init__(self, nc):
        self.nc = nc
        self.guards = []

    def sb(self, name, shape, dt):
        g = self.nc.sbuf_tensor(name, shape, dt)
        t = g.__enter__()
        self.guards.append(g)
        return t

    def ps(self, name, shape, dt=F32):
        g = self.nc.psum_tensor(name, shape, dt)
        t = g.__enter__()
        self.guards.append(g)
        return t

    def close(self):
        for g in reversed(self.guards):
            g.__exit__(None, None, None)
        self.guards = []


def barrier(f):
    engs = [f.pe, f.act, f.dve, f.pool, f.sp]
    toks = []
    for e in engs:
        if e.cnt > 0:
            toks.append((e.sem, e.cnt, None))
    for s, v in zip(f.dma_sems, f.dma_vals):
        if v > 0:
            toks.append((s, v, None))
    for e in engs:
        e._wait(toks)


_uid = [0]


def uname(p):
    _uid[0] += 1
    return "%s_%d" % (p, _uid[0])


def rms_stats(f, sc, xt, sqbuf, Bx, Bsq, ones_bf, eps_t, pss, Bpss, rstd, Brstd, ncols, inv_n):
    nc = f.nc
    Bsq = Bsq if isinstance(Bsq, list) else [Bsq]
    f.op(f.act, lambda: nc.scalar.activation(out=sqbuf, in_=xt, func=AF.Square), reads=[Bx], writes=Bsq)
    for c in range(KC):
        f.op(f.pe, lambda c=c: nc.tensor.matmul(pss, ones_bf, sqbuf[:, c, :], start=(c == 0), stop=(c == KC - 1)),
             reads=Bsq, writes=[Bpss])
    f.op(f.act, lambda: nc.scalar.activation(out=rstd, in_=pss, func=AF.Sqrt, bias=eps_t, scale=inv_n),
         reads=[Bpss], writes=[Brstd])
    f.op(f.dve, lambda: nc.vector.reciprocal(out=rstd, in_=rstd), reads=[Brstd], writes=[Brstd])


def load_w_bf16(f, eng, dst_sb, src_ap, bufs, piece):
    nc = f.nc
    A, Bn = src_ap.shape[1], src_ap.shape[2]
    toks = []
    i = 0
    for b0 in range(0, Bn, piece):
        b1 = min(Bn, b0 + piece)
        f.dma(eng, dst_sb[:, :, b0:b1], src_ap[:, :, b0:b1], writes=[bufs[i]])
        i += 1


def ffn_phase(f, src, dst, wn_d, wg_d, wu_d, wd_d, NT):
    nc = f.nc
    sc = Scope(nc)
    Wg = sc.sb(uname("Wg"), [128, KC, DFF], BF16)
    Wu = sc.sb(uname("Wu"), [128, KC, DFF], BF16)
    Wd = sc.sb(uname("Wd"), [128, FC, D], BF16)
    wn = sc.sb(uname("wn"), [128, KC], F32)
    xt = [sc.sb(uname("xt"), [128, KC, TT], F32) for _ in range(2)]
    hT = sc.sb(uname("hT"), [128, KC, TT], BF16)
    sq = [sc.sb(uname("sq"), [128, TT], BF16) for _ in range(2)]
    act = sc.sb(uname("act"), [128, FC, TT], BF16)
    rstd = sc.sb(uname("rstd"), [128, TT], F32)
    sg = [sc.sb(uname("sg"), [128, TT], F32) for _ in range(2)]
    ones_bf = sc.sb(uname("ones"), [128, 128], BF16)
    eps_t = sc.sb(uname("eps"), [128, 1], F32)
    pg = [sc.ps(uname("pg"), [128, TT]) for _ in range(2)]
    pu = [sc.ps(uname("pu"), [128, TT]) for _ in range(2)]
    pd = [sc.ps(uname("pd"), [128, TT]) for _ in range(2)]
    pss = sc.ps(uname("pss"), [128, TT])

    PW = 512
    npc = (DFF + PW - 1) // PW
    BWg = [Buf() for _ in range(npc)]
    BWu = [Buf() for _ in range(npc)]
    BWd = [Buf() for _ in range(FC)]
    Bc, Bh, Bpss, Brstd = Buf(), Buf(), Buf(), Buf()
    Bsq = [Buf(), Buf()]
    Bx = [Buf(), Buf()]
    Bact = [Buf() for _ in range(FC)]
    Bpg = [Buf(), Buf()]
    Bpu = [Buf(), Buf()]
    Bsg = [Buf(), Buf()]
    Bpd = [Buf(), Buf()]
    Bdst = Buf()

    f.op(f.dve, lambda: nc.vector.memset(ones_bf[:], 1.0), writes=[Bc])
    f.op(f.dve, lambda: nc.vector.memset(eps_t[:], EPS), writes=[Bc])
    f.dma(f.sp, wn[:], wn_d.rearrange("(c p) -> p c", p=128), writes=[Bc], allow_slow_non_contiguous=True)
    srcv = src.rearrange("(c p) t -> p c t", p=128)
    dstv = dst.rearrange("(c p) t -> p c t", p=128)
    ntile = NT // TT

    def load(t):
        f.dma(f.sp, xt[t % 2][:], srcv[:, :, t * TT:(t + 1) * TT], writes=[Bx[t % 2]])

    def norm(t):
        x_ = xt[t % 2]
        for c in range(KC):
            f.op(f.act, lambda c=c: nc.scalar.activation(out=sq[c % 2][:], in_=x_[:, c, :], func=AF.Square), reads=[Bx[t % 2]], writes=[Bsq[c % 2]])
            f.op(f.pe, lambda c=c: nc.tensor.matmul(pss[:], ones_bf[:], sq[c % 2][:], start=(c == 0), stop=(c == KC - 1)),
                 reads=[Bsq[c % 2], Bc], writes=[Bpss])
        f.op(f.act, lambda: nc.scalar.activation(out=rstd[:], in_=pss[:], func=AF.Sqrt, bias=eps_t[:], scale=1.0 / D),
             reads=[Bpss, Bc], writes=[Brstd])
        f.op(f.dve, lambda: nc.vector.reciprocal(out=rstd[:], in_=rstd[:]), reads=[Brstd], writes=[Brstd])
        for c in range(KC):
            f.op(f.dve, lambda c=c: nc.vector.scalar_tensor_tensor(out=hT[:, c, :], in0=x_[:, c, :], scalar=wn[:, c:c + 1],
                                                                 in1=rstd[:], op0=ALU.mult, op1=ALU.mult),
                 reads=[Bx[t % 2], Brstd, Bc], writes=[Bh])

    load(0)
    wgv = wg_d.rearrange("(kc p) f -> p kc f", p=128)
    wuv = wu_d.rearrange("(kc p) f -> p kc f", p=128)
    wdv = wd_d.rearrange("(fc p) d -> p fc d", p=128)
    for i in range(npc):
        b0, b1 = i * PW, min(DFF, (i + 1) * PW)
        f.dma(f.pool, Wg[:, :, b0:b1], wgv[:, :, b0:b1], writes=[BWg[i]])
        f.dma(f.pool, Wu[:, :, b0:b1], wuv[:, :, b0:b1], writes=[BWu[i]])
    for i in range(0, FC, 2):
        f.dma(f.pool, Wd[:, i:i + 2, :], wdv[:, i:i + 2, :], writes=[BWd[i], BWd[i + 1]])
    norm(0)
    for t in range(ntile):
        x_ = xt[t % 2]
        if t + 1 < ntile:
            load(t + 1)
        for fc in range(FC):
            b = fc % 2
            wi = (fc * 128) // PW
            for kc in range(KC):
                f.op(f.pe, lambda kc=kc, fc=fc, b=b: nc.tensor.matmul(pg[b][:], Wg[:, kc, fc * 128:(fc + 1) * 128], hT[:, kc, :],
                                                                      start=(kc == 0), stop=(kc == KC - 1)),
                     reads=[BWg[wi], Bh], writes=[Bpg[b]])
            for kc in range(KC):
                f.op(f.pe, lambda kc=kc, fc=fc, b=b: nc.tensor.matmul(pu[b][:], Wu[:, kc, fc * 128:(fc + 1) * 128], hT[:, kc, :],
                                                                      start=(kc == 0), stop=(kc == KC - 1)),
                     reads=[BWu[wi], Bh], writes=[Bpu[b]])
            f.op(f.act, lambda b=b: nc.scalar.activation(out=sg[b][:], in_=pg[b][:], func=AF.Silu), reads=[Bpg[b]], writes=[Bsg[b]])
            f.op(f.dve, lambda b=b, fc=fc: nc.vector.tensor_tensor(out=act[:, fc, :], in0=pu[b][:], in1=sg[b][:], op=ALU.mult),
                 reads=[Bpu[b], Bsg[b]], writes=[Bact[fc]])
        if t + 1 < ntile:
            norm(t + 1)
        for dc in range(KC):
            b = dc % 2
            for fc in range(FC):
                f.op(f.pe, lambda dc=dc, fc=fc, b=b: nc.tensor.matmul(pd[b][:], Wd[:, fc, dc * 128:(dc + 1) * 128], act[:, fc, :],
                                                                      start=(fc == 0), stop=(fc == FC - 1)),
                     reads=[BWd[fc], Bact[fc]], writes=[Bpd[b]])
            f.op(f.dve, lambda dc=dc, b=b: nc.vector.scalar_tensor_tensor(out=x_[:, dc, :], in0=pd[b][:], scalar=0.5, in1=x_[:, dc, :],
                                                                       op0=ALU.mult, op1=ALU.add),
                 reads=[Bpd[b]], writes=[Bx[t % 2]])
        f.dma(f.sp, dstv[:, :, t * TT:(t + 1) * TT], x_[:], reads=[Bx[t % 2]], writes=[Bdst])
    barrier(f)
    sc.close()


def final_phase(f, src, dst, wn_d, NT):
    nc = f.nc
    sc = Scope(nc)
    wn = sc.sb(uname("wn"), [128, KC], F32)
    xt = sc.sb(uname("xt"), [128, KC, TT], F32)
    sq = sc.sb(uname("sq"), [128, KC, TT], BF16)
    rstd = sc.sb(uname("rstd"), [128, TT], F32)
    ones_bf = sc.sb(uname("ones"), [128, 128], BF16)
    eps_t = sc.sb(uname("eps"), [128, 1], F32)
    pss = sc.ps(uname("pss"), [128, TT])
    Bc, Bx, Bsq, Bpss, Brstd, Bdst = [Buf() for _ in range(6)]
    f.op(f.dve, lambda: nc.vector.memset(ones_bf[:], 1.0), writes=[Bc])
    f.op(f.dve, lambda: nc.vector.memset(eps_t[:], EPS), writes=[Bc])
    f.dma(f.sp, wn[:], wn_d.rearrange("(c p) -> p c", p=128), writes=[Bc], allow_slow_non_contiguous=True)
    srcv = src.rearrange("(c p) t -> p c t", p=128)
    dstv = dst.rearrange("(c p) t -> p c t", p=128)
    for t in range(NT // TT):
        cs = slice(t * TT, (t + 1) * TT)
        f.dma(f.sp, xt[:], srcv[:, :, cs], writes=[Bx])
        rms_stats(f, sc, xt[:], sq[:], Bx, Bsq, ones_bf[:], eps_t[:], pss[:], Bpss, rstd[:], Brstd, TT, 1.0 / D)
        for c in range(KC):
            f.op(f.dve, lambda c=c: nc.vector.scalar_tensor_tensor(out=xt[:, c, :], in0=xt[:, c, :], scalar=wn[:, c:c + 1],
                                                                 in1=rstd[:], op0=ALU.mult, op1=ALU.mult),
                 reads=[Brstd, Bc], writes=[Bx])
        f.dma(f.sp, dstv[:, :, cs], xt[:], reads=[Bx], writes=[Bdst])
    barrier(f)
    sc.close()
    return Bdst


NEG = -30000.0


class Consts:
    def __init__(self, f, sc):
        nc = f.nc
        self.B = Buf()
        B = self.B
        mk = lambda n, shp, dt=F32: sc.sb(uname(n), shp, dt)
        self.ones32 = mk("ones32", [128, 128])
        self.ones_bf = mk("onesbf", [128, 128], BF16)
        self.tri = mk("tri", [128, 128])
        self.low = mk("low", [128, 128])
        self.ident = mk("ident", [128, 128])
        self.ident_bf = mk("identbf", [128, 128], BF16)
        self.negincT = mk("negincT", [128, 128])
        self.negstr = mk("negstr", [128, 128])
        self.strT01 = mk("strT01", [128, 128])
        self.bd = mk("bd", [128, 128])
        self.cind = mk("cind", [128, 2, 128])
        self.eps = mk("epsc", [128, 1])
        self.one = mk("onec", [128, 1])
        P = f.pool
        f.op(P, lambda: nc.gpsimd.memset(self.ones32[:], 1.0), writes=[B])
        f.op(P, lambda: nc.gpsimd.memset(self.ones_bf[:], 1.0), writes=[B])
        f.op(P, lambda: nc.gpsimd.memset(self.eps[:], EPS), writes=[B])
        f.op(P, lambda: nc.gpsimd.memset(self.one[:], 1.0), writes=[B])
        f.op(P, lambda: nc.gpsimd.affine_select(out=self.tri[:], in_=self.ones32[:], pattern=[[1, 128]], compare_op=ALU.is_ge,
                                                fill=0.0, base=0, channel_multiplier=-1), reads=[B], writes=[B])
        f.op(P, lambda: nc.gpsimd.memset(self.tri[0:64, 64:128], 0.0), writes=[B])
        f.op(P, lambda: nc.gpsimd.affine_select(out=self.low[:], in_=self.ones32[:], pattern=[[-1, 128]], compare_op=ALU.is_gt,
                                                fill=0.0, base=0, channel_multiplier=1), reads=[B], writes=[B])
        f.op(P, lambda: nc.gpsimd.memset(self.low[64:128, 0:64], 0.0), writes=[B])
        f.op(P, lambda: nc.gpsimd.affine_select(out=self.ident[:], in_=self.ones32[:], pattern=[[-1, 128]], compare_op=ALU.is_equal,
                                                fill=0.0, base=0, channel_multiplier=1), reads=[B], writes=[B])
        f.op(P, lambda: nc.gpsimd.tensor_copy(out=self.ident_bf[:], in_=self.ident[:]), reads=[B], writes=[B])
        f.op(P, lambda: nc.gpsimd.tensor_scalar(self.negincT[:], self.tri[:], -1.0, -NEG, ALU.add, ALU.mult), reads=[B], writes=[B])
        f.op(P, lambda: nc.gpsimd.tensor_scalar(self.negstr[:], self.low[:], -1.0, -NEG, ALU.add, ALU.mult), reads=[B], writes=[B])
        f.op(P, lambda: nc.gpsimd.tensor_tensor(out=self.strT01[:], in0=self.tri[:], in1=self.ident[:], op=ALU.subtract), reads=[B], writes=[B])
        f.op(P, lambda: nc.gpsimd.memset(self.bd[:], 0.0), writes=[B])
        f.op(P, lambda: nc.gpsimd.memset(self.bd[0:64, 0:64], 1.0), writes=[B])
        f.op(P, lambda: nc.gpsimd.memset(self.bd[64:128, 64:128], 1.0), writes=[B])
        f.op(P, lambda: nc.gpsimd.memset(self.cind[:], 0.0), writes=[B])
        f.op(P, lambda: nc.gpsimd.memset(self.cind[0:64, 0, :], 1.0), writes=[B])
        f.op(P, lambda: nc.gpsimd.memset(self.cind[64:128, 1, :], 1.0), writes=[B])


def bc_h(ap2, H=8):
    return ap2.unsqueeze(1).to_broadcast([ap2.shape[0], H, ap2.shape[1]])


def bc_i(ap2, n=128):
    return ap2.unsqueeze(2).to_broadcast([ap2.shape[0], ap2.shape[1], n])


GH = 8
ODIN = 4112


def gdn_proj_phase(f, X, wn_d, win_d, convw_d, alog_d, dtb_d, scr, NT, L):
    nc = f.nc
    sc = Scope(nc)
    C = Consts(f, sc)
    Win = sc.sb(uname("Win"), [128, KC, ODIN], BF16)
    wn = sc.sb(uname("wn"), [128, KC], F32)
    xt = sc.sb(uname("xt"), [128, KC, TT], F32)
    hT = sc.sb(uname("hT"), [128, KC, TT], BF16)
    sq = sc.sb(uname("sq"), [128, KC, TT], BF16)
    rstd = sc.sb(uname("rstd"), [128, TT], F32)
    cw = sc.sb(uname("cw"), [128, 24, 4], F32)
    halo = sc.sb(uname("halo"), [128, 24, 3], F32)
    pre = [sc.sb(uname("pre"), [128, TT + 3], F32) for _ in range(2)]
    cv = [sc.sb(uname("cv"), [128, TT], F32) for _ in range(2)]
    s32 = [sc.sb(uname("s32"), [128, TT], F32) for _ in range(2)]
    sq2 = [sc.sb(uname("sq2"), [128, TT], BF16) for _ in range(2)]
    r2 = [sc.sb(uname("r2"), [128, TT], F32) for _ in range(2)]
    ob = [sc.sb(uname("ob"), [128, TT], BF16) for _ in range(2)]
    tk = [sc.sb(uname("tk"), [128, 4, 128], BF16) for _ in range(2)]
    blt = sc.sb(uname("blt"), [128, 4, 16], F32)
    tmpb = sc.sb(uname("tmpb"), [128, 4, 8], F32)
    dtb = sc.sb(uname("dtb"), [128, 8], F32)
    negA = sc.sb(uname("negA"), [128, 8], F32)
    eps128 = sc.sb(uname("eps128"), [128, 1], F32)
    pp = [sc.ps(uname("pp"), [128, TT]) for _ in range(2)]
    pn = [sc.ps(uname("pn"), [128, TT]) for _ in range(2)]
    ptr = [sc.ps(uname("ptr"), [128, 4, 128], BF16) for _ in range(2)]
    pss = sc.ps(uname("pss"), [128, TT])
    pb = sc.ps(uname("pb"), [128, 4, 16])

    Bc, Bx, Bh, Bsq, Bpss, Brstd, Bhalo, Bpb, Bblt, Btmpb = [Buf() for _ in range(10)]
    NW = 9
    BW = [Buf() for _ in range(NW)]
    Bpp = [Buf(), Buf()]; Bpn = [Buf(), Buf()]; Bptr = [Buf(), Buf()]
    Bpre = [Buf(), Buf()]; Bcv = [Buf(), Buf()]; Bs32 = [Buf(), Buf()]; Bsq2 = [Buf(), Buf()]
    Bcvh = [[Buf(), Buf()], [Buf(), Buf()]]
    Br2 = [Buf(), Buf()]; Bob = [Buf(), Buf()]; Btk = [Buf(), Buf()]
    Bscr = Buf()

    f.dma(f.sp, wn[:], wn_d.rearrange("(c p) -> p c", p=128), writes=[Bc], allow_slow_non_contiguous=True)
    for j in range(4):
        f.dma(f.sp, cw[:, :, j], convw_d[j, :].rearrange("(c p) -> p c", p=128), writes=[Bc], allow_slow_non_contiguous=True)
    f.op(f.dve, lambda: nc.vector.memset(eps128[:], EPS * 128.0), writes=[Bc])
    lnscl = sc.sb(uname("lnscl"), [128, 1], F32)
    zero1 = sc.sb(uname("zero1"), [128, 1], F32)
    f.op(f.dve, lambda: nc.vector.memset(lnscl[:], -2.4260151319598084), writes=[Bc])
    f.op(f.dve, lambda: nc.vector.memset(zero1[:], 0.0), writes=[Bc])
    f.dma(f.sp, dtb[:], dtb_d.partition_broadcast(128), writes=[Bc])
    f.dma(f.sp, negA[:], alog_d.partition_broadcast(128), writes=[Bc])
    f.op(f.act, lambda: nc.scalar.activation(out=negA[:], in_=negA[:], func=AF.Exp), reads=[Bc], writes=[Bc])
    f.op(f.dve, lambda: nc.vector.tensor_scalar(negA[:], negA[:], -1.0, None, ALU.mult), reads=[Bc], writes=[Bc])
    winv = win_d.rearrange("(kc p) f -> p kc f", p=128)
    for i in range(NW):
        b0, b1 = i * 512, min(ODIN, (i + 1) * 512)
        f.dma(f.pool, Win[:, :, b0:b1], winv[:, :, b0:b1], writes=[BW[i]])

    Xv = X.rearrange("(c p) t -> p c t", p=128)
    qTv, kTv, gTv = scr["qT"], scr["kT"], scr["gT"]
    for t in range(NT // TT):
        cs = slice(t * TT, (t + 1) * TT)
        seq_start = (t * TT) % L == 0
        f.dma(f.sp, xt[:], Xv[:, :, cs], writes=[Bx])
        rms_stats(f, sc, xt[:], sq[:], Bx, Bsq, C.ones_bf[:], C.eps[:], pss[:], Bpss, rstd[:], Brstd, TT, 1.0 / D)
        for c in range(KC):
            f.op(f.dve, lambda c=c: nc.vector.scalar_tensor_tensor(out=hT[:, c, :], in0=xt[:, c, :], scalar=wn[:, c:c + 1],
                                                                 in1=rstd[:], op0=ALU.mult, op1=ALU.mult),
                 reads=[Bx, Brstd, Bc], writes=[Bh])
        if seq_start:
            f.op(f.dve, lambda: nc.vector.memset(halo[:], 0.0), writes=[Bhalo])
        HT = TT // 2

        def stA(oc):
            b = oc % 2
            wi = (oc * 128) // 512
            for kc in range(KC):
                f.op(f.pe, lambda kc=kc: nc.tensor.matmul(pp[b][:], Win[:, kc, oc * 128:(oc + 1) * 128], hT[:, kc, :],
                                                          start=(kc == 0), stop=(kc == KC - 1)),
                     reads=[BW[wi], Bh], writes=[Bpp[b]])
            f.op(f.act, lambda: nc.scalar.copy(out=pre[b][:, 3:TT + 3], in_=pp[b][:]), reads=[Bpp[b]], writes=[Bpre[b]])

        def stB(oc):
            b = oc % 2
            f.op(f.dve, lambda: nc.vector.tensor_copy(out=pre[b][:, 0:3], in_=halo[:, oc, :]), reads=[Bhalo], writes=[Bpre[b]])
            f.op(f.dve, lambda: nc.vector.tensor_copy(out=halo[:, oc, :], in_=pre[b][:, TT:TT + 3]), reads=[Bpre[b]], writes=[Bhalo])
            for j in range(4):
                for hf in range(2):
                    c0 = hf * HT
                    if j == 0:
                        f.op(f.dve, lambda c0=c0: nc.vector.tensor_scalar(cv[b][:, c0:c0 + HT], pre[b][:, c0:c0 + HT], cw[:, oc, 0:1], None, ALU.mult),
                             reads=[Bpre[b], Bc], writes=[Bcvh[b][hf]])
                    else:
                        f.op(f.dve, lambda c0=c0, j=j: nc.vector.scalar_tensor_tensor(out=cv[b][:, c0:c0 + HT], in0=pre[b][:, c0 + j:c0 + HT + j],
                                                                                      scalar=cw[:, oc, j:j + 1], in1=cv[b][:, c0:c0 + HT],
                                                                                      op0=ALU.mult, op1=ALU.add),
                             reads=[Bpre[b], Bc], writes=[Bcvh[b][hf]])
            if oc < 16:
                f.op(f.act, lambda: nc.scalar.activation(out=s32[b][:], in_=cv[b][:], func=AF.Silu), reads=Bcvh[b], writes=[Bs32[b]])
                f.op(f.act, lambda: nc.scalar.activation(out=sq2[b][:], in_=s32[b][:], func=AF.Square), reads=[Bs32[b]], writes=[Bsq2[b]])
            else:
                f.op(f.act, lambda: nc.scalar.activation(out=ob[b][:], in_=cv[b][:], func=AF.Silu), reads=Bcvh[b], writes=[Bob[b]])

        def stC1(oc):
            b = oc % 2
            if oc < 16:
                f.op(f.pe, lambda: nc.tensor.matmul(pn[b][:], C.ones_bf[:], sq2[b][:], start=True, stop=True), reads=[Bsq2[b], C.B], writes=[Bpn[b]])
                f.op(f.act, lambda: nc.scalar.activation(out=r2[b][:], in_=pn[b][:], func=AF.Ln, bias=C.eps[:], scale=1.0),
                     reads=[Bpn[b], C.B], writes=[Br2[b]])
                f.op(f.act, lambda: nc.scalar.activation(out=r2[b][:], in_=r2[b][:], func=AF.Exp, bias=(lnscl[:] if oc < 8 else zero1[:]), scale=-0.5),
                     reads=[Br2[b], Bc], writes=[Br2[b]])

        def stC2(oc):
            b = oc % 2
            hc = oc % 8
            if oc < 16:
                f.op(f.pool, lambda: nc.gpsimd.tensor_tensor(out=ob[b][:], in0=s32[b][:], in1=r2[b][:], op=ALU.mult),
                     reads=[Bs32[b], Br2[b]], writes=[Bob[b]])
                dstT = qTv if oc < 8 else kTv
                f.dma(f.sp, dstT[hc * 128:(hc + 1) * 128, cs], ob[b][:], reads=[Bob[b]], writes=[Bscr])
            if oc >= 8:
                for s_ in range(4):
                    f.op(f.pe, lambda s_=s_: nc.tensor.transpose(ptr[b][:, s_, :], ob[b][:, s_ * 128:(s_ + 1) * 128], C.ident_bf[:]),
                         reads=[Bob[b], C.B], writes=[Bptr[b]])
                f.op(f.act, lambda: nc.scalar.copy(out=tk[b][:], in_=ptr[b][:]), reads=[Bptr[b]], writes=[Btk[b]])
                dsttok = scr["ktok"] if oc < 16 else scr["vtok"]
                f.dma(f.sp, dsttok[cs, hc * 128:(hc + 1) * 128].rearrange("(s p) d -> p s d", p=128), tk[b][:], reads=[Btk[b]], writes=[Bscr])

        NQ = 24
        stA(0); stA(1); stB(0)
        for oc in range(NQ):
            if oc + 2 < NQ:
                stA(oc + 2)
            stC1(oc)
            if oc + 1 < NQ:
                stB(oc + 1)
            stC2(oc)
        for oc in range(24, 32):
            b = oc % 2
            wi = (oc * 128) // 512
            for kc in range(KC):
                f.op(f.pe, lambda kc=kc, oc=oc, b=b: nc.tensor.matmul(pp[b][:], Win[:, kc, oc * 128:(oc + 1) * 128], hT[:, kc, :],
                                                                      start=(kc == 0), stop=(kc == KC - 1)),
                     reads=[BW[wi], Bh], writes=[Bpp[b]])
            f.op(f.act, lambda b=b: nc.scalar.activation(out=ob[b][:], in_=pp[b][:], func=AF.Silu), reads=[Bpp[b]], writes=[Bob[b]])
            hc = oc - 24
            f.dma(f.sp, gTv[hc * 128:(hc + 1) * 128, cs], ob[b][:], reads=[Bob[b]], writes=[Bscr])
        for s in range(4):
            for kc in range(KC):
                f.op(f.pe, lambda kc=kc, s=s: nc.tensor.matmul(pb[:, s, :], hT[:, kc, s * 128:(s + 1) * 128], Win[:, kc, 4096:4112],
                                                               start=(kc == 0), stop=(kc == KC - 1)),
                     reads=[BW[8], Bh], writes=[Bpb])
        f.op(f.act, lambda: nc.scalar.activation(out=blt[:, :, 0:8], in_=pb[:, :, 0:8], func=AF.Sigmoid), reads=[Bpb], writes=[Bblt])
        f.op(f.dve, lambda: nc.vector.tensor_tensor(out=tmpb[:], in0=pb[:, :, 8:16], in1=dtb[:].unsqueeze(1).to_broadcast([128, 4, 8]), op=ALU.add),
             reads=[Bpb, Bc], writes=[Btmpb])
        f.op(f.act, lambda: nc.scalar.activation(out=tmpb[:], in_=tmpb[:], func=AF.Exp), reads=[Btmpb], writes=[Btmpb])
        f.op(f.act, lambda: nc.scalar.activation(out=tmpb[:], in_=tmpb[:], func=AF.Ln, bias=C.one[:], scale=1.0), reads=[Btmpb, C.B], writes=[Btmpb])
        f.op(f.dve, lambda: nc.vector.tensor_tensor(out=blt[:, :, 8:16], in0=tmpb[:], in1=negA[:].unsqueeze(1).to_broadcast([128, 4, 8]), op=ALU.mult),
             reads=[Btmpb, Bc], writes=[Bblt])
        f.dma(f.sp, scr["bl"][cs, :].rearrange("(s p) c -> p s c", p=128), blt[:], reads=[Bblt], writes=[Bscr])
    barrier(f)
    sc.close()


def gdn_core_phase(f, X, gnw_d, wout_d, scr, NT, L):
    nc = f.nc
    sc = Scope(nc)
    C = Consts(f, sc)
    H = GH
    mk = lambda n, shp, dt=F32: sc.sb(uname(n), shp, dt)
    gnw = mk("gnw", [128, 1])
    qTb = mk("qTb", [128, H, 128], BF16)
    kTb = mk("kTb", [128, H, 128], BF16)
    ktokb = mk("ktokb", [128, H, 128], BF16)
    vtokb = mk("vtokb", [128, H, 128], BF16)
    gTb = mk("gTb", [128, H, 128], BF16)
    bl = mk("bl", [128, 16])
    Xs = mk("Xs", [128, H, 128])
    sm = mk("sm", [128, 32])
    sme = mk("sme", [128, 40])
    nbeta = mk("nbeta", [128, 8])
    tmp1 = mk("tmp1", [128, H, 128])
    tmp2 = mk("tmp2", [128, H, 128])
    LmT = mk("LmT", [128, H, 128])
    LmS = mk("LmS", [128, H, 128])
    WTN = mk("WTN", [128, H, 128])
    E = mk("E", [128, H, 128])
    Pm = [mk("Pm", [128, H, 128], BF16)]
    PTm = [mk("PTm", [128, H, 128], BF16)]
    TTm = [mk("TTm", [128, H, 128]) for _ in range(2)]
    Pb = [mk("Pb", [128, H, 128], BF16) for _ in range(2)]
    PTb = [mk("PTb", [128, H, 128], BF16) for _ in range(2)]
    TTbb = [mk("TTbb", [128, H, 128], BF16) for _ in range(2)]
    attnT = mk("attnT", [128, H, 128], BF16)
    qgT = mk("qgT", [128, H, 128], BF16)
    ktil = mk("ktil", [128, H, 128], BF16)
    vb = mk("vb", [128, H, 128])
    R = mk("R", [128, H, 128], BF16)
    vnew = mk("vnew", [128, H, 128], BF16)
    S = mk("S", [128, H, 128])
    Sb = mk("Sb", [128, H, 128], BF16)
    sqo = mk("sqo", [128, H, 128], BF16)
    rs = mk("rs", [128, H, 128])
    of32 = mk("of32", [128, H, 128])
    ofb = mk("ofb", [128, H, 128], BF16)
    xt = mk("xtb", [128, KC, 128])
    PA = sc.ps(uname("PA"), [128, H, 128])
    PB = sc.ps(uname("PB"), [128, H, 128])
    PC = sc.ps(uname("PC"), [128, H, 128])
    PD = sc.ps(uname("PD"), [128, H, 128])
    shared = "c W bl sm sme nb xt scr qin kin ktin vtin gin"
    grouped = "Xs t1 t2 LmT LmS WTN E attnT qgT ktil vb R vnew S Sb sqo rs of32 ofb PA PB PC PD P0 PT0 TT0 TT1 Pb0 Pb1 PTb0 PTb1 TTb0 TTb1"
    Bf = {n: Buf(n) for n in shared.split()}
    for n in grouped.split():
        for gi in range(2):
            Bf[n + "#%d" % gi] = Buf(n)
    g = lambda *ns: [Bf[n] for n in ns]

    def gg(gi, *ns):
        return [Bf[n + "#%d" % gi] for n in ns]
    gall = lambda *ns: [Bf[n + "#%d" % gi] for n in ns for gi in range(2)]
    Bc = Bf["c"]

    f.dma(f.sp, gnw[:], gnw_d.rearrange("(p o) -> p o", o=1), writes=[Bc])
    woutv = wout_d.rearrange("(h p) d -> p h d", p=128)
    Xv = X.rearrange("(c p) t -> p c t", p=128)
    V, A, P_, G_ = f.dve, f.act, f.pe, f.pool
    HS = [slice(0, 4), slice(4, 8)]

    nblk = NT // 128
    for blk in range(nblk):
        t0 = blk * 128
        ts = slice(t0, t0 + 128)
        if t0 % L == 0:
            for gi in range(2):
                f.op(G_, lambda gi=gi: nc.gpsimd.memset(S[:, HS[gi], :], 0.0), writes=gg(gi, "S"))
                f.op(G_, lambda gi=gi: nc.gpsimd.memset(Sb[:, HS[gi], :], 0.0), writes=gg(gi, "Sb"))
        f.dma(f.sp, qTb[:], scr["qT"][:, ts].rearrange("(h p) t -> p h t", p=128), writes=g("qin"))
        f.dma(f.sp, kTb[:], scr["kT"][:, ts].rearrange("(h p) t -> p h t", p=128), writes=g("kin"))
        f.dma(f.sp, gTb[:], scr["gT"][:, ts].rearrange("(h p) t -> p h t", p=128), writes=g("gin"))
        f.dma(f.sp, ktokb[:], scr["ktok"][ts, :].rearrange("p (h d) -> p h d", d=128), writes=g("ktin"))
        f.dma(f.sp, vtokb[:], scr["vtok"][ts, :].rearrange("p (h d) -> p h d", d=128), writes=g("vtin"))
        f.dma(f.sp, bl[:], scr["bl"][ts, :], writes=g("bl"))
        beta = bl[:, 0:8]
        la = bl[:, 8:16]
        for gi in range(2):
            hs = HS[gi]
            f.op(V, lambda hs=hs: nc.vector.tensor_tensor(out=Xs[:, hs, :], in0=bc_i(la[:, hs]), in1=bc_h(C.tri[:], 4), op=ALU.mult),
                 reads=g("bl") + [C.B], writes=gg(gi, "Xs"))
            f.op(P_, lambda hs=hs: nc.tensor.matmul(PA[:, hs, :], C.ones32[:], Xs[:, hs, :], start=True, stop=True),
                 reads=gg(gi, "Xs") + [C.B], writes=gg(gi, "PA"))
        f.op(P_, lambda: nc.tensor.matmul(PD[:, 0, 0:8], C.tri[:], la, start=True, stop=True), reads=g("bl") + [C.B], writes=gg(0, "PD"))
        f.op(P_, lambda: nc.tensor.matmul(PD[:, 0, 8:16], C.bd[:], la, start=True, stop=True), reads=g("bl") + [C.B], writes=gg(0, "PD"))
        f.op(P_, lambda: nc.tensor.matmul(PD[:, 0, 16:24], C.cind[:, 0, :], la, start=True, stop=True), reads=g("bl") + [C.B], writes=gg(0, "PD"))
        f.op(P_, lambda: nc.tensor.matmul(PD[:, 0, 24:32], C.cind[:, 1, :], la, start=True, stop=True), reads=g("bl") + [C.B], writes=gg(0, "PD"))
        f.op(V, lambda: nc.vector.tensor_copy(out=sm[:], in_=PD[:, 0, 0:32]), reads=gg(0, "PD"), writes=g("sm"))
        gcol = sm[:, 0:8]
        f.op(A, lambda: nc.scalar.activation(out=sme[:, 0:8], in_=sm[:, 0:8], func=AF.Exp), reads=g("sm"), writes=g("sme"))
        f.op(V, lambda: nc.vector.tensor_tensor(out=sme[:, 8:16], in0=sm[:, 8:16], in1=sm[:, 0:8], op=ALU.subtract), reads=g("sm"), writes=g("sme"))
        f.op(A, lambda: nc.scalar.activation(out=sme[:, 8:16], in_=sme[:, 8:16], func=AF.Exp), reads=g("sme"), writes=g("sme"))
        f.op(A, lambda: nc.scalar.activation(out=sme[:, 16:32], in_=sm[:, 16:32], func=AF.Exp), reads=g("sm"), writes=g("sme"))
        f.op(V, lambda: nc.vector.scalar_tensor_tensor(out=sme[:, 32:40], in0=sme[:, 0:8], scalar=-1.0, in1=beta, op0=ALU.mult, op1=ALU.mult),
             reads=g("sme", "bl"), writes=g("sme"))
        f.op(V, lambda: nc.vector.tensor_scalar(nbeta[:], beta, -1.0, None, ALU.mult), reads=g("bl"), writes=g("nb"))
        for gi in range(2):
            hs = HS[gi]
            f.op(V, lambda hs=hs: nc.vector.tensor_tensor(out=tmp1[:, hs, :], in0=PA[:, hs, :], in1=bc_i(gcol[:, hs]), op=ALU.subtract),
                 reads=gg(gi, "PA") + g("sm"), writes=gg(gi, "t1"))
            f.op(V, lambda hs=hs: nc.vector.scalar_tensor_tensor(out=tmp2[:, hs, :], in0=tmp1[:, hs, :], scalar=-1.0, in1=bc_h(C.negstr[:], 4),
                                                                 op0=ALU.mult, op1=ALU.add),
                 reads=gg(gi, "t1") + [C.B], writes=gg(gi, "t2"))
            f.op(V, lambda hs=hs: nc.vector.tensor_tensor(out=tmp1[:, hs, :], in0=tmp1[:, hs, :], in1=bc_h(C.negincT[:], 4), op=ALU.add),
                 reads=gg(gi, "t2") + [C.B], writes=gg(gi, "t1"))
            f.op(A, lambda hs=hs: nc.scalar.activation(out=LmT[:, hs, :], in_=tmp1[:, hs, :], func=AF.Exp), reads=gg(gi, "t1"), writes=gg(gi, "LmT"))
            f.op(A, lambda hs=hs: nc.scalar.activation(out=LmS[:, hs, :], in_=tmp2[:, hs, :], func=AF.Exp), reads=gg(gi, "t2"), writes=gg(gi, "LmS"))
            f.op(A, lambda hs=hs: nc.scalar.activation(out=E[:, hs, :], in_=PA[:, hs, :], func=AF.Exp), reads=gg(gi, "PA"), writes=gg(gi, "E"))
            f.op(V, lambda hs=hs: nc.vector.tensor_tensor(out=LmS[:, hs, :], in0=LmS[:, hs, :], in1=bc_i(nbeta[:, hs]), op=ALU.mult),
                 reads=gg(gi, "LmS") + g("nb"), writes=gg(gi, "LmS"))
            f.op(V, lambda hs=hs: nc.vector.tensor_tensor(out=Xs[:, hs, :], in0=bc_i(beta[:, hs]), in1=bc_h(C.ident[:], 4), op=ALU.mult),
                 reads=g("bl") + [C.B], writes=gg(gi, "Xs"))
            f.op(P_, lambda hs=hs: nc.tensor.matmul(PB[:, hs, :], C.ones32[:], Xs[:, hs, :], start=True, stop=True),
                 reads=gg(gi, "Xs") + [C.B], writes=gg(gi, "PB"))
            f.op(V, lambda hs=hs: nc.vector.tensor_tensor(out=WTN[:, hs, :], in0=LmT[:, hs, :], in1=bc_h(C.strT01[:], 4), op=ALU.mult),
                 reads=gg(gi, "LmT") + [C.B], writes=gg(gi, "WTN"))
            f.op(V, lambda hs=hs: nc.vector.scalar_tensor_tensor(out=WTN[:, hs, :], in0=PB[:, hs, :], scalar=-1.0, in1=WTN[:, hs, :],
                                                                 op0=ALU.mult, op1=ALU.mult),
                 reads=gg(gi, "PB"), writes=gg(gi, "WTN"))
        for gi in range(2):
            hs = HS[gi]
            for h in range(4 * gi, 4 * gi + 4):
                f.op(P_, lambda h=h: nc.tensor.matmul(PC[:, h, :], kTb[:, h, :], kTb[:, h, :], start=True, stop=True), reads=g("kin"), writes=gg(gi, "PC"))
            f.op(V, lambda hs=hs: nc.vector.tensor_tensor(out=Pm[0][:, hs, :], in0=PC[:, hs, :], in1=LmS[:, hs, :], op=ALU.mult),
                 reads=gg(gi, "PC", "LmS"), writes=gg(gi, "P0"))
            f.op(V, lambda hs=hs: nc.vector.tensor_tensor(out=PTm[0][:, hs, :], in0=PC[:, hs, :], in1=WTN[:, hs, :], op=ALU.mult),
                 reads=gg(gi, "PC", "WTN"), writes=gg(gi, "PT0"))
            for h in range(4 * gi, 4 * gi + 4):
                f.op(P_, lambda h=h: nc.tensor.matmul(PB[:, h, :], kTb[:, h, :], qTb[:, h, :], start=True, stop=True), reads=g("kin", "qin"), writes=gg(gi, "PB"))
            f.op(V, lambda hs=hs: nc.vector.tensor_tensor(out=attnT[:, hs, :], in0=PB[:, hs, :], in1=LmT[:, hs, :], op=ALU.mult),
                 reads=gg(gi, "PB", "LmT"), writes=gg(gi, "attnT"))
            f.op(V, lambda hs=hs: nc.vector.tensor_tensor(out=TTm[0][:, hs, :], in0=PTm[0][:, hs, :], in1=bc_h(C.ident[:], 4), op=ALU.add),
                 reads=gg(gi, "PT0") + [C.B], writes=gg(gi, "TT0"))
        for gi in range(2):
            hs = HS[gi]
            f.op(V, lambda hs=hs: nc.vector.tensor_copy(out=TTbb[0][:, hs, :], in_=TTm[0][:, hs, :]), reads=gg(gi, "TT0"), writes=gg(gi, "TTb0"))
        for k in range(1, 6):
            cur, nxt = (k - 1) % 2, k % 2
            for gi in range(2):
                hs = HS[gi]
                for h in range(4 * gi, 4 * gi + 4):
                    if k == 1:
                        f.op(P_, lambda h=h: nc.tensor.matmul(PC[:, h, :], PTm[0][:, h, :], Pm[0][:, h, :], start=True, stop=True),
                             reads=gg(gi, "P0", "PT0"), writes=gg(gi, "PC"))
                    else:
                        f.op(P_, lambda h=h: nc.tensor.matmul(PC[:, h, :], PTb[cur][:, h, :], Pb[cur][:, h, :], start=True, stop=True),
                             reads=gg(gi, "Pb%d" % cur, "PTb%d" % cur), writes=gg(gi, "PC"))
                f.op(A, lambda hs=hs: nc.scalar.copy(out=Pb[nxt][:, hs, :], in_=PC[:, hs, :]), reads=gg(gi, "PC"), writes=gg(gi, "Pb%d" % nxt))
                if k < 5:
                    for h in range(4 * gi, 4 * gi + 4):
                        if k == 1:
                            f.op(P_, lambda h=h: nc.tensor.matmul(PB[:, h, :], Pm[0][:, h, :], PTm[0][:, h, :], start=True, stop=True),
                                 reads=gg(gi, "P0", "PT0"), writes=gg(gi, "PB"))
                        else:
                            f.op(P_, lambda h=h: nc.tensor.matmul(PB[:, h, :], Pb[cur][:, h, :], PTb[cur][:, h, :], start=True, stop=True),
                                 reads=gg(gi, "Pb%d" % cur, "PTb%d" % cur), writes=gg(gi, "PB"))
                    f.op(A, lambda hs=hs: nc.scalar.copy(out=PTb[nxt][:, hs, :], in_=PB[:, hs, :]), reads=gg(gi, "PB"), writes=gg(gi, "PTb%d" % nxt))
            for gi in range(2):
                hs = HS[gi]
                for h in range(4 * gi, 4 * gi + 4):
                    f.op(P_, lambda h=h: nc.tensor.matmul(PA[:, h, :], Pb[nxt][:, h, :], TTbb[cur][:, h, :], start=True, stop=True),
                         reads=gg(gi, "Pb%d" % nxt, "TTb%d" % cur), writes=gg(gi, "PA"))
                f.op(V, lambda hs=hs: nc.vector.tensor_tensor(out=TTbb[nxt][:, hs, :], in0=PA[:, hs, :], in1=TTm[cur][:, hs, :], op=ALU.add),
                     reads=gg(gi, "PA", "TT%d" % cur), writes=gg(gi, "TTb%d" % nxt))
                if k < 5:
                    f.op(V, lambda hs=hs: nc.vector.tensor_tensor(out=TTm[nxt][:, hs, :], in0=PA[:, hs, :], in1=TTm[cur][:, hs, :], op=ALU.add),
                         reads=gg(gi, "PA", "TT%d" % cur), writes=gg(gi, "TT%d" % nxt))
        TTb = TTbb[1]
        for gi in range(2):
            hs = HS[gi]
            f.op(G_, lambda hs=hs: nc.gpsimd.tensor_tensor(out=qgT[:, hs, :], in0=qTb[:, hs, :], in1=E[:, hs, :], op=ALU.mult),
                 reads=g("qin") + gg(gi, "E"), writes=gg(gi, "qgT"))
            f.op(G_, lambda hs=hs: nc.gpsimd.tensor_tensor(out=ktil[:, hs, :], in0=ktokb[:, hs, :], in1=bc_i(sme[:, 8:16][:, hs]), op=ALU.mult),
                 reads=g("ktin", "sme"), writes=gg(gi, "ktil"))
            f.op(G_, lambda hs=hs: nc.gpsimd.tensor_tensor(out=vb[:, hs, :], in0=vtokb[:, hs, :], in1=bc_i(beta[:, hs]), op=ALU.mult),
                 reads=g("vtin", "bl"), writes=gg(gi, "vb"))
        for c in range(2):
            r = slice(64 * c, 64 * c + 64)
            for gi in range(2):
                hs = HS[gi]
                for h in range(4 * gi, 4 * gi + 4):
                    f.op(P_, lambda h=h: nc.tensor.matmul(PC[r, h, :], kTb[:, h, r], Sb[:, h, :], start=True, stop=True),
                         reads=g("kin") + gg(gi, "Sb"), writes=gg(gi, "PC"))
            for gi in range(2):
                for h in range(4 * gi, 4 * gi + 4):
                    f.op(V, lambda h=h: nc.vector.scalar_tensor_tensor(out=R[r, h, :], in0=PC[r, h, :], scalar=sme[r, 32 + h:33 + h], in1=vb[r, h, :],
                                                                       op0=ALU.mult, op1=ALU.add),
                         reads=gg(gi, "PC", "vb") + g("sme"), writes=gg(gi, "R"))
                for h in range(4 * gi, 4 * gi + 4):
                    f.op(P_, lambda h=h: nc.tensor.matmul(PB[r, h, :], TTb[r, h, r], R[r, h, :], start=True, stop=True),
                         reads=gg(gi, "TTb1", "R"), writes=gg(gi, "PB"))
                f.op(A, lambda gi=gi: nc.scalar.copy(out=vnew[r, HS[gi], :], in_=PB[r, HS[gi], :]), reads=gg(gi, "PB"), writes=gg(gi, "vnew"))
            for gi in range(2):
                for h in range(4 * gi, 4 * gi + 4):
                    f.op(P_, lambda h=h: nc.tensor.matmul(PD[:, h, r], Sb[:, h, :], qgT[:, h, r], start=True, stop=False),
                         reads=gg(gi, "Sb", "qgT"), writes=gg(gi, "PD"))
                    f.op(P_, lambda h=h: nc.tensor.matmul(PD[:, h, r], vnew[r, h, :], attnT[r, h, r], start=False, stop=True),
                         reads=gg(gi, "vnew", "attnT"), writes=gg(gi, "PD"))
                for h in range(4 * gi, 4 * gi + 4):
                    f.op(P_, lambda h=h: nc.tensor.matmul(PA[:, h, :], ktil[r, h, :], vnew[r, h, :], start=True, stop=True),
                         reads=gg(gi, "ktil", "vnew"), writes=gg(gi, "PA"))
            for gi in range(2):
                for h in range(4 * gi, 4 * gi + 4):
                    f.op(V, lambda h=h: nc.vector.scalar_tensor_tensor(out=S[:, h, :], in0=S[:, h, :], scalar=sme[:, 16 + 8 * c + h:17 + 8 * c + h],
                                                                       in1=PA[:, h, :], op0=ALU.mult, op1=ALU.add),
                         reads=gg(gi, "PA") + g("sme"), writes=gg(gi, "S"))
                f.op(A, lambda gi=gi: nc.scalar.copy(out=Sb[:, HS[gi], :], in_=S[:, HS[gi], :]), reads=gg(gi, "S"), writes=gg(gi, "Sb"))
        for gi in range(2):
            hs = HS[gi]
            f.op(A, lambda hs=hs: nc.scalar.activation(out=sqo[:, hs, :], in_=PD[:, hs, :], func=AF.Square), reads=gg(gi, "PD"), writes=gg(gi, "sqo"))
            f.op(P_, lambda hs=hs: nc.tensor.matmul(PC[:, hs, :], C.ones_bf[:], sqo[:, hs, :], start=True, stop=True),
                 reads=gg(gi, "sqo") + [C.B], writes=gg(gi, "PC"))
            f.op(A, lambda hs=hs: nc.scalar.activation(out=rs[:, hs, :], in_=PC[:, hs, :], func=AF.Sqrt, bias=C.eps[:], scale=1.0 / 128),
                 reads=gg(gi, "PC") + [C.B], writes=gg(gi, "rs"))
            f.op(V, lambda hs=hs: nc.vector.reciprocal(out=rs[:, hs, :], in_=rs[:, hs, :]), reads=gg(gi, "rs"), writes=gg(gi, "rs"))
            f.op(V, lambda hs=hs: nc.vector.scalar_tensor_tensor(out=of32[:, hs, :], in0=PD[:, hs, :], scalar=gnw[:, 0:1], in1=rs[:, hs, :],
                                                                 op0=ALU.mult, op1=ALU.mult),
                 reads=gg(gi, "PD", "rs") + [Bc], writes=gg(gi, "of32"))
            f.op(G_, lambda hs=hs: nc.gpsimd.tensor_tensor(out=ofb[:, hs, :], in0=of32[:, hs, :], in1=gTb[:, hs, :], op=ALU.mult),
                 reads=gg(gi, "of32") + g("gin"), writes=gg(gi, "ofb"))
        f.dma(f.sp, scr["yT"][:, ts].rearrange("(h p) t -> p h t", p=128), ofb[:], reads=gall("ofb"), writes=g("scr"))
    barrier(f)
    sc.close()


EVIN = 2560
HH = 4


def ev_proj_phase(f, X, wn_d, win_d, lbl_d, j, scr, NT, L):
    nc = f.nc
    sc = Scope(nc)
    C = Consts(f, sc)
    mk = lambda n, shp, dt=F32: sc.sb(uname(n), shp, dt)
    Win = mk("Win", [128, KC, EVIN], BF16)
    wn = mk("wn", [128, KC])
    xt = mk("xt", [128, KC, TT])
    hT = mk("hT", [128, KC, TT], BF16)
    sq = mk("sq", [128, KC, TT], BF16)
    rstd = mk("rstd", [128, TT])
    lg = mk("lg", [128, 2, 4])
    lb = mk("lb", [128, 4])
    oml = mk("oml", [128, 4])
    ob = [mk("ob", [128, TT], BF16) for _ in range(2)]
    fs = [mk("fs", [128, TT]) for _ in range(2)]
    lf = [mk("lf", [128, TT]) for _ in range(2)]
    vt = [mk("vt", [128, 512], BF16) for _ in range(2)]
    pp = [sc.ps(uname("pp"), [128, TT]) for _ in range(2)]
    pv = [sc.ps(uname("pv"), [128, 512]) for _ in range(2)]
    pss = sc.ps(uname("pss"), [128, TT])
    Bc, Bx, Bh, Bsq, Bpss, Brstd, Bscr = [Buf() for _ in range(7)]
    BW = [Buf() for _ in range(5)]
    Bpp = [Buf(), Buf()]; Bpv = [Buf(), Buf()]; Bob = [Buf(), Buf()]; Bfs = [Buf(), Buf()]; Blf = [Buf(), Buf()]; Bvt = [Buf(), Buf()]
    V, A, P_ = f.dve, f.act, f.pe

    f.dma(f.sp, wn[:], wn_d.rearrange("(c p) -> p c", p=128), writes=[Bc], allow_slow_non_contiguous=True)
    for l in range(2):
        f.dma(f.sp, lg[:, l, :], lbl_d[l, :].rearrange("(c p) -> p c", p=128), writes=[Bc], allow_slow_non_contiguous=True)
    if j == 0:
        f.op(V, lambda: nc.vector.memset(lb[:], 0.0), writes=[Bc])
    else:
        f.op(V, lambda: nc.vector.tensor_tensor(out=lb[:], in0=lg[:, 1, :], in1=lg[:, 0, :], op=ALU.subtract), reads=[Bc], writes=[Bc])
        f.op(A, lambda: nc.scalar.activation(out=lb[:], in_=lb[:], func=AF.Sigmoid), reads=[Bc], writes=[Bc])
    f.op(V, lambda: nc.vector.tensor_scalar(oml[:], lb[:], -1.0, 1.0, ALU.mult, ALU.add), reads=[Bc], writes=[Bc])
    winv = win_d.rearrange("(kc p) f -> p kc f", p=128)
    for i in range(5):
        f.dma(f.pool, Win[:, :, i * 512:(i + 1) * 512], winv[:, :, i * 512:(i + 1) * 512], writes=[BW[i]])
    Xv = X.rearrange("(c p) t -> p c t", p=128)
    for t in range(NT // TT):
        cs = slice(t * TT, (t + 1) * TT)
        f.dma(f.sp, xt[:], Xv[:, :, cs], writes=[Bx])
        rms_stats(f, sc, xt[:], sq[:], Bx, Bsq, C.ones_bf[:], C.eps[:], pss[:], Bpss, rstd[:], Brstd, TT, 1.0 / D)
        for c in range(KC):
            f.op(V, lambda c=c: nc.vector.scalar_tensor_tensor(out=hT[:, c, :], in0=xt[:, c, :], scalar=wn[:, c:c + 1],
                                                             in1=rstd[:], op0=ALU.mult, op1=ALU.mult),
                 reads=[Bx, Brstd, Bc], writes=[Bh])
        for oc in list(range(0, 8)) + list(range(12, 20)):
            b = oc % 2
            wi = oc // 4
            hc = oc % 4
            for kc in range(KC):
                f.op(P_, lambda kc=kc, oc=oc, b=b: nc.tensor.matmul(pp[b][:], Win[:, kc, oc * 128:(oc + 1) * 128], hT[:, kc, :],
                                                                    start=(kc == 0), stop=(kc == KC - 1)),
                     reads=[BW[wi], Bh], writes=[Bpp[b]])
            rows = slice(hc * 128, (hc + 1) * 128)
            if oc < 4 or 12 <= oc < 16:
                f.op(A, lambda b=b: nc.scalar.activation(out=ob[b][:], in_=pp[b][:], func=AF.Silu), reads=[Bpp[b]], writes=[Bob[b]])
                dst = scr["qT"] if oc < 4 else scr["gT"]
                f.dma(f.sp, dst[rows, cs], ob[b][:], reads=[Bob[b]], writes=[Bscr])
            elif oc >= 16:
                f.op(A, lambda b=b: nc.scalar.copy(out=ob[b][:], in_=pp[b][:]), reads=[Bpp[b]], writes=[Bob[b]])
                f.dma(f.sp, scr["uT"][rows, cs], ob[b][:], reads=[Bob[b]], writes=[Bscr])
            else:
                f.op(A, lambda b=b: nc.scalar.activation(out=fs[b][:], in_=pp[b][:], func=AF.Sigmoid), reads=[Bpp[b]], writes=[Bfs[b]])
                f.op(V, lambda b=b, hc=hc: nc.vector.tensor_scalar(fs[b][:], fs[b][:], oml[:, hc:hc + 1], lb[:, hc:hc + 1], ALU.mult, ALU.add),
                     reads=[Bc], writes=[Bfs[b]])
                f.op(V, lambda b=b: nc.vector.tensor_scalar(ob[b][:], fs[b][:], -1.0, 1.0, ALU.mult, ALU.add), reads=[Bfs[b]], writes=[Bob[b]])
                f.dma(f.sp, scr["kT"][rows, cs], ob[b][:], reads=[Bob[b]], writes=[Bscr])
                f.op(V, lambda b=b: nc.vector.tensor_scalar(lf[b][:], fs[b][:], 1e-6, None, ALU.max), reads=[Bfs[b]], writes=[Blf[b]])
                f.op(A, lambda b=b: nc.scalar.activation(out=lf[b][:], in_=lf[b][:], func=AF.Ln), reads=[Blf[b]], writes=[Blf[b]])
                f.dma(f.sp, scr["lfT"][rows, cs], lf[b][:], reads=[Blf[b]], writes=[Bscr])
        for s in range(4):
            b = s % 2
            for kc in range(KC):
                f.op(P_, lambda kc=kc, s=s, b=b: nc.tensor.matmul(pv[b][:], hT[:, kc, s * 128:(s + 1) * 128], Win[:, kc, 1024:1536],
                                                                 start=(kc == 0), stop=(kc == KC - 1)),
                     reads=[BW[2], Bh], writes=[Bpv[b]])
            f.op(A, lambda b=b: nc.scalar.copy(out=vt[b][:], in_=pv[b][:]), reads=[Bpv[b]], writes=[Bvt[b]])
            f.dma(f.sp, scr["vtok"][t * TT + s * 128:t * TT + (s + 1) * 128, 0:512], vt[b][:], reads=[Bvt[b]], writes=[Bscr])
    barrier(f)
    sc.close()


def hgrn_core_phase(f, hnw_d, scr, NT, L):
    nc = f.nc
    sc = Scope(nc)
    C = Consts(f, sc)
    mk = lambda n, shp, dt=F32: sc.sb(uname(n), shp, dt)
    NCH = L // 64
    NB = L // 128
    NSQ = NT // L
    hnw = mk("hnw", [128, 1])
    onesL = mk("onesL", [128, L])
    V, A, P_, G_ = f.dve, f.act, f.pe, f.pool
    Bcst = Buf()
    f.dma(f.sp, hnw[:], hnw_d.rearrange("(p o) -> p o", o=1), writes=[Bcst])
    f.op(V, lambda: nc.vector.memset(onesL[:], 1.0), writes=[Bcst])

    class SeqState:
        pass
    SS = []
    for q_ in range(NSQ):
        st = SeqState()
        st.qh = mk("qh", [128, L], BF16); st.kh = mk("kh", [128, L], BF16); st.gh = mk("gh", [128, L], BF16)
        st.lfh = mk("lfh", [128, L]); st.Bcs = mk("Bcs", [128, L]); st.dif = mk("dif", [128, L])
        st.ee = mk("ee", [128, L])
        st.qt = mk("qt", [128, L], BF16); st.kt = mk("kt", [128, L], BF16)
        st.bprev = mk("bprev", [128, NCH]); st.sca = mk("sca", [128, 3, NCH])
        st.vb = [mk("vblk", [128, 128], BF16) for _ in range(2)]
        st.ktok = [mk("ktokh", [128, 128], BF16) for _ in range(2)]
        st.attnT = [mk("attnTh", [128, 128], BF16) for _ in range(2)]
        st.S = mk("Sh", [128, 128]); st.St = mk("Sth", [128, 128], BF16); st.dSs = mk("dSs", [128, 128])
        st.sqo = mk("sqoh", [128, 128], BF16); st.rs = mk("rsh", [128, 128]); st.o32 = mk("o32h", [128, 128])
        st.yo = [mk("yoh", [128, 128], BF16) for _ in range(2)]
        st.pat = sc.ps(uname("pat"), [128, 512])[:, 0:128]
        st.ptr = sc.ps(uname("ptrh"), [128, 1024], BF16)[:, 0:128]
        st.po = sc.ps(uname("po"), [128, 512])[:, 0:128]
        st.pds = sc.ps(uname("pds"), [128, 512])
        names = "q k g lf B dif ee qt kt bp sca S St dSs sqo rs o32 pds pn pat ptr po v0 v1 ktok0 ktok1 attnT0 attnT1 yo0 yo1 scr"
        st.Bf = {n: Buf(n) for n in names.split()}
        SS.append(st)

    for h in range(HH):
        rows = slice(h * 128, (h + 1) * 128)
        for q_, st in enumerate(SS):
            g = lambda *ns, st=st: [st.Bf[n] for n in ns]
            s0 = q_ * L
            f.dma(f.sp, st.qh[:], scr["qT"][rows, s0:s0 + L], writes=g("q"))
            f.dma(f.sp, st.kh[:], scr["kT"][rows, s0:s0 + L], writes=g("k"))
            f.dma(f.sp, st.gh[:], scr["gT"][rows, s0:s0 + L], writes=g("g"))
            f.dma(f.sp, st.lfh[:], scr["lfT"][rows, s0:s0 + L], writes=g("lf"))
            f.op(V, lambda st=st: nc.vector.tensor_tensor_scan(out=st.Bcs[:], data0=onesL[:], data1=st.lfh[:], initial=0.0, op0=ALU.mult, op1=ALU.add),
                 reads=g("lf") + [Bcst], writes=g("B"))
            B3 = st.Bcs[:].rearrange("p (c s) -> p c s", s=64)
            bmid = B3[:, :, 31]
            blast = B3[:, :, 63]
            f.op(G_, lambda st=st: nc.gpsimd.memset(st.bprev[:, 0:1], 0.0), writes=g("bp"))
            f.op(G_, lambda st=st, B3=B3: nc.gpsimd.tensor_copy(out=st.bprev[:, 1:NCH], in_=B3[:, 0:NCH - 1, 63]), reads=g("B"), writes=g("bp"))
            f.op(G_, lambda st=st, blast=blast: nc.gpsimd.tensor_tensor(out=st.sca[:, 0, :], in0=blast, in1=st.bprev[:], op=ALU.subtract), reads=g("B", "bp"), writes=g("sca"))
            f.op(G_, lambda st=st, blast=blast, bmid=bmid: nc.gpsimd.tensor_tensor(out=st.sca[:, 1, :], in0=blast, in1=bmid, op=ALU.subtract), reads=g("B"), writes=g("sca"))
            f.op(G_, lambda st=st, bmid=bmid: nc.gpsimd.tensor_tensor(out=st.sca[:, 2, :], in0=bmid, in1=st.bprev[:], op=ALU.subtract), reads=g("B", "bp"), writes=g("sca"))
            f.op(A, lambda st=st: nc.scalar.activation(out=st.sca[:], in_=st.sca[:], func=AF.Exp), reads=g("sca"), writes=g("sca"))
            f.op(G_, lambda st=st, B3=B3, bmid=bmid: nc.gpsimd.tensor_tensor(out=st.dif[:].rearrange("p (c s) -> p c s", s=64), in0=B3,
                                                                          in1=bmid.unsqueeze(2).to_broadcast([128, NCH, 64]), op=ALU.subtract),
                 reads=g("B"), writes=g("dif"))
            f.op(A, lambda st=st: nc.scalar.activation(out=st.ee[:], in_=st.dif[:], func=AF.Exp), reads=g("dif"), writes=g("ee"))
            f.op(V, lambda st=st: nc.vector.tensor_tensor(out=st.qt[:], in0=st.qh[:], in1=st.ee[:], op=ALU.mult), reads=g("q", "ee"), writes=g("qt"))
            f.op(A, lambda st=st: nc.scalar.activation(out=st.ee[:], in_=st.dif[:], func=AF.Exp, scale=-1.0), reads=g("dif", "qt"), writes=g("ee"))
            f.op(V, lambda st=st: nc.vector.tensor_tensor(out=st.kt[:], in0=st.kh[:], in1=st.ee[:], op=ALU.mult), reads=g("k", "ee"), writes=g("kt"))
            f.op(G_, lambda st=st: nc.gpsimd.memset(st.S[:], 0.0), writes=g("S"))
        for blk in range(NB):
            b = blk % 2
            bs = slice(blk * 128, (blk + 1) * 128)
            sb_ = str(b)
            for q_, st in enumerate(SS):
                g = lambda *ns, st=st: [st.Bf[n] for n in ns]
                s0 = q_ * L
                f.dma(f.sp, st.vb[b][:], scr["vtok"][s0 + blk * 128:s0 + (blk + 1) * 128, h * 128:(h + 1) * 128], writes=g("v" + sb_))
                f.op(P_, lambda st=st: nc.tensor.matmul(st.pat, st.kt[:, bs], st.qt[:, bs], start=True, stop=True), reads=g("kt", "qt"), writes=g("pat"))
                f.op(V, lambda st=st: nc.vector.tensor_tensor(out=st.attnT[b][:], in0=st.pat, in1=C.tri[:], op=ALU.mult),
                     reads=g("pat") + [C.B], writes=g("attnT" + sb_))
                f.op(P_, lambda st=st: nc.tensor.transpose(st.ptr, st.kt[:, bs], C.ident_bf[:]), reads=g("kt") + [C.B], writes=g("ptr"))
                f.op(A, lambda st=st: nc.scalar.copy(out=st.ktok[b][:], in_=st.ptr), reads=g("ptr"), writes=g("ktok" + sb_))
                f.op(P_, lambda st=st: nc.tensor.matmul(st.po, st.vb[b][:], st.attnT[b][:], start=True, stop=False),
                     reads=g("v" + sb_, "attnT" + sb_), writes=g("po"))
            for c in range(2):
                ci = blk * 2 + c
                r = slice(64 * c, 64 * c + 64)
                cols = slice(blk * 128 + 64 * c, blk * 128 + 64 * c + 64)
                for q_, st in enumerate(SS):
                    g = lambda *ns, st=st: [st.Bf[n] for n in ns]
                    f.op(V, lambda st=st: nc.vector.tensor_scalar(st.St[:], st.S[:], st.sca[:, 2, ci:ci + 1], None, ALU.mult), reads=g("S", "sca"), writes=g("St"))
                    f.op(P_, lambda st=st: nc.tensor.matmul(st.po[:, r], st.St[:], st.qt[:, cols], start=False, stop=(c == 1)),
                         reads=g("St", "qt"), writes=g("po"))
                    f.op(P_, lambda st=st: nc.tensor.matmul(st.pds[:, 0:128], st.ktok[b][r, :], st.vb[b][r, :], start=True, stop=True),
                         reads=g("ktok" + sb_, "v" + sb_), writes=g("pds"))
                for q_, st in enumerate(SS):
                    g = lambda *ns, st=st: [st.Bf[n] for n in ns]
                    f.op(A, lambda st=st: nc.scalar.activation(out=st.dSs[:], in_=st.pds[:, 0:128], func=AF.Identity, scale=st.sca[:, 1, ci:ci + 1]),
                         reads=g("pds", "sca"), writes=g("dSs"))
                for q_, st in enumerate(SS):
                    g = lambda *ns, st=st: [st.Bf[n] for n in ns]
                    f.op(V, lambda st=st: nc.vector.scalar_tensor_tensor(out=st.S[:], in0=st.S[:], scalar=st.sca[:, 0, ci:ci + 1], in1=st.dSs[:],
                                                                       op0=ALU.mult, op1=ALU.add), reads=g("dSs", "sca"), writes=g("S"))
            for q_, st in enumerate(SS):
                g = lambda *ns, st=st: [st.Bf[n] for n in ns]
                f.op(A, lambda st=st: nc.scalar.activation(out=st.sqo[:], in_=st.po, func=AF.Square), reads=g("po"), writes=g("sqo"))
                f.op(P_, lambda st=st: nc.tensor.matmul(st.pds[:, 128:256], C.ones_bf[:], st.sqo[:], start=True, stop=True), reads=g("sqo") + [C.B], writes=g("pds"))
                f.op(A, lambda st=st: nc.scalar.activation(out=st.rs[:], in_=st.pds[:, 128:256], func=AF.Sqrt, bias=C.eps[:], scale=1.0 / 128),
                     reads=g("pds") + [C.B], writes=g("rs"))
            for q_, st in enumerate(SS):
                g = lambda *ns, st=st: [st.Bf[n] for n in ns]
                s0 = q_ * L
                f.op(V, lambda st=st: nc.vector.reciprocal(out=st.rs[:], in_=st.rs[:]), reads=g("rs"), writes=g("rs"))
                f.op(V, lambda st=st: nc.vector.scalar_tensor_tensor(out=st.o32[:], in0=st.po, scalar=hnw[:, 0:1], in1=st.rs[:], op0=ALU.mult, op1=ALU.mult),
                     reads=g("po", "rs") + [Bcst], writes=g("o32"))
                f.op(G_, lambda st=st: nc.gpsimd.tensor_tensor(out=st.yo[b][:], in0=st.o32[:], in1=st.gh[:, bs], op=ALU.mult), reads=g("o32", "g"), writes=g("yo" + sb_))
                f.dma(f.sp, scr["yT"][h * 128:(h + 1) * 128, s0 + blk * 128:s0 + (blk + 1) * 128], st.yo[b][:], reads=g("yo" + sb_), writes=g("scr"))
    barrier(f)
    sc.close()


PI = 3.14159265358979


def s5_phase(f, p, j, scr, NT, L):
    nc = f.nc
    sc = Scope(nc)
    C = Consts(f, sc)
    mk = lambda n, shp, dt=F32: sc.sb(uname(n), shp, dt)
    V, A, P_ = f.dve, f.act, f.pe
    NS = 16
    NCH = NT // 64
    CPS = L // 64
    NSEQ = NT // L
    Bp = Buf("prep")
    gp = [Bp]
    ar = mk("ar", [128, NS]); ai = mk("ai", [128, NS]); nai = mk("nai", [128, NS])
    pwr = mk("pwr", [128, NS, 64]); pwi = mk("pwi", [128, NS, 64]); npwi = mk("npwi", [128, NS, 64])
    a64r = mk("a64r", [128, NS]); a64i = mk("a64i", [128, NS]); na64i = mk("na64i", [128, NS])
    Btab = [mk("Btab", [128, NS, 128], BF16) for _ in range(2)]
    TCre = mk("TCre", [128, NS, 128], BF16); TCimn = mk("TCimn", [128, NS, 128], BF16)
    dv = mk("dvec", [128, 4])
    Wglu = mk("Wglu", [128, 4, 512], BF16)
    scp = Scope(nc)
    mkp = lambda n, shp, dt=F32: scp.sb(uname(n), shp, dt)
    are = mkp("are", [128, NS]); aim = mkp("aim", [128, NS]); dtl = mkp("dtl", [128, NS])
    mag = mkp("mag", [128, NS]); ang = mkp("ang", [128, NS]); ang2 = mkp("ang2", [128, NS]); kk = mkp("kk", [128, NS]); tmpa = mkp("tmpa", [128, NS])
    cre = mkp("cre", [128, NS]); cim = mkp("cim", [128, NS]); den = mkp("den", [128, NS]); zr = mkp("zr", [128, NS])
    a2r = mkp("a2r", [128, NS]); a2i = mkp("a2i", [128, NS]); t3a = mkp("t3a", [128, NS, 32])
    Braw = [mkp("Braw", [128, NS, 128]) for _ in range(2)]
    Craw = [mkp("Craw", [128, NS, 128]) for _ in range(2)]
    c1 = mkp("c1", [128, NS, 128]); c2 = mkp("c2", [128, NS, 128])
    f.dma(f.sp, are[:], p["a_re"].rearrange("(t g) n -> (g n) t", g=2), writes=gp, allow_slow_non_contiguous=True)
    f.dma(f.sp, aim[:], p["a_im"].rearrange("(t g) n -> (g n) t", g=2), writes=gp, allow_slow_non_contiguous=True)
    ldv = p["log_dt"].rearrange("(t g) -> g t", g=2)
    for g2 in range(2):
        f.dma(f.sp, dtl[g2 * 64:(g2 + 1) * 64, :], ldv[g2:g2 + 1, :].to_broadcast([64, NS]), writes=gp, allow_slow_non_contiguous=True)
    op = lambda eng, fn: f.op(eng, fn, reads=gp, writes=gp)
    op(A, lambda: nc.scalar.activation(out=dtl[:], in_=dtl[:], func=AF.Exp))
    op(V, lambda: nc.vector.tensor_tensor(out=mag[:], in0=dtl[:], in1=are[:], op=ALU.mult))
    op(A, lambda: nc.scalar.activation(out=mag[:], in_=mag[:], func=AF.Exp))
    op(V, lambda: nc.vector.tensor_tensor(out=ang[:], in0=dtl[:], in1=aim[:], op=ALU.mult))
    op(V, lambda: nc.vector.tensor_scalar(ang2[:], ang[:], PI / 2, None, ALU.add))
    for a_ in (ang, ang2):
        op(V, lambda: nc.vector.memset(kk[:], 0.0))
        for m in (1, 3, 5, 7, 9):
            op(V, lambda a_=a_, m=m: nc.vector.tensor_scalar(tmpa[:], a_[:], m * PI, None, ALU.is_gt))
            op(V, lambda: nc.vector.tensor_tensor(out=kk[:], in0=kk[:], in1=tmpa[:], op=ALU.add))
        op(V, lambda a_=a_: nc.vector.scalar_tensor_tensor(out=a_[:], in0=kk[:], scalar=-2 * PI, in1=a_[:], op0=ALU.mult, op1=ALU.add))
        op(V, lambda a_=a_: nc.vector.tensor_scalar(a_[:], a_[:], PI, -PI, ALU.min, ALU.max))
    op(A, lambda: nc.scalar.activation(out=ai[:], in_=ang[:], func=AF.Sin))
    op(A, lambda: nc.scalar.activation(out=ar[:], in_=ang2[:], func=AF.Sin))
    op(V, lambda: nc.vector.tensor_tensor(out=ai[:], in0=ai[:], in1=mag[:], op=ALU.mult))
    op(V, lambda: nc.vector.tensor_tensor(out=ar[:], in0=ar[:], in1=mag[:], op=ALU.mult))
    op(V, lambda: nc.vector.tensor_scalar(nai[:], ai[:], -1.0, None, ALU.mult))
    op(V, lambda: nc.vector.tensor_tensor(out=den[:], in0=are[:], in1=are[:], op=ALU.mult))
    op(V, lambda: nc.vector.tensor_tensor(out=tmpa[:], in0=aim[:], in1=aim[:], op=ALU.mult))
    op(V, lambda: nc.vector.tensor_tensor(out=den[:], in0=den[:], in1=tmpa[:], op=ALU.add))
    op(V, lambda: nc.vector.reciprocal(out=den[:], in_=den[:]))
    op(V, lambda: nc.vector.tensor_scalar(zr[:], ar[:], -1.0, None, ALU.add))
    op(V, lambda: nc.vector.tensor_tensor(out=cre[:], in0=zr[:], in1=are[:], op=ALU.mult))
    op(V, lambda: nc.vector.tensor_tensor(out=tmpa[:], in0=ai[:], in1=aim[:], op=ALU.mult))
    op(V, lambda: nc.vector.tensor_tensor(out=cre[:], in0=cre[:], in1=tmpa[:], op=ALU.add))
    op(V, lambda: nc.vector.tensor_tensor(out=cre[:], in0=cre[:], in1=den[:], op=ALU.mult))
    op(V, lambda: nc.vector.tensor_tensor(out=cim[:], in0=ai[:], in1=are[:], op=ALU.mult))
    op(V, lambda: nc.vector.tensor_tensor(out=tmpa[:], in0=zr[:], in1=aim[:], op=ALU.mult))
    op(V, lambda: nc.vector.tensor_tensor(out=cim[:], in0=cim[:], in1=tmpa[:], op=ALU.subtract))
    op(V, lambda: nc.vector.tensor_tensor(out=cim[:], in0=cim[:], in1=den[:], op=ALU.mult))
    op(V, lambda: nc.vector.tensor_copy(out=pwr[:, :, 0], in_=ar[:]))
    op(V, lambda: nc.vector.tensor_copy(out=pwi[:, :, 0], in_=ai[:]))
    op(V, lambda: nc.vector.tensor_copy(out=a2r[:], in_=ar[:]))
    op(V, lambda: nc.vector.tensor_copy(out=a2i[:], in_=ai[:]))
    n = 1
    while n < 64:
        br = bc_i(a2r[:], n); bi = bc_i(a2i[:], n)
        op(V, lambda n=n, br=br: nc.vector.tensor_tensor(out=pwr[:, :, n:2 * n], in0=pwr[:, :, 0:n], in1=br, op=ALU.mult))
        op(V, lambda n=n, bi=bi: nc.vector.tensor_tensor(out=t3a[:, :, 0:n], in0=pwi[:, :, 0:n], in1=bi, op=ALU.mult))
        op(V, lambda n=n: nc.vector.tensor_tensor(out=pwr[:, :, n:2 * n], in0=pwr[:, :, n:2 * n], in1=t3a[:, :, 0:n], op=ALU.subtract))
        op(V, lambda n=n, bi=bi: nc.vector.tensor_tensor(out=pwi[:, :, n:2 * n], in0=pwr[:, :, 0:n], in1=bi, op=ALU.mult))
        op(V, lambda n=n, br=br: nc.vector.tensor_tensor(out=t3a[:, :, 0:n], in0=pwi[:, :, 0:n], in1=br, op=ALU.mult))
        op(V, lambda n=n: nc.vector.tensor_tensor(out=pwi[:, :, n:2 * n], in0=pwi[:, :, n:2 * n], in1=t3a[:, :, 0:n], op=ALU.add))
        op(V, lambda: nc.vector.tensor_tensor(out=tmpa[:], in0=a2r[:], in1=a2i[:], op=ALU.mult))
        op(V, lambda: nc.vector.tensor_tensor(out=kk[:], in0=a2i[:], in1=a2i[:], op=ALU.mult))
        op(V, lambda: nc.vector.tensor_tensor(out=a2r[:], in0=a2r[:], in1=a2r[:], op=ALU.mult))
        op(V, lambda: nc.vector.tensor_tensor(out=a2r[:], in0=a2r[:], in1=kk[:], op=ALU.subtract))
        op(V, lambda: nc.vector.tensor_scalar(a2i[:], tmpa[:], 2.0, None, ALU.mult))
        n *= 2
    op(V, lambda: nc.vector.tensor_scalar(npwi[:], pwi[:], -1.0, None, ALU.mult))
    op(V, lambda: nc.vector.tensor_copy(out=a64r[:], in_=pwr[:, :, 63]))
    op(V, lambda: nc.vector.tensor_copy(out=a64i[:], in_=pwi[:, :, 63]))
    op(V, lambda: nc.vector.tensor_scalar(na64i[:], a64i[:], -1.0, None, ALU.mult))
    for ri, key in enumerate(("b_re", "b_im")):
        op(V, lambda ri=ri: nc.vector.memset(Braw[ri][:], 0.0))
        for g_ in range(32):
            st_, p0, g2 = g_ // 2, (g_ % 8) * 16, g_ % 2
            f.dma(f.sp, Braw[ri][p0:p0 + 16, st_, g2 * 64:(g2 + 1) * 64], p[key][g_].rearrange("n q -> q n"), reads=gp, writes=gp,
                  allow_slow_non_contiguous=True)
        op(V, lambda ri=ri: nc.vector.tensor_copy(out=Btab[ri][:], in_=Braw[ri][:]))
    for ri, key in enumerate(("c_re", "c_im")):
        op(V, lambda ri=ri: nc.vector.memset(Craw[ri][:], 0.0))
        cv_ = p[key].rearrange("(t g) q n -> g n t q", g=2)
        for g2 in range(2):
            for t_ in range(NS):
                c0 = 32 * (t_ % 4) + 16 * g2
                f.dma(f.sp, Craw[ri][g2 * 64:(g2 + 1) * 64, t_, c0:c0 + 16], cv_[g2, :, t_, :], reads=gp, writes=gp,
                      allow_slow_non_contiguous=True)
    op(V, lambda: nc.vector.tensor_tensor(out=c1[:], in0=Craw[0][:], in1=bc_i(cre[:], 128), op=ALU.mult))
    op(V, lambda: nc.vector.tensor_tensor(out=c2[:], in0=Craw[1][:], in1=bc_i(cim[:], 128), op=ALU.mult))
    op(V, lambda: nc.vector.tensor_tensor(out=TCre[:], in0=c1[:], in1=c2[:], op=ALU.subtract))
    op(V, lambda: nc.vector.tensor_tensor(out=c1[:], in0=Craw[0][:], in1=bc_i(cim[:], 128), op=ALU.mult))
    op(V, lambda: nc.vector.tensor_tensor(out=c2[:], in0=Craw[1][:], in1=bc_i(cre[:], 128), op=ALU.mult))
    op(V, lambda: nc.vector.tensor_tensor(out=c1[:], in0=c1[:], in1=c2[:], op=ALU.add))
    op(V, lambda: nc.vector.tensor_scalar(TCimn[:], c1[:], -1.0, None, ALU.mult))
    f.dma(f.sp, dv[:], p["d"].rearrange("(c p) -> p c", p=128), writes=gp, allow_slow_non_contiguous=True)
    f.dma(f.pool, Wglu[:], p["w_glu"].rearrange("(kc p) f -> p kc f", p=128), writes=gp)
    barrier(f)
    scp.close()

    uT = mk("uTkt", [128, NT], BF16)
    yg = mk("yg", [128, 4, NT], BF16)
    bu = [mk("bu", [128, 64, NCH]) for _ in range(2)]
    hb = [[mk("hb", [128, 64, NCH], BF16) for _ in range(2)] for _ in range(4)]
    Hs = [mk("Hs", [128, NSEQ, CPS + 1]) for _ in range(2)]
    ysb = mk("ysb", [128, 512])
    x2 = mk("x2g", [128, 512]); zz = mk("zzg", [128, 512])
    pbu = [[sc.ps(uname("pbu"), [128, 512]) for _ in range(2)] for _ in range(2)]
    py = [sc.ps(uname("py"), [128, 512]) for _ in range(2)]
    pgl = [sc.ps(uname("pgl"), [128, 512]) for _ in range(2)]
    names = "u yg ysb x2 zz scr"
    Bf = {n: Buf(n) for n in names.split()}
    Bur = [Buf() for _ in range(64)]; Bui = [Buf() for _ in range(64)]
    BHr = [Buf() for _ in range(CPS + 1)]; BHi = [Buf() for _ in range(CPS + 1)]
    for n in ["pbu0", "pbu1", "py", "pgl"]:
        Bf[n + "0"] = Buf(); Bf[n + "1"] = Buf()
    for sl in range(4):
        Bf["hbr%d" % sl] = Buf(); Bf["hbi%d" % sl] = Buf()
    g = lambda *ns: [Bf[n] for n in ns]
    bun = (Bur, Bui)
    for ot in range(4):
      f.dma(f.sp, uT[:], scr["uT"][ot * 128:(ot + 1) * 128, :], writes=g("u"))
      for sl in range(4):
        st = 4 * ot + sl
        arS, aiS, naiS = ar[:, st:st + 1], ai[:, st:st + 1], nai[:, st:st + 1]
        hbn = ("hbr%d" % sl, "hbi%d" % sl)
        for pc in range(NT // 512):
            b = pc % 2
            for ri in range(2):
                f.op(P_, lambda ri=ri, b=b, pc=pc: nc.tensor.matmul(pbu[ri][b][:], Btab[ri][:, st, :], uT[:, pc * 512:(pc + 1) * 512],
                                                                    start=True, stop=True),
                     reads=g("u") + gp, writes=g("pbu%d%d" % (ri, b)))
                f.op(A, lambda ri=ri, b=b, pc=pc: nc.scalar.copy(out=bu[ri][:, :, pc * 8:(pc + 1) * 8].rearrange("p s c -> p c s"),
                                                                in_=pbu[ri][b][:].rearrange("p (c s) -> p c s", s=64)),
                     reads=g("pbu%d%d" % (ri, b)), writes=bun[ri])
        for tau in range(1, 64):
            X1 = lambda tau=tau: f.op(V, lambda: nc.vector.scalar_tensor_tensor(out=bu[0][:, tau, :], in0=bu[1][:, tau - 1, :], scalar=naiS, in1=bu[0][:, tau, :],
                                                                   op0=ALU.mult, op1=ALU.add), reads=[Bui[tau - 1]] + gp, writes=[Bur[tau]])
            X2 = lambda tau=tau: f.op(V, lambda: nc.vector.scalar_tensor_tensor(out=bu[1][:, tau, :], in0=bu[0][:, tau - 1, :], scalar=aiS, in1=bu[1][:, tau, :],
                                                                   op0=ALU.mult, op1=ALU.add), reads=[Bur[tau - 1]] + gp, writes=[Bui[tau]])
            D1 = lambda tau=tau: f.op(V, lambda: nc.vector.scalar_tensor_tensor(out=bu[0][:, tau, :], in0=bu[0][:, tau - 1, :], scalar=arS, in1=bu[0][:, tau, :],
                                                                   op0=ALU.mult, op1=ALU.add), reads=[Bur[tau - 1]] + gp, writes=[Bur[tau]])
            D2 = lambda tau=tau: f.op(V, lambda: nc.vector.scalar_tensor_tensor(out=bu[1][:, tau, :], in0=bu[1][:, tau - 1, :], scalar=arS, in1=bu[1][:, tau, :],
                                                                   op0=ALU.mult, op1=ALU.add), reads=[Bui[tau - 1]] + gp, writes=[Bui[tau]])
            for o_ in ((X1, X2, D1, D2) if tau % 2 == 1 else (X2, X1, D2, D1)):
                o_()
        f.op(V, lambda: nc.vector.memset(Hs[0][:, :, 0:1], 0.0), writes=[BHr[0]])
        f.op(V, lambda: nc.vector.memset(Hs[1][:, :, 0:1], 0.0), writes=[BHi[0]])
        lastr = bu[0][:, 63, :].rearrange("p (b c) -> p b c", c=CPS)
        lasti = bu[1][:, 63, :].rearrange("p (b c) -> p b c", c=CPS)
        A64r, A64i, NA64i = a64r[:, st:st + 1], a64i[:, st:st + 1], na64i[:, st:st + 1]
        for c in range(CPS):
            C1 = lambda c=c: f.op(V, lambda: nc.vector.scalar_tensor_tensor(out=Hs[0][:, :, c + 1], in0=Hs[1][:, :, c], scalar=NA64i, in1=lastr[:, :, c],
                                                               op0=ALU.mult, op1=ALU.add), reads=[Bur[63], BHi[c]] + gp, writes=[BHr[c + 1]])
            C2 = lambda c=c: f.op(V, lambda: nc.vector.scalar_tensor_tensor(out=Hs[0][:, :, c + 1], in0=Hs[0][:, :, c], scalar=A64r, in1=Hs[0][:, :, c + 1],
                                                               op0=ALU.mult, op1=ALU.add), reads=[BHr[c]] + gp, writes=[BHr[c + 1]])
            C3 = lambda c=c: f.op(V, lambda: nc.vector.scalar_tensor_tensor(out=Hs[1][:, :, c + 1], in0=Hs[0][:, :, c], scalar=A64i, in1=lasti[:, :, c],
                                                               op0=ALU.mult, op1=ALU.add), reads=[Bui[63], BHr[c]] + gp, writes=[BHi[c + 1]])
            C4 = lambda c=c: f.op(V, lambda: nc.vector.scalar_tensor_tensor(out=Hs[1][:, :, c + 1], in0=Hs[1][:, :, c], scalar=A64r, in1=Hs[1][:, :, c + 1],
                                                               op0=ALU.mult, op1=ALU.add), reads=[BHi[c]] + gp, writes=[BHi[c + 1]])
            for o_ in ((C1, C3, C2, C4) if c % 2 == 0 else (C3, C1, C4, C2)):
                o_()
        Hr = Hs[0][:, :, 0:CPS]; Hi = Hs[1][:, :, 0:CPS]
        for tau in range(64):
            pr, pi_, npi = pwr[:, st, tau:tau + 1], pwi[:, st, tau:tau + 1], npwi[:, st, tau:tau + 1]
            br3 = bu[0][:, tau, :].rearrange("p (b c) -> p b c", c=CPS)
            bi3 = bu[1][:, tau, :].rearrange("p (b c) -> p b c", c=CPS)
            hr3 = hb[sl][0][:, tau, :].rearrange("p (b c) -> p b c", c=CPS)
            hi3 = hb[sl][1][:, tau, :].rearrange("p (b c) -> p b c", c=CPS)
            f.op(V, lambda: nc.vector.scalar_tensor_tensor(out=br3, in0=Hi, scalar=npi, in1=br3, op0=ALU.mult, op1=ALU.add), reads=BHi[0:CPS] + gp, writes=[Bur[tau]])
            f.op(V, lambda: nc.vector.scalar_tensor_tensor(out=bi3, in0=Hr, scalar=pi_, in1=bi3, op0=ALU.mult, op1=ALU.add), reads=BHr[0:CPS] + gp, writes=[Bui[tau]])
            f.op(V, lambda: nc.vector.scalar_tensor_tensor(out=hr3, in0=Hr, scalar=pr, in1=br3, op0=ALU.mult, op1=ALU.add), reads=BHr[0:CPS] + [Bur[tau]] + gp, writes=g(hbn[0]))
            f.op(V, lambda: nc.vector.scalar_tensor_tensor(out=hi3, in0=Hi, scalar=pr, in1=bi3, op0=ALU.mult, op1=ALU.add), reads=BHi[0:CPS] + [Bui[tau]] + gp, writes=g(hbn[1]))
      tpp = 512 // NCH
      for pc in range(64 * NCH // 512):
        b = pc % 2
        k = 0
        for sl in range(4):
            st = 4 * ot + sl
            for ri, TC in enumerate((TCre, TCimn)):
                hbf = hb[sl][ri][:].rearrange("p s c -> p (s c)")
                f.op(P_, lambda pc=pc, b=b, TC=TC, st=st, hbf=hbf, k=k: nc.tensor.matmul(py[b][:], TC[:, st, :], hbf[:, pc * 512:(pc + 1) * 512],
                                                                                      start=(k == 0), stop=(k == 7)),
                     reads=g("hbr%d" % sl, "hbi%d" % sl) + gp, writes=g("py%d" % b))
                k += 1
        uview = uT[:, :].rearrange("p (c s) -> p s c", s=64)[:, pc * tpp:(pc + 1) * tpp, :]
        ygview = yg[:, ot, :].rearrange("p (c s) -> p s c", s=64)[:, pc * tpp:(pc + 1) * tpp, :]
        y3 = ysb[:, :].rearrange("p (s c) -> p s c", c=NCH)
        z3 = zz[:, :].rearrange("p (s c) -> p s c", c=NCH)
        f.op(V, lambda uview=uview, y3=y3, b=b: nc.vector.scalar_tensor_tensor(out=y3, in0=uview, scalar=dv[:, ot:ot + 1],
                                                                              in1=py[b][:].rearrange("p (s c) -> p s c", c=NCH),
                                                                              op0=ALU.mult, op1=ALU.add),
             reads=g("py%d" % b, "u") + gp, writes=g("ysb"))
        f.op(A, lambda: nc.scalar.activation(out=x2[:], in_=ysb[:], func=AF.Square), reads=g("ysb"), writes=g("x2"))
        f.op(V, lambda: nc.vector.tensor_scalar(x2[:], x2[:], 0.044715, 1.0, ALU.mult, ALU.add), reads=g("x2"), writes=g("x2"))
        f.op(V, lambda: nc.vector.tensor_tensor(out=zz[:], in0=x2[:], in1=ysb[:], op=ALU.mult), reads=g("x2", "ysb"), writes=g("zz"))
        f.op(A, lambda: nc.scalar.activation(out=zz[:], in_=zz[:], func=AF.Sigmoid, scale=1.5957691216), reads=g("zz"), writes=g("zz"))
        f.op(V, lambda ygview=ygview, y3=y3, z3=z3: nc.vector.tensor_tensor(out=ygview, in0=y3, in1=z3, op=ALU.mult), reads=g("zz", "ysb"), writes=g("yg"))
    sgl = [mk("sgl", [128, 512]) for _ in range(2)]
    og = [mk("og", [128, 512], BF16) for _ in range(2)]
    Bs = [Buf(), Buf()]; Bo = [Buf(), Buf()]
    for t in range(NT // 512):
        cs = slice(t * 512, (t + 1) * 512)
        for oc in range(4):
            b = oc % 2
            for kc in range(4):
                f.op(P_, lambda kc=kc, oc=oc, b=b, cs=cs: nc.tensor.matmul(pgl[b][:], Wglu[:, kc, oc * 128:(oc + 1) * 128], yg[:, kc, cs],
                                                                        start=(kc == 0), stop=(kc == 3)),
                     reads=g("yg") + gp, writes=g("pgl%d" % b))
            f.op(A, lambda b=b: nc.scalar.activation(out=sgl[b][:], in_=pgl[b][:], func=AF.Sigmoid), reads=g("pgl%d" % b), writes=[Bs[b]])
            f.op(V, lambda b=b, oc=oc, cs=cs: nc.vector.tensor_tensor(out=og[b][:], in0=sgl[b][:], in1=yg[:, oc, cs], op=ALU.mult),
                 reads=[Bs[b]] + g("yg"), writes=[Bo[b]])
            f.dma(f.sp, scr["yT"][512 + oc * 128:512 + (oc + 1) * 128, cs], og[b][:], reads=[Bo[b]], writes=g("scr"))
    barrier(f)
    sc.close()


def ev_out_phase(f, X, wout_d, scr, NT):
    nc = f.nc
    sc = Scope(nc)
    mk = lambda n, shp, dt=F32: sc.sb(uname(n), shp, dt)
    Wo = mk("Wo", [128, KC, D], BF16)
    yt = mk("yt", [128, KC, TT], BF16)
    xt = mk("xt", [128, KC, TT])
    po = [sc.ps(uname("pout"), [128, TT]) for _ in range(2)]
    BW, By, Bx, Bs = Buf(), Buf(), Buf(), Buf()
    Bp = [Buf(), Buf()]
    wv = wout_d.rearrange("(kc p) d -> p kc d", p=128)
    f.dma(f.pool, Wo[:, 0:4, :], wv[:, 0:4, :], writes=[BW])
    f.dma(f.pool, Wo[:, 4:8, :], wv[:, 4:8, :], writes=[BW])
    Xv = X.rearrange("(c p) t -> p c t", p=128)
    yv = scr["yT"].rearrange("(c p) t -> p c t", p=128)
    for t in range(NT // TT):
        cs = slice(t * TT, (t + 1) * TT)
        f.dma(f.sp, yt[:], yv[:, :, cs], writes=[By])
        f.dma(f.sp, xt[:], Xv[:, :, cs], writes=[Bx])
        for dc in range(KC):
            b = dc % 2
            for kc in range(KC):
                f.op(f.pe, lambda dc=dc, kc=kc, b=b: nc.tensor.matmul(po[b][:], Wo[:, kc, dc * 128:(dc + 1) * 128], yt[:, kc, :],
                                                                      start=(kc == 0), stop=(kc == KC - 1)),
                     reads=[BW, By], writes=[Bp[b]])
            f.op(f.dve, lambda dc=dc, b=b: nc.vector.tensor_tensor(out=xt[:, dc, :], in0=po[b][:], in1=xt[:, dc, :], op=ALU.add),
                 reads=[Bp[b]], writes=[Bx])
        f.dma(f.sp, Xv[:, :, cs], xt[:], reads=[Bx], writes=[Bs])
    barrier(f)
    sc.close()


SEQ = 2048
NSEQ_CORE = 2
NCORES = 8
DEPTH = 4

_IN_SHAPES = {
    "ffn1_norm": [4, 1024], "ffn1_w_gate": [4, 1024, 2816], "ffn1_w_up": [4, 1024, 2816], "ffn1_w_down": [4, 2816, 1024],
    "mix_norm": [4, 1024], "ffn2_norm": [4, 1024], "ffn2_w_gate": [4, 1024, 2816], "ffn2_w_up": [4, 1024, 2816],
    "ffn2_w_down": [4, 2816, 1024], "ev_w_in": [2, 1024, 2560], "hg_lb_logits": [2, 512], "hg_norm_w": [2, 128],
    "s5_a_re": [2, 32, 64], "s5_a_im": [2, 32, 64], "s5_b_re": [2, 32, 64, 16], "s5_b_im": [2, 32, 64, 16],
    "s5_c_re": [2, 32, 16, 64], "s5_c_im": [2, 32, 16, 64], "s5_d": [2, 512], "s5_log_dt": [2, 32], "s5_w_glu": [2, 512, 512],
    "ev_w_out": [2, 1024, 1024], "od_w_in": [2, 1024, 4112], "gdn_conv_w": [2, 4, 3072], "gdn_a_log": [2, 8], "gdn_dt_bias": [2, 8],
    "gdn_norm_w": [2, 128], "od_w_out": [2, 1024, 1024], "final_norm": [1024],
}


def build_program(L=SEQ, nseq=NSEQ_CORE, depth=DEPTH):
    NT = L * nseq
    f = FW()
    nc = f.nc
    I = {k: nc.dram_tensor(k, list(shp), F32, kind="ExternalInput").ap() for k, shp in _IN_SHAPES.items()}
    xT = nc.dram_tensor("xT", [D, NT], F32, kind="ExternalInput").ap()
    oT = nc.dram_tensor("oT", [D, NT], F32, kind="ExternalOutput").ap()
    X = nc.dram_tensor("Xres", [D, NT], F32).ap()
    dt_ = lambda n, shp, t: nc.dram_tensor(n, shp, t).ap()
    scr = {
        "qT": dt_("s_qT", [D, NT], BF16), "kT": dt_("s_kT", [D, NT], BF16), "gT": dt_("s_gT", [D, NT], BF16),
        "ktok": dt_("s_ktok", [NT, D], BF16), "vtok": dt_("s_vtok", [NT, D], BF16), "bl": dt_("s_bl", [NT, 16], F32),
        "uT": dt_("s_uT", [512, NT], BF16), "lfT": dt_("s_lfT", [512, NT], F32), "yT": dt_("s_yT", [D, NT], BF16),
    }
    src = xT
    for layer in range(depth):
        j = layer // 2
        ffn_phase(f, src, X, I["ffn1_norm"][layer], I["ffn1_w_gate"][layer], I["ffn1_w_up"][layer], I["ffn1_w_down"][layer], NT)
        src = X
        if layer % 2 == 0:
            ev_proj_phase(f, X, I["mix_norm"][layer], I["ev_w_in"][j], I["hg_lb_logits"], j, scr, NT, L)
            hgrn_core_phase(f, I["hg_norm_w"][j], scr, NT, L)
            p = {"a_re": I["s5_a_re"][j], "a_im": I["s5_a_im"][j], "b_re": I["s5_b_re"][j], "b_im": I["s5_b_im"][j],
                 "c_re": I["s5_c_re"][j], "c_im": I["s5_c_im"][j], "d": I["s5_d"][j], "log_dt": I["s5_log_dt"][j], "w_glu": I["s5_w_glu"][j]}
            s5_phase(f, p, j, scr, NT, L)
            ev_out_phase(f, X, I["ev_w_out"][j], scr, NT)
        else:
            gdn_proj_phase(f, X, I["mix_norm"][layer], I["od_w_in"][j], I["gdn_conv_w"][j], I["gdn_a_log"][j], I["gdn_dt_bias"][j], scr, NT, L)
            gdn_core_phase(f, X, I["gdn_norm_w"][j], I["od_w_out"][j], scr, NT, L)
            ev_out_phase(f, X, I["od_w_out"][j], scr, NT)
        ffn_phase(f, X, X, I["ffn2_norm"][layer], I["ffn2_w_gate"][layer], I["ffn2_w_up"][layer], I["ffn2_w_down"][layer], NT)
    Bo = final_phase(f, src, oT, I["final_norm"], NT)
    f.finish([Bo])
    return f


def kernel(**inputs):
    x = np.asarray(inputs["x"], dtype=np.float32)
    Bsz, L, Dm = x.shape
    nseq = Bsz // NCORES
    f = build_program(L, nseq, DEPTH)
    shared = {k: np.ascontiguousarray(np.asarray(inputs[k], dtype=np.float32)) for k in _IN_SHAPES}
    in_maps = []
    for c in range(NCORES):
        m = dict(shared)
        m["xT"] = np.ascontiguousarray(x[c * nseq:(c + 1) * nseq].reshape(nseq * L, Dm).T)
        in_maps.append(m)
    res = run_bass_kernel_spmd(f.nc, in_maps, core_ids=list(range(NCORES)))
    out = np.empty((Bsz, L, Dm), dtype=np.float32)
    for c in range(NCORES):
        oT = np.asarray(res.results[c]["oT"])
        out[c * nseq:(c + 1) * nseq] = oT.T.reshape(nseq, L, Dm)
    return out
```

```python
import numpy as np
import concourse.bass as bass
import concourse.mybir as mybir
from concourse.bass_utils import run_bass_kernel_spmd

F32 = mybir.dt.float32
BF16 = mybir.dt.bfloat16
AF = mybir.ActivationFunctionType
ALU = mybir.AluOpType

EPOCH = 16000


class Eng:
    def __init__(self, fw, e, name, self_sync=True):
        self.fw = fw
        self.e = e
        self.name = name
        self.self_sync = self_sync
        self.sem = fw.nc.alloc_semaphore(name + "_s0")
        self.cnt = 0
        self.nep = 0
        self.seen = {}
        self.total = 0

    def _wait(self, deps):
        for d in deps:
            if d is None:
                continue
            sem, val, own = d
            if own is self and not self.self_sync:
                continue
            k = id(sem)
            if self.seen.get(k, 0) >= val:
                continue
            self.e.wait_ge(sem, val)
            self.seen[k] = val

    def emit(self, fn, deps=()):
        self._wait(deps)
        if self.cnt >= EPOCH:
            self.nep += 1
            self.sem = self.fw.nc.alloc_semaphore("%s_s%d" % (self.name, self.nep))
            self.cnt = 0
        ins = fn()
        self.cnt += 1
        self.total += 1
        ins.then_inc(self.sem, 1)
        return (self.sem, self.cnt, self)

    def dma(self, out, in_, deps=(), **kw):
        fw = self.fw
        self._wait(deps)
        slot = fw.dma_rr % len(fw.dma_sems)
        fw.dma_rr += 1
        sem = fw.dma_sems[slot]
        prev = fw.dma_vals[slot]
        if prev > 0:
            k = id(sem)
            if self.seen.get(k, 0) < prev:
                self.e.wait_ge(sem, prev)
                self.seen[k] = prev
        ins = self.e.dma_start(out=out, in_=in_, **kw)
        val = prev + 16
        fw.dma_vals[slot] = val
        ins.then_inc(sem, 16)
        return (sem, val, None)


class Buf:
    def __init__(self, name=""):
        self.name = name
        self.w = None
        self.r = {}


class FW:
    def __init__(self, n_dma_sems=40):
        self.nc = bass.Bass("TRN2", target_bir_lowering=False)
        nc = self.nc
        self.pe = Eng(self, nc.tensor, "pe", self_sync=False)
        self.act = Eng(self, nc.scalar, "act")
        self.dve = Eng(self, nc.vector, "dve")
        self.pool = Eng(self, nc.gpsimd, "pool")
        self.sp = Eng(self, nc.sync, "sp")
        self.dma_sems = [nc.alloc_semaphore("dma%d" % i) for i in range(n_dma_sems)]
        self.dma_vals = [0] * n_dma_sems
        self.dma_rr = 0

    def _deps(self, reads, writes):
        deps = []
        for b in reads:
            if b.w is not None:
                deps.append(b.w)
        for b in writes:
            if b.w is not None:
                deps.append(b.w)
            deps.extend(b.r.values())
        return deps

    def _post(self, tok, reads, writes):
        for b in reads:
            k = id(tok[0])
            o = b.r.get(k)
            if o is None or o[1] < tok[1]:
                b.r[k] = tok
        for b in writes:
            b.w = tok
            b.r = {}

    def op(self, eng, fn, reads=(), writes=()):
        tok = eng.emit(fn, self._deps(reads, writes))
        self._post(tok, reads, writes)
        return tok

    def dma(self, eng, out, in_, reads=(), writes=(), **kw):
        tok = eng.dma(out, in_, self._deps(reads, writes), **kw)
        self._post(tok, reads, writes)
        return tok

    def finish(self, bufs):
        deps = []
        for b in bufs:
            if b.w is not None:
                deps.append(b.w)
        self.sp._wait(deps)


D = 1024
DFF = 2816
KC = D // 128
FC = DFF // 128
TT = 512
EPS = 1e-6


class Scope:
    def __init__(self, nc):
        self.nc = nc
        self.guards = []

    def sb(self, name, shape, dt):
        g = self.nc.sbuf_tensor(name, shape, dt)
        t = g.__enter__()
        self.guards.append(g)
        return t

    def ps(self, name, shape, dt=F32):
        g = self.nc.psum_tensor(name, shape, dt)
        t = g.__enter__()
        self.guards.append(g)
        return t

    def close(self):
        for g in reversed(self.guards):
            g.__exit__(None, None, None)
        self.guards = []


def barrier(f):
    engs = [f.pe, f.act, f.dve, f.pool, f.sp]
    toks = []
    for e in engs:
        if e.cnt > 0:
            toks.append((e.sem, e.cnt, None))
    for s, v in zip(f.dma_sems, f.dma_vals):
        if v > 0:
            toks.append((s, v, None))
    for e in engs:
        e._wait(toks)


_uid = [0]


def uname(p):
    _uid[0] += 1
    return "%s_%d" % (p, _uid[0])


def rms_stats(f, sc, xt, sqbuf, Bx, Bsq, ones_bf, eps_t, pss, Bpss, rstd, Brstd, ncols, inv_n):
    nc = f.nc
    Bsq = Bsq if isinstance(Bsq, list) else [Bsq]
    f.op(f.act, lambda: nc.scalar.activation(out=sqbuf, in_=xt, func=AF.Square), reads=[Bx], writes=Bsq)
    for c in range(KC):
        f.op(f.pe, lambda c=c: nc.tensor.matmul(pss, ones_bf, sqbuf[:, c, :], start=(c == 0), stop=(c == KC - 1)),
             reads=Bsq, writes=[Bpss])
    f.op(f.act, lambda: nc.scalar.activation(out=rstd, in_=pss, func=AF.Sqrt, bias=eps_t, scale=inv_n),
         reads=[Bpss], writes=[Brstd])
    f.op(f.dve, lambda: nc.vector.reciprocal(out=rstd, in_=rstd), reads=[Brstd], writes=[Brstd])


def load_w_bf16(f, eng, dst_sb, src_ap, bufs, piece):
    nc = f.nc
    A, Bn = src_ap.shape[1], src_ap.shape[2]
    toks = []
    i = 0
    for b0 in range(0, Bn, piece):
        b1 = min(Bn, b0 + piece)
        f.dma(eng, dst_sb[:, :, b0:b1], src_ap[:, :, b0:b1], writes=[bufs[i]])
        i += 1


def ffn_phase(f, src, dst, wn_d, wg_d, wu_d, wd_d, NT):
    nc = f.nc
    sc = Scope(nc)
    Wg = sc.sb(uname("Wg"), [128, KC, DFF], BF16)
    Wu = sc.sb(uname("Wu"), [128, KC, DFF], BF16)
    Wd = sc.sb(uname("Wd"), [128, FC, D], BF16)
    wn = sc.sb(uname("wn"), [128, KC], F32)
    xt = [sc.sb(uname("xt"), [128, KC, TT], F32) for _ in range(2)]
    hT = sc.sb(uname("hT"), [128, KC, TT], BF16)
    sq = [sc.sb(uname("sq"), [128, TT], BF16) for _ in range(2)]
    act = sc.sb(uname("act"), [128, FC, TT], BF16)
    rstd = sc.sb(uname("rstd"), [128, TT], F32)
    sg = [sc.sb(uname("sg"), [128, TT], F32) for _ in range(2)]
    ones_bf = sc.sb(uname("ones"), [128, 128], BF16)
    eps_t = sc.sb(uname("eps"), [128, 1], F32)
    pg = [sc.ps(uname("pg"), [128, TT]) for _ in range(2)]
    pu = [sc.ps(uname("pu"), [128, TT]) for _ in range(2)]
    pd = [sc.ps(uname("pd"), [128, TT]) for _ in range(2)]
    pss = sc.ps(uname("pss"), [128, TT])

    PW = 512
    npc = (DFF + PW - 1) // PW
    BWg = [Buf() for _ in range(npc)]
    BWu = [Buf() for _ in range(npc)]
    BWd = [Buf() for _ in range(FC)]
    Bc, Bh, Bpss, Brstd = Buf(), Buf(), Buf(), Buf()
    Bsq = [Buf(), Buf()]
    Bx = [Buf(), Buf()]
    Bact = [Buf() for _ in range(FC)]
    Bpg = [Buf(), Buf()]
    Bpu = [Buf(), Buf()]
    Bsg = [Buf(), Buf()]
    Bpd = [Buf(), Buf()]
    Bdst = Buf()

    f.op(f.dve, lambda: nc.vector.memset(ones_bf[:], 1.0), writes=[Bc])
    f.op(f.dve, lambda: nc.vector.memset(eps_t[:], EPS), writes=[Bc])
    f.dma(f.sp, wn[:], wn_d.rearrange("(c p) -> p c", p=128), writes=[Bc], allow_slow_non_contiguous=True)
    srcv = src.rearrange("(c p) t -> p c t", p=128)
    dstv = dst.rearrange("(c p) t -> p c t", p=128)
    ntile = NT // TT

    def load(t):
        f.dma(f.sp, xt[t % 2][:], srcv[:, :, t * TT:(t + 1) * TT], writes=[Bx[t % 2]])

    def norm(t):
        x_ = xt[t % 2]
        for c in range(KC):
            f.op(f.act, lambda c=c: nc.scalar.activation(out=sq[c % 2][:], in_=x_[:, c, :], func=AF.Square), reads=[Bx[t % 2]], writes=[Bsq[c % 2]])
            f.op(f.pe, lambda c=c: nc.tensor.matmul(pss[:], ones_bf[:], sq[c % 2][:], start=(c == 0), stop=(c == KC - 1)),
                 reads=[Bsq[c % 2], Bc], writes=[Bpss])
        f.op(f.act, lambda: nc.scalar.activation(out=rstd[:], in_=pss[:], func=AF.Sqrt, bias=eps_t[:], scale=1.0 / D),
             reads=[Bpss, Bc], writes=[Brstd])
        f.op(f.dve, lambda: nc.vector.reciprocal(out=rstd[:], in_=rstd[:]), reads=[Brstd], writes=[Brstd])
        for c in range(KC):
            f.op(f.dve, lambda c=c: nc.vector.scalar_tensor_tensor(out=hT[:, c, :], in0=x_[:, c, :], scalar=wn[:, c:c + 1],
                                                                 in1=rstd[:], op0=ALU.mult, op1=ALU.mult),
                 reads=[Bx[t % 2], Brstd, Bc], writes=[Bh])

    load(0)
    wgv = wg_d.rearrange("(kc p) f -> p kc f", p=128)
    wuv = wu_d.rearrange("(kc p) f -> p kc f", p=128)
    wdv = wd_d.rearrange("(fc p) d -> p fc d", p=128)
    for i in range(npc):
        b0, b1 = i * PW, min(DFF, (i + 1) * PW)
        f.dma(f.pool, Wg[:, :, b0:b1], wgv[:, :, b0:b1], writes=[BWg[i]])
        f.dma(f.pool, Wu[:, :, b0:b1], wuv[:, :, b0:b1], writes=[BWu[i]])
    for i in range(0, FC, 2):
        f.dma(f.pool, Wd[:, i:i + 2, :], wdv[:, i:i + 2, :], writes=[BWd[i], BWd[i + 1]])
    norm(0)
    for t in range(ntile):
        x_ = xt[t % 2]
        if t + 1 < ntile:
            load(t + 1)
        for fc in range(FC):
            b = fc % 2
            wi = (fc * 128) // PW
            for kc in range(KC):
                f.op(f.pe, lambda kc=kc, fc=fc, b=b: nc.tensor.matmul(pg[b][:], Wg[:, kc, fc * 128:(fc + 1) * 128], hT[:, kc, :],
                                                                      start=(kc == 0), stop=(kc == KC - 1)),
                     reads=[BWg[wi], Bh], writes=[Bpg[b]])
            for kc in range(KC):
                f.op(f.pe, lambda kc=kc, fc=fc, b=b: nc.tensor.matmul(pu[b][:], Wu[:, kc, fc * 128:(fc + 1) * 128], hT[:, kc, :],
                                                                      start=(kc == 0), stop=(kc == KC - 1)),
                     reads=[BWu[wi], Bh], writes=[Bpu[b]])
            f.op(f.act, lambda b=b: nc.scalar.activation(out=sg[b][:], in_=pg[b][:], func=AF.Silu), reads=[Bpg[b]], writes=[Bsg[b]])
            f.op(f.dve, lambda b=b, fc=fc: nc.vector.tensor_tensor(out=act[:, fc, :], in0=pu[b][:], in1=sg[b][:], op=ALU.mult),
                 reads=[Bpu[b], Bsg[b]], writes=[Bact[fc]])
        if t + 1 < ntile:
            norm(t + 1)
        for dc in range(KC):
            b = dc % 2
            for fc in range(FC):
                f.op(f.pe, lambda dc=dc, fc=fc, b=b: nc.tensor.matmul(pd[b][:], Wd[:, fc, dc * 128:(dc + 1) * 128], act[:, fc, :],
                                                                      start=(fc == 0), stop=(fc == FC - 1)),
                     reads=[BWd[fc], Bact[fc]], writes=[Bpd[b]])
            f.op(f.dve, lambda dc=dc, b=b: nc.vector.scalar_tensor_tensor(out=x_[:, dc, :], in0=pd[b][:], scalar=0.5, in1=x_[:, dc, :],
                                                                       op0=ALU.mult, op1=ALU.add),
                 reads=[Bpd[b]], writes=[Bx[t % 2]])
        f.dma(f.sp, dstv[:, :, t * TT:(t + 1) * TT], x_[:], reads=[Bx[t % 2]], writes=[Bdst])
    barrier(f)
    sc.close()


def final_phase(f, src, dst, wn_d, NT):
    nc = f.nc
    sc = Scope(nc)
    wn = sc.sb(uname("wn"), [128, KC], F32)
    xt = sc.sb(uname("xt"), [128, KC, TT], F32)
    sq = sc.sb(uname("sq"), [128, KC, TT], BF16)
    rstd = sc.sb(uname("rstd"), [128, TT], F32)
    ones_bf = sc.sb(uname("ones"), [128, 128], BF16)
    eps_t = sc.sb(uname("eps"), [128, 1], F32)
    pss = sc.ps(uname("pss"), [128, TT])
    Bc, Bx, Bsq, Bpss, Brstd, Bdst = [Buf() for _ in range(6)]
    f.op(f.dve, lambda: nc.vector.memset(ones_bf[:], 1.0), writes=[Bc])
    f.op(f.dve, lambda: nc.vector.memset(eps_t[:], EPS), writes=[Bc])
    f.dma(f.sp, wn[:], wn_d.rearrange("(c p) -> p c", p=128), writes=[Bc], allow_slow_non_contiguous=True)
    srcv = src.rearrange("(c p) t -> p c t", p=128)
    dstv = dst.rearrange("(c p) t -> p c t", p=128)
    for t in range(NT // TT):
        cs = slice(t * TT, (t + 1) * TT)
        f.dma(f.sp, xt[:], srcv[:, :, cs], writes=[Bx])
        rms_stats(f, sc, xt[:], sq[:], Bx, Bsq, ones_bf[:], eps_t[:], pss[:], Bpss, rstd[:], Brstd, TT, 1.0 / D)
        for c in range(KC):
            f.op(f.dve, lambda c=c: nc.vector.scalar_tensor_tensor(out=xt[:, c, :], in0=xt[:, c, :], scalar=wn[:, c:c + 1],
                                                                 in1=rstd[:], op0=ALU.mult, op1=ALU.mult),
                 reads=[Brstd, Bc], writes=[Bx])
        f.dma(f.sp, dstv[:, :, cs], xt[:], reads=[Bx], writes=[Bdst])
    barrier(f)
    sc.close()
    return Bdst


NEG = -30000.0


class Consts:
    def __init__(self, f, sc):
        nc = f.nc
        self.B = Buf()
        B = self.B
        mk = lambda n, shp, dt=F32: sc.sb(uname(n), shp, dt)
        self.ones32 = mk("ones32", [128, 128])
        self.ones_bf = mk("onesbf", [128, 128], BF16)
        self.tri = mk("tri", [128, 128])
        self.low = mk("low", [128, 128])
        self.ident = mk("ident", [128, 128])
        self.ident_bf = mk("identbf", [128, 128], BF16)
        self.negincT = mk("negincT", [128, 128])
        self.negstr = mk("negstr", [128, 128])
        self.strT01 = mk("strT01", [128, 128])
        self.bd = mk("bd", [128, 128])
        self.cind = mk("cind", [128, 2, 128])
        self.eps = mk("epsc", [128, 1])
        self.one = mk("onec", [128, 1])
        P = f.pool
        f.op(P, lambda: nc.gpsimd.memset(self.ones32[:], 1.0), writes=[B])
        f.op(P, lambda: nc.gpsimd.memset(self.ones_bf[:], 1.0), writes=[B])
        f.op(P, lambda: nc.gpsimd.memset(self.eps[:], EPS), writes=[B])
        f.op(P, lambda: nc.gpsimd.memset(self.one[:], 1.0), writes=[B])
        f.op(P, lambda: nc.gpsimd.affine_select(out=self.tri[:], in_=self.ones32[:], pattern=[[1, 128]], compare_op=ALU.is_ge,
                                                fill=0.0, base=0, channel_multiplier=-1), reads=[B], writes=[B])
        f.op(P, lambda: nc.gpsimd.memset(self.tri[0:64, 64:128], 0.0), writes=[B])
        f.op(P, lambda: nc.gpsimd.affine_select(out=self.low[:], in_=self.ones32[:], pattern=[[-1, 128]], compare_op=ALU.is_gt,
                                                fill=0.0, base=0, channel_multiplier=1), reads=[B], writes=[B])
        f.op(P, lambda: nc.gpsimd.memset(self.low[64:128, 0:64], 0.0), writes=[B])
        f.op(P, lambda: nc.gpsimd.affine_select(out=self.ident[:], in_=self.ones32[:], pattern=[[-1, 128]], compare_op=ALU.is_equal,
                                                fill=0.0, base=0, channel_multiplier=1), reads=[B], writes=[B])
        f.op(P, lambda: nc.gpsimd.tensor_copy(out=self.ident_bf[:], in_=self.ident[:]), reads=[B], writes=[B])
        f.op(P, lambda: nc.gpsimd.tensor_scalar(self.negincT[:], self.tri[:], -1.0, -NEG, ALU.add, ALU.mult), reads=[B], writes=[B])
        f.op(P, lambda: nc.gpsimd.tensor_scalar(self.negstr[:], self.low[:], -1.0, -NEG, ALU.add, ALU.mult), reads=[B], writes=[B])
        f.op(P, lambda: nc.gpsimd.tensor_tensor(out=self.strT01[:], in0=self.tri[:], in1=self.ident[:], op=ALU.subtract), reads=[B], writes=[B])
        f.op(P, lambda: nc.gpsimd.memset(self.bd[:], 0.0), writes=[B])
        f.op(P, lambda: nc.gpsimd.memset(self.bd[0:64, 0:64], 1.0), writes=[B])
        f.op(P, lambda: nc.gpsimd.memset(self.bd[64:128, 64:128], 1.0), writes=[B])
        f.op(P, lambda: nc.gpsimd.memset(self.cind[:], 0.0), writes=[B])
        f.op(P, lambda: nc.gpsimd.memset(self.cind[0:64, 0, :], 1.0), writes=[B])
        f.op(P, lambda: nc.gpsimd.memset(self.cind[64:128, 1, :], 1.0), writes=[B])


def bc_h(ap2, H=8):
    return ap2.unsqueeze(1).to_broadcast([ap2.shape[0], H, ap2.shape[1]])


def bc_i(ap2, n=128):
    return ap2.unsqueeze(2).to_broadcast([ap2.shape[0], ap2.shape[1], n])


GH = 8
ODIN = 4112


def gdn_proj_phase(f, X, wn_d, win_d, convw_d, alog_d, dtb_d, scr, NT, L):
    nc = f.nc
    sc = Scope(nc)
    C = Consts(f, sc)
    Win = sc.sb(uname("Win"), [128, KC, ODIN], BF16)
    wn = sc.sb(uname("wn"), [128, KC], F32)
    xt = sc.sb(uname("xt"), [128, KC, TT], F32)
    hT = sc.sb(uname("hT"), [128, KC, TT], BF16)
    sq = sc.sb(uname("sq"), [128, KC, TT], BF16)
    rstd = sc.sb(uname("rstd"), [128, TT], F32)
    cw = sc.sb(uname("cw"), [128, 24, 4], F32)
    halo = sc.sb(uname("halo"), [128, 24, 3], F32)
    pre = [sc.sb(uname("pre"), [128, TT + 3], F32) for _ in range(2)]
    cv = [sc.sb(uname("cv"), [128, TT], F32) for _ in range(2)]
    s32 = [sc.sb(uname("s32"), [128, TT], F32) for _ in range(2)]
    sq2 = [sc.sb(uname("sq2"), [128, TT], BF16) for _ in range(2)]
    r2 = [sc.sb(uname("r2"), [128, TT], F32) for _ in range(2)]
    ob = [sc.sb(uname("ob"), [128, TT], BF16) for _ in range(2)]
    tk = [sc.sb(uname("tk"), [128, 4, 128], BF16) for _ in range(2)]
    blt = sc.sb(uname("blt"), [128, 4, 16], F32)
    tmpb = sc.sb(uname("tmpb"), [128, 4, 8], F32)
    dtb = sc.sb(uname("dtb"), [128, 8], F32)
    negA = sc.sb(uname("negA"), [128, 8], F32)
    eps128 = sc.sb(uname("eps128"), [128, 1], F32)
    pp = [sc.ps(uname("pp"), [128, TT]) for _ in range(2)]
    pn = [sc.ps(uname("pn"), [128, TT]) for _ in range(2)]
    ptr = [sc.ps(uname("ptr"), [128, 4, 128], BF16) for _ in range(2)]
    pss = sc.ps(uname("pss"), [128, TT])
    pb = sc.ps(uname("pb"), [128, 4, 16])

    Bc, Bx, Bh, Bsq, Bpss, Brstd, Bhalo, Bpb, Bblt, Btmpb = [Buf() for _ in range(10)]
    NW = 9
    BW = [Buf() for _ in range(NW)]
    Bpp = [Buf(), Buf()]; Bpn = [Buf(), Buf()]; Bptr = [Buf(), Buf()]
    Bpre = [Buf(), Buf()]; Bcv = [Buf(), Buf()]; Bs32 = [Buf(), Buf()]; Bsq2 = [Buf(), Buf()]
    Bcvh = [[Buf(), Buf()], [Buf(), Buf()]]
    Br2 = [Buf(), Buf()]; Bob = [Buf(), Buf()]; Btk = [Buf(), Buf()]
    Bscr = Buf()

    f.dma(f.sp, wn[:], wn_d.rearrange("(c p) -> p c", p=128), writes=[Bc], allow_slow_non_contiguous=True)
    for j in range(4):
        f.dma(f.sp, cw[:, :, j], convw_d[j, :].rearrange("(c p) -> p c", p=128), writes=[Bc], allow_slow_non_contiguous=True)
    f.op(f.dve, lambda: nc.vector.memset(eps128[:], EPS * 128.0), writes=[Bc])
    lnscl = sc.sb(uname("lnscl"), [128, 1], F32)
    zero1 = sc.sb(uname("zero1"), [128, 1], F32)
    f.op(f.dve, lambda: nc.vector.memset(lnscl[:], -2.4260151319598084), writes=[Bc])
    f.op(f.dve, lambda: nc.vector.memset(zero1[:], 0.0), writes=[Bc])
    f.dma(f.sp, dtb[:], dtb_d.partition_broadcast(128), writes=[Bc])
    f.dma(f.sp, negA[:], alog_d.partition_broadcast(128), writes=[Bc])
    f.op(f.act, lambda: nc.scalar.activation(out=negA[:], in_=negA[:], func=AF.Exp), reads=[Bc], writes=[Bc])
    f.op(f.dve, lambda: nc.vector.tensor_scalar(negA[:], negA[:], -1.0, None, ALU.mult), reads=[Bc], writes=[Bc])
    winv = win_d.rearrange("(kc p) f -> p kc f", p=128)
    for i in range(NW):
        b0, b1 = i * 512, min(ODIN, (i + 1) * 512)
        f.dma(f.pool, Win[:, :, b0:b1], winv[:, :, b0:b1], writes=[BW[i]])

    Xv = X.rearrange("(c p) t -> p c t", p=128)
    qTv, kTv, gTv = scr["qT"], scr["kT"], scr["gT"]
    for t in range(NT // TT):
        cs = slice(t * TT, (t + 1) * TT)
        seq_start = (t * TT) % L == 0
        f.dma(f.sp, xt[:], Xv[:, :, cs], writes=[Bx])
        rms_stats(f, sc, xt[:], sq[:], Bx, Bsq, C.ones_bf[:], C.eps[:], pss[:], Bpss, rstd[:], Brstd, TT, 1.0 / D)
        for c in range(KC):
            f.op(f.dve, lambda c=c: nc.vector.scalar_tensor_tensor(out=hT[:, c, :], in0=xt[:, c, :], scalar=wn[:, c:c + 1],
                                                                 in1=rstd[:], op0=ALU.mult, op1=ALU.mult),
                 reads=[Bx, Brstd, Bc], writes=[Bh])
        if seq_start:
            f.op(f.dve, lambda: nc.vector.memset(halo[:], 0.0), writes=[Bhalo])
        HT = TT // 2

        def stA(oc):
            b = oc % 2
            wi = (oc * 128) // 512
            for kc in range(KC):
                f.op(f.pe, lambda kc=kc: nc.tensor.matmul(pp[b][:], Win[:, kc, oc * 128:(oc + 1) * 128], hT[:, kc, :],
                                                          start=(kc == 0), stop=(kc == KC - 1)),
                     reads=[BW[wi], Bh], writes=[Bpp[b]])
            f.op(f.act, lambda: nc.scalar.copy(out=pre[b][:, 3:TT + 3], in_=pp[b][:]), reads=[Bpp[b]], writes=[Bpre[b]])

        def stB(oc):
            b = oc % 2
            f.op(f.dve, lambda: nc.vector.tensor_copy(out=pre[b][:, 0:3], in_=halo[:, oc, :]), reads=[Bhalo], writes=[Bpre[b]])
            f.op(f.dve, lambda: nc.vector.tensor_copy(out=halo[:, oc, :], in_=pre[b][:, TT:TT + 3]), reads=[Bpre[b]], writes=[Bhalo])
            for j in range(4):
                for hf in range(2):
                    c0 = hf * HT
                    if j == 0:
                        f.op(f.dve, lambda c0=c0: nc.vector.tensor_scalar(cv[b][:, c0:c0 + HT], pre[b][:, c0:c0 + HT], cw[:, oc, 0:1], None, ALU.mult),
                             reads=[Bpre[b], Bc], writes=[Bcvh[b][hf]])
                    else:
                        f.op(f.dve, lambda c0=c0, j=j: nc.vector.scalar_tensor_tensor(out=cv[b][:, c0:c0 + HT], in0=pre[b][:, c0 + j:c0 + HT + j],
                                                                                      scalar=cw[:, oc, j:j + 1], in1=cv[b][:, c0:c0 + HT],
                                                                                      op0=ALU.mult, op1=ALU.add),
                             reads=[Bpre[b], Bc], writes=[Bcvh[b][hf]])
            if oc < 16:
                f.op(f.act, lambda: nc.scalar.activation(out=s32[b][:], in_=cv[b][:], func=AF.Silu), reads=Bcvh[b], writes=[Bs32[b]])
                f.op(f.act, lambda: nc.scalar.activation(out=sq2[b][:], in_=s32[b][:], func=AF.Square), reads=[Bs32[b]], writes=[Bsq2[b]])
            else:
                f.op(f.act, lambda: nc.scalar.activation(out=ob[b][:], in_=cv[b][:], func=AF.Silu), reads=Bcvh[b], writes=[Bob[b]])

        def stC1(oc):
            b = oc % 2
            if oc < 16:
                f.op(f.pe, lambda: nc.tensor.matmul(pn[b][:], C.ones_bf[:], sq2[b][:], start=True, stop=True), reads=[Bsq2[b], C.B], writes=[Bpn[b]])
                f.op(f.act, lambda: nc.scalar.activation(out=r2[b][:], in_=pn[b][:], func=AF.Ln, bias=C.eps[:], scale=1.0),
                     reads=[Bpn[b], C.B], writes=[Br2[b]])
                f.op(f.act, lambda: nc.scalar.activation(out=r2[b][:], in_=r2[b][:], func=AF.Exp, bias=(lnscl[:] if oc < 8 else zero1[:]), scale=-0.5),
                     reads=[Br2[b], Bc], writes=[Br2[b]])

        def stC2(oc):
            b = oc % 2
            hc = oc % 8
            if oc < 16:
                f.op(f.pool, lambda: nc.gpsimd.tensor_tensor(out=ob[b][:], in0=s32[b][:], in1=r2[b][:], op=ALU.mult),
                     reads=[Bs32[b], Br2[b]], writes=[Bob[b]])
                dstT = qTv if oc < 8 else kTv
                f.dma(f.sp, dstT[hc * 128:(hc + 1) * 128, cs], ob[b][:], reads=[Bob[b]], writes=[Bscr])
            if oc >= 8:
                for s_ in range(4):
                    f.op(f.pe, lambda s_=s_: nc.tensor.transpose(ptr[b][:, s_, :], ob[b][:, s_ * 128:(s_ + 1) * 128], C.ident_bf[:]),
                         reads=[Bob[b], C.B], writes=[Bptr[b]])
                f.op(f.act, lambda: nc.scalar.copy(out=tk[b][:], in_=ptr[b][:]), reads=[Bptr[b]], writes=[Btk[b]])
                dsttok = scr["ktok"] if oc < 16 else scr["vtok"]
                f.dma(f.sp, dsttok[cs, hc * 128:(hc + 1) * 128].rearrange("(s p) d -> p s d", p=128), tk[b][:], reads=[Btk[b]], writes=[Bscr])

        NQ = 24
        stA(0); stA(1); stB(0)
        for oc in range(NQ):
            if oc + 2 < NQ:
                stA(oc + 2)
            stC1(oc)
            if oc + 1 < NQ:
                stB(oc + 1)
            stC2(oc)
        for oc in range(24, 32):
            b = oc % 2
            wi = (oc * 128) // 512
            for kc in range(KC):
                f.op(f.pe, lambda kc=kc, oc=oc, b=b: nc.tensor.matmul(pp[b][:], Win[:, kc, oc * 128:(oc + 1) * 128], hT[:, kc, :],
                                                                      start=(kc == 0), stop=(kc == KC - 1)),
                     reads=[BW[wi], Bh], writes=[Bpp[b]])
            f.op(f.act, lambda b=b: nc.scalar.activation(out=ob[b][:], in_=pp[b][:], func=AF.Silu), reads=[Bpp[b]], writes=[Bob[b]])
            hc = oc - 24
            f.dma(f.sp, gTv[hc * 128:(hc + 1) * 128, cs], ob[b][:], reads=[Bob[b]], writes=[Bscr])
        for s in range(4):
            for kc in range(KC):
                f.op(f.pe, lambda kc=kc, s=s: nc.tensor.matmul(pb[:, s, :], hT[:, kc, s * 128:(s + 1) * 128], Win[:, kc, 4096:4112],
                                                               start=(kc == 0), stop=(kc == KC - 1)),
                     reads=[BW[8], Bh], writes=[Bpb])
        f.op(f.act, lambda: nc.scalar.activation(out=blt[:, :, 0:8], in_=pb[:, :, 0:8], func=AF.Sigmoid), reads=[Bpb], writes=[Bblt])
        f.op(f.dve, lambda: nc.vector.tensor_tensor(out=tmpb[:], in0=pb[:, :, 8:16], in1=dtb[:].unsqueeze(1).to_broadcast([128, 4, 8]), op=ALU.add),
             reads=[Bpb, Bc], writes=[Btmpb])
        f.op(f.act, lambda: nc.scalar.activation(out=tmpb[:], in_=tmpb[:], func=AF.Exp), reads=[Btmpb], writes=[Btmpb])
        f.op(f.act, lambda: nc.scalar.activation(out=tmpb[:], in_=tmpb[:], func=AF.Ln, bias=C.one[:], scale=1.0), reads=[Btmpb, C.B], writes=[Btmpb])
        f.op(f.dve, lambda: nc.vector.tensor_tensor(out=blt[:, :, 8:16], in0=tmpb[:], in1=negA[:].unsqueeze(1).to_broadcast([128, 4, 8]), op=ALU.mult),
             reads=[Btmpb, Bc], writes=[Bblt])
        f.dma(f.sp, scr["bl"][cs, :].rearrange("(s p) c -> p s c", p=128), blt[:], reads=[Bblt], writes=[Bscr])
    barrier(f)
    sc.close()


def gdn_core_phase(f, X, gnw_d, wout_d, scr, NT, L):
    nc = f.nc
    sc = Scope(nc)
    C = Consts(f, sc)
    H = GH
    mk = lambda n, shp, dt=F32: sc.sb(uname(n), shp, dt)
    gnw = mk("gnw", [128, 1])
    qTb = mk("qTb", [128, H, 128], BF16)
    kTb = mk("kTb", [128, H, 128], BF16)
    ktokb = mk("ktokb", [128, H, 128], BF16)
    vtokb = mk("vtokb", [128, H, 128], BF16)
    gTb = mk("gTb", [128, H, 128], BF16)
    bl = mk("bl", [128, 16])
    Xs = mk("Xs", [128, H, 128])
    sm = mk("sm", [128, 32])
    sme = mk("sme", [128, 40])
    nbeta = mk("nbeta", [128, 8])
    tmp1 = mk("tmp1", [128, H, 128])
    tmp2 = mk("tmp2", [128, H, 128])
    LmT = mk("LmT", [128, H, 128])
    LmS = mk("LmS", [128, H, 128])
    WTN = mk("WTN", [128, H, 128])
    E = mk("E", [128, H, 128])
    Pm = [mk("Pm", [128, H, 128], BF16)]
    PTm = [mk("PTm", [128, H, 128], BF16)]
    TTm = [mk("TTm", [128, H, 128]) for _ in range(2)]
    Pb = [mk("Pb", [128, H, 128], BF16) for _ in range(2)]
    PTb = [mk("PTb", [128, H, 128], BF16) for _ in range(2)]
    TTbb = [mk("TTbb", [128, H, 128], BF16) for _ in range(2)]
    attnT = mk("attnT", [128, H, 128], BF16)
    qgT = mk("qgT", [128, H, 128], BF16)
    ktil = mk("ktil", [128, H, 128], BF16)
    vb = mk("vb", [128, H, 128])
    R = mk("R", [128, H, 128], BF16)
    vnew = mk("vnew", [128, H, 128], BF16)
    S = mk("S", [128, H, 128])
    Sb = mk("Sb", [128, H, 128], BF16)
    sqo = mk("sqo", [128, H, 128], BF16)
    rs = mk("rs", [128, H, 128])
    of32 = mk("of32", [128, H, 128])
    ofb = mk("ofb", [128, H, 128], BF16)
    xt = mk("xtb", [128, KC, 128])
    PA = sc.ps(uname("PA"), [128, H, 128])
    PB = sc.ps(uname("PB"), [128, H, 128])
    PC = sc.ps(uname("PC"), [128, H, 128])
    PD = sc.ps(uname("PD"), [128, H, 128])
    shared = "c W bl sm sme nb xt scr qin kin ktin vtin gin"
    grouped = "Xs t1 t2 LmT LmS WTN E attnT qgT ktil vb R vnew S Sb sqo rs of32 ofb PA PB PC PD P0 PT0 TT0 TT1 Pb0 Pb1 PTb0 PTb1 TTb0 TTb1"
    Bf = {n: Buf(n) for n in shared.split()}
    for n in grouped.split():
        for gi in range(2):
            Bf[n + "#%d" % gi] = Buf(n)
    g = lambda *ns: [Bf[n] for n in ns]

    def gg(gi, *ns):
        return [Bf[n + "#%d" % gi] for n in ns]
    gall = lambda *ns: [Bf[n + "#%d" % gi] for n in ns for gi in range(2)]
    Bc = Bf["c"]

    f.dma(f.sp, gnw[:], gnw_d.rearrange("(p o) -> p o", o=1), writes=[Bc])
    woutv = wout_d.rearrange("(h p) d -> p h d", p=128)
    Xv = X.rearrange("(c p) t -> p c t", p=128)
    V, A, P_, G_ = f.dve, f.act, f.pe, f.pool
    HS = [slice(0, 4), slice(4, 8)]

    nblk = NT // 128
    for blk in range(nblk):
        t0 = blk * 128
        ts = slice(t0, t0 + 128)
        if t0 % L == 0:
            for gi in range(2):
                f.op(G_, lambda gi=gi: nc.gpsimd.memset(S[:, HS[gi], :], 0.0), writes=gg(gi, "S"))
                f.op(G_, lambda gi=gi: nc.gpsimd.memset(Sb[:, HS[gi], :], 0.0), writes=gg(gi, "Sb"))
        f.dma(f.sp, qTb[:], scr["qT"][:, ts].rearrange("(h p) t -> p h t", p=128), writes=g("qin"))
        f.dma(f.sp, kTb[:], scr["kT"][:, ts].rearrange("(h p) t -> p h t", p=128), writes=g("kin"))
        f.dma(f.sp, gTb[:], scr["gT"][:, ts].rearrange("(h p) t -> p h t", p=128), writes=g("gin"))
        f.dma(f.sp, ktokb[:], scr["ktok"][ts, :].rearrange("p (h d) -> p h d", d=128), writes=g("ktin"))
        f.dma(f.sp, vtokb[:], scr["vtok"][ts, :].rearrange("p (h d) -> p h d", d=128), writes=g("vtin"))
        f.dma(f.sp, bl[:], scr["bl"][ts, :], writes=g("bl"))
        beta = bl[:, 0:8]
        la = bl[:, 8:16]
        for gi in range(2):
            hs = HS[gi]
            f.op(V, lambda hs=hs: nc.vector.tensor_tensor(out=Xs[:, hs, :], in0=bc_i(la[:, hs]), in1=bc_h(C.tri[:], 4), op=ALU.mult),
                 reads=g("bl") + [C.B], writes=gg(gi, "Xs"))
            f.op(P_, lambda hs=hs: nc.tensor.matmul(PA[:, hs, :], C.ones32[:], Xs[:, hs, :], start=True, stop=True),
                 reads=gg(gi, "Xs") + [C.B], writes=gg(gi, "PA"))
        f.op(P_, lambda: nc.tensor.matmul(PD[:, 0, 0:8], C.tri[:], la, start=True, stop=True), reads=g("bl") + [C.B], writes=gg(0, "PD"))
        f.op(P_, lambda: nc.tensor.matmul(PD[:, 0, 8:16], C.bd[:], la, start=True, stop=True), reads=g("bl") + [C.B], writes=gg(0, "PD"))
        f.op(P_, lambda: nc.tensor.matmul(PD[:, 0, 16:24], C.cind[:, 0, :], la, start=True, stop=True), reads=g("bl") + [C.B], writes=gg(0, "PD"))
        f.op(P_, lambda: nc.tensor.matmul(PD[:, 0, 24:32], C.cind[:, 1, :], la, start=True, stop=True), reads=g("bl") + [C.B], writes=gg(0, "PD"))
        f.op(V, lambda: nc.vector.tensor_copy(out=sm[:], in_=PD[:, 0, 0:32]), reads=gg(0, "PD"), writes=g("sm"))
        gcol = sm[:, 0:8]
        f.op(A, lambda: nc.scalar.activation(out=sme[:, 0:8], in_=sm[:, 0:8], func=AF.Exp), reads=g("sm"), writes=g("sme"))
        f.op(V, lambda: nc.vector.tensor_tensor(out=sme[:, 8:16], in0=sm[:, 8:16], in1=sm[:, 0:8], op=ALU.subtract), reads=g("sm"), writes=g("sme"))
        f.op(A, lambda: nc.scalar.activation(out=sme[:, 8:16], in_=sme[:, 8:16], func=AF.Exp), reads=g("sme"), writes=g("sme"))
        f.op(A, lambda: nc.scalar.activation(out=sme[:, 16:32], in_=sm[:, 16:32], func=AF.Exp), reads=g("sm"), writes=g("sme"))
        f.op(V, lambda: nc.vector.scalar_tensor_tensor(out=sme[:, 32:40], in0=sme[:, 0:8], scalar=-1.0, in1=beta, op0=ALU.mult, op1=ALU.mult),
             reads=g("sme", "bl"), writes=g("sme"))
        f.op(V, lambda: nc.vector.tensor_scalar(nbeta[:], beta, -1.0, None, ALU.mult), reads=g("bl"), writes=g("nb"))
        for gi in range(2):
            hs = HS[gi]
            f.op(V, lambda hs=hs: nc.vector.tensor_tensor(out=tmp1[:, hs, :], in0=PA[:, hs, :], in1=bc_i(gcol[:, hs]), op=ALU.subtract),
                 reads=gg(gi, "PA") + g("sm"), writes=gg(gi, "t1"))
            f.op(V, lambda hs=hs: nc.vector.scalar_tensor_tensor(out=tmp2[:, hs, :], in0=tmp1[:, hs, :], scalar=-1.0, in1=bc_h(C.negstr[:], 4),
                                                                 op0=ALU.mult, op1=ALU.add),
                 reads=gg(gi, "t1") + [C.B], writes=gg(gi, "t2"))
            f.op(V, lambda hs=hs: nc.vector.tensor_tensor(out=tmp1[:, hs, :], in0=tmp1[:, hs, :], in1=bc_h(C.negincT[:], 4), op=ALU.add),
                 reads=gg(gi, "t2") + [C.B], writes=gg(gi, "t1"))
            f.op(A, lambda hs=hs: nc.scalar.activation(out=LmT[:, hs, :], in_=tmp1[:, hs, :], func=AF.Exp), reads=gg(gi, "t1"), writes=gg(gi, "LmT"))
            f.op(A, lambda hs=hs: nc.scalar.activation(out=LmS[:, hs, :], in_=tmp2[:, hs, :], func=AF.Exp), reads=gg(gi, "t2"), writes=gg(gi, "LmS"))
            f.op(A, lambda hs=hs: nc.scalar.activation(out=E[:, hs, :], in_=PA[:, hs, :], func=AF.Exp), reads=gg(gi, "PA"), writes=gg(gi, "E"))
            f.op(V, lambda hs=hs: nc.vector.tensor_tensor(out=LmS[:, hs, :], in0=LmS[:, hs, :], in1=bc_i(nbeta[:, hs]), op=ALU.mult),
                 reads=gg(gi, "LmS") + g("nb"), writes=gg(gi, "LmS"))
            f.op(V, lambda hs=hs: nc.vector.tensor_tensor(out=Xs[:, hs, :], in0=bc_i(beta[:, hs]), in1=bc_h(C.ident[:], 4), op=ALU.mult),
                 reads=g("bl") + [C.B], writes=gg(gi, "Xs"))
            f.op(P_, lambda hs=hs: nc.tensor.matmul(PB[:, hs, :], C.ones32[:], Xs[:, hs, :], start=True, stop=True),
                 reads=gg(gi, "Xs") + [C.B], writes=gg(gi, "PB"))
            f.op(V, lambda hs=hs: nc.vector.tensor_tensor(out=WTN[:, hs, :], in0=LmT[:, hs, :], in1=bc_h(C.strT01[:], 4), op=ALU.mult),
                 reads=gg(gi, "LmT") + [C.B], writes=gg(gi, "WTN"))
            f.op(V, lambda hs=hs: nc.vector.scalar_tensor_tensor(out=WTN[:, hs, :], in0=PB[:, hs, :], scalar=-1.0, in1=WTN[:, hs, :],
                                                                 op0=ALU.mult, op1=ALU.mult),
                 reads=gg(gi, "PB"), writes=gg(gi, "WTN"))
        for gi in range(2):
            hs = HS[gi]
            for h in range(4 * gi, 4 * gi + 4):
                f.op(P_, lambda h=h: nc.tensor.matmul(PC[:, h, :], kTb[:, h, :], kTb[:, h, :], start=True, stop=True), reads=g("kin"), writes=gg(gi, "PC"))
            f.op(V, lambda hs=hs: nc.vector.tensor_tensor(out=Pm[0][:, hs, :], in0=PC[:, hs, :], in1=LmS[:, hs, :], op=ALU.mult),
                 reads=gg(gi, "PC", "LmS"), writes=gg(gi, "P0"))
            f.op(V, lambda hs=hs: nc.vector.tensor_tensor(out=PTm[0][:, hs, :], in0=PC[:, hs, :], in1=WTN[:, hs, :], op=ALU.mult),
                 reads=gg(gi, "PC", "WTN"), writes=gg(gi, "PT0"))
            for h in range(4 * gi, 4 * gi + 4):
                f.op(P_, lambda h=h: nc.tensor.matmul(PB[:, h, :], kTb[:, h, :], qTb[:, h, :], start=True, stop=True), reads=g("kin", "qin"), writes=gg(gi, "PB"))
            f.op(V, lambda hs=hs: nc.vector.tensor_tensor(out=attnT[:, hs, :], in0=PB[:, hs, :], in1=LmT[:, hs, :], op=ALU.mult),
                 reads=gg(gi, "PB", "LmT"), writes=gg(gi, "attnT"))
            f.op(V, lambda hs=hs: nc.vector.tensor_tensor(out=TTm[0][:, hs, :], in0=PTm[0][:, hs, :], in1=bc_h(C.ident[:], 4), op=ALU.add),
                 reads=gg(gi, "PT0") + [C.B], writes=gg(gi, "TT0"))
        for gi in range(2):
            hs = HS[gi]
            f.op(V, lambda hs=hs: nc.vector.tensor_copy(out=TTbb[0][:, hs, :], in_=TTm[0][:, hs, :]), reads=gg(gi, "TT0"), writes=gg(gi, "TTb0"))
        for k in range(1, 6):
            cur, nxt = (k - 1) % 2, k % 2
            for gi in range(2):
                hs = HS[gi]
                for h in range(4 * gi, 4 * gi + 4):
                    if k == 1:
                        f.op(P_, lambda h=h: nc.tensor.matmul(PC[:, h, :], PTm[0][:, h, :], Pm[0][:, h, :], start=True, stop=True),
                             reads=gg(gi, "P0", "PT0"), writes=gg(gi, "PC"))
                    else:
                        f.op(P_, lambda h=h: nc.tensor.matmul(PC[:, h, :], PTb[cur][:, h, :], Pb[cur][:, h, :], start=True, stop=True),
                             reads=gg(gi, "Pb%d" % cur, "PTb%d" % cur), writes=gg(gi, "PC"))
                f.op(A, lambda hs=hs: nc.scalar.copy(out=Pb[nxt][:, hs, :], in_=PC[:, hs, :]), reads=gg(gi, "PC"), writes=gg(gi, "Pb%d" % nxt))
                if k < 5:
                    for h in range(4 * gi, 4 * gi + 4):
                        if k == 1:
                            f.op(P_, lambda h=h: nc.tensor.matmul(PB[:, h, :], Pm[0][:, h, :], PTm[0][:, h, :], start=True, stop=True),
                                 reads=gg(gi, "P0", "PT0"), writes=gg(gi, "PB"))
                        else:
                            f.op(P_, lambda h=h: nc.tensor.matmul(PB[:, h, :], Pb[cur][:, h, :], PTb[cur][:, h, :], start=True, stop=True),
                                 reads=gg(gi, "Pb%d" % cur, "PTb%d" % cur), writes=gg(gi, "PB"))
                    f.op(A, lambda hs=hs: nc.scalar.copy(out=PTb[nxt][:, hs, :], in_=PB[:, hs, :]), reads=gg(gi, "PB"), writes=gg(gi, "PTb%d" % nxt))
            for gi in range(2):
                hs = HS[gi]
                for h in range(4 * gi, 4 * gi + 4):
                    f.op(P_, lambda h=h: nc.tensor.matmul(PA[:, h, :], Pb[nxt][:, h, :], TTbb[cur][:, h, :], start=True, stop=True),
                         reads=gg(gi, "Pb%d" % nxt, "TTb%d" % cur), writes=gg(gi, "PA"))
                f.op(V, lambda hs=hs: nc.vector.tensor_tensor(out=TTbb[nxt][:, hs, :], in0=PA[:, hs, :], in1=TTm[cur][:, hs, :], op=ALU.add),
                     reads=gg(gi, "PA", "TT%d" % cur), writes=gg(gi, "TTb%d" % nxt))
                if k < 5:
                    f.op(V, lambda hs=hs: nc.vector.tensor_tensor(out=TTm[nxt][:, hs, :], in0=PA[:, hs, :], in1=TTm[cur][:, hs, :], op=ALU.add),
                         reads=gg(gi, "PA", "TT%d" % cur), writes=gg(gi, "TT%d" % nxt))
        TTb = TTbb[1]
        for gi in range(2):
            hs = HS[gi]
            f.op(G_, lambda hs=hs: nc.gpsimd.tensor_tensor(out=qgT[:, hs, :], in0=qTb[:, hs, :], in1=E[:, hs, :], op=ALU.mult),
                 reads=g("qin") + gg(gi, "E"), writes=gg(gi, "qgT"))
            f.op(G_, lambda hs=hs: nc.gpsimd.tensor_tensor(out=ktil[:, hs, :], in0=ktokb[:, hs, :], in1=bc_i(sme[:, 8:16][:, hs]), op=ALU.mult),
                 reads=g("ktin", "sme"), writes=gg(gi, "ktil"))
            f.op(G_, lambda hs=hs: nc.gpsimd.tensor_tensor(out=vb[:, hs, :], in0=vtokb[:, hs, :], in1=bc_i(beta[:, hs]), op=ALU.mult),
                 reads=g("vtin", "bl"), writes=gg(gi, "vb"))
        for c in range(2):
            r = slice(64 * c, 64 * c + 64)
            for gi in range(2):
                hs = HS[gi]
                for h in range(4 * gi, 4 * gi + 4):
                    f.op(P_, lambda h=h: nc.tensor.matmul(PC[r, h, :], kTb[:, h, r], Sb[:, h, :], start=True, stop=True),
                         reads=g("kin") + gg(gi, "Sb"), writes=gg(gi, "PC"))
            for gi in range(2):
                for h in range(4 * gi, 4 * gi + 4):
                    f.op(V, lambda h=h: nc.vector.scalar_tensor_tensor(out=R[r, h, :], in0=PC[r, h, :], scalar=sme[r, 32 + h:33 + h], in1=vb[r, h, :],
                                                                       op0=ALU.mult, op1=ALU.add),
                         reads=gg(gi, "PC", "vb") + g("sme"), writes=gg(gi, "R"))
                for h in range(4 * gi, 4 * gi + 4):
                    f.op(P_, lambda h=h: nc.tensor.matmul(PB[r, h, :], TTb[r, h, r], R[r, h, :], start=True, stop=True),
                         reads=gg(gi, "TTb1", "R"), writes=gg(gi, "PB"))
                f.op(A, lambda gi=gi: nc.scalar.copy(out=vnew[r, HS[gi], :], in_=PB[r, HS[gi], :]), reads=gg(gi, "PB"), writes=gg(gi, "vnew"))
            for gi in range(2):
                for h in range(4 * gi, 4 * gi + 4):
                    f.op(P_, lambda h=h: nc.tensor.matmul(PD[:, h, r], Sb[:, h, :], qgT[:, h, r], start=True, stop=False),
                         reads=gg(gi, "Sb", "qgT"), writes=gg(gi, "PD"))
                    f.op(P_, lambda h=h: nc.tensor.matmul(PD[:, h, r], vnew[r, h, :], attnT[r, h, r], start=False, stop=True),
                         reads=gg(gi, "vnew", "attnT"), writes=gg(gi, "PD"))
                for h in range(4 * gi, 4 * gi + 4):
                    f.op(P_, lambda h=h: nc.tensor.matmul(PA[:, h, :], ktil[r, h, :], vnew[r, h, :], start=True, stop=True),
                         reads=gg(gi, "ktil", "vnew"), writes=gg(gi, "PA"))
            for gi in range(2):
                for h in range(4 * gi, 4 * gi + 4):
                    f.op(V, lambda h=h: nc.vector.scalar_tensor_tensor(out=S[:, h, :], in0=S[:, h, :], scalar=sme[:, 16 + 8 * c + h:17 + 8 * c + h],
                                                                       in1=PA[:, h, :], op0=ALU.mult, op1=ALU.add),
                         reads=gg(gi, "PA") + g("sme"), writes=gg(gi, "S"))
                f.op(A, lambda gi=gi: nc.scalar.copy(out=Sb[:, HS[gi], :], in_=S[:, HS[gi], :]), reads=gg(gi, "S"), writes=gg(gi, "Sb"))
        for gi in range(2):
            hs = HS[gi]
            f.op(A, lambda hs=hs: nc.scalar.activation(out=sqo[:, hs, :], in_=PD[:, hs, :], func=AF.Square), reads=gg(gi, "PD"), writes=gg(gi, "sqo"))
            f.op(P_, lambda hs=hs: nc.tensor.matmul(PC[:, hs, :], C.ones_bf[:], sqo[:, hs, :], start=True, stop=True),
                 reads=gg(gi, "sqo") + [C.B], writes=gg(gi, "PC"))
            f.op(A, lambda hs=hs: nc.scalar.activation(out=rs[:, hs, :], in_=PC[:, hs, :], func=AF.Sqrt, bias=C.eps[:], scale=1.0 / 128),
                 reads=gg(gi, "PC") + [C.B], writes=gg(gi, "rs"))
            f.op(V, lambda hs=hs: nc.vector.reciprocal(out=rs[:, hs, :], in_=rs[:, hs, :]), reads=gg(gi, "rs"), writes=gg(gi, "rs"))
            f.op(V, lambda hs=hs: nc.vector.scalar_tensor_tensor(out=of32[:, hs, :], in0=PD[:, hs, :], scalar=gnw[:, 0:1], in1=rs[:, hs, :],
                                                                 op0=ALU.mult, op1=ALU.mult),
                 reads=gg(gi, "PD", "rs") + [Bc], writes=gg(gi, "of32"))
            f.op(G_, lambda hs=hs: nc.gpsimd.tensor_tensor(out=ofb[:, hs, :], in0=of32[:, hs, :], in1=gTb[:, hs, :], op=ALU.mult),
                 reads=gg(gi, "of32") + g("gin"), writes=gg(gi, "ofb"))
        f.dma(f.sp, scr["yT"][:, ts].rearrange("(h p) t -> p h t", p=128), ofb[:], reads=gall("ofb"), writes=g("scr"))
    barrier(f)
    sc.close()


EVIN = 2560
HH = 4


def ev_proj_phase(f, X, wn_d, win_d, lbl_d, j, scr, NT, L):
    nc = f.nc
    sc = Scope(nc)
    C = Consts(f, sc)
    mk = lambda n, shp, dt=F32: sc.sb(uname(n), shp, dt)
    Win = mk("Win", [128, KC, EVIN], BF16)
    wn = mk("wn", [128, KC])
    xt = mk("xt", [128, KC, TT])
    hT = mk("hT", [128, KC, TT], BF16)
    sq = mk("sq", [128, KC, TT], BF16)
    rstd = mk("rstd", [128, TT])
    lg = mk("lg", [128, 2, 4])
    lb = mk("lb", [128, 4])
    oml = mk("oml", [128, 4])
    ob = [mk("ob", [128, TT], BF16) for _ in range(2)]
    fs = [mk("fs", [128, TT]) for _ in range(2)]
    lf = [mk("lf", [128, TT]) for _ in range(2)]
    vt = [mk("vt", [128, 512], BF16) for _ in range(2)]
    pp = [sc.ps(uname("pp"), [128, TT]) for _ in range(2)]
    pv = [sc.ps(uname("pv"), [128, 512]) for _ in range(2)]
    pss = sc.ps(uname("pss"), [128, TT])
    Bc, Bx, Bh, Bsq, Bpss, Brstd, Bscr = [Buf() for _ in range(7)]
    BW = [Buf() for _ in range(5)]
    Bpp = [Buf(), Buf()]; Bpv = [Buf(), Buf()]; Bob = [Buf(), Buf()]; Bfs = [Buf(), Buf()]; Blf = [Buf(), Buf()]; Bvt = [Buf(), Buf()]
    V, A, P_ = f.dve, f.act, f.pe

    f.dma(f.sp, wn[:], wn_d.rearrange("(c p) -> p c", p=128), writes=[Bc], allow_slow_non_contiguous=True)
    for l in range(2):
        f.dma(f.sp, lg[:, l, :], lbl_d[l, :].rearrange("(c p) -> p c", p=128), writes=[Bc], allow_slow_non_contiguous=True)
    if j == 0:
        f.op(V, lambda: nc.vector.memset(lb[:], 0.0), writes=[Bc])
    else:
        f.op(V, lambda: nc.vector.tensor_tensor(out=lb[:], in0=lg[:, 1, :], in1=lg[:, 0, :], op=ALU.subtract), reads=[Bc], writes=[Bc])
        f.op(A, lambda: nc.scalar.activation(out=lb[:], in_=lb[:], func=AF.Sigmoid), reads=[Bc], writes=[Bc])
    f.op(V, lambda: nc.vector.tensor_scalar(oml[:], lb[:], -1.0, 1.0, ALU.mult, ALU.add), reads=[Bc], writes=[Bc])
    winv = win_d.rearrange("(kc p) f -> p kc f", p=128)
    for i in range(5):
        f.dma(f.pool, Win[:, :, i * 512:(i + 1) * 512], winv[:, :, i * 512:(i + 1) * 512], writes=[BW[i]])
    Xv = X.rearrange("(c p) t -> p c t", p=128)
    for t in range(NT // TT):
        cs = slice(t * TT, (t + 1) * TT)
        f.dma(f.sp, xt[:], Xv[:, :, cs], writes=[Bx])
        rms_stats(f, sc, xt[:], sq[:], Bx, Bsq, C.ones_bf[:], C.eps[:], pss[:], Bpss, rstd[:], Brstd, TT, 1.0 / D)
        for c in range(KC):
            f.op(V, lambda c=c: nc.vector.scalar_tensor_tensor(out=hT[:, c, :], in0=xt[:, c, :], scalar=wn[:, c:c + 1],
                                                             in1=rstd[:], op0=ALU.mult, op1=ALU.mult),
                 reads=[Bx, Brstd, Bc], writes=[Bh])
        for oc in list(range(0, 8)) + list(range(12, 20)):
            b = oc % 2
            wi = oc // 4
            hc = oc % 4
            for kc in range(KC):
                f.op(P_, lambda kc=kc, oc=oc, b=b: nc.tensor.matmul(pp[b][:], Win[:, kc, oc * 128:(oc + 1) * 128], hT[:, kc, :],
                                                                    start=(kc == 0), stop=(kc == KC - 1)),
                     reads=[BW[wi], Bh], writes=[Bpp[b]])
            rows = slice(hc * 128, (hc + 1) * 128)
            if oc < 4 or 12 <= oc < 16:
                f.op(A, lambda b=b: nc.scalar.activation(out=ob[b][:], in_=pp[b][:], func=AF.Silu), reads=[Bpp[b]], writes=[Bob[b]])
                dst = scr["qT"] if oc < 4 else scr["gT"]
                f.dma(f.sp, dst[rows, cs], ob[b][:], reads=[Bob[b]], writes=[Bscr])
            elif oc >= 16:
                f.op(A, lambda b=b: nc.scalar.copy(out=ob[b][:], in_=pp[b][:]), reads=[Bpp[b]], writes=[Bob[b]])
                f.dma(f.sp, scr["uT"][rows, cs], ob[b][:], reads=[Bob[b]], writes=[Bscr])
            else:
                f.op(A, lambda b=b: nc.scalar.activation(out=fs[b][:], in_=pp[b][:], func=AF.Sigmoid), reads=[Bpp[b]], writes=[Bfs[b]])
                f.op(V, lambda b=b, hc=hc: nc.vector.tensor_scalar(fs[b][:], fs[b][:], oml[:, hc:hc + 1], lb[:, hc:hc + 1], ALU.mult, ALU.add),
                     reads=[Bc], writes=[Bfs[b]])
                f.op(V, lambda b=b: nc.vector.tensor_scalar(ob[b][:], fs[b][:], -1.0, 1.0, ALU.mult, ALU.add), reads=[Bfs[b]], writes=[Bob[b]])
                f.dma(f.sp, scr["kT"][rows, cs], ob[b][:], reads=[Bob[b]], writes=[Bscr])
                f.op(V, lambda b=b: nc.vector.tensor_scalar(lf[b][:], fs[b][:], 1e-6, None, ALU.max), reads=[Bfs[b]], writes=[Blf[b]])
                f.op(A, lambda b=b: nc.scalar.activation(out=lf[b][:], in_=lf[b][:], func=AF.Ln), reads=[Blf[b]], writes=[Blf[b]])
                f.dma(f.sp, scr["lfT"][rows, cs], lf[b][:], reads=[Blf[b]], writes=[Bscr])
        for s in range(4):
            b = s % 2
            for kc in range(KC):
                f.op(P_, lambda kc=kc, s=s, b=b: nc.tensor.matmul(pv[b][:], hT[:, kc, s * 128:(s + 1) * 128], Win[:, kc, 1024:1536],
                                                                 start=(kc == 0), stop=(kc == KC - 1)),
                     reads=[BW[2], Bh], writes=[Bpv[b]])
            f.op(A, lambda b=b: nc.scalar.copy(out=vt[b][:], in_=pv[b][:]), reads=[Bpv[b]], writes=[Bvt[b]])
            f.dma(f.sp, scr["vtok"][t * TT + s * 128:t * TT + (s + 1) * 128, 0:512], vt[b][:], reads=[Bvt[b]], writes=[Bscr])
    barrier(f)
    sc.close()


def hgrn_core_phase(f, hnw_d, scr, NT, L):
    nc = f.nc
    sc = Scope(nc)
    C = Consts(f, sc)
    mk = lambda n, shp, dt=F32: sc.sb(uname(n), shp, dt)
    NCH = L // 64
    NB = L // 128
    NSQ = NT // L
    hnw = mk("hnw", [128, 1])
    onesL = mk("onesL", [128, L])
    V, A, P_, G_ = f.dve, f.act, f.pe, f.pool
    Bcst = Buf()
    f.dma(f.sp, hnw[:], hnw_d.rearrange("(p o) -> p o", o=1), writes=[Bcst])
    f.op(V, lambda: nc.vector.memset(onesL[:], 1.0), writes=[Bcst])

    class SeqState:
        pass
    SS = []
    for q_ in range(NSQ):
        st = SeqState()
        st.qh = mk("qh", [128, L], BF16); st.kh = mk("kh", [128, L], BF16); st.gh = mk("gh", [128, L], BF16)
        st.lfh = mk("lfh", [128, L]); st.Bcs = mk("Bcs", [128, L]); st.dif = mk("dif", [128, L])
        st.ee = mk("ee", [128, L])
        st.qt = mk("qt", [128, L], BF16); st.kt = mk("kt", [128, L], BF16)
        st.bprev = mk("bprev", [128, NCH]); st.sca = mk("sca", [128, 3, NCH])
        st.vb = [mk("vblk", [128, 128], BF16) for _ in range(2)]
        st.ktok = [mk("ktokh", [128, 128], BF16) for _ in range(2)]
        st.attnT = [mk("attnTh", [128, 128], BF16) for _ in range(2)]
        st.S = mk("Sh", [128, 128]); st.St = mk("Sth", [128, 128], BF16); st.dSs = mk("dSs", [128, 128])
        st.sqo = mk("sqoh", [128, 128], BF16); st.rs = mk("rsh", [128, 128]); st.o32 = mk("o32h", [128, 128])
        st.yo = [mk("yoh", [128, 128], BF16) for _ in range(2)]
        st.osb = [mk("osbh", [128, 128]) for _ in range(2)]
        st.pat = sc.ps(uname("pat"), [128, 512])[:, 0:128]
        st.ptr = sc.ps(uname("ptrh"), [128, 1024], BF16)[:, 0:128]
        st.po = sc.ps(uname("po"), [128, 512])[:, 0:128]
        st.pds = sc.ps(uname("pds"), [128, 512])
        names = "q k g lf B dif ee qt kt bp sca S St dSs sqo rs o32 pds pn pat ptr po v0 v1 ktok0 ktok1 attnT0 attnT1 yo0 yo1 osb0 osb1 scr"
        st.Bf = {n: Buf(n) for n in names.split()}
        SS.append(st)

    def epilogue(h, blk, b, bs):
        sb_ = str(b)
        for q_, st in enumerate(SS):
            g = lambda *ns, st=st: [st.Bf[n] for n in ns]
            f.op(A, lambda st=st: nc.scalar.activation(out=st.sqo[:], in_=st.osb[b][:], func=AF.Square), reads=g("osb" + sb_), writes=g("sqo"))
            f.op(P_, lambda st=st: nc.tensor.matmul(st.pds[:, 128:256], C.ones_bf[:], st.sqo[:], start=True, stop=True), reads=g("sqo") + [C.B], writes=g("pds"))
            f.op(A, lambda st=st: nc.scalar.activation(out=st.rs[:], in_=st.pds[:, 128:256], func=AF.Sqrt, bias=C.eps[:], scale=1.0 / 128),
                 reads=g("pds") + [C.B], writes=g("rs"))
        for q_, st in enumerate(SS):
            g = lambda *ns, st=st: [st.Bf[n] for n in ns]
            s0 = q_ * L
            f.op(V, lambda st=st: nc.vector.reciprocal(out=st.rs[:], in_=st.rs[:]), reads=g("rs"), writes=g("rs"))
            f.op(V, lambda st=st: nc.vector.scalar_tensor_tensor(out=st.o32[:], in0=st.osb[b][:], scalar=hnw[:, 0:1], in1=st.rs[:], op0=ALU.mult, op1=ALU.mult),
                 reads=g("osb" + sb_, "rs") + [Bcst], writes=g("o32"))
            f.op(G_, lambda st=st: nc.gpsimd.tensor_tensor(out=st.yo[b][:], in0=st.o32[:], in1=st.gh[:, bs], op=ALU.mult), reads=g("o32", "g"), writes=g("yo" + sb_))
            f.dma(f.sp, scr["yT"][h * 128:(h + 1) * 128, s0 + blk * 128:s0 + (blk + 1) * 128], st.yo[b][:], reads=g("yo" + sb_), writes=g("scr"))

    pending = None
    for h in range(HH):
        rows = slice(h * 128, (h + 1) * 128)
        for q_, st in enumerate(SS):
            g = lambda *ns, st=st: [st.Bf[n] for n in ns]
            s0 = q_ * L
            f.dma(f.sp, st.qh[:], scr["qT"][rows, s0:s0 + L], writes=g("q"))
            f.dma(f.sp, st.kh[:], scr["kT"][rows, s0:s0 + L], writes=g("k"))
            f.dma(f.sp, st.gh[:], scr["gT"][rows, s0:s0 + L], writes=g("g"))
            f.dma(f.sp, st.lfh[:], scr["lfT"][rows, s0:s0 + L], writes=g("lf"))
            f.op(V, lambda st=st: nc.vector.tensor_tensor_scan(out=st.Bcs[:], data0=onesL[:], data1=st.lfh[:], initial=0.0, op0=ALU.mult, op1=ALU.add),
                 reads=g("lf") + [Bcst], writes=g("B"))
            B3 = st.Bcs[:].rearrange("p (c s) -> p c s", s=64)
            bmid = B3[:, :, 31]
            blast = B3[:, :, 63]
            f.op(G_, lambda st=st: nc.gpsimd.memset(st.bprev[:, 0:1], 0.0), writes=g("bp"))
            f.op(G_, lambda st=st, B3=B3: nc.gpsimd.tensor_copy(out=st.bprev[:, 1:NCH], in_=B3[:, 0:NCH - 1, 63]), reads=g("B"), writes=g("bp"))
            f.op(G_, lambda st=st, blast=blast: nc.gpsimd.tensor_tensor(out=st.sca[:, 0, :], in0=blast, in1=st.bprev[:], op=ALU.subtract), reads=g("B", "bp"), writes=g("sca"))
            f.op(G_, lambda st=st, blast=blast, bmid=bmid: nc.gpsimd.tensor_tensor(out=st.sca[:, 1, :], in0=blast, in1=bmid, op=ALU.subtract), reads=g("B"), writes=g("sca"))
            f.op(G_, lambda st=st, bmid=bmid: nc.gpsimd.tensor_tensor(out=st.sca[:, 2, :], in0=bmid, in1=st.bprev[:], op=ALU.subtract), reads=g("B", "bp"), writes=g("sca"))
            f.op(A, lambda st=st: nc.scalar.activation(out=st.sca[:], in_=st.sca[:], func=AF.Exp), reads=g("sca"), writes=g("sca"))
            f.op(G_, lambda st=st, B3=B3, bmid=bmid: nc.gpsimd.tensor_tensor(out=st.dif[:].rearrange("p (c s) -> p c s", s=64), in0=B3,
                                                                          in1=bmid.unsqueeze(2).to_broadcast([128, NCH, 64]), op=ALU.subtract),
                 reads=g("B"), writes=g("dif"))
            f.op(A, lambda st=st: nc.scalar.activation(out=st.ee[:], in_=st.dif[:], func=AF.Exp), reads=g("dif"), writes=g("ee"))
            f.op(V, lambda st=st: nc.vector.tensor_tensor(out=st.qt[:], in0=st.qh[:], in1=st.ee[:], op=ALU.mult), reads=g("q", "ee"), writes=g("qt"))
            f.op(A, lambda st=st: nc.scalar.activation(out=st.ee[:], in_=st.dif[:], func=AF.Exp, scale=-1.0), reads=g("dif", "qt"), writes=g("ee"))
            f.op(V, lambda st=st: nc.vector.tensor_tensor(out=st.kt[:], in0=st.kh[:], in1=st.ee[:], op=ALU.mult), reads=g("k", "ee"), writes=g("kt"))
            f.op(G_, lambda st=st: nc.gpsimd.memset(st.S[:], 0.0), writes=g("S"))
        for blk in range(NB):
            b = blk % 2
            bs = slice(blk * 128, (blk + 1) * 128)
            sb_ = str(b)
            for q_, st in enumerate(SS):
                g = lambda *ns, st=st: [st.Bf[n] for n in ns]
                s0 = q_ * L
                f.dma(f.sp, st.vb[b][:], scr["vtok"][s0 + blk * 128:s0 + (blk + 1) * 128, h * 128:(h + 1) * 128], writes=g("v" + sb_))
                f.op(P_, lambda st=st: nc.tensor.matmul(st.pat, st.kt[:, bs], st.qt[:, bs], start=True, stop=True), reads=g("kt", "qt"), writes=g("pat"))
                f.op(V, lambda st=st: nc.vector.tensor_tensor(out=st.attnT[b][:], in0=st.pat, in1=C.tri[:], op=ALU.mult),
                     reads=g("pat") + [C.B], writes=g("attnT" + sb_))
                f.op(P_, lambda st=st: nc.tensor.transpose(st.ptr, st.kt[:, bs], C.ident_bf[:]), reads=g("kt") + [C.B], writes=g("ptr"))
                f.op(A, lambda st=st: nc.scalar.copy(out=st.ktok[b][:], in_=st.ptr), reads=g("ptr"), writes=g("ktok" + sb_))
                f.op(P_, lambda st=st: nc.tensor.matmul(st.po, st.vb[b][:], st.attnT[b][:], start=True, stop=False),
                     reads=g("v" + sb_, "attnT" + sb_), writes=g("po"))
            for c in range(2):
                ci = blk * 2 + c
                r = slice(64 * c, 64 * c + 64)
                cols = slice(blk * 128 + 64 * c, blk * 128 + 64 * c + 64)
                for q_, st in enumerate(SS):
                    g = lambda *ns, st=st: [st.Bf[n] for n in ns]
                    f.op(V, lambda st=st: nc.vector.tensor_scalar(st.St[:], st.S[:], st.sca[:, 2, ci:ci + 1], None, ALU.mult), reads=g("S", "sca"), writes=g("St"))
                    f.op(P_, lambda st=st: nc.tensor.matmul(st.po[:, r], st.St[:], st.qt[:, cols], start=False, stop=(c == 1)),
                         reads=g("St", "qt"), writes=g("po"))
                    f.op(P_, lambda st=st: nc.tensor.matmul(st.pds[:, 0:128], st.ktok[b][r, :], st.vb[b][r, :], start=True, stop=True),
                         reads=g("ktok" + sb_, "v" + sb_), writes=g("pds"))
                for q_, st in enumerate(SS):
                    g = lambda *ns, st=st: [st.Bf[n] for n in ns]
                    f.op(A, lambda st=st: nc.scalar.activation(out=st.dSs[:], in_=st.pds[:, 0:128], func=AF.Identity, scale=st.sca[:, 1, ci:ci + 1]),
                         reads=g("pds", "sca"), writes=g("dSs"))
                for q_, st in enumerate(SS):
                    g = lambda *ns, st=st: [st.Bf[n] for n in ns]
                    f.op(V, lambda st=st: nc.vector.scalar_tensor_tensor(out=st.S[:], in0=st.S[:], scalar=st.sca[:, 0, ci:ci + 1], in1=st.dSs[:],
                                                                       op0=ALU.mult, op1=ALU.add), reads=g("dSs", "sca"), writes=g("S"))
            for q_, st in enumerate(SS):
                g = lambda *ns, st=st: [st.Bf[n] for n in ns]
                f.op(A, lambda st=st: nc.scalar.copy(out=st.osb[b][:], in_=st.po), reads=g("po"), writes=g("osb" + sb_))
            if pending is not None:
                epilogue(*pending)
            pending = (h, blk, b, bs)
        epilogue(*pending)
        pending = None
    barrier(f)
    sc.close()


PI = 3.14159265358979


def s5_phase(f, p, j, scr, NT, L):
    nc = f.nc
    sc = Scope(nc)
    C = Consts(f, sc)
    mk = lambda n, shp, dt=F32: sc.sb(uname(n), shp, dt)
    V, A, P_ = f.dve, f.act, f.pe
    NS = 16
    NCH = NT // 64
    CPS = L // 64
    NSEQ = NT // L
    Bp = Buf("prep")
    gp = [Bp]
    ar = mk("ar", [128, NS]); ai = mk("ai", [128, NS]); nai = mk("nai", [128, NS])
    pwr = mk("pwr", [128, NS, 64]); pwi = mk("pwi", [128, NS, 64]); npwi = mk("npwi", [128, NS, 64])
    a64r = mk("a64r", [128, NS]); a64i = mk("a64i", [128, NS]); na64i = mk("na64i", [128, NS])
    NSTEP = max(1, (CPS - 1).bit_length())
    apr = mk("apr", [128, NSTEP, NS]); api = mk("api", [128, NSTEP, NS]); napi = mk("napi", [128, NSTEP, NS])
    Btab = [mk("Btab", [128, NS, 128], BF16) for _ in range(2)]
    TCre = mk("TCre", [128, NS, 128], BF16); TCimn = mk("TCimn", [128, NS, 128], BF16)
    dv = mk("dvec", [128, 4])
    Wglu = mk("Wglu", [128, 4, 512], BF16)
    scp = Scope(nc)
    mkp = lambda n, shp, dt=F32: scp.sb(uname(n), shp, dt)
    are = mkp("are", [128, NS]); aim = mkp("aim", [128, NS]); dtl = mkp("dtl", [128, NS])
    mag = mkp("mag", [128, NS]); ang = mkp("ang", [128, NS]); ang2 = mkp("ang2", [128, NS]); kk = mkp("kk", [128, NS]); tmpa = mkp("tmpa", [128, NS])
    cre = mkp("cre", [128, NS]); cim = mkp("cim", [128, NS]); den = mkp("den", [128, NS]); zr = mkp("zr", [128, NS])
    a2r = mkp("a2r", [128, NS]); a2i = mkp("a2i", [128, NS]); t3a = mkp("t3a", [128, NS, 32])
    Braw = [mkp("Braw", [128, NS, 128]) for _ in range(2)]
    Craw = [mkp("Craw", [128, NS, 128]) for _ in range(2)]
    c1 = mkp("c1", [128, NS, 128]); c2 = mkp("c2", [128, NS, 128])
    f.dma(f.sp, are[:], p["a_re"].rearrange("(t g) n -> (g n) t", g=2), writes=gp, allow_slow_non_contiguous=True)
    f.dma(f.sp, aim[:], p["a_im"].rearrange("(t g) n -> (g n) t", g=2), writes=gp, allow_slow_non_contiguous=True)
    ldv = p["log_dt"].rearrange("(t g) -> g t", g=2)
    for g2 in range(2):
        f.dma(f.sp, dtl[g2 * 64:(g2 + 1) * 64, :], ldv[g2:g2 + 1, :].to_broadcast([64, NS]), writes=gp, allow_slow_non_contiguous=True)
    op = lambda eng, fn: f.op(eng, fn, reads=gp, writes=gp)
    op(A, lambda: nc.scalar.activation(out=dtl[:], in_=dtl[:], func=AF.Exp))
    op(V, lambda: nc.vector.tensor_tensor(out=mag[:], in0=dtl[:], in1=are[:], op=ALU.mult))
    op(A, lambda: nc.scalar.activation(out=mag[:], in_=mag[:], func=AF.Exp))
    op(V, lambda: nc.vector.tensor_tensor(out=ang[:], in0=dtl[:], in1=aim[:], op=ALU.mult))
    op(V, lambda: nc.vector.tensor_scalar(ang2[:], ang[:], PI / 2, None, ALU.add))
    for a_ in (ang, ang2):
        op(V, lambda: nc.vector.memset(kk[:], 0.0))
        for m in (1, 3, 5, 7, 9):
            op(V, lambda a_=a_, m=m: nc.vector.tensor_scalar(tmpa[:], a_[:], m * PI, None, ALU.is_gt))
            op(V, lambda: nc.vector.tensor_tensor(out=kk[:], in0=kk[:], in1=tmpa[:], op=ALU.add))
        op(V, lambda a_=a_: nc.vector.scalar_tensor_tensor(out=a_[:], in0=kk[:], scalar=-2 * PI, in1=a_[:], op0=ALU.mult, op1=ALU.add))
        op(V, lambda a_=a_: nc.vector.tensor_scalar(a_[:], a_[:], PI, -PI, ALU.min, ALU.max))
    op(A, lambda: nc.scalar.activation(out=ai[:], in_=ang[:], func=AF.Sin))
    op(A, lambda: nc.scalar.activation(out=ar[:], in_=ang2[:], func=AF.Sin))
    op(V, lambda: nc.vector.tensor_tensor(out=ai[:], in0=ai[:], in1=mag[:], op=ALU.mult))
    op(V, lambda: nc.vector.tensor_tensor(out=ar[:], in0=ar[:], in1=mag[:], op=ALU.mult))
    op(V, lambda: nc.vector.tensor_scalar(nai[:], ai[:], -1.0, None, ALU.mult))
    op(V, lambda: nc.vector.tensor_tensor(out=den[:], in0=are[:], in1=are[:], op=ALU.mult))
    op(V, lambda: nc.vector.tensor_tensor(out=tmpa[:], in0=aim[:], in1=aim[:], op=ALU.mult))
    op(V, lambda: nc.vector.tensor_tensor(out=den[:], in0=den[:], in1=tmpa[:], op=ALU.add))
    op(V, lambda: nc.vector.reciprocal(out=den[:], in_=den[:]))
    op(V, lambda: nc.vector.tensor_scalar(zr[:], ar[:], -1.0, None, ALU.add))
    op(V, lambda: nc.vector.tensor_tensor(out=cre[:], in0=zr[:], in1=are[:], op=ALU.mult))
    op(V, lambda: nc.vector.tensor_tensor(out=tmpa[:], in0=ai[:], in1=aim[:], op=ALU.mult))
    op(V, lambda: nc.vector.tensor_tensor(out=cre[:], in0=cre[:], in1=tmpa[:], op=ALU.add))
    op(V, lambda: nc.vector.tensor_tensor(out=cre[:], in0=cre[:], in1=den[:], op=ALU.mult))
    op(V, lambda: nc.vector.tensor_tensor(out=cim[:], in0=ai[:], in1=are[:], op=ALU.mult))
    op(V, lambda: nc.vector.tensor_tensor(out=tmpa[:], in0=zr[:], in1=aim[:], op=ALU.mult))
    op(V, lambda: nc.vector.tensor_tensor(out=cim[:], in0=cim[:], in1=tmpa[:], op=ALU.subtract))
    op(V, lambda: nc.vector.tensor_tensor(out=cim[:], in0=cim[:], in1=den[:], op=ALU.mult))
    op(V, lambda: nc.vector.tensor_copy(out=pwr[:, :, 0], in_=ar[:]))
    op(V, lambda: nc.vector.tensor_copy(out=pwi[:, :, 0], in_=ai[:]))
    op(V, lambda: nc.vector.tensor_copy(out=a2r[:], in_=ar[:]))
    op(V, lambda: nc.vector.tensor_copy(out=a2i[:], in_=ai[:]))
    n = 1
    while n < 64:
        br = bc_i(a2r[:], n); bi = bc_i(a2i[:], n)
        op(V, lambda n=n, br=br: nc.vector.tensor_tensor(out=pwr[:, :, n:2 * n], in0=pwr[:, :, 0:n], in1=br, op=ALU.mult))
        op(V, lambda n=n, bi=bi: nc.vector.tensor_tensor(out=t3a[:, :, 0:n], in0=pwi[:, :, 0:n], in1=bi, op=ALU.mult))
        op(V, lambda n=n: nc.vector.tensor_tensor(out=pwr[:, :, n:2 * n], in0=pwr[:, :, n:2 * n], in1=t3a[:, :, 0:n], op=ALU.subtract))
        op(V, lambda n=n, bi=bi: nc.vector.tensor_tensor(out=pwi[:, :, n:2 * n], in0=pwr[:, :, 0:n], in1=bi, op=ALU.mult))
        op(V, lambda n=n, br=br: nc.vector.tensor_tensor(out=t3a[:, :, 0:n], in0=pwi[:, :, 0:n], in1=br, op=ALU.mult))
        op(V, lambda n=n: nc.vector.tensor_tensor(out=pwi[:, :, n:2 * n], in0=pwi[:, :, n:2 * n], in1=t3a[:, :, 0:n], op=ALU.add))
        op(V, lambda: nc.vector.tensor_tensor(out=tmpa[:], in0=a2r[:], in1=a2i[:], op=ALU.mult))
        op(V, lambda: nc.vector.tensor_tensor(out=kk[:], in0=a2i[:], in1=a2i[:], op=ALU.mult))
        op(V, lambda: nc.vector.tensor_tensor(out=a2r[:], in0=a2r[:], in1=a2r[:], op=ALU.mult))
        op(V, lambda: nc.vector.tensor_tensor(out=a2r[:], in0=a2r[:], in1=kk[:], op=ALU.subtract))
        op(V, lambda: nc.vector.tensor_scalar(a2i[:], tmpa[:], 2.0, None, ALU.mult))
        n *= 2
    op(V, lambda: nc.vector.tensor_scalar(npwi[:], pwi[:], -1.0, None, ALU.mult))
    op(V, lambda: nc.vector.tensor_copy(out=a64r[:], in_=pwr[:, :, 63]))
    op(V, lambda: nc.vector.tensor_copy(out=a64i[:], in_=pwi[:, :, 63]))
    op(V, lambda: nc.vector.tensor_scalar(na64i[:], a64i[:], -1.0, None, ALU.mult))
    op(V, lambda: nc.vector.tensor_copy(out=apr[:, 0, :], in_=a64r[:]))
    op(V, lambda: nc.vector.tensor_copy(out=api[:, 0, :], in_=a64i[:]))
    for k_ in range(1, NSTEP):
        op(V, lambda k_=k_: nc.vector.tensor_tensor(out=tmpa[:], in0=apr[:, k_ - 1, :], in1=api[:, k_ - 1, :], op=ALU.mult))
        op(V, lambda k_=k_: nc.vector.tensor_tensor(out=kk[:], in0=api[:, k_ - 1, :], in1=api[:, k_ - 1, :], op=ALU.mult))
        op(V, lambda k_=k_: nc.vector.tensor_tensor(out=apr[:, k_, :], in0=apr[:, k_ - 1, :], in1=apr[:, k_ - 1, :], op=ALU.mult))
        op(V, lambda k_=k_: nc.vector.tensor_tensor(out=apr[:, k_, :], in0=apr[:, k_, :], in1=kk[:], op=ALU.subtract))
        op(V, lambda k_=k_: nc.vector.tensor_scalar(api[:, k_, :], tmpa[:], 2.0, None, ALU.mult))
    op(V, lambda: nc.vector.tensor_scalar(napi[:], api[:], -1.0, None, ALU.mult))
    for ri, key in enumerate(("b_re", "b_im")):
        op(V, lambda ri=ri: nc.vector.memset(Braw[ri][:], 0.0))
        for g_ in range(32):
            st_, p0, g2 = g_ // 2, (g_ % 8) * 16, g_ % 2
            f.dma(f.sp, Braw[ri][p0:p0 + 16, st_, g2 * 64:(g2 + 1) * 64], p[key][g_].rearrange("n q -> q n"), reads=gp, writes=gp,
                  allow_slow_non_contiguous=True)
        op(V, lambda ri=ri: nc.vector.tensor_copy(out=Btab[ri][:], in_=Braw[ri][:]))
    for ri, key in enumerate(("c_re", "c_im")):
        op(V, lambda ri=ri: nc.vector.memset(Craw[ri][:], 0.0))
        cv_ = p[key].rearrange("(t g) q n -> g n t q", g=2)
        for g2 in range(2):
            for t_ in range(NS):
                c0 = 32 * (t_ % 4) + 16 * g2
                f.dma(f.sp, Craw[ri][g2 * 64:(g2 + 1) * 64, t_, c0:c0 + 16], cv_[g2, :, t_, :], reads=gp, writes=gp,
                      allow_slow_non_contiguous=True)
    op(V, lambda: nc.vector.tensor_tensor(out=c1[:], in0=Craw[0][:], in1=bc_i(cre[:], 128), op=ALU.mult))
    op(V, lambda: nc.vector.tensor_tensor(out=c2[:], in0=Craw[1][:], in1=bc_i(cim[:], 128), op=ALU.mult))
    op(V, lambda: nc.vector.tensor_tensor(out=TCre[:], in0=c1[:], in1=c2[:], op=ALU.subtract))
    op(V, lambda: nc.vector.tensor_tensor(out=c1[:], in0=Craw[0][:], in1=bc_i(cim[:], 128), op=ALU.mult))
    op(V, lambda: nc.vector.tensor_tensor(out=c2[:], in0=Craw[1][:], in1=bc_i(cre[:], 128), op=ALU.mult))
    op(V, lambda: nc.vector.tensor_tensor(out=c1[:], in0=c1[:], in1=c2[:], op=ALU.add))
    op(V, lambda: nc.vector.tensor_scalar(TCimn[:], c1[:], -1.0, None, ALU.mult))
    f.dma(f.sp, dv[:], p["d"].rearrange("(c p) -> p c", p=128), writes=gp, allow_slow_non_contiguous=True)
    f.dma(f.pool, Wglu[:], p["w_glu"].rearrange("(kc p) f -> p kc f", p=128), writes=gp)
    barrier(f)
    scp.close()

    uT = mk("uTkt", [128, NT], BF16)
    yg = mk("yg", [128, 4, NT], BF16)
    bu = [mk("bu", [128, 64, NCH]) for _ in range(2)]
    hb = [[mk("hb", [128, 64, NCH], BF16) for _ in range(2)] for _ in range(4)]
    Hs = [mk("Hs", [128, NSEQ, CPS + 1]) for _ in range(2)]
    Gs = [[mk("Gs", [128, NSEQ, CPS]) for _ in range(2)] for _ in range(2)]
    ysb = mk("ysb", [128, 512])
    x2 = mk("x2g", [128, 512]); zz = mk("zzg", [128, 512])
    pbu = [[sc.ps(uname("pbu"), [128, 512]) for _ in range(2)] for _ in range(2)]
    py = [sc.ps(uname("py"), [128, 512]) for _ in range(2)]
    pgl = [sc.ps(uname("pgl"), [128, 512]) for _ in range(2)]
    names = "u yg ysb x2 zz scr"
    Bf = {n: Buf(n) for n in names.split()}
    Bur = [Buf() for _ in range(64)]; Bui = [Buf() for _ in range(64)]
    BHs = [Buf(), Buf()]
    BG = [[Buf(), Buf()], [Buf(), Buf()]]
    for n in ["pbu0", "pbu1", "py", "pgl"]:
        Bf[n + "0"] = Buf(); Bf[n + "1"] = Buf()
    for sl in range(4):
        Bf["hbr%d" % sl] = Buf(); Bf["hbi%d" % sl] = Buf()
    g = lambda *ns: [Bf[n] for n in ns]
    bun = (Bur, Bui)
    for ot in range(4):
      f.dma(f.sp, uT[:], scr["uT"][ot * 128:(ot + 1) * 128, :], writes=g("u"))
      for sl in range(4):
        st = 4 * ot + sl
        arS, aiS, naiS = ar[:, st:st + 1], ai[:, st:st + 1], nai[:, st:st + 1]
        hbn = ("hbr%d" % sl, "hbi%d" % sl)
        for pc in range(NT // 512):
            b = pc % 2
            for ri in range(2):
                f.op(P_, lambda ri=ri, b=b, pc=pc: nc.tensor.matmul(pbu[ri][b][:], Btab[ri][:, st, :], uT[:, pc * 512:(pc + 1) * 512],
                                                                    start=True, stop=True),
                     reads=g("u") + gp, writes=g("pbu%d%d" % (ri, b)))
                f.op(A, lambda ri=ri, b=b, pc=pc: nc.scalar.copy(out=bu[ri][:, :, pc * 8:(pc + 1) * 8].rearrange("p s c -> p c s"),
                                                                in_=pbu[ri][b][:].rearrange("p (c s) -> p c s", s=64)),
                     reads=g("pbu%d%d" % (ri, b)), writes=bun[ri])
        for tau in range(1, 64):
            X1 = lambda tau=tau: f.op(V, lambda: nc.vector.scalar_tensor_tensor(out=bu[0][:, tau, :], in0=bu[1][:, tau - 1, :], scalar=naiS, in1=bu[0][:, tau, :],
                                                                   op0=ALU.mult, op1=ALU.add), reads=[Bui[tau - 1]] + gp, writes=[Bur[tau]])
            X2 = lambda tau=tau: f.op(V, lambda: nc.vector.scalar_tensor_tensor(out=bu[1][:, tau, :], in0=bu[0][:, tau - 1, :], scalar=aiS, in1=bu[1][:, tau, :],
                                                                   op0=ALU.mult, op1=ALU.add), reads=[Bur[tau - 1]] + gp, writes=[Bui[tau]])
            D1 = lambda tau=tau: f.op(V, lambda: nc.vector.scalar_tensor_tensor(out=bu[0][:, tau, :], in0=bu[0][:, tau - 1, :], scalar=arS, in1=bu[0][:, tau, :],
                                                                   op0=ALU.mult, op1=ALU.add), reads=[Bur[tau - 1]] + gp, writes=[Bur[tau]])
            D2 = lambda tau=tau: f.op(V, lambda: nc.vector.scalar_tensor_tensor(out=bu[1][:, tau, :], in0=bu[1][:, tau - 1, :], scalar=arS, in1=bu[1][:, tau, :],
                                                                   op0=ALU.mult, op1=ALU.add), reads=[Bui[tau - 1]] + gp, writes=[Bui[tau]])
            for o_ in ((X1, X2, D1, D2) if tau % 2 == 1 else (X2, X1, D2, D1)):
                o_()
        f.op(V, lambda: nc.vector.memset(Hs[0][:, :, 0:1], 0.0), writes=[BHs[0]])
        f.op(V, lambda: nc.vector.memset(Hs[1][:, :, 0:1], 0.0), writes=[BHs[1]])
        srcv = [bu[ri][:, 63, :].rearrange("p (b c) -> p b c", c=CPS) for ri in range(2)]
        srcB = [[Bur[63]], [Bui[63]]]
        for k_ in range(NSTEP):
            s_ = 1 << k_
            last = (k_ == NSTEP - 1)
            if last:
                dstv = [Hs[ri][:, :, 1:CPS + 1] for ri in range(2)]
                dstB = [[BHs[0]], [BHs[1]]]
            else:
                dstv = [Gs[k_ % 2][ri][:] for ri in range(2)]
                dstB = [[BG[k_ % 2][0]], [BG[k_ % 2][1]]]
            pr_, pi_, npi_ = apr[:, k_, st:st + 1], api[:, k_, st:st + 1], napi[:, k_, st:st + 1]
            hi_ = slice(s_, CPS); lo_ = slice(0, CPS - s_)
            f.op(V, lambda: nc.vector.scalar_tensor_tensor(out=dstv[0][:, :, hi_], in0=srcv[0][:, :, lo_], scalar=pr_, in1=srcv[0][:, :, hi_],
                                                           op0=ALU.mult, op1=ALU.add), reads=srcB[0] + gp, writes=dstB[0])
            f.op(V, lambda: nc.vector.scalar_tensor_tensor(out=dstv[1][:, :, hi_], in0=srcv[0][:, :, lo_], scalar=pi_, in1=srcv[1][:, :, hi_],
                                                           op0=ALU.mult, op1=ALU.add), reads=srcB[0] + srcB[1] + gp, writes=dstB[1])
            f.op(V, lambda: nc.vector.scalar_tensor_tensor(out=dstv[0][:, :, hi_], in0=srcv[1][:, :, lo_], scalar=npi_, in1=dstv[0][:, :, hi_],
                                                           op0=ALU.mult, op1=ALU.add), reads=srcB[1] + gp, writes=dstB[0])
            f.op(V, lambda: nc.vector.scalar_tensor_tensor(out=dstv[1][:, :, hi_], in0=srcv[1][:, :, lo_], scalar=pr_, in1=dstv[1][:, :, hi_],
                                                           op0=ALU.mult, op1=ALU.add), reads=srcB[1] + gp, writes=dstB[1])
            f.op(V, lambda: nc.vector.tensor_copy(out=dstv[0][:, :, 0:s_], in_=srcv[0][:, :, 0:s_]), reads=srcB[0], writes=dstB[0])
            f.op(V, lambda: nc.vector.tensor_copy(out=dstv[1][:, :, 0:s_], in_=srcv[1][:, :, 0:s_]), reads=srcB[1], writes=dstB[1])
            srcv, srcB = dstv, dstB
        Hr = Hs[0][:, :, 0:CPS]; Hi = Hs[1][:, :, 0:CPS]
        for tau in range(64):
            pr, pi_, npi = pwr[:, st, tau:tau + 1], pwi[:, st, tau:tau + 1], npwi[:, st, tau:tau + 1]
            br3 = bu[0][:, tau, :].rearrange("p (b c) -> p b c", c=CPS)
            bi3 = bu[1][:, tau, :].rearrange("p (b c) -> p b c", c=CPS)
            hr3 = hb[sl][0][:, tau, :].rearrange("p (b c) -> p b c", c=CPS)
            hi3 = hb[sl][1][:, tau, :].rearrange("p (b c) -> p b c", c=CPS)
            f.op(V, lambda: nc.vector.scalar_tensor_tensor(out=br3, in0=Hi, scalar=npi, in1=br3, op0=ALU.mult, op1=ALU.add), reads=[BHs[1]] + gp, writes=[Bur[tau]])
            f.op(V, lambda: nc.vector.scalar_tensor_tensor(out=bi3, in0=Hr, scalar=pi_, in1=bi3, op0=ALU.mult, op1=ALU.add), reads=[BHs[0]] + gp, writes=[Bui[tau]])
            f.op(V, lambda: nc.vector.scalar_tensor_tensor(out=hr3, in0=Hr, scalar=pr, in1=br3, op0=ALU.mult, op1=ALU.add), reads=[BHs[0]] + [Bur[tau]] + gp, writes=g(hbn[0]))
            f.op(V, lambda: nc.vector.scalar_tensor_tensor(out=hi3, in0=Hi, scalar=pr, in1=bi3, op0=ALU.mult, op1=ALU.add), reads=[BHs[1]] + [Bui[tau]] + gp, writes=g(hbn[1]))
      tpp = 512 // NCH
      for pc in range(64 * NCH // 512):
        b = pc % 2
        k = 0
        for sl in range(4):
            st = 4 * ot + sl
            for ri, TC in enumerate((TCre, TCimn)):
                hbf = hb[sl][ri][:].rearrange("p s c -> p (s c)")
                f.op(P_, lambda pc=pc, b=b, TC=TC, st=st, hbf=hbf, k=k: nc.tensor.matmul(py[b][:], TC[:, st, :], hbf[:, pc * 512:(pc + 1) * 512],
                                                                                      start=(k == 0), stop=(k == 7)),
                     reads=g("hbr%d" % sl, "hbi%d" % sl) + gp, writes=g("py%d" % b))
                k += 1
        uview = uT[:, :].rearrange("p (c s) -> p s c", s=64)[:, pc * tpp:(pc + 1) * tpp, :]
        ygview = yg[:, ot, :].rearrange("p (c s) -> p s c", s=64)[:, pc * tpp:(pc + 1) * tpp, :]
        y3 = ysb[:, :].rearrange("p (s c) -> p s c", c=NCH)
        z3 = zz[:, :].rearrange("p (s c) -> p s c", c=NCH)
        f.op(V, lambda uview=uview, y3=y3, b=b: nc.vector.scalar_tensor_tensor(out=y3, in0=uview, scalar=dv[:, ot:ot + 1],
                                                                              in1=py[b][:].rearrange("p (s c) -> p s c", c=NCH),
                                                                              op0=ALU.mult, op1=ALU.add),
             reads=g("py%d" % b, "u") + gp, writes=g("ysb"))
        f.op(A, lambda: nc.scalar.activation(out=x2[:], in_=ysb[:], func=AF.Square), reads=g("ysb"), writes=g("x2"))
        f.op(V, lambda: nc.vector.tensor_scalar(x2[:], x2[:], 0.044715, 1.0, ALU.mult, ALU.add), reads=g("x2"), writes=g("x2"))
        f.op(V, lambda: nc.vector.tensor_tensor(out=zz[:], in0=x2[:], in1=ysb[:], op=ALU.mult), reads=g("x2", "ysb"), writes=g("zz"))
        f.op(A, lambda: nc.scalar.activation(out=zz[:], in_=zz[:], func=AF.Sigmoid, scale=1.5957691216), reads=g("zz"), writes=g("zz"))
        f.op(V, lambda ygview=ygview, y3=y3, z3=z3: nc.vector.tensor_tensor(out=ygview, in0=y3, in1=z3, op=ALU.mult), reads=g("zz", "ysb"), writes=g("yg"))
    sgl = [mk("sgl", [128, 512]) for _ in range(2)]
    og = [mk("og", [128, 512], BF16) for _ in range(2)]
    Bs = [Buf(), Buf()]; Bo = [Buf(), Buf()]
    for t in range(NT // 512):
        cs = slice(t * 512, (t + 1) * 512)
        for oc in range(4):
            b = oc % 2
            for kc in range(4):
                f.op(P_, lambda kc=kc, oc=oc, b=b, cs=cs: nc.tensor.matmul(pgl[b][:], Wglu[:, kc, oc * 128:(oc + 1) * 128], yg[:, kc, cs],
                                                                        start=(kc == 0), stop=(kc == 3)),
                     reads=g("yg") + gp, writes=g("pgl%d" % b))
            f.op(A, lambda b=b: nc.scalar.activation(out=sgl[b][:], in_=pgl[b][:], func=AF.Sigmoid), reads=g("pgl%d" % b), writes=[Bs[b]])
            f.op(V, lambda b=b, oc=oc, cs=cs: nc.vector.tensor_tensor(out=og[b][:], in0=sgl[b][:], in1=yg[:, oc, cs], op=ALU.mult),
                 reads=[Bs[b]] + g("yg"), writes=[Bo[b]])
            f.dma(f.sp, scr["yT"][512 + oc * 128:512 + (oc + 1) * 128, cs], og[b][:], reads=[Bo[b]], writes=g("scr"))
    barrier(f)
    sc.close()


def ev_out_phase(f, X, wout_d, scr, NT):
    nc = f.nc
    sc = Scope(nc)
    mk = lambda n, shp, dt=F32: sc.sb(uname(n), shp, dt)
    Wo = mk("Wo", [128, KC, D], BF16)
    yt = mk("yt", [128, KC, TT], BF16)
    xt = mk("xt", [128, KC, TT])
    po = [sc.ps(uname("pout"), [128, TT]) for _ in range(2)]
    BW, By, Bx, Bs = Buf(), Buf(), Buf(), Buf()
    Bp = [Buf(), Buf()]
    wv = wout_d.rearrange("(kc p) d -> p kc d", p=128)
    f.dma(f.pool, Wo[:, 0:4, :], wv[:, 0:4, :], writes=[BW])
    f.dma(f.pool, Wo[:, 4:8, :], wv[:, 4:8, :], writes=[BW])
    Xv = X.rearrange("(c p) t -> p c t", p=128)
    yv = scr["yT"].rearrange("(c p) t -> p c t", p=128)
    for t in range(NT // TT):
        cs = slice(t * TT, (t + 1) * TT)
        f.dma(f.sp, yt[:], yv[:, :, cs], writes=[By])
        f.dma(f.sp, xt[:], Xv[:, :, cs], writes=[Bx])
        for dc in range(KC):
            b = dc % 2
            for kc in range(KC):
                f.op(f.pe, lambda dc=dc, kc=kc, b=b: nc.tensor.matmul(po[b][:], Wo[:, kc, dc * 128:(dc + 1) * 128], yt[:, kc, :],
                                                                      start=(kc == 0), stop=(kc == KC - 1)),
                     reads=[BW, By], writes=[Bp[b]])
            f.op(f.dve, lambda dc=dc, b=b: nc.vector.tensor_tensor(out=xt[:, dc, :], in0=po[b][:], in1=xt[:, dc, :], op=ALU.add),
                 reads=[Bp[b]], writes=[Bx])
        f.dma(f.sp, Xv[:, :, cs], xt[:], reads=[Bx], writes=[Bs])
    barrier(f)
    sc.close()


SEQ = 2048
NSEQ_CORE = 2
NCORES = 8
DEPTH = 4

_IN_SHAPES = {
    "ffn1_norm": [4, 1024], "ffn1_w_gate": [4, 1024, 2816], "ffn1_w_up": [4, 1024, 2816], "ffn1_w_down": [4, 2816, 1024],
    "mix_norm": [4, 1024], "ffn2_norm": [4, 1024], "ffn2_w_gate": [4, 1024, 2816], "ffn2_w_up": [4, 1024, 2816],
    "ffn2_w_down": [4, 2816, 1024], "ev_w_in": [2, 1024, 2560], "hg_lb_logits": [2, 512], "hg_norm_w": [2, 128],
    "s5_a_re": [2, 32, 64], "s5_a_im": [2, 32, 64], "s5_b_re": [2, 32, 64, 16], "s5_b_im": [2, 32, 64, 16],
    "s5_c_re": [2, 32, 16, 64], "s5_c_im": [2, 32, 16, 64], "s5_d": [2, 512], "s5_log_dt": [2, 32], "s5_w_glu": [2, 512, 512],
    "ev_w_out": [2, 1024, 1024], "od_w_in": [2, 1024, 4112], "gdn_conv_w": [2, 4, 3072], "gdn_a_log": [2, 8], "gdn_dt_bias": [2, 8],
    "gdn_norm_w": [2, 128], "od_w_out": [2, 1024, 1024], "final_norm": [1024],
}


def build_program(L=SEQ, nseq=NSEQ_CORE, depth=DEPTH):
    NT = L * nseq
    f = FW()
    nc = f.nc
    I = {k: nc.dram_tensor(k, list(shp), F32, kind="ExternalInput").ap() for k, shp in _IN_SHAPES.items()}
    xT = nc.dram_tensor("xT", [D, NT], F32, kind="ExternalInput").ap()
    oT = nc.dram_tensor("oT", [D, NT], F32, kind="ExternalOutput").ap()
    X = nc.dram_tensor("Xres", [D, NT], F32).ap()
    dt_ = lambda n, shp, t: nc.dram_tensor(n, shp, t).ap()
    scr = {
        "qT": dt_("s_qT", [D, NT], BF16), "kT": dt_("s_kT", [D, NT], BF16), "gT": dt_("s_gT", [D, NT], BF16),
        "ktok": dt_("s_ktok", [NT, D], BF16), "vtok": dt_("s_vtok", [NT, D], BF16), "bl": dt_("s_bl", [NT, 16], F32),
        "uT": dt_("s_uT", [512, NT], BF16), "lfT": dt_("s_lfT", [512, NT], F32), "yT": dt_("s_yT", [D, NT], BF16),
    }
    src = xT
    for layer in range(depth):
        j = layer // 2
        ffn_phase(f, src, X, I["ffn1_norm"][layer], I["ffn1_w_gate"][layer], I["ffn1_w_up"][layer], I["ffn1_w_down"][layer], NT)
        src = X
        if layer % 2 == 0:
            ev_proj_phase(f, X, I["mix_norm"][layer], I["ev_w_in"][j], I["hg_lb_logits"], j, scr, NT, L)
            hgrn_core_phase(f, I["hg_norm_w"][j], scr, NT, L)
            p = {"a_re": I["s5_a_re"][j], "a_im": I["s5_a_im"][j], "b_re": I["s5_b_re"][j], "b_im": I["s5_b_im"][j],
                 "c_re": I["s5_c_re"][j], "c_im": I["s5_c_im"][j], "d": I["s5_d"][j], "log_dt": I["s5_log_dt"][j], "w_glu": I["s5_w_glu"][j]}
            s5_phase(f, p, j, scr, NT, L)
            ev_out_phase(f, X, I["ev_w_out"][j], scr, NT)
        else:
            gdn_proj_phase(f, X, I["mix_norm"][layer], I["od_w_in"][j], I["gdn_conv_w"][j], I["gdn_a_log"][j], I["gdn_dt_bias"][j], scr, NT, L)
            gdn_core_phase(f, X, I["gdn_norm_w"][j], I["od_w_out"][j], scr, NT, L)
            ev_out_phase(f, X, I["od_w_out"][j], scr, NT)
        ffn_phase(f, X, X, I["ffn2_norm"][layer], I["ffn2_w_gate"][layer], I["ffn2_w_up"][layer], I["ffn2_w_down"][layer], NT)
    Bo = final_phase(f, src, oT, I["final_norm"], NT)
    f.finish([Bo])
    return f


def kernel(**inputs):
    x = np.asarray(inputs["x"], dtype=np.float32)
    Bsz, L, Dm = x.shape
    nseq = Bsz // NCORES
    f = build_program(L, nseq, DEPTH)
    shared = {k: np.ascontiguousarray(np.asarray(inputs[k], dtype=np.float32)) for k in _IN_SHAPES}
    in_maps = []
    for c in range(NCORES):
        m = dict(shared)
        m["xT"] = np.ascontiguousarray(x[c * nseq:(c + 1) * nseq].reshape(nseq * L, Dm).T)
        in_maps.append(m)
    res = run_bass_kernel_spmd(f.nc, in_maps, core_ids=list(range(NCORES)))
    out = np.empty((Bsz, L, Dm), dtype=np.float32)
    for c in range(NCORES):
        oT = np.asarray(res.results[c]["oT"])
        out[c * nseq:(c + 1) * nseq] = oT.T.reshape(nseq, L, Dm)
    return out
```
